# Optimizing a Trainium2 kernel written in Bass

```python
import jax, jax.numpy as jnp
from jax import lax
import numpy as np


D_MODEL = 1024
BATCH = 16
SEQ = 2048
DEPTH = 2

N_MIXERS = 2
POOL_EXPAND = 2
D_POOL = POOL_EXPAND * D_MODEL
POOL_WINDOWS = (2, 4, 8, 16)
N_POOL_GROUPS = len(POOL_WINDOWS)
POOL_GROUP = D_POOL // N_POOL_GROUPS
HEAD_DIM = 64
N_HEADS = D_MODEL // HEAD_DIM
N_KV = 4
HPG = N_HEADS // N_KV
D_ATT = N_HEADS * HEAD_DIM
D_KV = N_KV * HEAD_DIM
N_BRANCH = 3
CMP_BLOCK = 32
CMP_STRIDE = 16
CMP_HIDDEN = 4 * HEAD_DIM
SEL_BLOCK = 64
SEL_TOPN = 8
WINDOW = 256
Q_BLOCK = 64
NSA_SPLITS = (D_ATT, D_KV, D_KV, D_KV, D_KV, D_KV, D_KV, D_ATT, N_BRANCH * N_HEADS)
NSA_VALUE_SLOTS = (2, 4, 6)
D_NSA_IN = sum(NSA_SPLITS)

N_POOL_LAYERS = (DEPTH + N_MIXERS - 1) // N_MIXERS
N_NSA_LAYERS = DEPTH // N_MIXERS

DN_ALPHA = (2.0 * DEPTH) ** 0.25
DN_BETA = (8.0 * DEPTH) ** -0.25
LN_EPS = 1e-5
NEG_INF = -1e30
FORCE_SCORE = 1e9

kernel_name = 'hybrid_pool_nsa_deepnorm'


def layer_norm(x, g, b):
    xf = x.astype(jnp.float32)
    mu = jnp.mean(xf, axis=-1, keepdims=True)
    var = jnp.mean(jnp.square(xf - mu), axis=-1, keepdims=True)
    y = (xf - mu) * lax.rsqrt(var + LN_EPS) * g.astype(jnp.float32) + b.astype(jnp.float32)
    return y.astype(x.dtype)


def alibi_slopes():
    h = jnp.arange(1, N_HEADS + 1, dtype=jnp.float32)
    return (2.0 ** (-8.0 * h / N_HEADS)).reshape(N_KV, HPG)


def masked_softmax(s, valid):
    p = jax.nn.softmax(jnp.where(valid, s, NEG_INF), axis=-1)
    return jnp.where(valid, p, 0.0)


def pool_mixer(x, w_in, w_grp, scale, w_out):
    B, S, _ = x.shape
    h = x @ w_in
    u, z = h[..., :D_POOL], h[..., D_POOL:]
    u = u.astype(jnp.float32).reshape(B, S, N_POOL_GROUPS, POOL_GROUP)
    csum = jnp.cumsum(u, axis=1)
    t = jnp.arange(S)
    outs = []
    for g, w in enumerate(POOL_WINDOWS):
        c = csum[:, :, g]
        lag = jnp.pad(c, ((0, 0), (w, 0), (0, 0)))[:, :S]
        cnt = jnp.minimum(t + 1, w).astype(jnp.float32)[None, :, None]
        outs.append((c - lag) / cnt - u[:, :, g])
    m = jnp.stack(outs, axis=2).astype(x.dtype)
    m = jnp.einsum('bsgc,gcd->bsgd', m, w_grp).reshape(B, S, D_POOL) * scale
    return (m * jax.nn.silu(z)) @ w_out


def nsa_mixer(x, w_in, pos_k, w1_k, w2_k, pos_v, w1_v, w2_v, w_out):
    B, S, _ = x.shape
    dt = x.dtype
    h = x @ w_in
    q, kc, vc, ks, vs, kw, vw, z, gl = jnp.split(h, np.cumsum(NSA_SPLITS)[:-1].tolist(), axis=-1)
    q = q.reshape(B, S, N_KV, HPG, HEAD_DIM) * (HEAD_DIM ** -0.5)
    kc, vc, ks, vs, kw, vw = [a.reshape(B, S, N_KV, HEAD_DIM) for a in (kc, vc, ks, vs, kw, vw)]
    gates = jax.nn.sigmoid(gl.astype(jnp.float32)).reshape(B, S, N_KV, HPG, N_BRANCH)
    slopes = alibi_slopes()[None, :, :, None, None]

    n_cmp = (S - CMP_BLOCK) // CMP_STRIDE + 1
    cmp_idx = np.arange(n_cmp)[:, None] * CMP_STRIDE + np.arange(CMP_BLOCK)[None, :]
    cmp_end = jnp.asarray(cmp_idx[:, -1], dtype=jnp.int32)

    def compress(a, pos, w1, w2):
        blk = a[:, cmp_idx] + pos[None, None, :, None, :]
        blk = blk.transpose(0, 1, 3, 2, 4).reshape(B, n_cmp, N_KV, CMP_BLOCK * HEAD_DIM)
        return jax.nn.silu(blk @ w1) @ w2

    k_cmp = compress(kc, pos_k, w1_k, w2_k)
    v_cmp = compress(vc, pos_v, w1_v, w2_v)

    n_sel = S // SEL_BLOCK
    top_n = min(SEL_TOPN, n_sel)
    c0 = np.arange(n_cmp)[:, None] * CMP_STRIDE
    j0 = np.arange(n_sel)[None, :] * SEL_BLOCK
    overlap = jnp.asarray((c0 < j0 + SEL_BLOCK) & (c0 + CMP_BLOCK > j0), dtype=jnp.float32)
    ks_blk = ks.reshape(B, n_sel, SEL_BLOCK, N_KV, HEAD_DIM).transpose(0, 3, 1, 2, 4)
    vs_blk = vs.reshape(B, n_sel, SEL_BLOCK, N_KV, HEAD_DIM).transpose(0, 3, 1, 2, 4)
    b_ix = jnp.arange(B)[:, None, None, None]
    g_ix = jnp.arange(N_KV)[None, :, None, None]
    blk_ids = jnp.arange(n_sel)
    in_blk = jnp.arange(SEL_BLOCK)

    kw_pad = jnp.pad(kw, ((0, 0), (WINDOW, 0), (0, 0), (0, 0)))
    vw_pad = jnp.pad(vw, ((0, 0), (WINDOW, 0), (0, 0), (0, 0)))
    win_off = jnp.arange(WINDOW + Q_BLOCK) - WINDOW

    def block(i):
        q0 = i * Q_BLOCK
        qb = lax.dynamic_slice_in_dim(q, q0, Q_BLOCK, axis=1)
        gb = lax.dynamic_slice_in_dim(gates, q0, Q_BLOCK, axis=1)
        t = q0 + jnp.arange(Q_BLOCK)

        dist = t[:, None] - cmp_end[None, :]
        s = jnp.einsum('bqghd,bcgd->bghqc', qb, k_cmp).astype(jnp.float32)
        s = s - slopes * dist.astype(jnp.float32)
        p_cmp = masked_softmax(s, dist >= 0)
        o_cmp = jnp.einsum('bghqc,bcgd->bqghd', p_cmp.astype(dt), v_cmp)

        imp = jnp.einsum('bghqc,cn->bgqn', p_cmp, overlap)
        cur = t // SEL_BLOCK
        forced = (blk_ids[None] == 0) | (blk_ids[None] == cur[:, None]) | (blk_ids[None] == cur[:, None] - 1)
        future = blk_ids[None] > cur[:, None]
        imp = jnp.where(forced, FORCE_SCORE, jnp.where(future, NEG_INF, imp))
        _, sel = lax.top_k(imp, top_n)
        k_g = ks_blk[b_ix, g_ix, sel].reshape(B, N_KV, Q_BLOCK, top_n * SEL_BLOCK, HEAD_DIM)
        v_g = vs_blk[b_ix, g_ix, sel].reshape(B, N_KV, Q_BLOCK, top_n * SEL_BLOCK, HEAD_DIM)
        pos = (sel[..., None] * SEL_BLOCK + in_blk).reshape(B, N_KV, Q_BLOCK, top_n * SEL_BLOCK)
        dist = t[None, None, :, None] - pos
        s = jnp.einsum('bqghd,bgqkd->bghqk', qb, k_g).astype(jnp.float32)
        s = s - slopes * dist[:, :, None].astype(jnp.float32)
        p = masked_softmax(s, (dist >= 0)[:, :, None])
        o_sel = jnp.einsum('bghqk,bgqkd->bqghd', p.astype(dt), v_g)

        kwb = lax.dynamic_slice_in_dim(kw_pad, q0, WINDOW + Q_BLOCK, axis=1)
        vwb = lax.dynamic_slice_in_dim(vw_pad, q0, WINDOW + Q_BLOCK, axis=1)
        spos = q0 + win_off
        dist = t[:, None] - spos[None, :]
        valid = (dist >= 0) & (dist < WINDOW) & (spos[None, :] >= 0)
        s = jnp.einsum('bqghd,bkgd->bghqk', qb, kwb).astype(jnp.float32)
        s = s - slopes * dist.astype(jnp.float32)
        p = masked_softmax(s, valid)
        o_win = jnp.einsum('bghqk,bkgd->bqghd', p.astype(dt), vwb)

        o = gb[..., 0:1] * o_cmp + gb[..., 1:2] * o_sel + gb[..., 2:3] * o_win
        return o.astype(dt).reshape(B, Q_BLOCK, D_ATT)

    o = lax.map(block, jnp.arange(S // Q_BLOCK))
    o = o.transpose(1, 0, 2, 3).reshape(B, S, D_ATT)
    return (o * jax.nn.silu(z)) @ w_out


def setup_inputs(seed: int = 0) -> dict:
    key = jax.random.key(seed)
    k = jax.random.split(key, 16)
    nA, nB = N_POOL_LAYERS, N_NSA_LAYERS

    def nrm(kk, shape, scale):
        return jax.random.normal(kk, shape, jnp.float32) * scale

    pool_col = jnp.asarray(np.concatenate([np.full(D_POOL, DN_BETA), np.ones(D_POOL)]), dtype=jnp.float32)
    nsa_col = jnp.asarray(np.concatenate([np.full(n, DN_BETA if s in NSA_VALUE_SLOTS else 1.0)
                                          for s, n in enumerate(NSA_SPLITS)]), dtype=jnp.float32)
    return {
        'x': nrm(k[0], (BATCH, SEQ, D_MODEL), 1.0),
        'ln_g': 1.0 + nrm(k[1], (DEPTH, D_MODEL), 0.02),
        'ln_b': nrm(k[2], (DEPTH, D_MODEL), 0.02),
        'pool_w_in': nrm(k[3], (nA, D_MODEL, 2 * D_POOL), D_MODEL ** -0.5) * pool_col,
        'pool_w_grp': nrm(k[4], (nA, N_POOL_GROUPS, POOL_GROUP, POOL_GROUP), POOL_GROUP ** -0.5),
        'pool_scale': 1.0 + nrm(k[5], (nA, D_POOL), 0.02),
        'pool_w_out': nrm(k[6], (nA, D_POOL, D_MODEL), D_POOL ** -0.5 * DN_BETA),
        'nsa_w_in': nrm(k[7], (nB, D_MODEL, D_NSA_IN), D_MODEL ** -0.5) * nsa_col,
        'nsa_cmp_pos_k': nrm(k[8], (nB, CMP_BLOCK, HEAD_DIM), 0.02),
        'nsa_cmp_w1_k': nrm(k[9], (nB, CMP_BLOCK * HEAD_DIM, CMP_HIDDEN), (CMP_BLOCK * HEAD_DIM) ** -0.5),
        'nsa_cmp_w2_k': nrm(k[10], (nB, CMP_HIDDEN, HEAD_DIM), CMP_HIDDEN ** -0.5),
        'nsa_cmp_pos_v': nrm(k[11], (nB, CMP_BLOCK, HEAD_DIM), 0.02),
        'nsa_cmp_w1_v': nrm(k[12], (nB, CMP_BLOCK * HEAD_DIM, CMP_HIDDEN), (CMP_BLOCK * HEAD_DIM) ** -0.5),
        'nsa_cmp_w2_v': nrm(k[13], (nB, CMP_HIDDEN, HEAD_DIM), CMP_HIDDEN ** -0.5),
        'nsa_w_out': nrm(k[14], (nB, D_ATT, D_MODEL), D_ATT ** -0.5 * DN_BETA),
    }


def reference(x, ln_g, ln_b, pool_w_in, pool_w_grp, pool_scale, pool_w_out,
              nsa_w_in, nsa_cmp_pos_k, nsa_cmp_w1_k, nsa_cmp_w2_k,
              nsa_cmp_pos_v, nsa_cmp_w1_v, nsa_cmp_w2_v, nsa_w_out):
    for i in range(DEPTH):
        j = i // N_MIXERS
        if i % N_MIXERS == 0:
            y = pool_mixer(x, pool_w_in[j], pool_w_grp[j], pool_scale[j], pool_w_out[j])
        else:
            y = nsa_mixer(x, nsa_w_in[j], nsa_cmp_pos_k[j], nsa_cmp_w1_k[j], nsa_cmp_w2_k[j],
                          nsa_cmp_pos_v[j], nsa_cmp_w1_v[j], nsa_cmp_w2_v[j], nsa_w_out[j])
        x = layer_norm(DN_ALPHA * x + y, ln_g[i], ln_b[i])
    return x
```

```python
import contextlib
import numpy as np
import ml_dtypes
import concourse.bass as bass
import concourse.mybir as mybir
from concourse.bass_utils import run_bass_kernel_spmd

F32 = mybir.dt.float32
BF16 = mybir.dt.bfloat16
AF = mybir.ActivationFunctionType
ALU = mybir.AluOpType

D = 1024
S = 2048
NSEQ = 2
NCORES = 8
DN_ALPHA = float((2.0 * 2) ** 0.25)
LN_EPS = 1e-5
POOL_WINDOWS = (2, 4, 8, 16)
NEGM = -30000.0


class _Op:
    __slots__ = ("eng", "fn", "deps", "is_dma", "key", "awaited", "count", "idx")


class Prog:
    ENGS = ("pe", "act", "dve", "pool", "sp")

    def __init__(self, nc):
        self.nc = nc
        self.ops = {e: [] for e in self.ENGS}
        self.last_w = {}
        self.readers = {}
        self.dma_counts = {}
        self.last_dma = {}
        self.bar = {e: [] for e in self.ENGS}

    def _dep(self, op, a):
        if a is None or a is op:
            return
        if (not a.is_dma) and a.eng == op.eng and a.eng == "pe":
            return
        if a.is_dma and op.is_dma and a.key == op.key:
            return
        op.deps.append(a)
        if not a.is_dma:
            a.awaited = True

    def add(self, eng, fn, reads=(), writes=(), dma_key=None):
        op = _Op()
        op.eng = eng
        op.fn = fn
        op.deps = []
        op.is_dma = dma_key is not None
        op.key = dma_key
        op.awaited = False
        op.count = None
        for a in self.bar[eng]:
            self._dep(op, a)
        self.bar[eng] = []
        if eng != "pe":
            extra = [("psx", r[1]) for r in reads if isinstance(r, tuple) and r[0] == "ps"]
            extra += [("psx", w[1]) for w in writes if isinstance(w, tuple) and w[0] == "ps"]
            writes = list(writes) + extra
        for r in reads:
            self._dep(op, self.last_w.get(r))
        for w in writes:
            self._dep(op, self.last_w.get(w))
            for a in self.readers.get(w, ()):
                self._dep(op, a)
        for r in reads:
            self.readers.setdefault(r, []).append(op)
        for w in writes:
            self.last_w[w] = op
            self.readers[w] = []
        if op.is_dma:
            c = self.dma_counts.get(dma_key, 0) + 1
            self.dma_counts[dma_key] = c
            op.count = c
            self.last_dma[dma_key] = op
        op.idx = len(self.ops[eng])
        self.ops[eng].append(op)
        return op

    def barrier(self):
        deps = []
        for e in self.ENGS:
            for op in reversed(self.ops[e]):
                if not op.is_dma:
                    deps.append(op)
                    break
        deps.extend(self.last_dma.values())
        for e in self.ENGS:
            self.bar[e] = list(deps)

    def emit(self, stack):
        nc = self.nc
        esem = {e: stack.enter_context(nc.semaphore("s_" + e)) for e in self.ENGS}
        dsem = {k: stack.enter_context(nc.semaphore("d_%d" % i)) for i, k in enumerate(self.dma_counts)}
        for e in self.ENGS:
            c = 0
            for op in self.ops[e]:
                if (not op.is_dma) and op.awaited:
                    c += 1
                    op.count = c
        block = stack.enter_context(nc.Block())
        final = [(dsem[k], 16 * c) for k, c in self.dma_counts.items()]

        def run(ename, eh, is_last=False):
            waited = {}
            for op in self.ops[ename]:
                for a in op.deps:
                    if a.is_dma:
                        sem, val = dsem[a.key], 16 * a.count
                    else:
                        sem, val = esem[a.eng], a.count
                    sid = id(sem)
                    if waited.get(sid, 0) >= val:
                        continue
                    waited[sid] = val
                    eh.wait_ge(sem, val)
                ins = op.fn(eh)
                if op.is_dma:
                    ins.then_inc(dsem[op.key], 16)
                elif op.awaited:
                    ins.then_inc(esem[ename], 1)
            if is_last:
                for sem, val in final:
                    eh.wait_ge(sem, val)

        @block.tensor
        def _(eh):
            run("pe", eh)

        @block.scalar
        def _(eh):
            run("act", eh)

        @block.vector
        def _(eh):
            run("dve", eh)

        @block.gpsimd
        def _(eh):
            run("pool", eh)

        @block.sync
        def _(eh):
            run("sp", eh, is_last=True)


class Arena:
    def __init__(self, ap, ncols):
        self.ap = ap
        self.n = ncols
        self.off = 0

    def reset(self):
        self.off = 0

    def _shape(self, v, shape):
        if len(shape) == 2:
            v = v.rearrange("p (a b) -> p a b", a=shape[0])
        elif len(shape) == 3:
            v = v.rearrange("p (a b c) -> p a b c", a=shape[0], b=shape[1])
        return v

    def f(self, *shape):
        cols = int(np.prod(shape))
        assert self.off + cols <= self.n, ("arena overflow", self.off, cols, self.n)
        v = self.ap[:, self.off:self.off + cols]
        self.off += cols
        return self._shape(v, shape)

    def b(self, *shape):
        cols = int(np.prod(shape))
        c32 = (cols + 1) // 2
        assert self.off + c32 <= self.n, ("arena overflow", self.off, c32, self.n)
        v = self.ap[:, self.off:self.off + c32].bitcast(BF16)[:, 0:cols]
        self.off += c32
        return self._shape(v, shape)


def _bf(a):
    return np.ascontiguousarray(np.asarray(a, dtype=np.float32).astype(ml_dtypes.bfloat16))


def _pool_tables():
    A = np.zeros((4, 3, 128, 128), np.float32)
    inv = np.zeros((4, 128), np.float32)
    for g, w in enumerate(POOL_WINDOWS):
        for t in range(128):
            for tp in range(t - w + 1, t + 1):
                if tp >= 0:
                    A[g, 0, tp, t] += 1.0 / w
                else:
                    A[g, 1, tp + 128, t] += 1.0 / w
            A[g, 0, t, t] -= 1.0
            cnt = min(t + 1, w)
            for tp in range(max(0, t - w + 1), t + 1):
                A[g, 2, tp, t] += 1.0
            A[g, 2, t, t] -= cnt
            inv[g, t] = 1.0 / cnt
    At = np.transpose(A, (2, 0, 1, 3)).reshape(128, 4 * 3 * 128)
    invb = np.broadcast_to(inv.reshape(1, 4 * 128), (128, 4 * 128))
    return _bf(At), np.ascontiguousarray(invb, dtype=np.float32)


def layer_norm_tile(P, r_ap, rkey, lng, lnb, stat, skey, epsc):
    st6 = stat[:, 0:12].rearrange("p (a b) -> p a b", a=2)
    mv = stat[:, 12:14]
    P.add("dve", lambda e: e.bn_stats(out=st6[:, 0, :], in_=r_ap[:, 0:512]), reads=[rkey], writes=[skey])
    P.add("dve", lambda e: e.bn_stats(out=st6[:, 1, :], in_=r_ap[:, 512:1024]), reads=[rkey], writes=[skey])
    P.add("dve", lambda e: e.bn_aggr(out=mv, in_=stat[:, 0:12]), reads=[skey], writes=[skey])
    P.add("act", lambda e: e.activation(out=stat[:, 14:15], in_=stat[:, 13:14], func=AF.Sqrt,
                                        bias=epsc, scale=1.0), reads=[skey, "consts"], writes=[skey])
    P.add("dve", lambda e: e.reciprocal(out=stat[:, 14:15], in_=stat[:, 14:15]), reads=[skey], writes=[skey])
    P.add("dve", lambda e: e.scalar_tensor_tensor(out=stat[:, 15:16], in0=stat[:, 12:13], scalar=-1.0,
                                                  in1=stat[:, 14:15], op0=ALU.mult, op1=ALU.mult),
          reads=[skey], writes=[skey])
    P.add("act", lambda e: e.activation(out=r_ap, in_=r_ap, func=AF.Identity, bias=stat[:, 15:16],
                                        scale=stat[:, 14:15]), reads=[skey, rkey], writes=[rkey])
    P.add("pool", lambda e: e.tensor_tensor(out=r_ap, in0=r_ap, in1=lng, op=ALU.mult),
          reads=[rkey, "lnc"], writes=[rkey])
    P.add("pool", lambda e: e.tensor_tensor(out=r_ap, in0=r_ap, in1=lnb, op=ALU.add),
          reads=[rkey, "lnc"], writes=[rkey])


def _slopes():
    h = np.arange(1, 17, dtype=np.float32)
    return (2.0 ** (-8.0 * h / 16.0)).astype(np.float32)


def _l1_tables():
    t = {}
    sl = _slopes()
    tq = np.arange(S, dtype=np.float64)
    qal = np.zeros((16, 7, S), np.float32)
    for h in range(16):
        s0 = float(sl[h])
        s1 = float(np.float32(s0).astype(ml_dtypes.bfloat16))
        s2 = float(np.float32(s0 - s1).astype(ml_dtypes.bfloat16))
        s3 = float(np.float32(s0 - s1 - s2).astype(ml_dtypes.bfloat16))
        qal[h, 0] = -s0 * tq
        for i, si in enumerate((s1, s2, s3)):
            qal[h, 1 + i] = 64.0 * si
            qal[h, 4 + i] = si
    t["qalibi"] = _bf(qal)
    kal = np.zeros((7, S), np.float32)
    kal[0] = 1.0
    kal[1:4] = (np.arange(S) // 64)[None, :]
    kal[4:7] = (np.arange(S) % 64)[None, :]
    t["kalibi"] = _bf(kal)
    kc = np.zeros((7, 128), np.float32)
    ce = np.arange(127) * 16 + 31
    kc[0] = 1.0
    kc[1:4, :127] = (ce // 64)[None, :]
    kc[4:7, :127] = (ce % 64)[None, :]
    t["kalibic"] = _bf(kc)
    E = np.zeros((32, S), np.float32)
    E[np.arange(S) // 64, np.arange(S)] = 1.0
    t["eoh"] = _bf(E)
    cm = np.full((128, S), NEGM, np.float32)
    cm[:127] = np.where(np.arange(S)[None, :] >= ce[:, None], 0.0, NEGM)
    t["cmpmask"] = _bf(cm)
    dk = np.arange(128)[:, None]
    dq = np.arange(384)[None, :]
    t["winmask"] = _bf(np.where((dq - dk >= 0) & (dq - dk < 256), 0.0, NEGM))
    dq = np.arange(128)[None, :]
    t["trimask"] = _bf(np.where(dk <= dq, 0.0, NEGM))
    vcc = np.zeros((128, 33), np.float32)
    vcc[:, 0] = 1.0
    c0 = np.arange(127)[:, None] * 16
    j0 = np.arange(32)[None, :] * 64
    vcc[:127, 1:] = ((c0 < j0 + 64) & (c0 + 32 > j0)).astype(np.float32)
    t["vcc"] = _bf(vcc)
    q = np.arange(128)[:, None, None]
    qt = np.arange(16)[None, :, None]
    j = np.arange(32)[None, None, :]
    cur = (qt * 128 + q) // 64
    forced = (j == 0) | (j == cur) | (j == cur - 1)
    t["forced"] = np.ascontiguousarray(np.where(forced, 1e9, 0.0).astype(np.float32).reshape(128, 512))
    t["future"] = np.ascontiguousarray(np.where(j > cur, -1e30, 3e38).astype(np.float32).reshape(128, 512))
    return t


def _l1_dram(nc, din):
    L = {}
    L["wg1"] = din("wg1", [4, 128, 8, 640])
    L["wg2"] = din("wg2", [4, 128, 8, 268])
    L["w1k"] = din("w1k", [64, 32, 256])
    L["w1v"] = din("w1v", [64, 32, 256])
    L["w2k"] = din("w2k", [128, 2, 64])
    L["w2v"] = din("w2v", [128, 2, 64])
    L["posk"] = din("posk", [64, 32])
    L["posv"] = din("posv", [64, 32])
    L["wout1"] = din("wout1", [128, 8, 1024])
    L["qalibi"] = din("qalibi", [16, 7, S], BF16)
    L["kalibi"] = din("kalibi", [7, S], BF16)
    L["kalibic"] = din("kalibic", [7, 128], BF16)
    L["eoh"] = din("eoh", [32, S], BF16)
    L["cmpmask"] = din("cmpmask", [128, S], BF16)
    L["winmask"] = din("winmask", [128, 384], BF16)
    L["trimask"] = din("trimask", [128, 128], BF16)
    L["vcc"] = din("vcc", [128, 33], BF16)
    L["forced"] = din("forced", [128, 512])
    L["future"] = din("future", [128, 512])
    return L


def _host_l1(inputs):
    f = lambda a: np.ascontiguousarray(np.asarray(a, dtype=np.float32))
    m = {}
    W = f(inputs["nsa_w_in"])[0].reshape(8, 128, 3632).transpose(1, 0, 2)
    wg1 = np.zeros((4, 128, 8, 640), np.float32)
    wg2 = np.zeros((4, 128, 8, 268), np.float32)
    for g in range(4):
        wg1[g, :, :, 0:256] = W[:, :, 256 * g:256 * g + 256]
        wg1[g, :, :, 256:320] = W[:, :, 1024 + 64 * g:1024 + 64 * g + 64]
        wg1[g, :, :, 320:384] = W[:, :, 1280 + 64 * g:1280 + 64 * g + 64]
        wg1[g, :, :, 384:448] = W[:, :, 1536 + 64 * g:1536 + 64 * g + 64]
        wg1[g, :, :, 448:512] = W[:, :, 2048 + 64 * g:2048 + 64 * g + 64]
        wg1[g, :, :, 512:576] = W[:, :, 1792 + 64 * g:1792 + 64 * g + 64]
        wg1[g, :, :, 576:640] = W[:, :, 2304 + 64 * g:2304 + 64 * g + 64]
        wg2[g, :, :, 0:256] = W[:, :, 2560 + 256 * g:2560 + 256 * g + 256]
        wg2[g, :, :, 256:268] = W[:, :, 3584 + 12 * g:3584 + 12 * g + 12]
    m["wg1"] = wg1
    m["wg2"] = wg2
    m["w1k"] = np.ascontiguousarray(f(inputs["nsa_cmp_w1_k"])[0].reshape(32, 64, 256).transpose(1, 0, 2))
    m["w1v"] = np.ascontiguousarray(f(inputs["nsa_cmp_w1_v"])[0].reshape(32, 64, 256).transpose(1, 0, 2))
    m["w2k"] = np.ascontiguousarray(f(inputs["nsa_cmp_w2_k"])[0].reshape(2, 128, 64).transpose(1, 0, 2))
    m["w2v"] = np.ascontiguousarray(f(inputs["nsa_cmp_w2_v"])[0].reshape(2, 128, 64).transpose(1, 0, 2))
    m["posk"] = np.ascontiguousarray(f(inputs["nsa_cmp_pos_k"])[0].T)
    m["posv"] = np.ascontiguousarray(f(inputs["nsa_cmp_pos_v"])[0].T)
    m["wout1"] = np.ascontiguousarray(f(inputs["nsa_w_out"])[0].reshape(8, 128, 1024).transpose(1, 0, 2))
    m.update(_l1_tables())
    return m


L1_STAGE = 99


def _build_l1(nc, P, AR, psum, L, x1_d, out_d, lng_d, lnb_d, identf_d, identb_d):
    STG = L1_STAGE
    def PS(i):
        return ("ps", i)

    def MM(out, lhsT, rhs, start, stop, reads, writes):
        P.add("pe", lambda e: e.matmul(out, lhsT=lhsT, rhs=rhs, start=start, stop=stop), reads=reads, writes=writes)

    def ACT(out, in_, func, reads, writes, **kw):
        P.add("act", lambda e: e.activation(out=out, in_=in_, func=func, **kw), reads=reads, writes=writes)

    def TT(eng, out, in0, in1, op, reads, writes):
        P.add(eng, lambda e: e.tensor_tensor(out=out, in0=in0, in1=in1, op=op), reads=reads, writes=writes)

    def TS(out, in0, s1, s2, op0, op1, reads, writes):
        if op1 is None:
            P.add("dve", lambda e: e.tensor_scalar(out=out, in0=in0, scalar1=s1, scalar2=None, op0=op0),
                  reads=reads, writes=writes)
        else:
            P.add("dve", lambda e: e.tensor_scalar(out=out, in0=in0, scalar1=s1, scalar2=s2, op0=op0, op1=op1),
                  reads=reads, writes=writes)

    def DMA(q, out, in_, reads, writes, key):
        P.add(q, lambda e: e.dma_start(out=out, in_=in_), reads=reads, writes=writes, dma_key=key)

    W1 = AR.b(8, 640)
    W2 = [AR.b(8, 268) for _ in range(2)]
    w1 = AR.b(32, 256)
    w2 = [AR.b(2, 64) for _ in range(2)]
    pos = AR.b(32)
    wout1 = AR.b(8, 1024)
    identb = AR.b(128)
    x1T = AR.b(8, S)
    ogT = AR.b(8, S)
    q_aug = [AR.b(S) for _ in range(4)]
    ksel = AR.b(S)
    kwin = AR.b(S)
    kcmp = AR.b(128)
    pair = AR.b(S)
    vsel = AR.b(16, 65)
    vwin = AR.b(16, 65)
    vcmp = AR.b(97)
    hs = [AR.b(2, 128) for _ in range(2)]
    cmpmask = AR.b(S)
    winmask = AR.b(384)
    trimask = AR.b(128)
    Pb = [AR.b(512) for _ in range(3)]
    selm_w = AR.b(4, 128)
    ogb = [AR.b(256) for _ in range(2)]
    identf = AR.f(128)
    lng1 = AR.f(1024)
    lnb1 = AR.f(1024)
    xr = [AR.f(1024) for _ in range(2)]
    stat = [AR.f(16) for _ in range(2)]
    o_tm = [AR.f(4, 256) for _ in range(2)]
    gates = AR.f(16, 12)
    imp = AR.f(4, 32)
    impm = AR.f(4, 32)
    forced = AR.f(16, 32)
    future = AR.f(16, 32)
    m8 = AR.f(4, 8)
    thr = AR.f(4)
    rden = [AR.f(4) for _ in range(2)]
    fcol = [AR.f(4) for _ in range(2)]
    tmp_o = [AR.f(4, 64) for _ in range(2)]
    tmp_i = AR.f(4, 32)
    sz = [AR.f(256) for _ in range(2)]
    bh = AR.f(4)
    epsc = AR.f(1)

    for t_, k_ in ((ksel, "ksel_c"), (kwin, "kwin_c"), (kcmp, "kcmp_c"), (vcmp, "vcmp_c"),
                   (selm_w.rearrange("p a b -> p (a b)"), "selm")):
        P.add("pool", lambda e, t_=t_: e.memset(t_, 0.0), writes=[k_])
    for hl in range(4):
        P.add("pool", lambda e, hl=hl: e.memset(q_aug[hl], 0.0), writes=[("qa_q", hl), ("qa_s", hl), ("qa_a", hl)])
    P.add("pool", lambda e: e.memset(vsel[:, :, 64:65], 1.0), writes=["vsel_c"])
    P.add("pool", lambda e: e.memset(vwin[:, :, 64:65], 1.0), writes=["vwin_c"])
    P.add("dve", lambda e: e.memset(epsc, LN_EPS), writes=["consts"])
    DMA("sp", identb, identb_d, [], ["consts"], "c1")
    DMA("sp", identf, identf_d, [], ["consts"], "c1")
    DMA("sp", lng1, lng_d[1], [], ["lnc"], "c1")
    DMA("sp", lnb1, lnb_d[1], [], ["lnc"], "c1")
    DMA("sp", cmpmask, L["cmpmask"], [], ["consts"], "c1")
    DMA("sp", winmask, L["winmask"], [], ["consts"], "c1")
    DMA("sp", trimask, L["trimask"], [], ["consts"], "c1")
    DMA("sp", forced.rearrange("p a b -> p (a b)"), L["forced"], [], ["consts"], "c1")
    DMA("sp", future.rearrange("p a b -> p (a b)"), L["future"], [], ["consts"], "c1")
    DMA("sp", ksel[64:96, :], L["eoh"], [], ["ksel_c"], "c2")
    DMA("sp", ksel[96:103, :], L["kalibi"], [], ["ksel_c"], "c2")
    DMA("sp", kwin[96:103, :], L["kalibi"], [], ["kwin_c"], "c2")
    DMA("sp", kcmp[96:103, :], L["kalibic"], [], ["kcmp_c"], "c2")
    DMA("sp", vcmp[:, 64:97], L["vcc"], [], ["vcmp_c"], "c2")
    for kv, (wn, w2n, pn) in enumerate((("w1k", "w2k", "posk"), ("w1v", "w2v", "posv"))):
        for i in range(0, 32, 8):
            DMA("pool", w1[64 * kv:64 * kv + 64, i:i + 8, :], L[wn][:, i:i + 8, :], [], ["wcmp"], "wcmp")
        DMA("pool", w2[kv], L[w2n], [], ["wcmp"], "wcmp")
        DMA("pool", pos[64 * kv:64 * kv + 64, :], L[pn], [], ["wcmp"], "wcmp")
    for c in range(8):
        DMA("pool", wout1[:, c, :], L["wout1"][:, c, :], [], ["wout1"], "wout1")
    for kv in range(2):
        for hc in range(2):
            col = kv * 2 + hc
            for i in range(32):
                MM(psum[0][:, col:col + 1], w1[64 * kv:64 * kv + 64, i, hc * 128:(hc + 1) * 128],
                   pos[64 * kv:64 * kv + 64, i:i + 1], i == 0, i == 31, ["wcmp"], [PS(0)])
    ACT(bh, psum[0][:, 0:4], AF.Copy, [PS(0)], ["bh"])

    if STG < 1:
        return
    rr = {"s": 0, "o": 0, "p": 0, "a": 0, "e": 0}

    def nxt(k, n):
        v = rr[k]
        rr[k] = (v + 1) % n
        return v

    for s in range(NSEQ):
        for j in range(16):
            xs_ = j % 2
            xk = ("xr", xs_)
            DMA("sp", xr[xs_], x1_d[s, j * 128:(j + 1) * 128, :], [], [xk], "xr%d" % xs_)
            for q4 in range(2):
                pb = nxt("a", 2)
                for i4 in range(4):
                    kc = q4 * 4 + i4
                    P.add("pe", lambda e, kc=kc, i4=i4, pb=pb, xs_=xs_: e.transpose(
                        out=psum[pb][:, i4 * 128:(i4 + 1) * 128], in_=xr[xs_][:, kc * 128:(kc + 1) * 128],
                        identity=identf), reads=[xk, "consts"], writes=[PS(pb)])
                ACT(x1T[:, q4 * 4:(q4 + 1) * 4, j * 128:(j + 1) * 128],
                    psum[pb][:, :].rearrange("p (a b) -> p a b", a=4), AF.Copy, [PS(pb)], ["x1T"])

        if STG < 2:
            return
        for g in range(4):
            gi = s * 4 + g
            w2s = gi % 2
            w2k_ = ("W2", w2s)
            for kc in range(0, 8, 2):
                DMA("pool", W1[:, kc:kc + 2, :], L["wg1"][g, :, kc:kc + 2, :], [], ["W1"], "W1")
            for kc in range(0, 8, 4):
                DMA("pool", W2[w2s][:, kc:kc + 4, :], L["wg2"][g, :, kc:kc + 4, :], [], [w2k_], "W2%d" % w2s)
            for hl in range(4):
                DMA("sp", q_aug[hl][96:103, :], L["qalibi"][4 * g + hl], [], [("qa_a", hl)], "qal")
            for pc in range(2):
                for tb in range(4):
                    pb = nxt("a", 2)
                    for kc in range(8):
                        MM(psum[pb][:, :], W1[:, kc, pc * 128:(pc + 1) * 128], x1T[:, kc, tb * 512:(tb + 1) * 512],
                           kc == 0, kc == 7, ["W1", "x1T"], [PS(pb)])
                    ACT(q_aug[2 * pc][0:64, tb * 512:(tb + 1) * 512], psum[pb][0:64, :], AF.Copy,
                        [PS(pb)], [("qa_q", 2 * pc)], scale=0.125)
                    ACT(q_aug[2 * pc + 1][0:64, tb * 512:(tb + 1) * 512], psum[pb][64:128, :], AF.Copy,
                        [PS(pb)], [("qa_q", 2 * pc + 1)], scale=0.125)
            if STG < 3:
                return
            for tb in range(4):
                c0, c1 = tb * 512, (tb + 1) * 512
                pb = nxt("a", 2)
                for kc in range(8):
                    MM(psum[pb][:, :], W1[:, kc, 256:384], x1T[:, kc, c0:c1], kc == 0, kc == 7, ["W1", "x1T"], [PS(pb)])
                ACT(pair[:, c0:c1], psum[pb][:, :], AF.Copy, [PS(pb)], ["pair"])
                pb = nxt("a", 2)
                for kc in range(8):
                    MM(psum[pb][:, :], W1[:, kc, 384:512], x1T[:, kc, c0:c1], kc == 0, kc == 7, ["W1", "x1T"], [PS(pb)])
                P.add("dve", lambda e, pb=pb, c0=c0, c1=c1: e.tensor_copy(out=ksel[0:64, c0:c1], in_=psum[pb][0:64, :]),
                      reads=[PS(pb)], writes=["ksel"])
                ACT(kwin[0:64, c0:c1], psum[pb][64:128, :], AF.Copy, [PS(pb)], ["kwin"])
            if STG < 4:
                return
            for j4 in range(4):
                pb = nxt("a", 2)
                for jj in range(4):
                    j = j4 * 4 + jj
                    for kc in range(8):
                        MM(psum[pb][:, jj * 128:(jj + 1) * 128], x1T[:, kc, j * 128:(j + 1) * 128], W1[:, kc, 512:640],
                           kc == 0, kc == 7, ["W1", "x1T"], [PS(pb)])
                pv = psum[pb][:, :].rearrange("p (a b) -> p a b", a=4)
                P.add("dve", lambda e, pv=pv, j4=j4: e.tensor_copy(out=vsel[:, j4 * 4:(j4 + 1) * 4, 0:64], in_=pv[:, :, 0:64]),
                      reads=[PS(pb)], writes=["vsel"])
                ACT(vwin[:, j4 * 4:(j4 + 1) * 4, 0:64], pv[:, :, 64:128], AF.Copy, [PS(pb)], ["vwin"])
            if STG < 5:
                return
            pb = nxt("a", 2)
            for j in range(16):
                for kc in range(8):
                    MM(psum[pb][:, j * 12:(j + 1) * 12], x1T[:, kc, j * 128:(j + 1) * 128], W2[w2s][:, kc, 256:268],
                       kc == 0, kc == 7, [w2k_, "x1T"], [PS(pb)])
            ACT(gates.rearrange("p a b -> p (a b)"), psum[pb][:, 0:192], AF.Sigmoid, [PS(pb)], ["gates"])
            if STG < 6:
                return
            for kv in range(2):
                for hc in range(2):
                    pb = nxt("a", 2)
                    for i in range(32):
                        MM(psum[pb][:, 0:127], w1[64 * kv:64 * kv + 64, i, hc * 128:(hc + 1) * 128],
                           pair[64 * kv:64 * kv + 64, i:i + 16 * 126 + 1:16], i == 0, i == 31, ["wcmp", "pair"], [PS(pb)])
                    ACT(hs[kv][:, hc, 0:127], psum[pb][:, 0:127], AF.Silu, [PS(pb), "bh"], [("hs", kv)],
                        bias=bh[:, kv * 2 + hc:kv * 2 + hc + 1])
            pb = nxt("a", 2)
            for hc in range(2):
                MM(psum[pb][0:64, 0:127], w2[0][:, hc, :], hs[0][:, hc, 0:127], hc == 0, hc == 1,
                   ["wcmp", ("hs", 0)], [PS(pb)])
            ACT(kcmp[0:64, 0:127], psum[pb][0:64, 0:127], AF.Copy, [PS(pb)], ["kcmp"])
            pb = nxt("a", 2)
            for hc in range(2):
                MM(psum[pb][0:127, 0:64], hs[1][:, hc, 0:127], w2[1][:, hc, :], hc == 0, hc == 1,
                   ["wcmp", ("hs", 1)], [PS(pb)])
            ACT(vcmp[0:127, 0:64], psum[pb][0:127, 0:64], AF.Copy, [PS(pb)], ["vcmp"])

            if STG < 7:
                return
            def branch_epilogue(hl, br, ob, Wd, qt, os_, first, with_imp):
                O = psum[ob][:, 0:4 * Wd].rearrange("p (a b) -> p a b", a=4)
                e_ = nxt("e", 2)
                rk, fk, tk = ("rden", e_), ("fcol", e_), ("tmp_o", e_)
                TS(rden[e_].unsqueeze(2), O[:, :, 64:65], 1e-30, None, ALU.max, None, [PS(ob)], [rk])
                P.add("dve", lambda e: e.reciprocal(out=rden[e_], in_=rden[e_]), reads=[rk], writes=[rk])
                if with_imp:
                    rb = rden[e_].unsqueeze(2).broadcast_to([128, 4, 32])
                    if hl == 0:
                        TT("dve", imp, O[:, :, 65:97], rb, ALU.mult, [PS(ob), rk], ["imp"])
                    else:
                        TT("dve", tmp_i, O[:, :, 65:97], rb, ALU.mult, [PS(ob), rk], ["tmp_i"])
                        TT("pool", imp, imp, tmp_i, ALU.add, ["tmp_i", "imp"], ["imp"])
                gcol = gates[:, qt * 4:(qt + 1) * 4, hl * 3 + br]
                TT("dve", fcol[e_], rden[e_], gcol, ALU.mult, [rk, "gates"], [fk])
                fb = fcol[e_].unsqueeze(2).broadcast_to([128, 4, 64])
                odst = o_tm[os_][:, :, hl * 64:(hl + 1) * 64]
                ok = ("o_tm", os_, hl)
                if first:
                    TT("dve", odst, O[:, :, 0:64], fb, ALU.mult, [PS(ob), fk], [ok])
                else:
                    TT("dve", tmp_o[e_], O[:, :, 0:64], fb, ALU.mult, [PS(ob), fk], [tk])
                    TT("pool", odst, odst, tmp_o[e_], ALU.add, [tk, ok], [ok])

            for qt in range(4):
                Q0 = qt * 512
                os_ = (gi * 4 + qt) % 2
                for hl in range(4):
                    sb = 2 + nxt("s", 2)
                    pi = nxt("p", 3)
                    qk = [("qa_q", hl), ("qa_s", hl), ("qa_a", hl)]
                    MM(psum[sb][:, :], kcmp[0:103, :], q_aug[hl][0:103, Q0:Q0 + 512], True, False,
                       ["kcmp", "kcmp_c"] + qk, [PS(sb)])
                    MM(psum[sb][:, :], identb, cmpmask[:, Q0:Q0 + 512], False, True, ["consts"], [PS(sb)])
                    ACT(Pb[pi], psum[sb][:, :], AF.Exp, [PS(sb)], [("Pb", pi)])
                    ob = 4 + nxt("o", 2)
                    for qs in range(4):
                        MM(psum[ob][:, qs * 97:(qs + 1) * 97], Pb[pi][:, qs * 128:(qs + 1) * 128], vcmp[:, 0:97],
                           True, True, [("Pb", pi), "vcmp", "vcmp_c"], [PS(ob)])
                    branch_epilogue(hl, 0, ob, 97, qt, os_, True, True)
                if STG < 8:
                    return
                TT("dve", impm, imp, forced[:, qt * 4:(qt + 1) * 4, :], ALU.max, ["imp", "consts"], ["impm"])
                TT("dve", impm, impm, future[:, qt * 4:(qt + 1) * 4, :], ALU.min, ["impm", "consts"], ["impm"])
                for qs in range(4):
                    P.add("dve", lambda e, qs=qs: e.max(out=m8[:, qs, :], in_=impm[:, qs, :]), reads=["impm"], writes=["m8"])
                TS(thr.unsqueeze(2), m8[:, :, 7:8], 0.0, None, ALU.max, None, ["m8"], ["thr"])
                for qs in range(4):
                    TS(selm_w[:, qs, 64:96], impm[:, qs, :], thr[:, qs:qs + 1], 1.0, ALU.is_ge, ALU.subtract,
                       ["impm", "thr"], ["selm"])
                if STG < 9:
                    return
                for hl in range(4):
                    qk = [("qa_q", hl), ("qa_s", hl), ("qa_a", hl)]
                    ob = 4 + nxt("o", 2)
                    kts = [kt for kt in range(max(0, 4 * qt - 2), 4 * qt + 4)]
                    plan = []
                    for kt in kts:
                        K0 = kt * 128
                        lo, hi = max(K0, Q0), min(K0 + 384, Q0 + 512)
                        if hi > lo:
                            plan.append((kt, K0, lo, hi))
                    lastk = {}
                    for (kt, K0, lo, hi) in plan:
                        for qs in range((lo - Q0) // 128, (hi - Q0) // 128):
                            lastk[qs] = kt
                    firstmm = True
                    for (kt, K0, lo, hi) in plan:
                        sb = 2 + nxt("s", 2)
                        pi = nxt("p", 3)
                        c0, c1, m0 = lo - Q0, hi - Q0, lo - K0
                        MM(psum[sb][:, c0:c1], kwin[0:103, K0:K0 + 128], q_aug[hl][0:103, lo:hi], True, False,
                           ["kwin", "kwin_c"] + qk, [PS(sb)])
                        MM(psum[sb][:, c0:c1], identb, winmask[:, m0:m0 + (hi - lo)], False, True, ["consts"], [PS(sb)])
                        ACT(Pb[pi][:, c0:c1], psum[sb][:, c0:c1], AF.Exp, [PS(sb)], [("Pb", pi)])
                        for qs in range(c0 // 128, c1 // 128):
                            MM(psum[ob][:, qs * 65:(qs + 1) * 65], Pb[pi][:, qs * 128:(qs + 1) * 128], vwin[:, kt, :],
                               firstmm, lastk[qs] == kt, [("Pb", pi), "vwin", "vwin_c"], [PS(ob)])
                            firstmm = False
                    branch_epilogue(hl, 2, ob, 65, qt, os_, False, False)
                if STG < 10:
                    return
                for qs in range(4):
                    MM(psum[6][:, qs * 128:(qs + 1) * 128], selm_w[:, qs, :], identb, True, True, ["selm", "consts"], [PS(6)])
                for hl in range(4):
                    ACT(q_aug[hl][64:96, Q0:Q0 + 512], psum[6][64:96, :], AF.Copy, [PS(6)], [("qa_s", hl)], scale=30000.0)
                if STG < 11:
                    return
                for hl in range(4):
                    qk = [("qa_q", hl), ("qa_s", hl), ("qa_a", hl)]
                    ob = 4 + nxt("o", 2)
                    nk = 4 * qt + 4
                    firstmm = True
                    for kt in range(nk):
                        K0 = kt * 128
                        dq_ = kt - 4 * qt
                        c0 = max(dq_, 0) * 128
                        sb = 2 + nxt("s", 2)
                        pi = nxt("p", 3)
                        MM(psum[sb][:, c0:512], ksel[0:103, K0:K0 + 128], q_aug[hl][0:103, Q0 + c0:Q0 + 512], True, dq_ < 0,
                           ["ksel", "ksel_c"] + qk, [PS(sb)])
                        if dq_ >= 0:
                            MM(psum[sb][:, c0:c0 + 128], identb, trimask, False, True, ["consts"], [PS(sb)])
                        ACT(Pb[pi][:, c0:512], psum[sb][:, c0:512], AF.Exp, [PS(sb)], [("Pb", pi)])
                        for qs in range(c0 // 128, 4):
                            lastkt = 4 * qt + qs
                            MM(psum[ob][:, qs * 65:(qs + 1) * 65], Pb[pi][:, qs * 128:(qs + 1) * 128], vsel[:, kt, :],
                               firstmm, kt == lastkt, [("Pb", pi), "vsel", "vsel_c"], [PS(ob)])
                            firstmm = False
                    branch_epilogue(hl, 1, ob, 65, qt, os_, False, False)
                if STG < 12:
                    return
                for qs in range(4):
                    jt = 4 * qt + qs
                    k_ = (gi * 16 + jt) % 2
                    for kc in range(8):
                        MM(psum[7][:, 0:256], x1T[:, kc, jt * 128:(jt + 1) * 128], W2[w2s][:, kc, 0:256], kc == 0, kc == 7,
                           [w2k_, "x1T"], [PS(7)])
                    ACT(sz[k_], psum[7][:, 0:256], AF.Silu, [PS(7)], [("sz", k_)])
                    TT("dve", ogb[k_], o_tm[os_][:, qs, :], sz[k_], ALU.mult,
                       [("sz", k_)] + [("o_tm", os_, h_) for h_ in range(4)], [("ogb", k_)])
                    p6b = psum[6][:, :].bitcast(BF16)
                    for cc in range(2):
                        P.add("pe", lambda e, cc=cc, k_=k_, p6b=p6b: e.transpose(
                            out=p6b[:, cc * 128:(cc + 1) * 128], in_=ogb[k_][:, cc * 128:(cc + 1) * 128], identity=identb),
                            reads=[("ogb", k_), "consts"], writes=[PS(6)])
                    ACT(ogT[:, 2 * g:2 * g + 2, jt * 128:(jt + 1) * 128],
                        p6b[:, 0:256].rearrange("p (a b) -> p a b", a=2), AF.Copy, [PS(6)], [("ogT", g)])

        for jt in range(16):
            xs_ = jt % 2
            xk = ("xr", xs_)
            DMA("sp", xr[xs_], x1_d[s, jt * 128:(jt + 1) * 128, :], [], [xk], "xr%d" % xs_)
            for hh in range(2):
                pb = nxt("a", 2)
                for c in range(8):
                    MM(psum[pb][:, :], ogT[:, c, jt * 128:(jt + 1) * 128], wout1[:, c, hh * 512:(hh + 1) * 512],
                       c == 0, c == 7, [("ogT", c // 2), "wout1"], [PS(pb)])
                P.add("dve", lambda e, hh=hh, pb=pb, xs_=xs_: e.scalar_tensor_tensor(
                    out=xr[xs_][:, hh * 512:(hh + 1) * 512], in0=xr[xs_][:, hh * 512:(hh + 1) * 512],
                    scalar=DN_ALPHA, in1=psum[pb][:, :], op0=ALU.mult, op1=ALU.add), reads=[PS(pb), xk], writes=[xk])
            layer_norm_tile(P, xr[xs_], xk, lng1, lnb1, stat[xs_], ("stat1", xs_), epsc)
            DMA("sp", out_d[s, jt * 128:(jt + 1) * 128, :], xr[xs_], [xk], [], "ost%d" % xs_)


def build_program(do_l0=True, do_l1=True):
    nc = bass.Bass("TRN2", target_bir_lowering=False)
    dt = {}

    def din(name, shape, dtype=F32):
        dt[name] = nc.dram_tensor(name, list(shape), dtype, kind="ExternalInput").ap()
        return dt[name]

    x_d = din("x", [NSEQ, S, D])
    lng_d = din("lng", [2, 128, D])
    lnb_d = din("lnb", [2, 128, D])
    w_in0_d = din("w_in0", [128, 8, 4096])
    w_grp_d = din("w_grp", [128, 16, 512])
    w_out0_d = din("w_out0", [128, 16, 1024])
    scale_d = din("pscale", [128, 16])
    poolA_d = din("poolA", [128, 4 * 3 * 128], BF16)
    invc_d = din("invc", [128, 4 * 128])
    identf_d = din("identf", [128, 128])
    identb_d = din("identb", [128, 128], BF16)
    x1kind = "Internal" if (do_l0 and do_l1) else ("ExternalOutput" if do_l0 else "ExternalInput")
    x1_d = nc.dram_tensor("x1s", [NSEQ, S, D], F32, kind=x1kind).ap()
    out_d = nc.dram_tensor("out", [NSEQ, S, D], F32, kind="ExternalOutput").ap()
    L1D = _l1_dram(nc, din)

    stack = contextlib.ExitStack()
    with stack:
        ACOLS = 52800
        ar_t = stack.enter_context(nc.sbuf_tensor("arena", [128, ACOLS], F32))
        AR = Arena(ar_t[:], ACOLS)
        psum = [stack.enter_context(nc.psum_tensor("ps%d" % i, [128, 512], F32)) for i in range(8)]
        P = Prog(nc)

        def PS(i):
            return ("ps", i)

        if do_l0:
            AR.reset()
            w_in0 = AR.b(8, 4096)
            w_grp = AR.b(16, 512)
            w_out0 = AR.b(16, 1024)
            poolA = AR.b(12, 128)
            identb = AR.b(128)
            xT0 = [AR.b(8, 256) for _ in range(2)]
            u_tm = AR.b(12, 512)
            mT = [AR.b(4, 256) for _ in range(2)]
            gT = AR.b(16, 256)
            identf = AR.f(128)
            lng0 = AR.f(1024)
            lnb0 = AR.f(1024)
            invc = AR.f(4, 128)
            pscale = AR.f(16)
            epsc = AR.f(1)
            xs = [AR.f(2, 1024) for _ in range(2)]
            siluz = [AR.f(4, 256) for _ in range(2)]
            rbuf = [AR.f(1024) for _ in range(2)]
            stat = [AR.f(16) for _ in range(2)]

            P.add("sp", lambda e: e.dma_start(out=identf, in_=identf_d), writes=["consts"], dma_key="c0")
            P.add("sp", lambda e: e.dma_start(out=identb, in_=identb_d), writes=["consts"], dma_key="c0")
            P.add("sp", lambda e: e.dma_start(out=poolA.rearrange("p a b -> p (a b)"), in_=poolA_d), writes=["consts"], dma_key="c0")
            P.add("sp", lambda e: e.dma_start(out=invc.rearrange("p a b -> p (a b)"), in_=invc_d), writes=["consts"], dma_key="c0")
            P.add("sp", lambda e: e.dma_start(out=pscale, in_=scale_d), writes=["consts"], dma_key="c0")
            P.add("sp", lambda e: e.dma_start(out=lng0, in_=lng_d[0]), writes=["lnc"], dma_key="c0")
            P.add("sp", lambda e: e.dma_start(out=lnb0, in_=lnb_d[0]), writes=["lnc"], dma_key="c0")
            P.add("dve", lambda e: e.memset(epsc, LN_EPS), writes=["consts"])
            for kc in range(8):
                for hh in range(2):
                    P.add("pool", lambda e, kc=kc, hh=hh: e.dma_start(out=w_in0[:, kc, hh * 2048:(hh + 1) * 2048],
                                                                      in_=w_in0_d[:, kc, hh * 2048:(hh + 1) * 2048]),
                          writes=["w_in0"], dma_key="w_in0")
            for c in range(16):
                P.add("pool", lambda e, c=c: e.dma_start(out=w_grp[:, c, :], in_=w_grp_d[:, c, :]),
                      writes=["w_grp"], dma_key="w_grp")
            for c in range(16):
                P.add("pool", lambda e, c=c: e.dma_start(out=w_out0[:, c, :], in_=w_out0_d[:, c, :]),
                      writes=["w_out0"], dma_key="w_out0")

            NB = S // 256
            for s in range(NSEQ):
                for b in range(NB):
                    it = s * NB + b
                    sl = it % 2
                    t0 = b * 256
                    xk = ("xs", sl)
                    for j in range(2):
                        P.add("sp", lambda e, j=j, sl=sl, s=s, t0=t0: e.dma_start(
                            out=xs[sl][:, j, :], in_=x_d[s, t0 + j * 128:t0 + (j + 1) * 128, :]),
                            writes=[xk], dma_key="xs%d" % sl)
                    xtk = ("xT0", sl)
                    for j in range(2):
                        for q4 in range(2):
                            pb = (j * 2 + q4) % 2
                            for i4 in range(4):
                                kc = q4 * 4 + i4
                                P.add("pe", lambda e, j=j, kc=kc, i4=i4, pb=pb, sl=sl: e.transpose(
                                    out=psum[pb][:, i4 * 128:(i4 + 1) * 128], in_=xs[sl][:, j, kc * 128:(kc + 1) * 128],
                                    identity=identf), reads=[xk, "consts"], writes=[PS(pb)])
                            P.add("act", lambda e, j=j, q4=q4, pb=pb, sl=sl: e.activation(
                                out=xT0[sl][:, q4 * 4:(q4 + 1) * 4, j * 128:(j + 1) * 128],
                                in_=psum[pb][:, :].rearrange("p (a b) -> p a b", a=4), func=AF.Copy),
                                reads=[PS(pb)], writes=[xtk])
                    for g in range(4):
                        gi = it * 4 + g
                        gs = gi % 2
                        szk = ("siluz", gs)
                        mk = ("mT", gs)
                        for j in range(2):
                            n = b * 2 + j
                            slot = n % 3
                            pb = 2 + (j % 2)
                            for kc in range(8):
                                P.add("pe", lambda e, kc=kc, j=j, g=g, pb=pb, sl=sl: e.matmul(
                                    psum[pb][:, :], lhsT=xT0[sl][:, kc, j * 128:(j + 1) * 128],
                                    rhs=w_in0[:, kc, g * 512:(g + 1) * 512], start=(kc == 0), stop=(kc == 7)),
                                    reads=[xtk, "w_in0"], writes=[PS(pb)])
                            P.add("act", lambda e, g=g, slot=slot, pb=pb: e.activation(
                                out=u_tm[:, g * 3 + slot, :], in_=psum[pb][:, :], func=AF.Copy),
                                reads=[PS(pb)], writes=[("u", g, slot)])
                        for c2 in range(2):
                            pb = 4 + c2
                            for ci in range(2):
                                c = c2 * 2 + ci
                                col = 2048 + g * 512 + c * 128
                                for kc in range(8):
                                    P.add("pe", lambda e, kc=kc, col=col, ci=ci, pb=pb, sl=sl: e.matmul(
                                        psum[pb][:, ci * 256:(ci + 1) * 256], lhsT=w_in0[:, kc, col:col + 128],
                                        rhs=xT0[sl][:, kc, :], start=(kc == 0), stop=(kc == 7)),
                                        reads=[xtk, "w_in0"], writes=[PS(pb)])
                            P.add("act", lambda e, c2=c2, pb=pb, gs=gs: e.activation(
                                out=siluz[gs][:, c2 * 2:(c2 + 1) * 2, :],
                                in_=psum[pb][:, :].rearrange("p (a b) -> p a b", a=2), func=AF.Silu),
                                reads=[PS(pb)], writes=[szk])
                        for c2 in range(2):
                            pb = 6 + c2
                            for ci in range(2):
                                c = c2 * 2 + ci
                                for j in range(2):
                                    n = b * 2 + j
                                    slot = n % 3
                                    pslot = (n - 1) % 3
                                    first = (n == 0)
                                    o = psum[pb][:, ci * 256 + j * 128: ci * 256 + (j + 1) * 128]
                                    if first:
                                        P.add("pe", lambda e, o=o, g=g, slot=slot, c=c: e.matmul(
                                            o, lhsT=u_tm[:, g * 3 + slot, c * 128:(c + 1) * 128],
                                            rhs=poolA[:, g * 3 + 2, :], start=True, stop=True),
                                            reads=[("u", g, slot), "consts"], writes=[PS(pb)])
                                    else:
                                        P.add("pe", lambda e, o=o, g=g, slot=slot, c=c: e.matmul(
                                            o, lhsT=u_tm[:, g * 3 + slot, c * 128:(c + 1) * 128],
                                            rhs=poolA[:, g * 3 + 0, :], start=True, stop=False),
                                            reads=[("u", g, slot), "consts"], writes=[PS(pb)])
                                        P.add("pe", lambda e, o=o, g=g, pslot=pslot, c=c: e.matmul(
                                            o, lhsT=u_tm[:, g * 3 + pslot, c * 128:(c + 1) * 128],
                                            rhs=poolA[:, g * 3 + 1, :], start=False, stop=True),
                                            reads=[("u", g, pslot), "consts"], writes=[PS(pb)])
                            P.add("act", lambda e, c2=c2, pb=pb, gs=gs: e.activation(
                                out=mT[gs][:, c2 * 2:(c2 + 1) * 2, :],
                                in_=psum[pb][:, :].rearrange("p (a b) -> p a b", a=2), func=AF.Copy),
                                reads=[PS(pb)], writes=[mk])
                            if b == 0:
                                P.add("dve", lambda e, c2=c2, pb=pb, gs=gs, g=g: e.tensor_tensor(
                                    out=mT[gs][:, c2 * 2:(c2 + 1) * 2, 0:128],
                                    in0=psum[pb][:, :].rearrange("p (a b) -> p a b", a=2)[:, :, 0:128],
                                    in1=invc[:, g:g + 1, :].broadcast_to([128, 2, 128]), op=ALU.mult),
                                    reads=[PS(pb), "consts"], writes=[mk])
                        for d2 in range(2):
                            pb = d2
                            for di in range(2):
                                d = d2 * 2 + di
                                for cc in range(4):
                                    P.add("pe", lambda e, cc=cc, d=d, di=di, g=g, pb=pb, gs=gs: e.matmul(
                                        psum[pb][:, di * 256:(di + 1) * 256],
                                        lhsT=w_grp[:, g * 4 + cc, d * 128:(d + 1) * 128], rhs=mT[gs][:, cc, :],
                                        start=(cc == 0), stop=(cc == 3)), reads=[mk, "w_grp"], writes=[PS(pb)])
                            for di in range(2):
                                d = d2 * 2 + di
                                ch = g * 4 + d
                                P.add("dve", lambda e, d=d, di=di, ch=ch, pb=pb, gs=gs: e.scalar_tensor_tensor(
                                    out=gT[:, ch, :], in0=psum[pb][:, di * 256:(di + 1) * 256],
                                    scalar=pscale[:, ch:ch + 1], in1=siluz[gs][:, d, :], op0=ALU.mult, op1=ALU.mult),
                                    reads=[PS(pb), szk, "consts"], writes=[("gT", ch)])
                    for j in range(2):
                        rs = (it * 2 + j) % 2
                        rk = ("r", rs)
                        for hh in range(2):
                            pb = 2 + hh
                            for ch in range(16):
                                P.add("pe", lambda e, ch=ch, j=j, hh=hh, pb=pb: e.matmul(
                                    psum[pb][:, :], lhsT=gT[:, ch, j * 128:(j + 1) * 128],
                                    rhs=w_out0[:, ch, hh * 512:(hh + 1) * 512], start=(ch == 0), stop=(ch == 15)),
                                    reads=[("gT", ch), "w_out0"], writes=[PS(pb)])
                            P.add("dve", lambda e, j=j, hh=hh, pb=pb, rs=rs, sl=sl: e.scalar_tensor_tensor(
                                out=rbuf[rs][:, hh * 512:(hh + 1) * 512], in0=xs[sl][:, j, hh * 512:(hh + 1) * 512],
                                scalar=DN_ALPHA, in1=psum[pb][:, :], op0=ALU.mult, op1=ALU.add),
                                reads=[PS(pb), xk], writes=[rk])
                        layer_norm_tile(P, rbuf[rs], rk, lng0, lnb0, stat[rs], ("stat", rs), epsc)
                        P.add("sp", lambda e, rs=rs, s=s, t0=t0, j=j: e.dma_start(
                            out=x1_d[s, t0 + j * 128:t0 + (j + 1) * 128, :], in_=rbuf[rs]),
                            reads=[rk], dma_key="st%d" % rs)

        P.barrier()
        if do_l1:
            AR.reset()
            _build_l1(nc, P, AR, psum, L1D, x1_d, out_d, lng_d, lnb_d, identf_d, identb_d)
        P.emit(stack)
    return nc


def _host_common(inputs):
    f = lambda a: np.ascontiguousarray(np.asarray(a, dtype=np.float32))
    m = {}
    lng = f(inputs["ln_g"])
    lnb = f(inputs["ln_b"])
    m["lng"] = np.ascontiguousarray(np.broadcast_to(lng[:, None, :], (2, 128, D)))
    m["lnb"] = np.ascontiguousarray(np.broadcast_to(lnb[:, None, :], (2, 128, D)))
    w = f(inputs["pool_w_in"])[0]
    m["w_in0"] = np.ascontiguousarray(w.reshape(8, 128, 4096).transpose(1, 0, 2))
    wg = f(inputs["pool_w_grp"])[0]
    m["w_grp"] = np.ascontiguousarray(wg.reshape(4, 4, 128, 512).transpose(2, 0, 1, 3).reshape(128, 16, 512))
    wo = f(inputs["pool_w_out"])[0]
    m["w_out0"] = np.ascontiguousarray(wo.reshape(16, 128, 1024).transpose(1, 0, 2))
    m["pscale"] = np.ascontiguousarray(f(inputs["pool_scale"])[0].reshape(16, 128).T)
    pa, inv = _pool_tables()
    m["poolA"] = pa
    m["invc"] = inv
    m["identf"] = np.eye(128, dtype=np.float32)
    m["identb"] = _bf(np.eye(128))
    m.update(_host_l1(inputs))
    return m


_NC_CACHE = {}


def kernel(**inputs):
    x = np.ascontiguousarray(np.asarray(inputs["x"], dtype=np.float32))
    common = _host_common(inputs)
    if "nc" not in _NC_CACHE:
        _NC_CACHE["nc"] = build_program()
    nc = _NC_CACHE["nc"]
    in_maps = []
    for c in range(NCORES):
        m = dict(common)
        m["x"] = x[c * NSEQ:(c + 1) * NSEQ]
        in_maps.append(m)
    res = run_bass_kernel_spmd(nc, in_maps, core_ids=list(range(NCORES)))
    out = np.concatenate([np.asarray(r["out"]) for r in res.results], axis=0)
    return out.astype(np.float32)
```

```python
import contextlib
import numpy as np
import ml_dtypes
import concourse.bass as bass
import concourse.mybir as mybir
from concourse.bass_utils import run_bass_kernel_spmd

F32 = mybir.dt.float32
BF16 = mybir.dt.bfloat16
AF = mybir.ActivationFunctionType
ALU = mybir.AluOpType

D = 1024
S = 2048
NSEQ = 2
NCORES = 8
DN_ALPHA = float((2.0 * 2) ** 0.25)
LN_EPS = 1e-5
POOL_WINDOWS = (2, 4, 8, 16)
NEGM = -30000.0


class _Op:
    __slots__ = ("eng", "fn", "deps", "is_dma", "key", "awaited", "count", "idx")


class Prog:
    ENGS = ("pe", "act", "dve", "pool", "sp")

    def __init__(self, nc):
        self.nc = nc
        self.ops = {e: [] for e in self.ENGS}
        self.last_w = {}
        self.readers = {}
        self.dma_counts = {}
        self.last_dma = {}
        self.bar = {e: [] for e in self.ENGS}

    def _dep(self, op, a):
        if a is None or a is op:
            return
        if (not a.is_dma) and a.eng == op.eng and a.eng == "pe":
            return
        if a.is_dma and op.is_dma and a.key == op.key:
            return
        op.deps.append(a)
        if not a.is_dma:
            a.awaited = True

    def add(self, eng, fn, reads=(), writes=(), dma_key=None):
        op = _Op()
        op.eng = eng
        op.fn = fn
        op.deps = []
        op.is_dma = dma_key is not None
        op.key = dma_key
        op.awaited = False
        op.count = None
        for a in self.bar[eng]:
            self._dep(op, a)
        self.bar[eng] = []
        if eng != "pe":
            extra = [("psx", r[1]) for r in reads if isinstance(r, tuple) and r[0] == "ps"]
            extra += [("psx", w[1]) for w in writes if isinstance(w, tuple) and w[0] == "ps"]
            writes = list(writes) + extra
        for r in reads:
            self._dep(op, self.last_w.get(r))
        for w in writes:
            self._dep(op, self.last_w.get(w))
            for a in self.readers.get(w, ()):
                self._dep(op, a)
        for r in reads:
            self.readers.setdefault(r, []).append(op)
        for w in writes:
            self.last_w[w] = op
            self.readers[w] = []
        if op.is_dma:
            c = self.dma_counts.get(dma_key, 0) + 1
            self.dma_counts[dma_key] = c
            op.count = c
            self.last_dma[dma_key] = op
        op.idx = len(self.ops[eng])
        self.ops[eng].append(op)
        return op

    def barrier(self):
        deps = []
        for e in self.ENGS:
            for op in reversed(self.ops[e]):
                if not op.is_dma:
                    deps.append(op)
                    break
        deps.extend(self.last_dma.values())
        for e in self.ENGS:
            self.bar[e] = list(deps)

    def emit(self, stack):
        nc = self.nc
        esem = {e: stack.enter_context(nc.semaphore("s_" + e)) for e in self.ENGS}
        dsem = {k: stack.enter_context(nc.semaphore("d_%d" % i)) for i, k in enumerate(self.dma_counts)}
        for e in self.ENGS:
            c = 0
            for op in self.ops[e]:
                if (not op.is_dma) and op.awaited:
                    c += 1
                    op.count = c
        block = stack.enter_context(nc.Block())
        final = [(dsem[k], 16 * c) for k, c in self.dma_counts.items()]

        def run(ename, eh, is_last=False):
            waited = {}
            for op in self.ops[ename]:
                for a in op.deps:
                    if a.is_dma:
                        sem, val = dsem[a.key], 16 * a.count
                    else:
                        sem, val = esem[a.eng], a.count
                    sid = id(sem)
                    if waited.get(sid, 0) >= val:
                        continue
                    waited[sid] = val
                    eh.wait_ge(sem, val)
                ins = op.fn(eh)
                if op.is_dma:
                    ins.then_inc(dsem[op.key], 16)
                elif op.awaited:
                    ins.then_inc(esem[ename], 1)
            if is_last:
                for sem, val in final:
                    eh.wait_ge(sem, val)

        @block.tensor
        def _(eh):
            run("pe", eh)

        @block.scalar
        def _(eh):
            run("act", eh)

        @block.vector
        def _(eh):
            run("dve", eh)

        @block.gpsimd
        def _(eh):
            run("pool", eh)

        @block.sync
        def _(eh):
            run("sp", eh, is_last=True)


class Arena:
    def __init__(self, ap, ncols):
        self.ap = ap
        self.n = ncols
        self.off = 0

    def reset(self):
        self.off = 0

    def _shape(self, v, shape):
        if len(shape) == 2:
            v = v.rearrange("p (a b) -> p a b", a=shape[0])
        elif len(shape) == 3:
            v = v.rearrange("p (a b c) -> p a b c", a=shape[0], b=shape[1])
        return v

    def f(self, *shape):
        cols = int(np.prod(shape))
        assert self.off + cols <= self.n, ("arena overflow", self.off, cols, self.n)
        v = self.ap[:, self.off:self.off + cols]
        self.off += cols
        return self._shape(v, shape)

    def b(self, *shape):
        cols = int(np.prod(shape))
        c32 = (cols + 1) // 2
        assert self.off + c32 <= self.n, ("arena overflow", self.off, c32, self.n)
        v = self.ap[:, self.off:self.off + c32].bitcast(BF16)[:, 0:cols]
        self.off += c32
        return self._shape(v, shape)


def _bf(a):
    return np.ascontiguousarray(np.asarray(a, dtype=np.float32).astype(ml_dtypes.bfloat16))


def _pool_tables():
    A = np.zeros((4, 3, 128, 128), np.float32)
    inv = np.zeros((4, 128), np.float32)
    for g, w in enumerate(POOL_WINDOWS):
        for t in range(128):
            for tp in range(t - w + 1, t + 1):
                if tp >= 0:
                    A[g, 0, tp, t] += 1.0 / w
                else:
                    A[g, 1, tp + 128, t] += 1.0 / w
            A[g, 0, t, t] -= 1.0
            cnt = min(t + 1, w)
            for tp in range(max(0, t - w + 1), t + 1):
                A[g, 2, tp, t] += 1.0
            A[g, 2, t, t] -= cnt
            inv[g, t] = 1.0 / cnt
    At = np.transpose(A, (2, 0, 1, 3)).reshape(128, 4 * 3 * 128)
    invb = np.broadcast_to(inv.reshape(1, 4 * 128), (128, 4 * 128))
    return _bf(At), np.ascontiguousarray(invb, dtype=np.float32)


def layer_norm_tile(P, r_ap, rkey, lng, lnb, stat, skey, epsc):
    st6 = stat[:, 0:12].rearrange("p (a b) -> p a b", a=2)
    mv = stat[:, 12:14]
    P.add("dve", lambda e: e.bn_stats(out=st6[:, 0, :], in_=r_ap[:, 0:512]), reads=[rkey], writes=[skey])
    P.add("dve", lambda e: e.bn_stats(out=st6[:, 1, :], in_=r_ap[:, 512:1024]), reads=[rkey], writes=[skey])
    P.add("dve", lambda e: e.bn_aggr(out=mv, in_=stat[:, 0:12]), reads=[skey], writes=[skey])
    P.add("act", lambda e: e.activation(out=stat[:, 14:15], in_=stat[:, 13:14], func=AF.Sqrt,
                                        bias=epsc, scale=1.0), reads=[skey, "consts"], writes=[skey])
    P.add("dve", lambda e: e.reciprocal(out=stat[:, 14:15], in_=stat[:, 14:15]), reads=[skey], writes=[skey])
    P.add("dve", lambda e: e.scalar_tensor_tensor(out=stat[:, 15:16], in0=stat[:, 12:13], scalar=-1.0,
                                                  in1=stat[:, 14:15], op0=ALU.mult, op1=ALU.mult),
          reads=[skey], writes=[skey])
    P.add("act", lambda e: e.activation(out=r_ap, in_=r_ap, func=AF.Identity, bias=stat[:, 15:16],
                                        scale=stat[:, 14:15]), reads=[skey, rkey], writes=[rkey])
    P.add("pool", lambda e: e.tensor_tensor(out=r_ap, in0=r_ap, in1=lng, op=ALU.mult),
          reads=[rkey, "lnc"], writes=[rkey])
    P.add("pool", lambda e: e.tensor_tensor(out=r_ap, in0=r_ap, in1=lnb, op=ALU.add),
          reads=[rkey, "lnc"], writes=[rkey])


def _slopes():
    h = np.arange(1, 17, dtype=np.float32)
    return (2.0 ** (-8.0 * h / 16.0)).astype(np.float32)


def _l1_tables():
    t = {}
    sl = _slopes()
    tq = np.arange(S, dtype=np.float64)
    qal = np.zeros((16, 7, S), np.float32)
    for h in range(16):
        s0 = float(sl[h])
        s1 = float(np.float32(s0).astype(ml_dtypes.bfloat16))
        s2 = float(np.float32(s0 - s1).astype(ml_dtypes.bfloat16))
        s3 = float(np.float32(s0 - s1 - s2).astype(ml_dtypes.bfloat16))
        qal[h, 0] = -s0 * tq
        for i, si in enumerate((s1, s2, s3)):
            qal[h, 1 + i] = 64.0 * si
            qal[h, 4 + i] = si
    t["qalibi"] = _bf(qal)
    kal = np.zeros((7, S), np.float32)
    kal[0] = 1.0
    kal[1:4] = (np.arange(S) // 64)[None, :]
    kal[4:7] = (np.arange(S) % 64)[None, :]
    t["kalibi"] = _bf(kal)
    kc = np.zeros((7, 128), np.float32)
    ce = np.arange(127) * 16 + 31
    kc[0] = 1.0
    kc[1:4, :127] = (ce // 64)[None, :]
    kc[4:7, :127] = (ce % 64)[None, :]
    t["kalibic"] = _bf(kc)
    E = np.zeros((32, S), np.float32)
    E[np.arange(S) // 64, np.arange(S)] = 1.0
    t["eoh"] = _bf(E)
    cm = np.full((128, S), NEGM, np.float32)
    cm[:127] = np.where(np.arange(S)[None, :] >= ce[:, None], 0.0, NEGM)
    t["cmpmask"] = _bf(cm)
    dk = np.arange(128)[:, None]
    dq = np.arange(384)[None, :]
    t["winmask"] = _bf(np.where((dq - dk >= 0) & (dq - dk < 256), 0.0, NEGM))
    dq = np.arange(128)[None, :]
    t["trimask"] = _bf(np.where(dk <= dq, 0.0, NEGM))
    vcc = np.zeros((128, 33), np.float32)
    vcc[:, 0] = 1.0
    c0 = np.arange(127)[:, None] * 16
    j0 = np.arange(32)[None, :] * 64
    vcc[:127, 1:] = ((c0 < j0 + 64) & (c0 + 32 > j0)).astype(np.float32)
    t["vcc"] = _bf(vcc)
    q = np.arange(128)[:, None, None]
    qt = np.arange(16)[None, :, None]
    j = np.arange(32)[None, None, :]
    cur = (qt * 128 + q) // 64
    forced = (j == 0) | (j == cur) | (j == cur - 1)
    t["forced"] = np.ascontiguousarray(np.where(forced, 1e9, 0.0).astype(np.float32).reshape(128, 512))
    t["future"] = np.ascontiguousarray(np.where(j > cur, -1e30, 3e38).astype(np.float32).reshape(128, 512))
    return t


def _l1_dram(nc, din):
    L = {}
    L["wg1"] = din("wg1", [4, 128, 8, 640])
    L["wg2"] = din("wg2", [4, 128, 8, 268])
    L["w1k"] = din("w1k", [64, 32, 256])
    L["w1v"] = din("w1v", [64, 32, 256])
    L["w2k"] = din("w2k", [128, 2, 64])
    L["w2v"] = din("w2v", [128, 2, 64])
    L["posk"] = din("posk", [64, 32])
    L["posv"] = din("posv", [64, 32])
    L["wout1"] = din("wout1", [128, 8, 1024])
    L["qalibi"] = din("qalibi", [16, 7, S], BF16)
    L["kalibi"] = din("kalibi", [7, S], BF16)
    L["kalibic"] = din("kalibic", [7, 128], BF16)
    L["eoh"] = din("eoh", [32, S], BF16)
    L["cmpmask"] = din("cmpmask", [128, S], BF16)
    L["winmask"] = din("winmask", [128, 384], BF16)
    L["trimask"] = din("trimask", [128, 128], BF16)
    L["vcc"] = din("vcc", [128, 33], BF16)
    L["forced"] = din("forced", [128, 512])
    L["future"] = din("future", [128, 512])
    return L


def _host_l1(inputs):
    f = lambda a: np.ascontiguousarray(np.asarray(a, dtype=np.float32))
    m = {}
    W = f(inputs["nsa_w_in"])[0].reshape(8, 128, 3632).transpose(1, 0, 2)
    wg1 = np.zeros((4, 128, 8, 640), np.float32)
    wg2 = np.zeros((4, 128, 8, 268), np.float32)
    for g in range(4):
        wg1[g, :, :, 0:256] = W[:, :, 256 * g:256 * g + 256]
        wg1[g, :, :, 256:320] = W[:, :, 1024 + 64 * g:1024 + 64 * g + 64]
        wg1[g, :, :, 320:384] = W[:, :, 1280 + 64 * g:1280 + 64 * g + 64]
        wg1[g, :, :, 384:448] = W[:, :, 1536 + 64 * g:1536 + 64 * g + 64]
        wg1[g, :, :, 448:512] = W[:, :, 2048 + 64 * g:2048 + 64 * g + 64]
        wg1[g, :, :, 512:576] = W[:, :, 1792 + 64 * g:1792 + 64 * g + 64]
        wg1[g, :, :, 576:640] = W[:, :, 2304 + 64 * g:2304 + 64 * g + 64]
        wg2[g, :, :, 0:256] = W[:, :, 2560 + 256 * g:2560 + 256 * g + 256]
        wg2[g, :, :, 256:268] = W[:, :, 3584 + 12 * g:3584 + 12 * g + 12]
    m["wg1"] = wg1
    m["wg2"] = wg2
    m["w1k"] = np.ascontiguousarray(f(inputs["nsa_cmp_w1_k"])[0].reshape(32, 64, 256).transpose(1, 0, 2))
    m["w1v"] = np.ascontiguousarray(f(inputs["nsa_cmp_w1_v"])[0].reshape(32, 64, 256).transpose(1, 0, 2))
    m["w2k"] = np.ascontiguousarray(f(inputs["nsa_cmp_w2_k"])[0].reshape(2, 128, 64).transpose(1, 0, 2))
    m["w2v"] = np.ascontiguousarray(f(inputs["nsa_cmp_w2_v"])[0].reshape(2, 128, 64).transpose(1, 0, 2))
    m["posk"] = np.ascontiguousarray(f(inputs["nsa_cmp_pos_k"])[0].T)
    m["posv"] = np.ascontiguousarray(f(inputs["nsa_cmp_pos_v"])[0].T)
    m["wout1"] = np.ascontiguousarray(f(inputs["nsa_w_out"])[0].reshape(8, 128, 1024).transpose(1, 0, 2))
    m.update(_l1_tables())
    return m


L1_STAGE = 99


def _build_l1(nc, P, AR, psum, L, x1_d, out_d, lng_d, lnb_d, identf_d, identb_d):
    STG = L1_STAGE
    def PS(i):
        return ("ps", i)

    def MM(out, lhsT, rhs, start, stop, reads, writes):
        P.add("pe", lambda e: e.matmul(out, lhsT=lhsT, rhs=rhs, start=start, stop=stop), reads=reads, writes=writes)

    def ACT(out, in_, func, reads, writes, **kw):
        P.add("act", lambda e: e.activation(out=out, in_=in_, func=func, **kw), reads=reads, writes=writes)

    def TT(eng, out, in0, in1, op, reads, writes):
        P.add(eng, lambda e: e.tensor_tensor(out=out, in0=in0, in1=in1, op=op), reads=reads, writes=writes)

    def TS(out, in0, s1, s2, op0, op1, reads, writes):
        if op1 is None:
            P.add("dve", lambda e: e.tensor_scalar(out=out, in0=in0, scalar1=s1, scalar2=None, op0=op0),
                  reads=reads, writes=writes)
        else:
            P.add("dve", lambda e: e.tensor_scalar(out=out, in0=in0, scalar1=s1, scalar2=s2, op0=op0, op1=op1),
                  reads=reads, writes=writes)

    def DMA(q, out, in_, reads, writes, key):
        P.add(q, lambda e: e.dma_start(out=out, in_=in_), reads=reads, writes=writes, dma_key=key)

    W1 = AR.b(8, 640)
    W2 = [AR.b(8, 268) for _ in range(2)]
    w1 = AR.b(32, 256)
    w2 = [AR.b(2, 64) for _ in range(2)]
    pos = AR.b(32)
    wout1 = AR.b(8, 1024)
    identb = AR.b(128)
    x1T = AR.b(8, S)
    ogT = AR.b(8, S)
    q_aug = [AR.b(S) for _ in range(4)]
    ksel = AR.b(S)
    kwin = AR.b(S)
    kcmp = AR.b(128)
    pair = AR.b(S)
    vsel = AR.b(16, 65)
    vwin = AR.b(16, 65)
    vcmp = AR.b(97)
    hs = [AR.b(2, 128) for _ in range(2)]
    cmpmask = AR.b(S)
    winmask = AR.b(384)
    trimask = AR.b(128)
    Pb = [AR.b(512) for _ in range(4)]
    selm_w = AR.b(4, 128)
    ogb = [AR.b(256) for _ in range(2)]
    identf = AR.f(128)
    lng1 = AR.f(1024)
    lnb1 = AR.f(1024)
    xr = [AR.f(1024) for _ in range(2)]
    stat = [AR.f(16) for _ in range(2)]
    o_tm = [AR.f(4, 256) for _ in range(2)]
    gates = AR.f(16, 12)
    imp = AR.f(4, 32)
    impm = AR.f(4, 32)
    forced = AR.f(16, 32)
    future = AR.f(16, 32)
    m8 = AR.f(4, 8)
    thr = AR.f(4)
    rden = [AR.f(4) for _ in range(2)]
    fcol = [AR.f(4) for _ in range(2)]
    tmp_o = [AR.f(4, 64) for _ in range(2)]
    tmp_i = AR.f(4, 32)
    sz = [AR.f(256) for _ in range(2)]
    bh = AR.f(4)
    epsc = AR.f(1)

    for t_, k_ in ((ksel, "ksel_c"), (kwin, "kwin_c"), (kcmp, "kcmp_c"), (vcmp, "vcmp_c"),
                   (selm_w.rearrange("p a b -> p (a b)"), "selm")):
        P.add("pool", lambda e, t_=t_: e.memset(t_, 0.0), writes=[k_])
    for hl in range(4):
        P.add("pool", lambda e, hl=hl: e.memset(q_aug[hl], 0.0), writes=[("qa_q", hl), ("qa_s", hl), ("qa_a", hl)])
    P.add("pool", lambda e: e.memset(vsel[:, :, 64:65], 1.0), writes=["vsel_c"])
    P.add("pool", lambda e: e.memset(vwin[:, :, 64:65], 1.0), writes=["vwin_c"])
    P.add("dve", lambda e: e.memset(epsc, LN_EPS), writes=["consts"])
    DMA("sp", identb, identb_d, [], ["consts"], "c1")
    DMA("sp", identf, identf_d, [], ["consts"], "c1")
    DMA("sp", lng1, lng_d[1], [], ["lnc"], "c1")
    DMA("sp", lnb1, lnb_d[1], [], ["lnc"], "c1")
    DMA("sp", cmpmask, L["cmpmask"], [], ["consts"], "c1")
    DMA("sp", winmask, L["winmask"], [], ["consts"], "c1")
    DMA("sp", trimask, L["trimask"], [], ["consts"], "c1")
    DMA("sp", forced.rearrange("p a b -> p (a b)"), L["forced"], [], ["consts"], "c1")
    DMA("sp", future.rearrange("p a b -> p (a b)"), L["future"], [], ["consts"], "c1")
    DMA("sp", ksel[64:96, :], L["eoh"], [], ["ksel_c"], "c2")
    DMA("sp", ksel[96:103, :], L["kalibi"], [], ["ksel_c"], "c2")
    DMA("sp", kwin[96:103, :], L["kalibi"], [], ["kwin_c"], "c2")
    DMA("sp", kcmp[96:103, :], L["kalibic"], [], ["kcmp_c"], "c2")
    DMA("sp", vcmp[:, 64:97], L["vcc"], [], ["vcmp_c"], "c2")
    for kv, (wn, w2n, pn) in enumerate((("w1k", "w2k", "posk"), ("w1v", "w2v", "posv"))):
        for i in range(0, 32, 8):
            DMA("pool", w1[64 * kv:64 * kv + 64, i:i + 8, :], L[wn][:, i:i + 8, :], [], ["wcmp"], "wcmp")
        DMA("pool", w2[kv], L[w2n], [], ["wcmp"], "wcmp")
        DMA("pool", pos[64 * kv:64 * kv + 64, :], L[pn], [], ["wcmp"], "wcmp")
    for c in range(8):
        DMA("pool", wout1[:, c, :], L["wout1"][:, c, :], [], ["wout1"], "wout1")
    for kv in range(2):
        for hc in range(2):
            col = kv * 2 + hc
            for i in range(32):
                MM(psum[0][:, col:col + 1], w1[64 * kv:64 * kv + 64, i, hc * 128:(hc + 1) * 128],
                   pos[64 * kv:64 * kv + 64, i:i + 1], i == 0, i == 31, ["wcmp"], [PS(0)])
    ACT(bh, psum[0][:, 0:4], AF.Copy, [PS(0)], ["bh"])

    if STG < 1:
        return
    rr = {"s": 0, "o": 0, "p": 0, "a": 0, "e": 0}

    def nxt(k, n):
        v = rr[k]
        rr[k] = (v + 1) % n
        return v

    for s in range(NSEQ):
        for j in range(16):
            xs_ = j % 2
            xk = ("xr", xs_)
            DMA("sp", xr[xs_], x1_d[s, j * 128:(j + 1) * 128, :], [], [xk], "xr%d" % xs_)
            for q4 in range(2):
                pb = nxt("a", 2)
                for i4 in range(4):
                    kc = q4 * 4 + i4
                    P.add("pe", lambda e, kc=kc, i4=i4, pb=pb, xs_=xs_: e.transpose(
                        out=psum[pb][:, i4 * 128:(i4 + 1) * 128], in_=xr[xs_][:, kc * 128:(kc + 1) * 128],
                        identity=identf), reads=[xk, "consts"], writes=[PS(pb)])
                ACT(x1T[:, q4 * 4:(q4 + 1) * 4, j * 128:(j + 1) * 128],
                    psum[pb][:, :].rearrange("p (a b) -> p a b", a=4), AF.Copy, [PS(pb)], ["x1T"])

        if STG < 2:
            return
        for g in range(4):
            gi = s * 4 + g
            w2s = gi % 2
            w2k_ = ("W2", w2s)
            for kc in range(0, 8, 2):
                DMA("pool", W1[:, kc:kc + 2, :], L["wg1"][g, :, kc:kc + 2, :], [], ["W1"], "W1")
            for kc in range(0, 8, 4):
                DMA("pool", W2[w2s][:, kc:kc + 4, :], L["wg2"][g, :, kc:kc + 4, :], [], [w2k_], "W2%d" % w2s)
            for hl in range(4):
                DMA("sp", q_aug[hl][96:103, :], L["qalibi"][4 * g + hl], [], [("qa_a", hl)], "qal")
            for pc in range(2):
                for tb in range(4):
                    pb = nxt("a", 2)
                    for kc in range(8):
                        MM(psum[pb][:, :], W1[:, kc, pc * 128:(pc + 1) * 128], x1T[:, kc, tb * 512:(tb + 1) * 512],
                           kc == 0, kc == 7, ["W1", "x1T"], [PS(pb)])
                    ACT(q_aug[2 * pc][0:64, tb * 512:(tb + 1) * 512], psum[pb][0:64, :], AF.Copy,
                        [PS(pb)], [("qa_q", 2 * pc)], scale=0.125)
                    ACT(q_aug[2 * pc + 1][0:64, tb * 512:(tb + 1) * 512], psum[pb][64:128, :], AF.Copy,
                        [PS(pb)], [("qa_q", 2 * pc + 1)], scale=0.125)
            if STG < 3:
                return
            for tb in range(4):
                c0, c1 = tb * 512, (tb + 1) * 512
                pb = nxt("a", 2)
                for kc in range(8):
                    MM(psum[pb][:, :], W1[:, kc, 256:384], x1T[:, kc, c0:c1], kc == 0, kc == 7, ["W1", "x1T"], [PS(pb)])
                ACT(pair[:, c0:c1], psum[pb][:, :], AF.Copy, [PS(pb)], ["pair"])
                pb = nxt("a", 2)
                for kc in range(8):
                    MM(psum[pb][:, :], W1[:, kc, 384:512], x1T[:, kc, c0:c1], kc == 0, kc == 7, ["W1", "x1T"], [PS(pb)])
                P.add("dve", lambda e, pb=pb, c0=c0, c1=c1: e.tensor_copy(out=ksel[0:64, c0:c1], in_=psum[pb][0:64, :]),
                      reads=[PS(pb)], writes=["ksel"])
                ACT(kwin[0:64, c0:c1], psum[pb][64:128, :], AF.Copy, [PS(pb)], ["kwin"])
            if STG < 4:
                return
            for j4 in range(4):
                pb = nxt("a", 2)
                for jj in range(4):
                    j = j4 * 4 + jj
                    for kc in range(8):
                        MM(psum[pb][:, jj * 128:(jj + 1) * 128], x1T[:, kc, j * 128:(j + 1) * 128], W1[:, kc, 512:640],
                           kc == 0, kc == 7, ["W1", "x1T"], [PS(pb)])
                pv = psum[pb][:, :].rearrange("p (a b) -> p a b", a=4)
                P.add("dve", lambda e, pv=pv, j4=j4: e.tensor_copy(out=vsel[:, j4 * 4:(j4 + 1) * 4, 0:64], in_=pv[:, :, 0:64]),
                      reads=[PS(pb)], writes=["vsel"])
                ACT(vwin[:, j4 * 4:(j4 + 1) * 4, 0:64], pv[:, :, 64:128], AF.Copy, [PS(pb)], ["vwin"])
            if STG < 5:
                return
            pb = nxt("a", 2)
            for j in range(16):
                for kc in range(8):
                    MM(psum[pb][:, j * 12:(j + 1) * 12], x1T[:, kc, j * 128:(j + 1) * 128], W2[w2s][:, kc, 256:268],
                       kc == 0, kc == 7, [w2k_, "x1T"], [PS(pb)])
            ACT(gates.rearrange("p a b -> p (a b)"), psum[pb][:, 0:192], AF.Sigmoid, [PS(pb)], ["gates"])
            if STG < 6:
                return
            for kv in range(2):
                for hc in range(2):
                    pb = nxt("a", 2)
                    for i in range(32):
                        MM(psum[pb][:, 0:127], w1[64 * kv:64 * kv + 64, i, hc * 128:(hc + 1) * 128],
                           pair[64 * kv:64 * kv + 64, i:i + 16 * 126 + 1:16], i == 0, i == 31, ["wcmp", "pair"], [PS(pb)])
                    ACT(hs[kv][:, hc, 0:127], psum[pb][:, 0:127], AF.Silu, [PS(pb), "bh"], [("hs", kv)],
                        bias=bh[:, kv * 2 + hc:kv * 2 + hc + 1])
            pb = nxt("a", 2)
            for hc in range(2):
                MM(psum[pb][0:64, 0:127], w2[0][:, hc, :], hs[0][:, hc, 0:127], hc == 0, hc == 1,
                   ["wcmp", ("hs", 0)], [PS(pb)])
            ACT(kcmp[0:64, 0:127], psum[pb][0:64, 0:127], AF.Copy, [PS(pb)], ["kcmp"])
            pb = nxt("a", 2)
            for hc in range(2):
                MM(psum[pb][0:127, 0:64], hs[1][:, hc, 0:127], w2[1][:, hc, :], hc == 0, hc == 1,
                   ["wcmp", ("hs", 1)], [PS(pb)])
            ACT(vcmp[0:127, 0:64], psum[pb][0:127, 0:64], AF.Copy, [PS(pb)], ["vcmp"])

            if STG < 7:
                return
            def branch_epilogue(hl, br, ob, Wd, qt, os_, first, with_imp):
                O = psum[ob][:, 0:4 * Wd].rearrange("p (a b) -> p a b", a=4)
                e_ = nxt("e", 2)
                rk, fk, tk = ("rden", e_), ("fcol", e_), ("tmp_o", e_)
                TS(rden[e_].unsqueeze(2), O[:, :, 64:65], 1e-30, None, ALU.max, None, [PS(ob)], [rk])
                P.add("dve", lambda e: e.reciprocal(out=rden[e_], in_=rden[e_]), reads=[rk], writes=[rk])
                if with_imp:
                    rb = rden[e_].unsqueeze(2).broadcast_to([128, 4, 32])
                    if hl == 0:
                        TT("dve", imp, O[:, :, 65:97], rb, ALU.mult, [PS(ob), rk], ["imp"])
                    else:
                        TT("dve", tmp_i, O[:, :, 65:97], rb, ALU.mult, [PS(ob), rk], ["tmp_i"])
                        TT("pool", imp, imp, tmp_i, ALU.add, ["tmp_i", "imp"], ["imp"])
                gcol = gates[:, qt * 4:(qt + 1) * 4, hl * 3 + br]
                TT("dve", fcol[e_], rden[e_], gcol, ALU.mult, [rk, "gates"], [fk])
                fb = fcol[e_].unsqueeze(2).broadcast_to([128, 4, 64])
                odst = o_tm[os_][:, :, hl * 64:(hl + 1) * 64]
                ok = ("o_tm", os_, hl)
                if first:
                    TT("dve", odst, O[:, :, 0:64], fb, ALU.mult, [PS(ob), fk], [ok])
                else:
                    TT("dve", tmp_o[e_], O[:, :, 0:64], fb, ALU.mult, [PS(ob), fk], [tk])
                    TT("pool", odst, odst, tmp_o[e_], ALU.add, [tk, ok], [ok])

            SBK = [2, 3, 0, 1]
            tiles = []

            def mk_tile(qk=None, act=None, pv=None, post=None):
                tiles.append((qk, act, pv, post))

            def sel_chain(qt):
                TT("dve", impm, imp, forced[:, qt * 4:(qt + 1) * 4, :], ALU.max, ["imp", "consts"], ["impm"])
                TT("dve", impm, impm, future[:, qt * 4:(qt + 1) * 4, :], ALU.min, ["impm", "consts"], ["impm"])
                for qs in range(4):
                    P.add("dve", lambda e, qs=qs: e.max(out=m8[:, qs, :], in_=impm[:, qs, :]), reads=["impm"], writes=["m8"])
                TS(thr.unsqueeze(2), m8[:, :, 7:8], 0.0, None, ALU.max, None, ["m8"], ["thr"])
                for qs in range(4):
                    TS(selm_w[:, qs, 64:96], impm[:, qs, :], thr[:, qs:qs + 1], 1.0, ALU.is_ge, ALU.subtract,
                       ["impm", "thr"], ["selm"])

            def sel_transposes(Q0):
                for qs in range(4):
                    MM(psum[6][:, qs * 128:(qs + 1) * 128], selm_w[:, qs, :], identb, True, True, ["selm", "consts"], [PS(6)])
                for hl in range(4):
                    ACT(q_aug[hl][64:96, Q0:Q0 + 512], psum[6][64:96, :], AF.Copy, [PS(6)], [("qa_s", hl)], scale=30000.0)

            def gate_path(qt, os_):
                for qs in range(4):
                    jt = 4 * qt + qs
                    k_ = (gi * 16 + jt) % 2
                    for kc in range(8):
                        MM(psum[7][:, 0:256], x1T[:, kc, jt * 128:(jt + 1) * 128], W2[w2s][:, kc, 0:256], kc == 0, kc == 7,
                           [w2k_, "x1T"], [PS(7)])
                    ACT(sz[k_], psum[7][:, 0:256], AF.Silu, [PS(7)], [("sz", k_)])
                    TT("dve", ogb[k_], o_tm[os_][:, qs, :], sz[k_], ALU.mult,
                       [("sz", k_)] + [("o_tm", os_, h_) for h_ in range(4)], [("ogb", k_)])
                    p6b = psum[6][:, :].bitcast(BF16)
                    for cc in range(2):
                        P.add("pe", lambda e, cc=cc, k_=k_, p6b=p6b: e.transpose(
                            out=p6b[:, cc * 128:(cc + 1) * 128], in_=ogb[k_][:, cc * 128:(cc + 1) * 128], identity=identb),
                            reads=[("ogb", k_), "consts"], writes=[PS(6)])
                    ACT(ogT[:, 2 * g:2 * g + 2, jt * 128:(jt + 1) * 128],
                        p6b[:, 0:256].rearrange("p (a b) -> p a b", a=2), AF.Copy, [PS(6)], [("ogT", g)])

            for qt in range(4):
                Q0 = qt * 512
                os_ = (gi * 4 + qt) % 2
                for hl in range(4):
                    sb = SBK[nxt("s", 4)]
                    pi = nxt("p", 4)
                    ob = 4 + nxt("o", 2)
                    qk = [("qa_q", hl), ("qa_s", hl), ("qa_a", hl)]

                    def qk_f(sb=sb, hl=hl, Q0=Q0, qk=qk):
                        MM(psum[sb][:, :], kcmp[0:103, :], q_aug[hl][0:103, Q0:Q0 + 512], True, False,
                           ["kcmp", "kcmp_c"] + qk, [PS(sb)])
                        MM(psum[sb][:, :], identb, cmpmask[:, Q0:Q0 + 512], False, True, ["consts"], [PS(sb)])

                    def act_f(sb=sb, pi=pi):
                        ACT(Pb[pi], psum[sb][:, :], AF.Exp, [PS(sb)], [("Pb", pi)])

                    def pv_f(pi=pi, ob=ob):
                        for qs in range(4):
                            MM(psum[ob][:, qs * 97:(qs + 1) * 97], Pb[pi][:, qs * 128:(qs + 1) * 128], vcmp[:, 0:97],
                               True, True, [("Pb", pi), "vcmp", "vcmp_c"], [PS(ob)])

                    def post_f(hl=hl, ob=ob, qt=qt, os_=os_):
                        branch_epilogue(hl, 0, ob, 97, qt, os_, True, True)
                        if hl == 3:
                            sel_chain(qt)

                    mk_tile(qk_f, act_f, pv_f, post_f)
                for hl in range(4):
                    qk = [("qa_q", hl), ("qa_s", hl), ("qa_a", hl)]
                    ob = 4 + nxt("o", 2)
                    plan = []
                    for kt in range(max(0, 4 * qt - 2), 4 * qt + 4):
                        K0 = kt * 128
                        lo, hi = max(K0, Q0), min(K0 + 384, Q0 + 512)
                        if hi > lo:
                            plan.append((kt, K0, lo, hi))
                    lastk = {}
                    for (kt, K0, lo, hi) in plan:
                        for qs in range((lo - Q0) // 128, (hi - Q0) // 128):
                            lastk[qs] = kt
                    firstmm = True
                    for ti, (kt, K0, lo, hi) in enumerate(plan):
                        sb = SBK[nxt("s", 4)]
                        pi = nxt("p", 4)
                        c0, c1, m0 = lo - Q0, hi - Q0, lo - K0

                        def qk_f(sb=sb, hl=hl, qk=qk, K0=K0, lo=lo, hi=hi, c0=c0, c1=c1, m0=m0):
                            MM(psum[sb][:, c0:c1], kwin[0:103, K0:K0 + 128], q_aug[hl][0:103, lo:hi], True, False,
                               ["kwin", "kwin_c"] + qk, [PS(sb)])
                            MM(psum[sb][:, c0:c1], identb, winmask[:, m0:m0 + (hi - lo)], False, True, ["consts"], [PS(sb)])

                        def act_f(sb=sb, pi=pi, c0=c0, c1=c1):
                            ACT(Pb[pi][:, c0:c1], psum[sb][:, c0:c1], AF.Exp, [PS(sb)], [("Pb", pi)])

                        pvl = []
                        for qs in range(c0 // 128, c1 // 128):
                            pvl.append((qs, firstmm, lastk[qs] == kt))
                            firstmm = False

                        def pv_f(pi=pi, ob=ob, kt=kt, pvl=pvl):
                            for (qs, st_, sp_) in pvl:
                                MM(psum[ob][:, qs * 65:(qs + 1) * 65], Pb[pi][:, qs * 128:(qs + 1) * 128], vwin[:, kt, :],
                                   st_, sp_, [("Pb", pi), "vwin", "vwin_c"], [PS(ob)])

                        post_f = None
                        if ti == len(plan) - 1:
                            def post_f(hl=hl, ob=ob, qt=qt, os_=os_):
                                branch_epilogue(hl, 2, ob, 65, qt, os_, False, False)
                        mk_tile(qk_f, act_f, pv_f, post_f)
                mk_tile(qk=(lambda Q0=Q0: sel_transposes(Q0)))
                for hl in range(4):
                    qk = [("qa_q", hl), ("qa_s", hl), ("qa_a", hl)]
                    ob = 4 + nxt("o", 2)
                    nk = 4 * qt + 4
                    firstmm = True
                    for kt in range(nk):
                        K0 = kt * 128
                        dq_ = kt - 4 * qt
                        c0 = max(dq_, 0) * 128
                        sb = SBK[nxt("s", 4)]
                        pi = nxt("p", 4)

                        def qk_f(sb=sb, hl=hl, qk=qk, K0=K0, Q0=Q0, c0=c0, dq_=dq_):
                            MM(psum[sb][:, c0:512], ksel[0:103, K0:K0 + 128], q_aug[hl][0:103, Q0 + c0:Q0 + 512], True, dq_ < 0,
                               ["ksel", "ksel_c"] + qk, [PS(sb)])
                            if dq_ >= 0:
                                MM(psum[sb][:, c0:c0 + 128], identb, trimask, False, True, ["consts"], [PS(sb)])

                        def act_f(sb=sb, pi=pi, c0=c0):
                            ACT(Pb[pi][:, c0:512], psum[sb][:, c0:512], AF.Exp, [PS(sb)], [("Pb", pi)])

                        pvl = []
                        for qs in range(c0 // 128, 4):
                            pvl.append((qs, firstmm, kt == 4 * qt + qs))
                            firstmm = False

                        def pv_f(pi=pi, ob=ob, kt=kt, pvl=pvl):
                            for (qs, st_, sp_) in pvl:
                                MM(psum[ob][:, qs * 65:(qs + 1) * 65], Pb[pi][:, qs * 128:(qs + 1) * 128], vsel[:, kt, :],
                                   st_, sp_, [("Pb", pi), "vsel", "vsel_c"], [PS(ob)])

                        post_f = None
                        if kt == nk - 1:
                            def post_f(hl=hl, ob=ob, qt=qt, os_=os_):
                                branch_epilogue(hl, 1, ob, 65, qt, os_, False, False)
                                if hl == 3:
                                    gate_path(qt, os_)
                        mk_tile(qk_f, act_f, pv_f, post_f)

            SKEW = 2
            for i_, (qk_f, act_f, pv_f, post_f) in enumerate(tiles):
                if qk_f is not None:
                    qk_f()
                if act_f is not None:
                    act_f()
                if i_ >= SKEW:
                    t2 = tiles[i_ - SKEW]
                    if t2[2] is not None:
                        t2[2]()
                    if t2[3] is not None:
                        t2[3]()
            for t2 in tiles[max(0, len(tiles) - SKEW):]:
                if t2[2] is not None:
                    t2[2]()
                if t2[3] is not None:
                    t2[3]()

        for jt in range(16):
            xs_ = jt % 2
            xk = ("xr", xs_)
            DMA("sp", xr[xs_], x1_d[s, jt * 128:(jt + 1) * 128, :], [], [xk], "xr%d" % xs_)
            for hh in range(2):
                pb = nxt("a", 2)
                for c in range(8):
                    MM(psum[pb][:, :], ogT[:, c, jt * 128:(jt + 1) * 128], wout1[:, c, hh * 512:(hh + 1) * 512],
                       c == 0, c == 7, [("ogT", c // 2), "wout1"], [PS(pb)])
                P.add("dve", lambda e, hh=hh, pb=pb, xs_=xs_: e.scalar_tensor_tensor(
                    out=xr[xs_][:, hh * 512:(hh + 1) * 512], in0=xr[xs_][:, hh * 512:(hh + 1) * 512],
                    scalar=DN_ALPHA, in1=psum[pb][:, :], op0=ALU.mult, op1=ALU.add), reads=[PS(pb), xk], writes=[xk])
            layer_norm_tile(P, xr[xs_], xk, lng1, lnb1, stat[xs_], ("stat1", xs_), epsc)
            DMA("sp", out_d[s, jt * 128:(jt + 1) * 128, :], xr[xs_], [xk], [], "ost%d" % xs_)


def build_program(do_l0=True, do_l1=True):
    nc = bass.Bass("TRN2", target_bir_lowering=False)
    dt = {}

    def din(name, shape, dtype=F32):
        dt[name] = nc.dram_tensor(name, list(shape), dtype, kind="ExternalInput").ap()
        return dt[name]

    x_d = din("x", [NSEQ, S, D])
    lng_d = din("lng", [2, 128, D])
    lnb_d = din("lnb", [2, 128, D])
    w_in0_d = din("w_in0", [128, 8, 4096])
    w_grp_d = din("w_grp", [128, 16, 512])
    w_out0_d = din("w_out0", [128, 16, 1024])
    scale_d = din("pscale", [128, 16])
    poolA_d = din("poolA", [128, 4 * 3 * 128], BF16)
    invc_d = din("invc", [128, 4 * 128])
    identf_d = din("identf", [128, 128])
    identb_d = din("identb", [128, 128], BF16)
    x1kind = "Internal" if (do_l0 and do_l1) else ("ExternalOutput" if do_l0 else "ExternalInput")
    x1_d = nc.dram_tensor("x1s", [NSEQ, S, D], F32, kind=x1kind).ap()
    out_d = nc.dram_tensor("out", [NSEQ, S, D], F32, kind="ExternalOutput").ap()
    L1D = _l1_dram(nc, din)

    stack = contextlib.ExitStack()
    with stack:
        ACOLS = 52800
        ar_t = stack.enter_context(nc.sbuf_tensor("arena", [128, ACOLS], F32))
        AR = Arena(ar_t[:], ACOLS)
        psum = [stack.enter_context(nc.psum_tensor("ps%d" % i, [128, 512], F32)) for i in range(8)]
        P = Prog(nc)

        def PS(i):
            return ("ps", i)

        if do_l0:
            AR.reset()
            w_in0 = AR.b(8, 4096)
            w_grp = AR.b(16, 512)
            w_out0 = AR.b(16, 1024)
            poolA = AR.b(12, 128)
            identb = AR.b(128)
            xT0 = [AR.b(8, 256) for _ in range(2)]
            u_tm = AR.b(12, 512)
            mT = [AR.b(4, 256) for _ in range(2)]
            gT = AR.b(16, 256)
            identf = AR.f(128)
            lng0 = AR.f(1024)
            lnb0 = AR.f(1024)
            invc = AR.f(4, 128)
            pscale = AR.f(16)
            epsc = AR.f(1)
            xs = [AR.f(2, 1024) for _ in range(2)]
            siluz = [AR.f(4, 256) for _ in range(2)]
            rbuf = [AR.f(1024) for _ in range(2)]
            stat = [AR.f(16) for _ in range(2)]

            P.add("sp", lambda e: e.dma_start(out=identf, in_=identf_d), writes=["consts"], dma_key="c0")
            P.add("sp", lambda e: e.dma_start(out=identb, in_=identb_d), writes=["consts"], dma_key="c0")
            P.add("sp", lambda e: e.dma_start(out=poolA.rearrange("p a b -> p (a b)"), in_=poolA_d), writes=["consts"], dma_key="c0")
            P.add("sp", lambda e: e.dma_start(out=invc.rearrange("p a b -> p (a b)"), in_=invc_d), writes=["consts"], dma_key="c0")
            P.add("sp", lambda e: e.dma_start(out=pscale, in_=scale_d), writes=["consts"], dma_key="c0")
            P.add("sp", lambda e: e.dma_start(out=lng0, in_=lng_d[0]), writes=["lnc"], dma_key="c0")
            P.add("sp", lambda e: e.dma_start(out=lnb0, in_=lnb_d[0]), writes=["lnc"], dma_key="c0")
            P.add("dve", lambda e: e.memset(epsc, LN_EPS), writes=["consts"])
            for kc in range(8):
                for hh in range(2):
                    P.add("pool", lambda e, kc=kc, hh=hh: e.dma_start(out=w_in0[:, kc, hh * 2048:(hh + 1) * 2048],
                                                                      in_=w_in0_d[:, kc, hh * 2048:(hh + 1) * 2048]),
                          writes=["w_in0"], dma_key="w_in0")
            for c in range(16):
                P.add("pool", lambda e, c=c: e.dma_start(out=w_grp[:, c, :], in_=w_grp_d[:, c, :]),
                      writes=["w_grp"], dma_key="w_grp")
            for c in range(16):
                P.add("pool", lambda e, c=c: e.dma_start(out=w_out0[:, c, :], in_=w_out0_d[:, c, :]),
                      writes=["w_out0"], dma_key="w_out0")

            NB = S // 256
            for s in range(NSEQ):
                for b in range(NB):
                    it = s * NB + b
                    sl = it % 2
                    t0 = b * 256
                    xk = ("xs", sl)
                    for j in range(2):
                        P.add("sp", lambda e, j=j, sl=sl, s=s, t0=t0: e.dma_start(
                            out=xs[sl][:, j, :], in_=x_d[s, t0 + j * 128:t0 + (j + 1) * 128, :]),
                            writes=[xk], dma_key="xs%d" % sl)
                    xtk = ("xT0", sl)
                    for j in range(2):
                        for q4 in range(2):
                            pb = (j * 2 + q4) % 2
                            for i4 in range(4):
                                kc = q4 * 4 + i4
                                P.add("pe", lambda e, j=j, kc=kc, i4=i4, pb=pb, sl=sl: e.transpose(
                                    out=psum[pb][:, i4 * 128:(i4 + 1) * 128], in_=xs[sl][:, j, kc * 128:(kc + 1) * 128],
                                    identity=identf), reads=[xk, "consts"], writes=[PS(pb)])
                            P.add("act", lambda e, j=j, q4=q4, pb=pb, sl=sl: e.activation(
                                out=xT0[sl][:, q4 * 4:(q4 + 1) * 4, j * 128:(j + 1) * 128],
                                in_=psum[pb][:, :].rearrange("p (a b) -> p a b", a=4), func=AF.Copy),
                                reads=[PS(pb)], writes=[xtk])
                    for g in range(4):
                        gi = it * 4 + g
                        gs = gi % 2
                        szk = ("siluz", gs)
                        mk = ("mT", gs)
                        for j in range(2):
                            n = b * 2 + j
                            slot = n % 3
                            pb = 2 + (j % 2)
                            for kc in range(8):
                                P.add("pe", lambda e, kc=kc, j=j, g=g, pb=pb, sl=sl: e.matmul(
                                    psum[pb][:, :], lhsT=xT0[sl][:, kc, j * 128:(j + 1) * 128],
                                    rhs=w_in0[:, kc, g * 512:(g + 1) * 512], start=(kc == 0), stop=(kc == 7)),
                                    reads=[xtk, "w_in0"], writes=[PS(pb)])
                            P.add("act", lambda e, g=g, slot=slot, pb=pb: e.activation(
                                out=u_tm[:, g * 3 + slot, :], in_=psum[pb][:, :], func=AF.Copy),
                                reads=[PS(pb)], writes=[("u", g, slot)])
                        for c2 in range(2):
                            pb = 4 + c2
                            for ci in range(2):
                                c = c2 * 2 + ci
                                col = 2048 + g * 512 + c * 128
                                for kc in range(8):
                                    P.add("pe", lambda e, kc=kc, col=col, ci=ci, pb=pb, sl=sl: e.matmul(
                                        psum[pb][:, ci * 256:(ci + 1) * 256], lhsT=w_in0[:, kc, col:col + 128],
                                        rhs=xT0[sl][:, kc, :], start=(kc == 0), stop=(kc == 7)),
                                        reads=[xtk, "w_in0"], writes=[PS(pb)])
                            P.add("act", lambda e, c2=c2, pb=pb, gs=gs: e.activation(
                                out=siluz[gs][:, c2 * 2:(c2 + 1) * 2, :],
                                in_=psum[pb][:, :].rearrange("p (a b) -> p a b", a=2), func=AF.Silu),
                                reads=[PS(pb)], writes=[szk])
                        for c2 in range(2):
                            pb = 6 + c2
                            for ci in range(2):
                                c = c2 * 2 + ci
                                for j in range(2):
                                    n = b * 2 + j
                                    slot = n % 3
                                    pslot = (n - 1) % 3
                                    first = (n == 0)
                                    o = psum[pb][:, ci * 256 + j * 128: ci * 256 + (j + 1) * 128]
                                    if first:
                                        P.add("pe", lambda e, o=o, g=g, slot=slot, c=c: e.matmul(
                                            o, lhsT=u_tm[:, g * 3 + slot, c * 128:(c + 1) * 128],
                                            rhs=poolA[:, g * 3 + 2, :], start=True, stop=True),
                                            reads=[("u", g, slot), "consts"], writes=[PS(pb)])
                                    else:
                                        P.add("pe", lambda e, o=o, g=g, slot=slot, c=c: e.matmul(
                                            o, lhsT=u_tm[:, g * 3 + slot, c * 128:(c + 1) * 128],
                                            rhs=poolA[:, g * 3 + 0, :], start=True, stop=False),
                                            reads=[("u", g, slot), "consts"], writes=[PS(pb)])
                                        P.add("pe", lambda e, o=o, g=g, pslot=pslot, c=c: e.matmul(
                                            o, lhsT=u_tm[:, g * 3 + pslot, c * 128:(c + 1) * 128],
                                            rhs=poolA[:, g * 3 + 1, :], start=False, stop=True),
                                            reads=[("u", g, pslot), "consts"], writes=[PS(pb)])
                            P.add("act", lambda e, c2=c2, pb=pb, gs=gs: e.activation(
                                out=mT[gs][:, c2 * 2:(c2 + 1) * 2, :],
                                in_=psum[pb][:, :].rearrange("p (a b) -> p a b", a=2), func=AF.Copy),
                                reads=[PS(pb)], writes=[mk])
                            if b == 0:
                                P.add("dve", lambda e, c2=c2, pb=pb, gs=gs, g=g: e.tensor_tensor(
                                    out=mT[gs][:, c2 * 2:(c2 + 1) * 2, 0:128],
                                    in0=psum[pb][:, :].rearrange("p (a b) -> p a b", a=2)[:, :, 0:128],
                                    in1=invc[:, g:g + 1, :].broadcast_to([128, 2, 128]), op=ALU.mult),
                                    reads=[PS(pb), "consts"], writes=[mk])
                        for d2 in range(2):
                            pb = d2
                            for di in range(2):
                                d = d2 * 2 + di
                                for cc in range(4):
                                    P.add("pe", lambda e, cc=cc, d=d, di=di, g=g, pb=pb, gs=gs: e.matmul(
                                        psum[pb][:, di * 256:(di + 1) * 256],
                                        lhsT=w_grp[:, g * 4 + cc, d * 128:(d + 1) * 128], rhs=mT[gs][:, cc, :],
                                        start=(cc == 0), stop=(cc == 3)), reads=[mk, "w_grp"], writes=[PS(pb)])
                            for di in range(2):
                                d = d2 * 2 + di
                                ch = g * 4 + d
                                P.add("dve", lambda e, d=d, di=di, ch=ch, pb=pb, gs=gs: e.scalar_tensor_tensor(
                                    out=gT[:, ch, :], in0=psum[pb][:, di * 256:(di + 1) * 256],
                                    scalar=pscale[:, ch:ch + 1], in1=siluz[gs][:, d, :], op0=ALU.mult, op1=ALU.mult),
                                    reads=[PS(pb), szk, "consts"], writes=[("gT", ch)])
                    for j in range(2):
                        rs = (it * 2 + j) % 2
                        rk = ("r", rs)
                        for hh in range(2):
                            pb = 2 + hh
                            for ch in range(16):
                                P.add("pe", lambda e, ch=ch, j=j, hh=hh, pb=pb: e.matmul(
                                    psum[pb][:, :], lhsT=gT[:, ch, j * 128:(j + 1) * 128],
                                    rhs=w_out0[:, ch, hh * 512:(hh + 1) * 512], start=(ch == 0), stop=(ch == 15)),
                                    reads=[("gT", ch), "w_out0"], writes=[PS(pb)])
                            P.add("dve", lambda e, j=j, hh=hh, pb=pb, rs=rs, sl=sl: e.scalar_tensor_tensor(
                                out=rbuf[rs][:, hh * 512:(hh + 1) * 512], in0=xs[sl][:, j, hh * 512:(hh + 1) * 512],
                                scalar=DN_ALPHA, in1=psum[pb][:, :], op0=ALU.mult, op1=ALU.add),
                                reads=[PS(pb), xk], writes=[rk])
                        layer_norm_tile(P, rbuf[rs], rk, lng0, lnb0, stat[rs], ("stat", rs), epsc)
                        P.add("sp", lambda e, rs=rs, s=s, t0=t0, j=j: e.dma_start(
                            out=x1_d[s, t0 + j * 128:t0 + (j + 1) * 128, :], in_=rbuf[rs]),
                            reads=[rk], dma_key="st%d" % rs)

        P.barrier()
        if do_l1:
            AR.reset()
            _build_l1(nc, P, AR, psum, L1D, x1_d, out_d, lng_d, lnb_d, identf_d, identb_d)
        P.emit(stack)
    return nc


def _host_common(inputs):
    f = lambda a: np.ascontiguousarray(np.asarray(a, dtype=np.float32))
    m = {}
    lng = f(inputs["ln_g"])
    lnb = f(inputs["ln_b"])
    m["lng"] = np.ascontiguousarray(np.broadcast_to(lng[:, None, :], (2, 128, D)))
    m["lnb"] = np.ascontiguousarray(np.broadcast_to(lnb[:, None, :], (2, 128, D)))
    w = f(inputs["pool_w_in"])[0]
    m["w_in0"] = np.ascontiguousarray(w.reshape(8, 128, 4096).transpose(1, 0, 2))
    wg = f(inputs["pool_w_grp"])[0]
    m["w_grp"] = np.ascontiguousarray(wg.reshape(4, 4, 128, 512).transpose(2, 0, 1, 3).reshape(128, 16, 512))
    wo = f(inputs["pool_w_out"])[0]
    m["w_out0"] = np.ascontiguousarray(wo.reshape(16, 128, 1024).transpose(1, 0, 2))
    m["pscale"] = np.ascontiguousarray(f(inputs["pool_scale"])[0].reshape(16, 128).T)
    pa, inv = _pool_tables()
    m["poolA"] = pa
    m["invc"] = inv
    m["identf"] = np.eye(128, dtype=np.float32)
    m["identb"] = _bf(np.eye(128))
    m.update(_host_l1(inputs))
    return m


_NC_CACHE = {}


def kernel(**inputs):
    x = np.ascontiguousarray(np.asarray(inputs["x"], dtype=np.float32))
    common = _host_common(inputs)
    if "nc" not in _NC_CACHE:
        _NC_CACHE["nc"] = build_program()
    nc = _NC_CACHE["nc"]
    in_maps = []
    for c in range(NCORES):
        m = dict(common)
        m["x"] = x[c * NSEQ:(c + 1) * NSEQ]
        in_maps.append(m)
    res = run_bass_kernel_spmd(nc, in_maps, core_ids=list(range(NCORES)))
    out = np.concatenate([np.asarray(r["out"]) for r in res.results], axis=0)
    return out.astype(np.float32)
```

```python
import contextlib
import numpy as np
import ml_dtypes
import concourse.bass as bass
import concourse.mybir as mybir
from concourse.bass_utils import run_bass_kernel_spmd

F32 = mybir.dt.float32
BF16 = mybir.dt.bfloat16
AF = mybir.ActivationFunctionType
ALU = mybir.AluOpType

D = 1024
S = 2048
NSEQ = 2
NCORES = 8
DN_ALPHA = float((2.0 * 2) ** 0.25)
LN_EPS = 1e-5
POOL_WINDOWS = (2, 4, 8, 16)
NEGM = -30000.0


class _Op:
    __slots__ = ("eng", "fn", "deps", "is_dma", "key", "awaited", "count", "idx")


class Prog:
    ENGS = ("pe", "act", "dve", "pool", "sp")

    def __init__(self, nc):
        self.nc = nc
        self.ops = {e: [] for e in self.ENGS}
        self.last_w = {}
        self.readers = {}
        self.dma_counts = {}
        self.last_dma = {}
        self.bar = {e: [] for e in self.ENGS}

    def _dep(self, op, a):
        if a is None or a is op:
            return
        if (not a.is_dma) and a.eng == op.eng and a.eng == "pe":
            return
        if a.is_dma and op.is_dma and a.key == op.key:
            return
        op.deps.append(a)
        if not a.is_dma:
            a.awaited = True

    def add(self, eng, fn, reads=(), writes=(), dma_key=None):
        op = _Op()
        op.eng = eng
        op.fn = fn
        op.deps = []
        op.is_dma = dma_key is not None
        op.key = dma_key
        op.awaited = False
        op.count = None
        for a in self.bar[eng]:
            self._dep(op, a)
        self.bar[eng] = []
        if eng != "pe":
            extra = [("psx", r[1]) for r in reads if isinstance(r, tuple) and r[0] == "ps"]
            extra += [("psx", w[1]) for w in writes if isinstance(w, tuple) and w[0] == "ps"]
            writes = list(writes) + extra
        for r in reads:
            self._dep(op, self.last_w.get(r))
        for w in writes:
            self._dep(op, self.last_w.get(w))
            for a in self.readers.get(w, ()):
                self._dep(op, a)
        for r in reads:
            self.readers.setdefault(r, []).append(op)
        for w in writes:
            self.last_w[w] = op
            self.readers[w] = []
        if op.is_dma:
            c = self.dma_counts.get(dma_key, 0) + 1
            self.dma_counts[dma_key] = c
            op.count = c
            self.last_dma[dma_key] = op
        op.idx = len(self.ops[eng])
        self.ops[eng].append(op)
        return op

    def barrier(self):
        deps = []
        for e in self.ENGS:
            for op in reversed(self.ops[e]):
                if not op.is_dma:
                    deps.append(op)
                    break
        deps.extend(self.last_dma.values())
        for e in self.ENGS:
            self.bar[e] = list(deps)

    def emit(self, stack):
        nc = self.nc
        esem = {e: stack.enter_context(nc.semaphore("s_" + e)) for e in self.ENGS}
        dsem = {k: stack.enter_context(nc.semaphore("d_%d" % i)) for i, k in enumerate(self.dma_counts)}
        for e in self.ENGS:
            c = 0
            for op in self.ops[e]:
                if (not op.is_dma) and op.awaited:
                    c += 1
                    op.count = c
        block = stack.enter_context(nc.Block())
        final = [(dsem[k], 16 * c) for k, c in self.dma_counts.items()]

        def run(ename, eh, is_last=False):
            waited = {}
            for op in self.ops[ename]:
                for a in op.deps:
                    if a.is_dma:
                        sem, val = dsem[a.key], 16 * a.count
                    else:
                        sem, val = esem[a.eng], a.count
                    sid = id(sem)
                    if waited.get(sid, 0) >= val:
                        continue
                    waited[sid] = val
                    eh.wait_ge(sem, val)
                ins = op.fn(eh)
                if op.is_dma:
                    ins.then_inc(dsem[op.key], 16)
                elif op.awaited:
                    ins.then_inc(esem[ename], 1)
            if is_last:
                for sem, val in final:
                    eh.wait_ge(sem, val)

        @block.tensor
        def _(eh):
            run("pe", eh)

        @block.scalar
        def _(eh):
            run("act", eh)

        @block.vector
        def _(eh):
            run("dve", eh)

        @block.gpsimd
        def _(eh):
            run("pool", eh)

        @block.sync
        def _(eh):
            run("sp", eh, is_last=True)


class Arena:
    def __init__(self, ap, ncols):
        self.ap = ap
        self.n = ncols
        self.off = 0

    def reset(self):
        self.off = 0

    def _shape(self, v, shape):
        if len(shape) == 2:
            v = v.rearrange("p (a b) -> p a b", a=shape[0])
        elif len(shape) == 3:
            v = v.rearrange("p (a b c) -> p a b c", a=shape[0], b=shape[1])
        return v

    def f(self, *shape):
        cols = int(np.prod(shape))
        assert self.off + cols <= self.n, ("arena overflow", self.off, cols, self.n)
        v = self.ap[:, self.off:self.off + cols]
        self.off += cols
        return self._shape(v, shape)

    def b(self, *shape):
        cols = int(np.prod(shape))
        c32 = (cols + 1) // 2
        assert self.off + c32 <= self.n, ("arena overflow", self.off, c32, self.n)
        v = self.ap[:, self.off:self.off + c32].bitcast(BF16)[:, 0:cols]
        self.off += c32
        return self._shape(v, shape)


def _bf(a):
    return np.ascontiguousarray(np.asarray(a, dtype=np.float32).astype(ml_dtypes.bfloat16))


def _pool_tables():
    A = np.zeros((4, 3, 128, 128), np.float32)
    inv = np.zeros((4, 128), np.float32)
    for g, w in enumerate(POOL_WINDOWS):
        for t in range(128):
            for tp in range(t - w + 1, t + 1):
                if tp >= 0:
                    A[g, 0, tp, t] += 1.0 / w
                else:
                    A[g, 1, tp + 128, t] += 1.0 / w
            A[g, 0, t, t] -= 1.0
            cnt = min(t + 1, w)
            for tp in range(max(0, t - w + 1), t + 1):
                A[g, 2, tp, t] += 1.0
            A[g, 2, t, t] -= cnt
            inv[g, t] = 1.0 / cnt
    At = np.transpose(A, (2, 0, 1, 3)).reshape(128, 4 * 3 * 128)
    invb = np.broadcast_to(inv.reshape(1, 4 * 128), (128, 4 * 128))
    return _bf(At), np.ascontiguousarray(invb, dtype=np.float32)


def layer_norm_tile(P, r_ap, rkey, lng, lnb, stat, skey, epsc):
    st6 = stat[:, 0:12].rearrange("p (a b) -> p a b", a=2)
    mv = stat[:, 12:14]
    P.add("dve", lambda e: e.bn_stats(out=st6[:, 0, :], in_=r_ap[:, 0:512]), reads=[rkey], writes=[skey])
    P.add("dve", lambda e: e.bn_stats(out=st6[:, 1, :], in_=r_ap[:, 512:1024]), reads=[rkey], writes=[skey])
    P.add("dve", lambda e: e.bn_aggr(out=mv, in_=stat[:, 0:12]), reads=[skey], writes=[skey])
    P.add("act", lambda e: e.activation(out=stat[:, 14:15], in_=stat[:, 13:14], func=AF.Sqrt,
                                        bias=epsc, scale=1.0), reads=[skey, "consts"], writes=[skey])
    P.add("dve", lambda e: e.reciprocal(out=stat[:, 14:15], in_=stat[:, 14:15]), reads=[skey], writes=[skey])
    P.add("dve", lambda e: e.scalar_tensor_tensor(out=stat[:, 15:16], in0=stat[:, 12:13], scalar=-1.0,
                                                  in1=stat[:, 14:15], op0=ALU.mult, op1=ALU.mult),
          reads=[skey], writes=[skey])
    P.add("act", lambda e: e.activation(out=r_ap, in_=r_ap, func=AF.Identity, bias=stat[:, 15:16],
                                        scale=stat[:, 14:15]), reads=[skey, rkey], writes=[rkey])
    P.add("pool", lambda e: e.tensor_tensor(out=r_ap, in0=r_ap, in1=lng, op=ALU.mult),
          reads=[rkey, "lnc"], writes=[rkey])
    P.add("pool", lambda e: e.tensor_tensor(out=r_ap, in0=r_ap, in1=lnb, op=ALU.add),
          reads=[rkey, "lnc"], writes=[rkey])


def _slopes():
    h = np.arange(1, 17, dtype=np.float32)
    return (2.0 ** (-8.0 * h / 16.0)).astype(np.float32)


def _l1_tables():
    t = {}
    sl = _slopes()
    tq = np.arange(S, dtype=np.float64)
    qal = np.zeros((16, 7, S), np.float32)
    for h in range(16):
        s0 = float(sl[h])
        s1 = float(np.float32(s0).astype(ml_dtypes.bfloat16))
        s2 = float(np.float32(s0 - s1).astype(ml_dtypes.bfloat16))
        s3 = float(np.float32(s0 - s1 - s2).astype(ml_dtypes.bfloat16))
        qal[h, 0] = -s0 * tq
        for i, si in enumerate((s1, s2, s3)):
            qal[h, 1 + i] = 64.0 * si
            qal[h, 4 + i] = si
    t["qalibi"] = _bf(qal)
    kal = np.zeros((7, S), np.float32)
    kal[0] = 1.0
    kal[1:4] = (np.arange(S) // 64)[None, :]
    kal[4:7] = (np.arange(S) % 64)[None, :]
    t["kalibi"] = _bf(kal)
    kc = np.zeros((7, 128), np.float32)
    ce = np.arange(127) * 16 + 31
    kc[0] = 1.0
    kc[1:4, :127] = (ce // 64)[None, :]
    kc[4:7, :127] = (ce % 64)[None, :]
    t["kalibic"] = _bf(kc)
    E = np.zeros((32, S), np.float32)
    E[np.arange(S) // 64, np.arange(S)] = 1.0
    t["eoh"] = _bf(E)
    cm = np.full((128, S), NEGM, np.float32)
    cm[:127] = np.where(np.arange(S)[None, :] >= ce[:, None], 0.0, NEGM)
    t["cmpmask"] = _bf(cm)
    dk = np.arange(128)[:, None]
    dq = np.arange(384)[None, :]
    t["winmask"] = _bf(np.where((dq - dk >= 0) & (dq - dk < 256), 0.0, NEGM))
    dq = np.arange(128)[None, :]
    t["trimask"] = _bf(np.where(dk <= dq, 0.0, NEGM))
    vcc = np.zeros((128, 33), np.float32)
    vcc[:, 0] = 1.0
    c0 = np.arange(127)[:, None] * 16
    j0 = np.arange(32)[None, :] * 64
    vcc[:127, 1:] = ((c0 < j0 + 64) & (c0 + 32 > j0)).astype(np.float32)
    t["vcc"] = _bf(vcc)
    q = np.arange(128)[:, None, None]
    qt = np.arange(16)[None, :, None]
    j = np.arange(32)[None, None, :]
    cur = (qt * 128 + q) // 64
    forced = (j == 0) | (j == cur) | (j == cur - 1)
    t["forced"] = np.ascontiguousarray(np.where(forced, 1e9, 0.0).astype(np.float32).reshape(128, 512))
    t["future"] = np.ascontiguousarray(np.where(j > cur, -1e30, 3e38).astype(np.float32).reshape(128, 512))
    return t


def _l1_dram(nc, din):
    L = {}
    L["wg1"] = din("wg1", [4, 128, 8, 640])
    L["wg2"] = din("wg2", [4, 128, 8, 268])
    L["w1k"] = din("w1k", [64, 32, 256])
    L["w1v"] = din("w1v", [64, 32, 256])
    L["w2k"] = din("w2k", [128, 2, 64])
    L["w2v"] = din("w2v", [128, 2, 64])
    L["posk"] = din("posk", [64, 32])
    L["posv"] = din("posv", [64, 32])
    L["wout1"] = din("wout1", [128, 8, 1024])
    L["qalibi"] = din("qalibi", [16, 7, S], BF16)
    L["kalibi"] = din("kalibi", [7, S], BF16)
    L["kalibic"] = din("kalibic", [7, 128], BF16)
    L["eoh"] = din("eoh", [32, S], BF16)
    L["cmpmask"] = din("cmpmask", [128, S], BF16)
    L["winmask"] = din("winmask", [128, 384], BF16)
    L["trimask"] = din("trimask", [128, 128], BF16)
    L["vcc"] = din("vcc", [128, 33], BF16)
    L["forced"] = din("forced", [128, 512])
    L["future"] = din("future", [128, 512])
    return L


def _host_l1(inputs):
    f = lambda a: np.ascontiguousarray(np.asarray(a, dtype=np.float32))
    m = {}
    W = f(inputs["nsa_w_in"])[0].reshape(8, 128, 3632).transpose(1, 0, 2)
    wg1 = np.zeros((4, 128, 8, 640), np.float32)
    wg2 = np.zeros((4, 128, 8, 268), np.float32)
    for g in range(4):
        wg1[g, :, :, 0:256] = W[:, :, 256 * g:256 * g + 256]
        wg1[g, :, :, 256:320] = W[:, :, 1024 + 64 * g:1024 + 64 * g + 64]
        wg1[g, :, :, 320:384] = W[:, :, 1280 + 64 * g:1280 + 64 * g + 64]
        wg1[g, :, :, 384:448] = W[:, :, 1536 + 64 * g:1536 + 64 * g + 64]
        wg1[g, :, :, 448:512] = W[:, :, 2048 + 64 * g:2048 + 64 * g + 64]
        wg1[g, :, :, 512:576] = W[:, :, 1792 + 64 * g:1792 + 64 * g + 64]
        wg1[g, :, :, 576:640] = W[:, :, 2304 + 64 * g:2304 + 64 * g + 64]
        wg2[g, :, :, 0:256] = W[:, :, 2560 + 256 * g:2560 + 256 * g + 256]
        wg2[g, :, :, 256:268] = W[:, :, 3584 + 12 * g:3584 + 12 * g + 12]
    m["wg1"] = wg1
    m["wg2"] = wg2
    m["w1k"] = np.ascontiguousarray(f(inputs["nsa_cmp_w1_k"])[0].reshape(32, 64, 256).transpose(1, 0, 2))
    m["w1v"] = np.ascontiguousarray(f(inputs["nsa_cmp_w1_v"])[0].reshape(32, 64, 256).transpose(1, 0, 2))
    m["w2k"] = np.ascontiguousarray(f(inputs["nsa_cmp_w2_k"])[0].reshape(2, 128, 64).transpose(1, 0, 2))
    m["w2v"] = np.ascontiguousarray(f(inputs["nsa_cmp_w2_v"])[0].reshape(2, 128, 64).transpose(1, 0, 2))
    m["posk"] = np.ascontiguousarray(f(inputs["nsa_cmp_pos_k"])[0].T)
    m["posv"] = np.ascontiguousarray(f(inputs["nsa_cmp_pos_v"])[0].T)
    m["wout1"] = np.ascontiguousarray(f(inputs["nsa_w_out"])[0].reshape(8, 128, 1024).transpose(1, 0, 2))
    m.update(_l1_tables())
    return m


L1_STAGE = 99


def _build_l1(nc, P, AR, psum, L, x1_d, out_d, lng_d, lnb_d, identf_d, identb_d):
    STG = L1_STAGE
    def PS(i):
        return ("ps", i)

    def MM(out, lhsT, rhs, start, stop, reads, writes):
        P.add("pe", lambda e: e.matmul(out, lhsT=lhsT, rhs=rhs, start=start, stop=stop), reads=reads, writes=writes)

    def ACT(out, in_, func, reads, writes, **kw):
        P.add("act", lambda e: e.activation(out=out, in_=in_, func=func, **kw), reads=reads, writes=writes)

    def TT(eng, out, in0, in1, op, reads, writes):
        P.add(eng, lambda e: e.tensor_tensor(out=out, in0=in0, in1=in1, op=op), reads=reads, writes=writes)

    def TS(out, in0, s1, s2, op0, op1, reads, writes):
        if op1 is None:
            P.add("dve", lambda e: e.tensor_scalar(out=out, in0=in0, scalar1=s1, scalar2=None, op0=op0),
                  reads=reads, writes=writes)
        else:
            P.add("dve", lambda e: e.tensor_scalar(out=out, in0=in0, scalar1=s1, scalar2=s2, op0=op0, op1=op1),
                  reads=reads, writes=writes)

    def CP(out, in_, reads, writes, scale=None):
        if scale is None:
            P.add("dve", lambda e: e.tensor_copy(out=out, in_=in_), reads=reads, writes=writes)
        else:
            P.add("dve", lambda e: e.tensor_scalar(out=out, in0=in_, scalar1=scale, scalar2=None, op0=ALU.mult),
                  reads=reads, writes=writes)

    def DMA(q, out, in_, reads, writes, key):
        P.add(q, lambda e: e.dma_start(out=out, in_=in_), reads=reads, writes=writes, dma_key=key)

    W1 = AR.b(8, 640)
    W2 = [AR.b(8, 268) for _ in range(2)]
    w1 = AR.b(32, 256)
    w2 = [AR.b(2, 64) for _ in range(2)]
    pos = AR.b(32)
    wout1 = AR.b(8, 1024)
    identb = AR.b(128)
    x1T = AR.b(8, S)
    ogT = AR.b(8, S)
    q_aug = [AR.b(S) for _ in range(4)]
    ksel = AR.b(S)
    kwin = AR.b(S)
    kcmp = AR.b(128)
    pair = AR.b(S)
    vsel = AR.b(16, 65)
    vwin = AR.b(16, 65)
    vcmp = AR.b(97)
    hs = [AR.b(2, 128) for _ in range(2)]
    cmpmask = AR.b(S)
    winmask = AR.b(384)
    trimask = AR.b(128)
    Pb = [AR.b(512) for _ in range(4)]
    selm_w = AR.b(4, 128)
    ogb = [AR.b(256) for _ in range(2)]
    identf = AR.f(128)
    lng1 = AR.f(1024)
    lnb1 = AR.f(1024)
    xr = [AR.f(1024) for _ in range(2)]
    xs2 = [AR.f(1024) for _ in range(2)]
    stat = [AR.f(16) for _ in range(2)]
    o_tm = [AR.f(4, 256) for _ in range(2)]
    gates = AR.f(16, 12)
    imp = AR.f(4, 32)
    impm = AR.f(4, 32)
    forced = AR.f(16, 32)
    future = AR.f(16, 32)
    m8 = AR.f(4, 8)
    thr = AR.f(4)
    rden = [AR.f(4) for _ in range(2)]
    fcol = [AR.f(4) for _ in range(2)]
    tmp_o = [AR.f(4, 64) for _ in range(2)]
    tmp_i = AR.f(4, 32)
    sz = [AR.f(256) for _ in range(2)]
    bh = AR.f(4)
    epsc = AR.f(1)

    for t_, k_ in ((ksel, "ksel_c"), (kwin, "kwin_c"), (kcmp, "kcmp_c"), (vcmp, "vcmp_c"),
                   (selm_w.rearrange("p a b -> p (a b)"), "selm")):
        P.add("pool", lambda e, t_=t_: e.memset(t_, 0.0), writes=[k_])
    for hl in range(4):
        P.add("pool", lambda e, hl=hl: e.memset(q_aug[hl], 0.0), writes=[("qa_q", hl), ("qa_s", hl), ("qa_a", hl)])
    P.add("pool", lambda e: e.memset(vsel[:, :, 64:65], 1.0), writes=["vsel_c"])
    P.add("pool", lambda e: e.memset(vwin[:, :, 64:65], 1.0), writes=["vwin_c"])
    P.add("dve", lambda e: e.memset(epsc, LN_EPS), writes=["consts"])
    DMA("sp", identb, identb_d, [], ["consts"], "c1")
    DMA("sp", identf, identf_d, [], ["consts"], "c1")
    DMA("sp", lng1, lng_d[1], [], ["lnc"], "c1")
    DMA("sp", lnb1, lnb_d[1], [], ["lnc"], "c1")
    DMA("sp", cmpmask, L["cmpmask"], [], ["consts"], "c1")
    DMA("sp", winmask, L["winmask"], [], ["consts"], "c1")
    DMA("sp", trimask, L["trimask"], [], ["consts"], "c1")
    DMA("sp", forced.rearrange("p a b -> p (a b)"), L["forced"], [], ["consts"], "c1")
    DMA("sp", future.rearrange("p a b -> p (a b)"), L["future"], [], ["consts"], "c1")
    DMA("sp", ksel[64:96, :], L["eoh"], [], ["ksel_c"], "c2")
    DMA("sp", ksel[96:103, :], L["kalibi"], [], ["ksel_c"], "c2")
    DMA("sp", kwin[96:103, :], L["kalibi"], [], ["kwin_c"], "c2")
    DMA("sp", kcmp[96:103, :], L["kalibic"], [], ["kcmp_c"], "c2")
    DMA("sp", vcmp[:, 64:97], L["vcc"], [], ["vcmp_c"], "c2")
    for kc in range(0, 8, 2):
        DMA("pool", W1[:, kc:kc + 2, :], L["wg1"][0, :, kc:kc + 2, :], [], ["W1"], "W1")
    for kc in range(0, 8, 4):
        DMA("pool", W2[0][:, kc:kc + 4, :], L["wg2"][0, :, kc:kc + 4, :], [], [("W2", 0)], "W20")
    for kv, (wn, w2n, pn) in enumerate((("w1k", "w2k", "posk"), ("w1v", "w2v", "posv"))):
        for i in range(0, 32, 8):
            DMA("pool", w1[64 * kv:64 * kv + 64, i:i + 8, :], L[wn][:, i:i + 8, :], [], ["wcmp"], "wcmp")
        DMA("pool", w2[kv], L[w2n], [], ["wcmp"], "wcmp")
        DMA("pool", pos[64 * kv:64 * kv + 64, :], L[pn], [], ["wcmp"], "wcmp")
    for c in range(8):
        DMA("pool", wout1[:, c, :], L["wout1"][:, c, :], [], ["wout1"], "wout1")
    for kv in range(2):
        for hc in range(2):
            col = kv * 2 + hc
            for i in range(32):
                MM(psum[0][:, col:col + 1], w1[64 * kv:64 * kv + 64, i, hc * 128:(hc + 1) * 128],
                   pos[64 * kv:64 * kv + 64, i:i + 1], i == 0, i == 31, ["wcmp"], [PS(0)])
    ACT(bh, psum[0][:, 0:4], AF.Copy, [PS(0)], ["bh"])

    if STG < 1:
        return
    rr = {"s": 0, "o": 0, "p": 0, "a": 0, "e": 0}

    def nxt(k, n):
        v = rr[k]
        rr[k] = (v + 1) % n
        return v

    def x1T_tile(s, j):
        xs_ = j % 2
        xk = ("xs2", xs_)
        DMA("sp", xs2[xs_], x1_d[s, j * 128:(j + 1) * 128, :], [], [xk], "xs2%d" % xs_)
        for q4 in range(2):
            pb = nxt("a", 2)
            for i4 in range(4):
                kc = q4 * 4 + i4
                P.add("pe", lambda e, kc=kc, i4=i4, pb=pb, xs_=xs_: e.transpose(
                    out=psum[pb][:, i4 * 128:(i4 + 1) * 128], in_=xs2[xs_][:, kc * 128:(kc + 1) * 128],
                    identity=identf), reads=[xk, "consts"], writes=[PS(pb)])
            CP(x1T[:, q4 * 4:(q4 + 1) * 4, j * 128:(j + 1) * 128],
               psum[pb][:, :].rearrange("p (a b) -> p a b", a=4), [PS(pb)], ["x1T"])

    xr.append(o_tm[0].rearrange("p a b -> p (a b)"))
    xr.append(o_tm[1].rearrange("p a b -> p (a b)"))
    stat.append(AR.f(16))
    stat.append(AR.f(16))

    def outproj_tile(s, jt):
        xs_ = jt % 4
        xk = ("xr", xs_)
        alias = [("o_tm", xs_ - 2, h_) for h_ in range(4)] if xs_ >= 2 else []
        DMA("sp", xr[xs_], x1_d[s, jt * 128:(jt + 1) * 128, :], [], [xk] + alias, "xr%d" % xs_)
        for hh in range(2):
            pb = nxt("a", 2)
            for c in range(8):
                MM(psum[pb][:, :], ogT[:, c, jt * 128:(jt + 1) * 128], wout1[:, c, hh * 512:(hh + 1) * 512],
                   c == 0, c == 7, [("ogT", c // 2), "wout1"], [PS(pb)])
            P.add("dve", lambda e, hh=hh, pb=pb, xs_=xs_: e.scalar_tensor_tensor(
                out=xr[xs_][:, hh * 512:(hh + 1) * 512], in0=xr[xs_][:, hh * 512:(hh + 1) * 512],
                scalar=DN_ALPHA, in1=psum[pb][:, :], op0=ALU.mult, op1=ALU.add), reads=[PS(pb), xk], writes=[xk])
        layer_norm_tile(P, xr[xs_], xk, lng1, lnb1, stat[xs_], ("stat1", xs_), epsc)
        DMA("sp", out_d[s, jt * 128:(jt + 1) * 128, :], xr[xs_], [xk], [], "ost%d" % xs_)

    def load_group_weights(gi_):
        g_ = gi_ % 4
        for kc in range(0, 8, 2):
            DMA("pool", W1[:, kc:kc + 2, :], L["wg1"][g_, :, kc:kc + 2, :], [], ["W1"], "W1")
        for kc in range(0, 8, 4):
            DMA("pool", W2[gi_ % 2][:, kc:kc + 4, :], L["wg2"][g_, :, kc:kc + 4, :], [], [("W2", gi_ % 2)], "W2%d" % (gi_ % 2))

    pending = []

    def drain_pending(n):
        for _ in range(n):
            if pending:
                s_, jt_ = pending.pop(0)
                outproj_tile(s_, jt_)

    for s in range(NSEQ):
        if s == 0:
            for j in range(16):
                x1T_tile(0, j)
        if STG < 2:
            return
        for g in range(4):
            gi = s * 4 + g
            w2s = gi % 2
            w2k_ = ("W2", w2s)
            for hl in range(4):
                DMA("sp", q_aug[hl][96:103, :], L["qalibi"][4 * g + hl], [], [("qa_a", hl)], "qal")
            for pc in range(2):
                for tb in range(4):
                    pb = nxt("a", 2)
                    for kc in range(8):
                        MM(psum[pb][:, :], W1[:, kc, pc * 128:(pc + 1) * 128], x1T[:, kc, tb * 512:(tb + 1) * 512],
                           kc == 0, kc == 7, ["W1", "x1T"], [PS(pb)])
                    CP(q_aug[2 * pc][0:64, tb * 512:(tb + 1) * 512], psum[pb][0:64, :],
                       [PS(pb)], [("qa_q", 2 * pc)], scale=0.125)
                    ACT(q_aug[2 * pc + 1][0:64, tb * 512:(tb + 1) * 512], psum[pb][64:128, :], AF.Copy,
                        [PS(pb)], [("qa_q", 2 * pc + 1)], scale=0.125)
                    drain_pending(1)
            if STG < 3:
                return
            for tb in range(4):
                c0, c1 = tb * 512, (tb + 1) * 512
                pb = nxt("a", 2)
                for kc in range(8):
                    MM(psum[pb][:, :], W1[:, kc, 256:384], x1T[:, kc, c0:c1], kc == 0, kc == 7, ["W1", "x1T"], [PS(pb)])
                CP(pair.rearrange("p (ph c) -> p ph c", ph=16)[:, :, tb * 32:(tb + 1) * 32],
                   psum[pb][:, :].rearrange("p (c ph) -> p ph c", ph=16), [PS(pb)], ["pair"])
                pb = nxt("a", 2)
                for kc in range(8):
                    MM(psum[pb][:, :], W1[:, kc, 384:512], x1T[:, kc, c0:c1], kc == 0, kc == 7, ["W1", "x1T"], [PS(pb)])
                P.add("dve", lambda e, pb=pb, c0=c0, c1=c1: e.tensor_copy(out=ksel[0:64, c0:c1], in_=psum[pb][0:64, :]),
                      reads=[PS(pb)], writes=["ksel"])
                ACT(kwin[0:64, c0:c1], psum[pb][64:128, :], AF.Copy, [PS(pb)], ["kwin"])
            if STG < 4:
                return
            for j4 in range(4):
                pb = nxt("a", 2)
                for jj in range(4):
                    j = j4 * 4 + jj
                    for kc in range(8):
                        MM(psum[pb][:, jj * 128:(jj + 1) * 128], x1T[:, kc, j * 128:(j + 1) * 128], W1[:, kc, 512:640],
                           kc == 0, kc == 7, ["W1", "x1T"], [PS(pb)])
                pv = psum[pb][:, :].rearrange("p (a b) -> p a b", a=4)
                P.add("dve", lambda e, pv=pv, j4=j4: e.tensor_copy(out=vsel[:, j4 * 4:(j4 + 1) * 4, 0:64], in_=pv[:, :, 0:64]),
                      reads=[PS(pb)], writes=["vsel"])
                ACT(vwin[:, j4 * 4:(j4 + 1) * 4, 0:64], pv[:, :, 64:128], AF.Copy, [PS(pb)], ["vwin"])
            if STG < 5:
                return
            if gi + 1 < NSEQ * 4:
                load_group_weights(gi + 1)
            pb = nxt("a", 2)
            for j in range(16):
                for kc in range(8):
                    MM(psum[pb][:, j * 12:(j + 1) * 12], x1T[:, kc, j * 128:(j + 1) * 128], W2[w2s][:, kc, 256:268],
                       kc == 0, kc == 7, [w2k_, "x1T"], [PS(pb)])
            ACT(gates.rearrange("p a b -> p (a b)"), psum[pb][:, 0:192], AF.Sigmoid, [PS(pb)], ["gates"])
            if STG < 6:
                return
            for kv in range(2):
                for hc in range(2):
                    pb = nxt("a", 2)
                    for i in range(32):
                        MM(psum[pb][:, 0:127], w1[64 * kv:64 * kv + 64, i, hc * 128:(hc + 1) * 128],
                           pair[64 * kv:64 * kv + 64, (i % 16) * 128 + i // 16:(i % 16) * 128 + i // 16 + 127],
                           i == 0, i == 31, ["wcmp", "pair"], [PS(pb)])
                    ACT(hs[kv][:, hc, 0:127], psum[pb][:, 0:127], AF.Silu, [PS(pb), "bh"], [("hs", kv)],
                        bias=bh[:, kv * 2 + hc:kv * 2 + hc + 1])
            pb = nxt("a", 2)
            for hc in range(2):
                MM(psum[pb][0:64, 0:127], w2[0][:, hc, :], hs[0][:, hc, 0:127], hc == 0, hc == 1,
                   ["wcmp", ("hs", 0)], [PS(pb)])
            CP(kcmp[0:64, 0:127], psum[pb][0:64, 0:127], [PS(pb)], ["kcmp"])
            pb = nxt("a", 2)
            for hc in range(2):
                MM(psum[pb][0:127, 0:64], hs[1][:, hc, 0:127], w2[1][:, hc, :], hc == 0, hc == 1,
                   ["wcmp", ("hs", 1)], [PS(pb)])
            CP(vcmp[0:127, 0:64], psum[pb][0:127, 0:64], [PS(pb)], ["vcmp"])

            if STG < 7:
                return
            def branch_epilogue(hl, br, ob, Wd, qt, os_, first, with_imp):
                O = psum[ob][:, 0:4 * Wd].rearrange("p (a b) -> p a b", a=4)
                e_ = nxt("e", 2)
                rk, fk, tk = ("rden", e_), ("fcol", e_), ("tmp_o", e_)
                TS(rden[e_].unsqueeze(2), O[:, :, 64:65], 1e-30, None, ALU.max, None, [PS(ob)], [rk])
                P.add("dve", lambda e: e.reciprocal(out=rden[e_], in_=rden[e_]), reads=[rk], writes=[rk])
                if with_imp:
                    rb = rden[e_].unsqueeze(2).broadcast_to([128, 4, 32])
                    if hl == 0:
                        TT("dve", imp, O[:, :, 65:97], rb, ALU.mult, [PS(ob), rk], ["imp"])
                    else:
                        TT("dve", tmp_i, O[:, :, 65:97], rb, ALU.mult, [PS(ob), rk], ["tmp_i"])
                        TT("pool", imp, imp, tmp_i, ALU.add, ["tmp_i", "imp"], ["imp"])
                gcol = gates[:, qt * 4:(qt + 1) * 4, hl * 3 + br]
                TT("dve", fcol[e_], rden[e_], gcol, ALU.mult, [rk, "gates"], [fk])
                fb = fcol[e_].unsqueeze(2).broadcast_to([128, 4, 64])
                odst = o_tm[os_][:, :, hl * 64:(hl + 1) * 64]
                ok = ("o_tm", os_, hl)
                if first:
                    TT("dve", odst, O[:, :, 0:64], fb, ALU.mult, [PS(ob), fk], [ok])
                else:
                    TT("dve", tmp_o[e_], O[:, :, 0:64], fb, ALU.mult, [PS(ob), fk], [tk])
                    TT("pool", odst, odst, tmp_o[e_], ALU.add, [tk, ok], [ok])

            SBK = [2, 3, 0, 1]
            tiles = []

            def mk_tile(qk=None, act=None, pv=None, post=None):
                tiles.append((qk, act, pv, post))

            def sel_chain(qt):
                TT("dve", impm, imp, forced[:, qt * 4:(qt + 1) * 4, :], ALU.max, ["imp", "consts"], ["impm"])
                TT("dve", impm, impm, future[:, qt * 4:(qt + 1) * 4, :], ALU.min, ["impm", "consts"], ["impm"])
                for qs in range(4):
                    P.add("dve", lambda e, qs=qs: e.max(out=m8[:, qs, :], in_=impm[:, qs, :]), reads=["impm"], writes=["m8"])
                TS(thr.unsqueeze(2), m8[:, :, 7:8], 0.0, None, ALU.max, None, ["m8"], ["thr"])
                for qs in range(4):
                    TS(selm_w[:, qs, 64:96], impm[:, qs, :], thr[:, qs:qs + 1], 1.0, ALU.is_ge, ALU.subtract,
                       ["impm", "thr"], ["selm"])

            def sel_transposes(Q0):
                for qs in range(4):
                    MM(psum[6][:, qs * 128:(qs + 1) * 128], selm_w[:, qs, :], identb, True, True, ["selm", "consts"], [PS(6)])
                for hl in range(4):
                    if hl % 2 == 0:
                        ACT(q_aug[hl][64:96, Q0:Q0 + 512], psum[6][64:96, :], AF.Copy, [PS(6)], [("qa_s", hl)], scale=30000.0)
                    else:
                        CP(q_aug[hl][64:96, Q0:Q0 + 512], psum[6][64:96, :], [PS(6)], [("qa_s", hl)], scale=30000.0)

            def gate_path(qt, os_):
                for qs in range(4):
                    jt = 4 * qt + qs
                    k_ = (gi * 16 + jt) % 2
                    for kc in range(8):
                        MM(psum[6][:, 256:512], x1T[:, kc, jt * 128:(jt + 1) * 128], W2[w2s][:, kc, 0:256], kc == 0, kc == 7,
                           [w2k_, "x1T"], [PS(6)])
                    ACT(sz[k_], psum[6][:, 256:512], AF.Silu, [PS(6)], [("sz", k_)])
                    TT("dve", ogb[k_], o_tm[os_][:, qs, :], sz[k_], ALU.mult,
                       [("sz", k_)] + [("o_tm", os_, h_) for h_ in range(4)], [("ogb", k_)])
                    p6b = psum[6][:, :].bitcast(BF16)
                    for cc in range(2):
                        P.add("pe", lambda e, cc=cc, k_=k_, p6b=p6b: e.transpose(
                            out=p6b[:, cc * 128:(cc + 1) * 128], in_=ogb[k_][:, cc * 128:(cc + 1) * 128], identity=identb),
                            reads=[("ogb", k_), "consts"], writes=[PS(6)])
                    CP(ogT[:, 2 * g:2 * g + 2, jt * 128:(jt + 1) * 128],
                       p6b[:, 0:256].rearrange("p (a b) -> p a b", a=2), [PS(6)], [("ogT", g)])

            for qt in range(4):
                Q0 = qt * 512
                os_ = (gi * 4 + qt) % 2
                for hl in range(4):
                    sb = SBK[nxt("s", 4)]
                    pi = nxt("p", 4)
                    ob = (4, 5, 7)[nxt("o", 3)]
                    qk = [("qa_q", hl), ("qa_s", hl), ("qa_a", hl)]

                    def qk_f(sb=sb, hl=hl, Q0=Q0, qk=qk):
                        MM(psum[sb][:, :], kcmp[0:103, :], q_aug[hl][0:103, Q0:Q0 + 512], True, False,
                           ["kcmp", "kcmp_c"] + qk, [PS(sb)])
                        MM(psum[sb][:, :], identb, cmpmask[:, Q0:Q0 + 512], False, True, ["consts"], [PS(sb)])

                    def act_f(sb=sb, pi=pi):
                        ACT(Pb[pi], psum[sb][:, :], AF.Exp, [PS(sb)], [("Pb", pi)])

                    def pv_f(pi=pi, ob=ob):
                        for qs in range(4):
                            MM(psum[ob][:, qs * 97:(qs + 1) * 97], Pb[pi][:, qs * 128:(qs + 1) * 128], vcmp[:, 0:97],
                               True, True, [("Pb", pi), "vcmp", "vcmp_c"], [PS(ob)])

                    def post_f(hl=hl, ob=ob, qt=qt, os_=os_):
                        branch_epilogue(hl, 0, ob, 97, qt, os_, True, True)
                        if hl == 3:
                            sel_chain(qt)

                    mk_tile(qk_f, act_f, pv_f, post_f)
                for hl in range(4):
                    qk = [("qa_q", hl), ("qa_s", hl), ("qa_a", hl)]
                    ob = (4, 5, 7)[nxt("o", 3)]
                    plan = []
                    for kt in range(max(0, 4 * qt - 2), 4 * qt + 4):
                        K0 = kt * 128
                        lo, hi = max(K0, Q0), min(K0 + 384, Q0 + 512)
                        if hi > lo:
                            plan.append((kt, K0, lo, hi))
                    lastk = {}
                    for (kt, K0, lo, hi) in plan:
                        for qs in range((lo - Q0) // 128, (hi - Q0) // 128):
                            lastk[qs] = kt
                    firstmm = True
                    for ti, (kt, K0, lo, hi) in enumerate(plan):
                        sb = SBK[nxt("s", 4)]
                        pi = nxt("p", 4)
                        c0, c1, m0 = lo - Q0, hi - Q0, lo - K0

                        def qk_f(sb=sb, hl=hl, qk=qk, K0=K0, lo=lo, hi=hi, c0=c0, c1=c1, m0=m0):
                            MM(psum[sb][:, c0:c1], kwin[0:103, K0:K0 + 128], q_aug[hl][0:103, lo:hi], True, False,
                               ["kwin", "kwin_c"] + qk, [PS(sb)])
                            MM(psum[sb][:, c0:c1], identb, winmask[:, m0:m0 + (hi - lo)], False, True, ["consts"], [PS(sb)])

                        def act_f(sb=sb, pi=pi, c0=c0, c1=c1):
                            ACT(Pb[pi][:, c0:c1], psum[sb][:, c0:c1], AF.Exp, [PS(sb)], [("Pb", pi)])

                        pvl = []
                        for qs in range(c0 // 128, c1 // 128):
                            pvl.append((qs, firstmm, lastk[qs] == kt))
                            firstmm = False

                        def pv_f(pi=pi, ob=ob, kt=kt, pvl=pvl):
                            for (qs, st_, sp_) in pvl:
                                MM(psum[ob][:, qs * 65:(qs + 1) * 65], Pb[pi][:, qs * 128:(qs + 1) * 128], vwin[:, kt, :],
                                   st_, sp_, [("Pb", pi), "vwin", "vwin_c"], [PS(ob)])

                        post_f = None
                        if ti == len(plan) - 1:
                            def post_f(hl=hl, ob=ob, qt=qt, os_=os_):
                                branch_epilogue(hl, 2, ob, 65, qt, os_, False, False)
                        mk_tile(qk_f, act_f, pv_f, post_f)
                mk_tile(qk=(lambda Q0=Q0: sel_transposes(Q0)))
                for hl in range(4):
                    qk = [("qa_q", hl), ("qa_s", hl), ("qa_a", hl)]
                    ob = (4, 5, 7)[nxt("o", 3)]
                    nk = 4 * qt + 4
                    firstmm = True
                    for kt in range(nk):
                        K0 = kt * 128
                        dq_ = kt - 4 * qt
                        c0 = max(dq_, 0) * 128
                        sb = SBK[nxt("s", 4)]
                        pi = nxt("p", 4)

                        def qk_f(sb=sb, hl=hl, qk=qk, K0=K0, Q0=Q0, c0=c0, dq_=dq_):
                            MM(psum[sb][:, c0:512], ksel[0:103, K0:K0 + 128], q_aug[hl][0:103, Q0 + c0:Q0 + 512], True, dq_ < 0,
                               ["ksel", "ksel_c"] + qk, [PS(sb)])
                            if dq_ >= 0:
                                MM(psum[sb][:, c0:c0 + 128], identb, trimask, False, True, ["consts"], [PS(sb)])

                        def act_f(sb=sb, pi=pi, c0=c0):
                            ACT(Pb[pi][:, c0:512], psum[sb][:, c0:512], AF.Exp, [PS(sb)], [("Pb", pi)])

                        pvl = []
                        for qs in range(c0 // 128, 4):
                            pvl.append((qs, firstmm, kt == 4 * qt + qs))
                            firstmm = False

                        def pv_f(pi=pi, ob=ob, kt=kt, pvl=pvl):
                            for (qs, st_, sp_) in pvl:
                                MM(psum[ob][:, qs * 65:(qs + 1) * 65], Pb[pi][:, qs * 128:(qs + 1) * 128], vsel[:, kt, :],
                                   st_, sp_, [("Pb", pi), "vsel", "vsel_c"], [PS(ob)])

                        post_f = None
                        if kt == nk - 1:
                            def post_f(hl=hl, ob=ob, qt=qt, os_=os_):
                                branch_epilogue(hl, 1, ob, 65, qt, os_, False, False)
                                if hl == 3:
                                    gate_path(qt, os_)
                        mk_tile(qk_f, act_f, pv_f, post_f)

            SKEW = 3
            for i_, (qk_f, act_f, pv_f, post_f) in enumerate(tiles):
                if qk_f is not None:
                    qk_f()
                if act_f is not None:
                    act_f()
                if i_ >= SKEW:
                    t2 = tiles[i_ - SKEW]
                    if t2[2] is not None:
                        t2[2]()
                    if t2[3] is not None:
                        t2[3]()
            for t2 in tiles[max(0, len(tiles) - SKEW):]:
                if t2[2] is not None:
                    t2[2]()
                if t2[3] is not None:
                    t2[3]()

        for jt in range(16):
            pending.append((s, jt))
        if s + 1 < NSEQ:
            for jt in range(16):
                x1T_tile(s + 1, jt)
                if jt % 2 == 1:
                    drain_pending(1)
        else:
            drain_pending(16)


def build_program(do_l0=True, do_l1=True):
    nc = bass.Bass("TRN2", target_bir_lowering=False)
    dt = {}

    def din(name, shape, dtype=F32):
        dt[name] = nc.dram_tensor(name, list(shape), dtype, kind="ExternalInput").ap()
        return dt[name]

    x_d = din("x", [NSEQ, S, D])
    lng_d = din("lng", [2, 128, D])
    lnb_d = din("lnb", [2, 128, D])
    w_in0_d = din("w_in0", [128, 8, 4096])
    w_grp_d = din("w_grp", [128, 16, 512])
    w_out0_d = din("w_out0", [128, 16, 1024])
    scale_d = din("pscale", [128, 16])
    poolA_d = din("poolA", [128, 4 * 3 * 128], BF16)
    invc_d = din("invc", [128, 4 * 128])
    identf_d = din("identf", [128, 128])
    identb_d = din("identb", [128, 128], BF16)
    x1kind = "Internal" if (do_l0 and do_l1) else ("ExternalOutput" if do_l0 else "ExternalInput")
    x1_d = nc.dram_tensor("x1s", [NSEQ, S, D], F32, kind=x1kind).ap()
    out_d = nc.dram_tensor("out", [NSEQ, S, D], F32, kind="ExternalOutput").ap()
    L1D = _l1_dram(nc, din)

    stack = contextlib.ExitStack()
    with stack:
        ACOLS = 52800
        ar_t = stack.enter_context(nc.sbuf_tensor("arena", [128, ACOLS], F32))
        AR = Arena(ar_t[:], ACOLS)
        psum = [stack.enter_context(nc.psum_tensor("ps%d" % i, [128, 512], F32)) for i in range(8)]
        P = Prog(nc)

        def PS(i):
            return ("ps", i)

        if do_l0:
            AR.reset()
            w_in0 = AR.b(8, 4096)
            w_grp = AR.b(16, 512)
            w_out0 = AR.b(16, 1024)
            poolA = AR.b(12, 128)
            identb = AR.b(128)
            xT0 = [AR.b(8, 256) for _ in range(2)]
            u_tm = AR.b(12, 512)
            mT = [AR.b(4, 256) for _ in range(2)]
            gT = [AR.b(16, 256) for _ in range(2)]
            identf = AR.f(128)
            lng0 = AR.f(1024)
            lnb0 = AR.f(1024)
            invc = AR.f(4, 128)
            pscale = AR.f(16)
            epsc = AR.f(1)
            xs = [AR.f(2, 1024) for _ in range(2)]
            siluz = [AR.f(4, 256) for _ in range(2)]
            rbuf = [AR.f(1024) for _ in range(2)]
            stat = [AR.f(16) for _ in range(2)]

            P.add("sp", lambda e: e.dma_start(out=identf, in_=identf_d), writes=["consts"], dma_key="c0")
            P.add("sp", lambda e: e.dma_start(out=identb, in_=identb_d), writes=["consts"], dma_key="c0")
            P.add("sp", lambda e: e.dma_start(out=poolA.rearrange("p a b -> p (a b)"), in_=poolA_d), writes=["consts"], dma_key="c0")
            P.add("sp", lambda e: e.dma_start(out=invc.rearrange("p a b -> p (a b)"), in_=invc_d), writes=["consts"], dma_key="c0")
            P.add("sp", lambda e: e.dma_start(out=pscale, in_=scale_d), writes=["consts"], dma_key="c0")
            P.add("sp", lambda e: e.dma_start(out=lng0, in_=lng_d[0]), writes=["lnc"], dma_key="c0")
            P.add("sp", lambda e: e.dma_start(out=lnb0, in_=lnb_d[0]), writes=["lnc"], dma_key="c0")
            P.add("dve", lambda e: e.memset(epsc, LN_EPS), writes=["consts"])
            for gq in range(4):
                for cb in (gq, 4 + gq):
                    P.add("pool", lambda e, cb=cb: e.dma_start(out=w_in0[:, :, cb * 512:(cb + 1) * 512],
                                                             in_=w_in0_d[:, :, cb * 512:(cb + 1) * 512]),
                          writes=[("w_in0", cb)], dma_key="w_in0_%d" % cb)
                P.add("pool", lambda e, gq=gq: e.dma_start(out=w_grp[:, gq * 4:(gq + 1) * 4, :],
                                                         in_=w_grp_d[:, gq * 4:(gq + 1) * 4, :]),
                      writes=[("w_grp", gq)], dma_key="w_grp_%d" % gq)
            for c in range(0, 16, 2):
                P.add("pool", lambda e, c=c: e.dma_start(out=w_out0[:, c:c + 2, :], in_=w_out0_d[:, c:c + 2, :]),
                      writes=["w_out0"], dma_key="w_out0")

            NB = S // 256
            NIT = NSEQ * NB

            def geo(it):
                s, b = divmod(it, NB)
                return s, b, it % 2, b * 256

            def stage_T(it):
                s, b, sl, t0 = geo(it)
                xk, xtk = ("xs", sl), ("xT0", sl)
                for j in range(2):
                    P.add("sp", lambda e, j=j: e.dma_start(
                        out=xs[sl][:, j, :], in_=x_d[s, t0 + j * 128:t0 + (j + 1) * 128, :]),
                        writes=[xk], dma_key="xs%d" % sl)
                for j in range(2):
                    for q4 in range(2):
                        pb = (j * 2 + q4) % 2
                        for i4 in range(4):
                            kc = q4 * 4 + i4
                            P.add("pe", lambda e, j=j, kc=kc, i4=i4, pb=pb: e.transpose(
                                out=psum[pb][:, i4 * 128:(i4 + 1) * 128], in_=xs[sl][:, j, kc * 128:(kc + 1) * 128],
                                identity=identf), reads=[xk, "consts"], writes=[PS(pb)])
                        P.add("dve", lambda e, j=j, q4=q4, pb=pb: e.tensor_copy(
                            out=xT0[sl][:, q4 * 4:(q4 + 1) * 4, j * 128:(j + 1) * 128],
                            in_=psum[pb][:, :].rearrange("p (a b) -> p a b", a=4)),
                            reads=[PS(pb)], writes=[xtk])

            def stage_U(it, g):
                s, b, sl, t0 = geo(it)
                xtk = ("xT0", sl)
                for j in range(2):
                    n = b * 2 + j
                    slot = n % 3
                    pb = 2 + (j % 2)
                    for kc in range(8):
                        P.add("pe", lambda e, kc=kc, j=j, pb=pb: e.matmul(
                            psum[pb][:, :], lhsT=xT0[sl][:, kc, j * 128:(j + 1) * 128],
                            rhs=w_in0[:, kc, g * 512:(g + 1) * 512], start=(kc == 0), stop=(kc == 7)),
                            reads=[xtk, ("w_in0", g)], writes=[PS(pb)])
                    P.add("act", lambda e, slot=slot, pb=pb: e.activation(
                        out=u_tm[:, g * 3 + slot, :], in_=psum[pb][:, :], func=AF.Copy),
                        reads=[PS(pb)], writes=[("u", g, slot)])

            def stage_Z(it, g):
                s, b, sl, t0 = geo(it)
                xtk = ("xT0", sl)
                gs = (it * 4 + g) % 2
                szk = ("siluz", gs)
                for c2 in range(2):
                    pb = 4 + c2
                    for ci in range(2):
                        c = c2 * 2 + ci
                        col = 2048 + g * 512 + c * 128
                        for kc in range(8):
                            P.add("pe", lambda e, kc=kc, col=col, ci=ci, pb=pb: e.matmul(
                                psum[pb][:, ci * 256:(ci + 1) * 256], lhsT=w_in0[:, kc, col:col + 128],
                                rhs=xT0[sl][:, kc, :], start=(kc == 0), stop=(kc == 7)),
                                reads=[xtk, ("w_in0", 4 + g)], writes=[PS(pb)])
                    P.add("act", lambda e, c2=c2, pb=pb: e.activation(
                        out=siluz[gs][:, c2 * 2:(c2 + 1) * 2, :],
                        in_=psum[pb][:, :].rearrange("p (a b) -> p a b", a=2), func=AF.Silu),
                        reads=[PS(pb)], writes=[szk])

            def stage_PM(it, g):
                s, b, sl, t0 = geo(it)
                gs = (it * 4 + g) % 2
                mk = ("mT", gs)
                for c2 in range(2):
                    pb = 6 + c2
                    for ci in range(2):
                        c = c2 * 2 + ci
                        for j in range(2):
                            n = b * 2 + j
                            slot = n % 3
                            pslot = (n - 1) % 3
                            o = psum[pb][:, ci * 256 + j * 128: ci * 256 + (j + 1) * 128]
                            if n == 0:
                                P.add("pe", lambda e, o=o, slot=slot, c=c: e.matmul(
                                    o, lhsT=u_tm[:, g * 3 + slot, c * 128:(c + 1) * 128],
                                    rhs=poolA[:, g * 3 + 2, :], start=True, stop=True),
                                    reads=[("u", g, slot), "consts"], writes=[PS(pb)])
                            else:
                                P.add("pe", lambda e, o=o, slot=slot, c=c: e.matmul(
                                    o, lhsT=u_tm[:, g * 3 + slot, c * 128:(c + 1) * 128],
                                    rhs=poolA[:, g * 3 + 0, :], start=True, stop=False),
                                    reads=[("u", g, slot), "consts"], writes=[PS(pb)])
                                P.add("pe", lambda e, o=o, pslot=pslot, c=c: e.matmul(
                                    o, lhsT=u_tm[:, g * 3 + pslot, c * 128:(c + 1) * 128],
                                    rhs=poolA[:, g * 3 + 1, :], start=False, stop=True),
                                    reads=[("u", g, pslot), "consts"], writes=[PS(pb)])
                    P.add("act", lambda e, c2=c2, pb=pb: e.activation(
                        out=mT[gs][:, c2 * 2:(c2 + 1) * 2, :],
                        in_=psum[pb][:, :].rearrange("p (a b) -> p a b", a=2), func=AF.Copy),
                        reads=[PS(pb)], writes=[mk])
                    if b == 0:
                        P.add("dve", lambda e, c2=c2, pb=pb: e.tensor_tensor(
                            out=mT[gs][:, c2 * 2:(c2 + 1) * 2, 0:128],
                            in0=psum[pb][:, :].rearrange("p (a b) -> p a b", a=2)[:, :, 0:128],
                            in1=invc[:, g:g + 1, :].broadcast_to([128, 2, 128]), op=ALU.mult),
                            reads=[PS(pb), "consts"], writes=[mk])

            def stage_G(it, g):
                gs = (it * 4 + g) % 2
                szk, mk = ("siluz", gs), ("mT", gs)
                gb = it % 2
                for d2 in range(2):
                    pb = d2
                    for di in range(2):
                        d = d2 * 2 + di
                        for cc in range(4):
                            P.add("pe", lambda e, cc=cc, d=d, di=di, pb=pb: e.matmul(
                                psum[pb][:, di * 256:(di + 1) * 256],
                                lhsT=w_grp[:, g * 4 + cc, d * 128:(d + 1) * 128], rhs=mT[gs][:, cc, :],
                                start=(cc == 0), stop=(cc == 3)), reads=[mk, ("w_grp", g)], writes=[PS(pb)])
                    for di in range(2):
                        d = d2 * 2 + di
                        ch = g * 4 + d
                        P.add("dve", lambda e, d=d, di=di, ch=ch, pb=pb: e.scalar_tensor_tensor(
                            out=gT[gb][:, ch, :], in0=psum[pb][:, di * 256:(di + 1) * 256],
                            scalar=pscale[:, ch:ch + 1], in1=siluz[gs][:, d, :], op0=ALU.mult, op1=ALU.mult),
                            reads=[PS(pb), szk, "consts"], writes=[("gT", gb, ch)])

            def stage_O(it):
                s, b, sl, t0 = geo(it)
                xk = ("xs", sl)
                gb = it % 2
                for j in range(2):
                    rs = (it * 2 + j) % 2
                    rk = ("r", rs)
                    for hh in range(2):
                        pb = 2 + hh
                        for ch in range(16):
                            P.add("pe", lambda e, ch=ch, j=j, hh=hh, pb=pb: e.matmul(
                                psum[pb][:, :], lhsT=gT[gb][:, ch, j * 128:(j + 1) * 128],
                                rhs=w_out0[:, ch, hh * 512:(hh + 1) * 512], start=(ch == 0), stop=(ch == 15)),
                                reads=[("gT", gb, ch), "w_out0"], writes=[PS(pb)])
                        P.add("dve", lambda e, j=j, hh=hh, pb=pb, rs=rs: e.scalar_tensor_tensor(
                            out=rbuf[rs][:, hh * 512:(hh + 1) * 512], in0=xs[sl][:, j, hh * 512:(hh + 1) * 512],
                            scalar=DN_ALPHA, in1=psum[pb][:, :], op0=ALU.mult, op1=ALU.add),
                            reads=[PS(pb), xk], writes=[rk])
                    layer_norm_tile(P, rbuf[rs], rk, lng0, lnb0, stat[rs], ("stat", rs), epsc)
                    P.add("sp", lambda e, rs=rs, j=j: e.dma_start(
                        out=x1_d[s, t0 + j * 128:t0 + (j + 1) * 128, :], in_=rbuf[rs]),
                        reads=[rk], dma_key="st%d" % rs)

            for it in range(NIT):
                stage_T(it)
                stage_U(it, 0)
                if it > 0:
                    stage_G(it - 1, 3)
                stage_Z(it, 0)
                if it > 0:
                    stage_O(it - 1)
                stage_PM(it, 0)
                for g in range(1, 4):
                    stage_U(it, g)
                    stage_Z(it, g)
                    stage_G(it, g - 1)
                    stage_PM(it, g)
            stage_G(NIT - 1, 3)
            stage_O(NIT - 1)

        P.barrier()
        if do_l1:
            AR.reset()
            _build_l1(nc, P, AR, psum, L1D, x1_d, out_d, lng_d, lnb_d, identf_d, identb_d)
        P.emit(stack)
    return nc


def _host_common(inputs):
    f = lambda a: np.ascontiguousarray(np.asarray(a, dtype=np.float32))
    m = {}
    lng = f(inputs["ln_g"])
    lnb = f(inputs["ln_b"])
    m["lng"] = np.ascontiguousarray(np.broadcast_to(lng[:, None, :], (2, 128, D)))
    m["lnb"] = np.ascontiguousarray(np.broadcast_to(lnb[:, None, :], (2, 128, D)))
    w = f(inputs["pool_w_in"])[0]
    m["w_in0"] = np.ascontiguousarray(w.reshape(8, 128, 4096).transpose(1, 0, 2))
    wg = f(inputs["pool_w_grp"])[0]
    m["w_grp"] = np.ascontiguousarray(wg.reshape(4, 4, 128, 512).transpose(2, 0, 1, 3).reshape(128, 16, 512))
    wo = f(inputs["pool_w_out"])[0]
    m["w_out0"] = np.ascontiguousarray(wo.reshape(16, 128, 1024).transpose(1, 0, 2))
    m["pscale"] = np.ascontiguousarray(f(inputs["pool_scale"])[0].reshape(16, 128).T)
    pa, inv = _pool_tables()
    m["poolA"] = pa
    m["invc"] = inv
    m["identf"] = np.eye(128, dtype=np.float32)
    m["identb"] = _bf(np.eye(128))
    m.update(_host_l1(inputs))
    return m


_NC_CACHE = {}


def kernel(**inputs):
    x = np.ascontiguousarray(np.asarray(inputs["x"], dtype=np.float32))
    common = _host_common(inputs)
    if "nc" not in _NC_CACHE:
        _NC_CACHE["nc"] = build_program()
    nc = _NC_CACHE["nc"]
    in_maps = []
    for c in range(NCORES):
        m = dict(common)
        m["x"] = x[c * NSEQ:(c + 1) * NSEQ]
        in_maps.append(m)
    res = run_bass_kernel_spmd(nc, in_maps, core_ids=list(range(NCORES)))
    out = np.concatenate([np.asarray(r["out"]) for r in res.results], axis=0)
    return out.astype(np.float32)
```

```python
import contextlib
import numpy as np
import ml_dtypes
import concourse.bass as bass
import concourse.mybir as mybir
from concourse.bass_utils import run_bass_kernel_spmd

F32 = mybir.dt.float32
BF16 = mybir.dt.bfloat16
AF = mybir.ActivationFunctionType
ALU = mybir.AluOpType

D = 1024
S = 2048
NSEQ = 2
NCORES = 8
DN_ALPHA = float((2.0 * 2) ** 0.25)
LN_EPS = 1e-5
POOL_WINDOWS = (2, 4, 8, 16)
NEGM = -30000.0


class _Op:
    __slots__ = ("eng", "fn", "deps", "is_dma", "key", "awaited", "count", "idx")


class Prog:
    ENGS = ("pe", "act", "dve", "pool", "sp")

    def __init__(self, nc):
        self.nc = nc
        self.ops = {e: [] for e in self.ENGS}
        self.last_w = {}
        self.readers = {}
        self.dma_counts = {}
        self.last_dma = {}
        self.bar = {e: [] for e in self.ENGS}

    def _dep(self, op, a):
        if a is None or a is op:
            return
        if (not a.is_dma) and a.eng == op.eng and a.eng == "pe":
            return
        if a.is_dma and op.is_dma and a.key == op.key:
            return
        op.deps.append(a)
        if not a.is_dma:
            a.awaited = True

    def add(self, eng, fn, reads=(), writes=(), dma_key=None):
        op = _Op()
        op.eng = eng
        op.fn = fn
        op.deps = []
        op.is_dma = dma_key is not None
        op.key = dma_key
        op.awaited = False
        op.count = None
        for a in self.bar[eng]:
            self._dep(op, a)
        self.bar[eng] = []
        if eng != "pe":
            extra = [("psx", r[1]) for r in reads if isinstance(r, tuple) and r[0] == "ps"]
            extra += [("psx", w[1]) for w in writes if isinstance(w, tuple) and w[0] == "ps"]
            writes = list(writes) + extra
        for r in reads:
            self._dep(op, self.last_w.get(r))
        for w in writes:
            self._dep(op, self.last_w.get(w))
            for a in self.readers.get(w, ()):
                self._dep(op, a)
        for r in reads:
            self.readers.setdefault(r, []).append(op)
        for w in writes:
            self.last_w[w] = op
            self.readers[w] = []
        if op.is_dma:
            c = self.dma_counts.get(dma_key, 0) + 1
            self.dma_counts[dma_key] = c
            op.count = c
            self.last_dma[dma_key] = op
        op.idx = len(self.ops[eng])
        self.ops[eng].append(op)
        return op

    def barrier(self):
        deps = []
        for e in self.ENGS:
            for op in reversed(self.ops[e]):
                if not op.is_dma:
                    deps.append(op)
                    break
        deps.extend(self.last_dma.values())
        for e in self.ENGS:
            self.bar[e] = list(deps)

    def emit(self, stack):
        nc = self.nc
        esem = {e: stack.enter_context(nc.semaphore("s_" + e)) for e in self.ENGS}
        dsem = {k: stack.enter_context(nc.semaphore("d_%d" % i)) for i, k in enumerate(self.dma_counts)}
        for e in self.ENGS:
            c = 0
            for op in self.ops[e]:
                if (not op.is_dma) and op.awaited:
                    c += 1
                    op.count = c
        block = stack.enter_context(nc.Block())
        final = [(dsem[k], 16 * c) for k, c in self.dma_counts.items()]

        def run(ename, eh, is_last=False):
            waited = {}
            for op in self.ops[ename]:
                for a in op.deps:
                    if a.is_dma:
                        sem, val = dsem[a.key], 16 * a.count
                    else:
                        sem, val = esem[a.eng], a.count
                    sid = id(sem)
                    if waited.get(sid, 0) >= val:
                        continue
                    waited[sid] = val
                    eh.wait_ge(sem, val)
                ins = op.fn(eh)
                if op.is_dma:
                    ins.then_inc(dsem[op.key], 16)
                elif op.awaited:
                    ins.then_inc(esem[ename], 1)
            if is_last:
                for sem, val in final:
                    eh.wait_ge(sem, val)

        @block.tensor
        def _(eh):
            run("pe", eh)

        @block.scalar
        def _(eh):
            run("act", eh)

        @block.vector
        def _(eh):
            run("dve", eh)

        @block.gpsimd
        def _(eh):
            run("pool", eh)

        @block.sync
        def _(eh):
            run("sp", eh, is_last=True)


class Arena:
    def __init__(self, ap, ncols):
        self.ap = ap
        self.n = ncols
        self.off = 0

    def reset(self):
        self.off = 0

    def _shape(self, v, shape):
        if len(shape) == 2:
            v = v.rearrange("p (a b) -> p a b", a=shape[0])
        elif len(shape) == 3:
            v = v.rearrange("p (a b c) -> p a b c", a=shape[0], b=shape[1])
        return v

    def f(self, *shape):
        cols = int(np.prod(shape))
        assert self.off + cols <= self.n, ("arena overflow", self.off, cols, self.n)
        v = self.ap[:, self.off:self.off + cols]
        self.off += cols
        return self._shape(v, shape)

    def b(self, *shape):
        cols = int(np.prod(shape))
        c32 = (cols + 1) // 2
        assert self.off + c32 <= self.n, ("arena overflow", self.off, c32, self.n)
        v = self.ap[:, self.off:self.off + c32].bitcast(BF16)[:, 0:cols]
        self.off += c32
        return self._shape(v, shape)


def _bf(a):
    return np.ascontiguousarray(np.asarray(a, dtype=np.float32).astype(ml_dtypes.bfloat16))


def _pool_tables():
    A = np.zeros((4, 3, 128, 128), np.float32)
    inv = np.zeros((4, 128), np.float32)
    for g, w in enumerate(POOL_WINDOWS):
        for t in range(128):
            for tp in range(t - w + 1, t + 1):
                if tp >= 0:
                    A[g, 0, tp, t] += 1.0 / w
                else:
                    A[g, 1, tp + 128, t] += 1.0 / w
            A[g, 0, t, t] -= 1.0
            cnt = min(t + 1, w)
            for tp in range(max(0, t - w + 1), t + 1):
                A[g, 2, tp, t] += 1.0
            A[g, 2, t, t] -= cnt
            inv[g, t] = 1.0 / cnt
    At = np.transpose(A, (2, 0, 1, 3)).reshape(128, 4 * 3 * 128)
    invb = np.broadcast_to(inv.reshape(1, 4 * 128), (128, 4 * 128))
    return _bf(At), np.ascontiguousarray(invb, dtype=np.float32)


def layer_norm_tile(P, r_ap, rkey, lng, lnb, stat, skey, epsc):
    st6 = stat[:, 0:12].rearrange("p (a b) -> p a b", a=2)
    mv = stat[:, 12:14]
    P.add("dve", lambda e: e.bn_stats(out=st6[:, 0, :], in_=r_ap[:, 0:512]), reads=[rkey], writes=[skey])
    P.add("dve", lambda e: e.bn_stats(out=st6[:, 1, :], in_=r_ap[:, 512:1024]), reads=[rkey], writes=[skey])
    P.add("dve", lambda e: e.bn_aggr(out=mv, in_=stat[:, 0:12]), reads=[skey], writes=[skey])
    P.add("act", lambda e: e.activation(out=stat[:, 14:15], in_=stat[:, 13:14], func=AF.Sqrt,
                                        bias=epsc, scale=1.0), reads=[skey, "consts"], writes=[skey])
    P.add("dve", lambda e: e.reciprocal(out=stat[:, 14:15], in_=stat[:, 14:15]), reads=[skey], writes=[skey])
    P.add("dve", lambda e: e.scalar_tensor_tensor(out=stat[:, 15:16], in0=stat[:, 12:13], scalar=-1.0,
                                                  in1=stat[:, 14:15], op0=ALU.mult, op1=ALU.mult),
          reads=[skey], writes=[skey])
    P.add("act", lambda e: e.activation(out=r_ap, in_=r_ap, func=AF.Identity, bias=stat[:, 15:16],
                                        scale=stat[:, 14:15]), reads=[skey, rkey], writes=[rkey])
    P.add("pool", lambda e: e.tensor_tensor(out=r_ap, in0=r_ap, in1=lng, op=ALU.mult),
          reads=[rkey, "lnc"], writes=[rkey])
    P.add("pool", lambda e: e.tensor_tensor(out=r_ap, in0=r_ap, in1=lnb, op=ALU.add),
          reads=[rkey, "lnc"], writes=[rkey])


def _slopes():
    h = np.arange(1, 17, dtype=np.float32)
    return (2.0 ** (-8.0 * h / 16.0)).astype(np.float32)


def _l1_tables():
    t = {}
    sl = _slopes()
    tq = np.arange(S, dtype=np.float64)
    qal = np.zeros((16, 7, S), np.float32)
    for h in range(16):
        s0 = float(sl[h])
        s1 = float(np.float32(s0).astype(ml_dtypes.bfloat16))
        s2 = float(np.float32(s0 - s1).astype(ml_dtypes.bfloat16))
        s3 = float(np.float32(s0 - s1 - s2).astype(ml_dtypes.bfloat16))
        qal[h, 0] = -s0 * tq
        for i, si in enumerate((s1, s2, s3)):
            qal[h, 1 + i] = 64.0 * si
            qal[h, 4 + i] = si
    t["qalibi"] = _bf(qal)
    kal = np.zeros((7, S), np.float32)
    kal[0] = 1.0
    kal[1:4] = (np.arange(S) // 64)[None, :]
    kal[4:7] = (np.arange(S) % 64)[None, :]
    t["kalibi"] = _bf(kal)
    kc = np.zeros((7, 128), np.float32)
    ce = np.arange(127) * 16 + 31
    kc[0] = 1.0
    kc[1:4, :127] = (ce // 64)[None, :]
    kc[4:7, :127] = (ce % 64)[None, :]
    t["kalibic"] = _bf(kc)
    E = np.zeros((32, S), np.float32)
    E[np.arange(S) // 64, np.arange(S)] = 1.0
    t["eoh"] = _bf(E)
    cm = np.full((128, S), NEGM, np.float32)
    cm[:127] = np.where(np.arange(S)[None, :] >= ce[:, None], 0.0, NEGM)
    t["cmpmask"] = _bf(cm)
    dk = np.arange(128)[:, None]
    dq = np.arange(384)[None, :]
    t["winmask"] = _bf(np.where((dq - dk >= 0) & (dq - dk < 256), 0.0, NEGM))
    dq = np.arange(128)[None, :]
    t["trimask"] = _bf(np.where(dk <= dq, 0.0, NEGM))
    vcc = np.zeros((128, 33), np.float32)
    vcc[:, 0] = 1.0
    c0 = np.arange(127)[:, None] * 16
    j0 = np.arange(32)[None, :] * 64
    vcc[:127, 1:] = ((c0 < j0 + 64) & (c0 + 32 > j0)).astype(np.float32)
    t["vcc"] = _bf(vcc)
    q = np.arange(128)[:, None, None]
    qt = np.arange(16)[None, :, None]
    j = np.arange(32)[None, None, :]
    cur = (qt * 128 + q) // 64
    forced = (j == 0) | (j == cur) | (j == cur - 1)
    t["forced"] = np.ascontiguousarray(np.where(forced, 1e9, 0.0).astype(np.float32).reshape(128, 512))
    t["future"] = np.ascontiguousarray(np.where(j > cur, -1e30, 3e38).astype(np.float32).reshape(128, 512))
    return t


def _l1_dram(nc, din):
    L = {}
    L["wg1"] = din("wg1", [4, 128, 8, 640])
    L["wg2"] = din("wg2", [4, 128, 8, 268])
    L["w1k"] = din("w1k", [64, 32, 256])
    L["w1v"] = din("w1v", [64, 32, 256])
    L["w2k"] = din("w2k", [128, 2, 64])
    L["w2v"] = din("w2v", [128, 2, 64])
    L["posk"] = din("posk", [64, 32])
    L["posv"] = din("posv", [64, 32])
    L["wout1"] = din("wout1", [128, 8, 1024])
    L["qalibi"] = din("qalibi", [16, 7, S], BF16)
    L["kalibi"] = din("kalibi", [7, S], BF16)
    L["kalibic"] = din("kalibic", [7, 128], BF16)
    L["eoh"] = din("eoh", [32, S], BF16)
    L["cmpmask"] = din("cmpmask", [128, S], BF16)
    L["winmask"] = din("winmask", [128, 384], BF16)
    L["trimask"] = din("trimask", [128, 128], BF16)
    L["vcc"] = din("vcc", [128, 33], BF16)
    L["forced"] = din("forced", [128, 512])
    L["future"] = din("future", [128, 512])
    return L


def _host_l1(inputs):
    f = lambda a: np.ascontiguousarray(np.asarray(a, dtype=np.float32))
    m = {}
    W = f(inputs["nsa_w_in"])[0].reshape(8, 128, 3632).transpose(1, 0, 2)
    wg1 = np.zeros((4, 128, 8, 640), np.float32)
    wg2 = np.zeros((4, 128, 8, 268), np.float32)
    for g in range(4):
        wg1[g, :, :, 0:256] = W[:, :, 256 * g:256 * g + 256]
        wg1[g, :, :, 256:320] = W[:, :, 1024 + 64 * g:1024 + 64 * g + 64]
        wg1[g, :, :, 320:384] = W[:, :, 1280 + 64 * g:1280 + 64 * g + 64]
        wg1[g, :, :, 384:448] = W[:, :, 1536 + 64 * g:1536 + 64 * g + 64]
        wg1[g, :, :, 448:512] = W[:, :, 2048 + 64 * g:2048 + 64 * g + 64]
        wg1[g, :, :, 512:576] = W[:, :, 1792 + 64 * g:1792 + 64 * g + 64]
        wg1[g, :, :, 576:640] = W[:, :, 2304 + 64 * g:2304 + 64 * g + 64]
        wg2[g, :, :, 0:256] = W[:, :, 2560 + 256 * g:2560 + 256 * g + 256]
        wg2[g, :, :, 256:268] = W[:, :, 3584 + 12 * g:3584 + 12 * g + 12]
    m["wg1"] = wg1
    m["wg2"] = wg2
    m["w1k"] = np.ascontiguousarray(f(inputs["nsa_cmp_w1_k"])[0].reshape(32, 64, 256).transpose(1, 0, 2))
    m["w1v"] = np.ascontiguousarray(f(inputs["nsa_cmp_w1_v"])[0].reshape(32, 64, 256).transpose(1, 0, 2))
    m["w2k"] = np.ascontiguousarray(f(inputs["nsa_cmp_w2_k"])[0].reshape(2, 128, 64).transpose(1, 0, 2))
    m["w2v"] = np.ascontiguousarray(f(inputs["nsa_cmp_w2_v"])[0].reshape(2, 128, 64).transpose(1, 0, 2))
    m["posk"] = np.ascontiguousarray(f(inputs["nsa_cmp_pos_k"])[0].T)
    m["posv"] = np.ascontiguousarray(f(inputs["nsa_cmp_pos_v"])[0].T)
    m["wout1"] = np.ascontiguousarray(f(inputs["nsa_w_out"])[0].reshape(8, 128, 1024).transpose(1, 0, 2))
    m.update(_l1_tables())
    return m


L1_STAGE = 99


def _build_l1(nc, P, AR, psum, psum2, L, x1_d, out_d, lng_d, lnb_d, identf_d, identb_d):
    STG = L1_STAGE
    def PS(i):
        return ("ps", i)

    def MM(out, lhsT, rhs, start, stop, reads, writes):
        P.add("pe", lambda e: e.matmul(out, lhsT=lhsT, rhs=rhs, start=start, stop=stop), reads=reads, writes=writes)

    def ACT(out, in_, func, reads, writes, **kw):
        P.add("act", lambda e: e.activation(out=out, in_=in_, func=func, **kw), reads=reads, writes=writes)

    def TT(eng, out, in0, in1, op, reads, writes):
        P.add(eng, lambda e: e.tensor_tensor(out=out, in0=in0, in1=in1, op=op), reads=reads, writes=writes)

    def TS(out, in0, s1, s2, op0, op1, reads, writes):
        if op1 is None:
            P.add("dve", lambda e: e.tensor_scalar(out=out, in0=in0, scalar1=s1, scalar2=None, op0=op0),
                  reads=reads, writes=writes)
        else:
            P.add("dve", lambda e: e.tensor_scalar(out=out, in0=in0, scalar1=s1, scalar2=s2, op0=op0, op1=op1),
                  reads=reads, writes=writes)

    def CP(out, in_, reads, writes, scale=None):
        if scale is None:
            P.add("dve", lambda e: e.tensor_copy(out=out, in_=in_), reads=reads, writes=writes)
        else:
            P.add("dve", lambda e: e.tensor_scalar(out=out, in0=in_, scalar1=scale, scalar2=None, op0=ALU.mult),
                  reads=reads, writes=writes)

    def DMA(q, out, in_, reads, writes, key):
        P.add(q, lambda e: e.dma_start(out=out, in_=in_), reads=reads, writes=writes, dma_key=key)

    W1 = AR.b(8, 640)
    W2 = [AR.b(8, 268) for _ in range(2)]
    w1 = AR.b(32, 256)
    w2 = [AR.b(2, 64) for _ in range(2)]
    pos = AR.b(32)
    wout1 = AR.b(8, 1024)
    identb = AR.b(128)
    x1T = AR.b(8, S)
    ogT = AR.b(8, S)
    q_aug = [AR.b(S) for _ in range(4)]
    ksel = AR.b(S)
    kwin = AR.b(S)
    kcmp = AR.b(128)
    pair = AR.b(S)
    vsel = AR.b(16, 65)
    vwin = AR.b(16, 65)
    vcmp = AR.b(97)
    hs = [AR.b(2, 128) for _ in range(2)]
    cmpmask = AR.b(S)
    winmask = AR.b(384)
    trimask = AR.b(128)
    Pbig = AR.b(2048)
    selm_w = AR.b(4, 128)
    ogb = [AR.b(256) for _ in range(4)]
    identf = AR.f(128)
    lng1 = AR.f(1024)
    lnb1 = AR.f(1024)
    xr = [AR.f(1024) for _ in range(2)]
    xs2 = [AR.f(1024) for _ in range(2)]
    stat = [AR.f(16) for _ in range(2)]
    o_tm = [AR.f(4, 256) for _ in range(2)]
    gates = AR.f(16, 12)
    imp = AR.f(4, 32)
    impm = AR.f(4, 32)
    forced = AR.f(16, 32)
    future = AR.f(16, 32)
    m8 = AR.f(4, 8)
    thr = AR.f(4)
    rden = [AR.f(4) for _ in range(2)]
    fcol = [AR.f(4) for _ in range(2)]
    tmp_o = [AR.f(4, 64) for _ in range(2)]
    tmp_i = AR.f(4, 32)
    sz = [AR.f(256) for _ in range(4)]
    bh = AR.f(4)
    epsc = AR.f(1)

    for t_, k_ in ((ksel, "ksel_c"), (kwin, "kwin_c"), (kcmp, "kcmp_c"), (vcmp, "vcmp_c"),
                   (selm_w.rearrange("p a b -> p (a b)"), "selm")):
        P.add("pool", lambda e, t_=t_: e.memset(t_, 0.0), writes=[k_])
    for hl in range(4):
        P.add("pool", lambda e, hl=hl: e.memset(q_aug[hl], 0.0), writes=[("qa_q", hl), ("qa_s", hl), ("qa_a", hl)])
    P.add("pool", lambda e: e.memset(vsel[:, :, 64:65], 1.0), writes=["vsel_c"])
    P.add("pool", lambda e: e.memset(vwin[:, :, 64:65], 1.0), writes=["vwin_c"])
    P.add("dve", lambda e: e.memset(epsc, LN_EPS), writes=["consts"])
    DMA("sp", identb, identb_d, [], ["consts"], "c1")
    DMA("sp", identf, identf_d, [], ["consts"], "c1")
    DMA("sp", lng1, lng_d[1], [], ["lnc"], "c1")
    DMA("sp", lnb1, lnb_d[1], [], ["lnc"], "c1")
    DMA("sp", cmpmask, L["cmpmask"], [], ["consts"], "c1")
    DMA("sp", winmask, L["winmask"], [], ["consts"], "c1")
    DMA("sp", trimask, L["trimask"], [], ["consts"], "c1")
    DMA("sp", forced.rearrange("p a b -> p (a b)"), L["forced"], [], ["consts"], "c1")
    DMA("sp", future.rearrange("p a b -> p (a b)"), L["future"], [], ["consts"], "c1")
    DMA("sp", ksel[64:96, :], L["eoh"], [], ["ksel_c"], "c2")
    DMA("sp", ksel[96:103, :], L["kalibi"], [], ["ksel_c"], "c2")
    DMA("sp", kwin[96:103, :], L["kalibi"], [], ["kwin_c"], "c2")
    DMA("sp", kcmp[96:103, :], L["kalibic"], [], ["kcmp_c"], "c2")
    DMA("sp", vcmp[:, 64:97], L["vcc"], [], ["vcmp_c"], "c2")
    for kc in range(0, 8, 2):
        DMA("pool", W1[:, kc:kc + 2, :], L["wg1"][0, :, kc:kc + 2, :], [], ["W1"], "W1")
    for kc in range(0, 8, 4):
        DMA("pool", W2[0][:, kc:kc + 4, :], L["wg2"][0, :, kc:kc + 4, :], [], [("W2", 0)], "W20")
    for kv, (wn, w2n, pn) in enumerate((("w1k", "w2k", "posk"), ("w1v", "w2v", "posv"))):
        for i in range(0, 32, 8):
            DMA("pool", w1[64 * kv:64 * kv + 64, i:i + 8, :], L[wn][:, i:i + 8, :], [], ["wcmp"], "wcmp")
        DMA("pool", w2[kv], L[w2n], [], ["wcmp"], "wcmp")
        DMA("pool", pos[64 * kv:64 * kv + 64, :], L[pn], [], ["wcmp"], "wcmp")
    for c in range(8):
        DMA("pool", wout1[:, c, :], L["wout1"][:, c, :], [], ["wout1"], "wout1")
    for kv in range(2):
        for hc in range(2):
            col = kv * 2 + hc
            for i in range(32):
                MM(psum[0][:, col:col + 1], w1[64 * kv:64 * kv + 64, i, hc * 128:(hc + 1) * 128],
                   pos[64 * kv:64 * kv + 64, i:i + 1], i == 0, i == 31, ["wcmp"], [PS(0)])
    ACT(bh, psum[0][:, 0:4], AF.Copy, [PS(0)], ["bh"])

    if STG < 1:
        return
    rr = {"s": 0, "o": 0, "p": 0, "a": 0, "e": 0, "ring": 0}

    def nxt(k, n):
        v = rr[k]
        rr[k] = (v + 1) % n
        return v

    def x1T_tile(s, j):
        xs_ = j % 2
        xk = ("xs2", xs_)
        DMA("sp", xs2[xs_], x1_d[s, j * 128:(j + 1) * 128, :], [], [xk], "xs2%d" % xs_)
        for q4 in range(2):
            pb = nxt("a", 2)
            for i4 in range(4):
                kc = q4 * 4 + i4
                P.add("pe", lambda e, kc=kc, i4=i4, pb=pb, xs_=xs_: e.transpose(
                    out=psum[pb][:, i4 * 128:(i4 + 1) * 128], in_=xs2[xs_][:, kc * 128:(kc + 1) * 128],
                    identity=identf), reads=[xk, "consts"], writes=[PS(pb)])
            CP(x1T[:, q4 * 4:(q4 + 1) * 4, j * 128:(j + 1) * 128],
               psum[pb][:, :].rearrange("p (a b) -> p a b", a=4), [PS(pb)], ["x1T"])

    xr.append(o_tm[0].rearrange("p a b -> p (a b)"))
    xr.append(o_tm[1].rearrange("p a b -> p (a b)"))
    stat.append(AR.f(16))
    stat.append(AR.f(16))

    def outproj_tile(s, jt):
        xs_ = jt % 4
        xk = ("xr", xs_)
        alias = [("o_tm", xs_ - 2, h_) for h_ in range(4)] if xs_ >= 2 else []
        DMA("sp", xr[xs_], x1_d[s, jt * 128:(jt + 1) * 128, :], [], [xk] + alias, "xr%d" % xs_)
        for hh in range(2):
            pb = nxt("a", 2)
            for c in range(8):
                MM(psum[pb][:, :], ogT[:, c, jt * 128:(jt + 1) * 128], wout1[:, c, hh * 512:(hh + 1) * 512],
                   c == 0, c == 7, [("ogT", c // 2), "wout1"], [PS(pb)])
            P.add("dve", lambda e, hh=hh, pb=pb, xs_=xs_: e.scalar_tensor_tensor(
                out=xr[xs_][:, hh * 512:(hh + 1) * 512], in0=xr[xs_][:, hh * 512:(hh + 1) * 512],
                scalar=DN_ALPHA, in1=psum[pb][:, :], op0=ALU.mult, op1=ALU.add), reads=[PS(pb), xk], writes=[xk])
        layer_norm_tile(P, xr[xs_], xk, lng1, lnb1, stat[xs_], ("stat1", xs_), epsc)
        DMA("pool", out_d[s, jt * 128:(jt + 1) * 128, :], xr[xs_], [xk], [], "ost%d" % xs_)

    def load_group_weights(gi_):
        g_ = gi_ % 4
        for kc in range(0, 8, 2):
            DMA("pool", W1[:, kc:kc + 2, :], L["wg1"][g_, :, kc:kc + 2, :], [], ["W1"], "W1")
        for kc in range(0, 8, 4):
            DMA("pool", W2[gi_ % 2][:, kc:kc + 4, :], L["wg2"][g_, :, kc:kc + 4, :], [], [("W2", gi_ % 2)], "W2%d" % (gi_ % 2))

    pending = []

    def drain_pending(n):
        for _ in range(n):
            if pending:
                s_, jt_ = pending.pop(0)
                outproj_tile(s_, jt_)

    for s in range(NSEQ):
        if s == 0:
            for j in range(16):
                x1T_tile(0, j)
        if STG < 2:
            return
        for g in range(4):
            gi = s * 4 + g
            w2s = gi % 2
            w2k_ = ("W2", w2s)
            for hl in range(4):
                DMA("sp", q_aug[hl][96:103, :], L["qalibi"][4 * g + hl], [], [("qa_a", hl)], "qal")
            for pc in range(2):
                for tb in range(4):
                    pb = nxt("a", 2)
                    for kc in range(8):
                        MM(psum[pb][:, :], W1[:, kc, pc * 128:(pc + 1) * 128], x1T[:, kc, tb * 512:(tb + 1) * 512],
                           kc == 0, kc == 7, ["W1", "x1T"], [PS(pb)])
                    CP(q_aug[2 * pc][0:64, tb * 512:(tb + 1) * 512], psum[pb][0:64, :],
                       [PS(pb)], [("qa_q", 2 * pc)], scale=0.125)
                    CP(q_aug[2 * pc + 1][0:64, tb * 512:(tb + 1) * 512], psum[pb][64:128, :],
                       [PS(pb)], [("qa_q", 2 * pc + 1)], scale=0.125)
                    drain_pending(1)
            if STG < 3:
                return
            for tb in range(4):
                c0, c1 = tb * 512, (tb + 1) * 512
                pb = nxt("a", 2)
                for kc in range(8):
                    MM(psum[pb][:, :], W1[:, kc, 256:384], x1T[:, kc, c0:c1], kc == 0, kc == 7, ["W1", "x1T"], [PS(pb)])
                CP(pair.rearrange("p (ph c) -> p ph c", ph=16)[:, :, tb * 32:(tb + 1) * 32],
                   psum[pb][:, :].rearrange("p (c ph) -> p ph c", ph=16), [PS(pb)], ["pair"])
                pb = nxt("a", 2)
                for kc in range(8):
                    MM(psum[pb][:, :], W1[:, kc, 384:512], x1T[:, kc, c0:c1], kc == 0, kc == 7, ["W1", "x1T"], [PS(pb)])
                P.add("dve", lambda e, pb=pb, c0=c0, c1=c1: e.tensor_copy(out=ksel[0:64, c0:c1], in_=psum[pb][0:64, :]),
                      reads=[PS(pb)], writes=["ksel"])
                CP(kwin[0:64, c0:c1], psum[pb][64:128, :], [PS(pb)], ["kwin"])
            if STG < 4:
                return
            for j4 in range(4):
                pb = nxt("a", 2)
                for jj in range(4):
                    j = j4 * 4 + jj
                    for kc in range(8):
                        MM(psum[pb][:, jj * 128:(jj + 1) * 128], x1T[:, kc, j * 128:(j + 1) * 128], W1[:, kc, 512:640],
                           kc == 0, kc == 7, ["W1", "x1T"], [PS(pb)])
                pv = psum[pb][:, :].rearrange("p (a b) -> p a b", a=4)
                P.add("dve", lambda e, pv=pv, j4=j4: e.tensor_copy(out=vsel[:, j4 * 4:(j4 + 1) * 4, 0:64], in_=pv[:, :, 0:64]),
                      reads=[PS(pb)], writes=["vsel"])
                ACT(vwin[:, j4 * 4:(j4 + 1) * 4, 0:64], pv[:, :, 64:128], AF.Copy, [PS(pb)], ["vwin"])
            if STG < 5:
                return
            if gi + 1 < NSEQ * 4:
                load_group_weights(gi + 1)
            pb = nxt("a", 2)
            for j in range(16):
                for kc in range(8):
                    MM(psum[pb][:, j * 12:(j + 1) * 12], x1T[:, kc, j * 128:(j + 1) * 128], W2[w2s][:, kc, 256:268],
                       kc == 0, kc == 7, [w2k_, "x1T"], [PS(pb)])
            ACT(gates.rearrange("p a b -> p (a b)"), psum[pb][:, 0:192], AF.Sigmoid, [PS(pb)], ["gates"])
            if STG < 6:
                return
            for kv in range(2):
                for hc in range(2):
                    pb = nxt("a", 2)
                    for i in range(32):
                        MM(psum[pb][:, 0:127], w1[64 * kv:64 * kv + 64, i, hc * 128:(hc + 1) * 128],
                           pair[64 * kv:64 * kv + 64, (i % 16) * 128 + i // 16:(i % 16) * 128 + i // 16 + 127],
                           i == 0, i == 31, ["wcmp", "pair"], [PS(pb)])
                    ACT(hs[kv][:, hc, 0:127], psum[pb][:, 0:127], AF.Silu, [PS(pb), "bh"], [("hs", kv)],
                        bias=bh[:, kv * 2 + hc:kv * 2 + hc + 1])
            pb = nxt("a", 2)
            for hc in range(2):
                MM(psum[pb][0:64, 0:127], w2[0][:, hc, :], hs[0][:, hc, 0:127], hc == 0, hc == 1,
                   ["wcmp", ("hs", 0)], [PS(pb)])
            CP(kcmp[0:64, 0:127], psum[pb][0:64, 0:127], [PS(pb)], ["kcmp"])
            pb = nxt("a", 2)
            for hc in range(2):
                MM(psum[pb][0:127, 0:64], hs[1][:, hc, 0:127], w2[1][:, hc, :], hc == 0, hc == 1,
                   ["wcmp", ("hs", 1)], [PS(pb)])
            CP(vcmp[0:127, 0:64], psum[pb][0:127, 0:64], [PS(pb)], ["vcmp"])

            if STG < 7:
                return
            def branch_epilogue(hl, br, ob, Wd, qt, os_, first, with_imp):
                O = psum[ob][:, 0:4 * Wd].rearrange("p (a b) -> p a b", a=4)
                e_ = nxt("e", 2)
                rk, fk, tk = ("rden", e_), ("fcol", e_), ("tmp_o", e_)
                TS(rden[e_].unsqueeze(2), O[:, :, 64:65], 1e-30, None, ALU.max, None, [PS(ob)], [rk])
                P.add("dve", lambda e: e.reciprocal(out=rden[e_], in_=rden[e_]), reads=[rk], writes=[rk])
                if with_imp:
                    rb = rden[e_].unsqueeze(2).broadcast_to([128, 4, 32])
                    if hl == 0:
                        TT("dve", imp, O[:, :, 65:97], rb, ALU.mult, [PS(ob), rk], ["imp"])
                    else:
                        TT("dve", tmp_i, O[:, :, 65:97], rb, ALU.mult, [PS(ob), rk], ["tmp_i"])
                        TT("pool", imp, imp, tmp_i, ALU.add, ["tmp_i", "imp"], ["imp"])
                gcol = gates[:, qt * 4:(qt + 1) * 4, hl * 3 + br]
                TT("dve", fcol[e_], rden[e_], gcol, ALU.mult, [rk, "gates"], [fk])
                fb = fcol[e_].unsqueeze(2).broadcast_to([128, 4, 64])
                odst = o_tm[os_][:, :, hl * 64:(hl + 1) * 64]
                ok = ("o_tm", os_, hl)
                if first:
                    TT("dve", odst, O[:, :, 0:64], fb, ALU.mult, [PS(ob), fk], [ok])
                else:
                    TT("dve", tmp_o[e_], O[:, :, 0:64], fb, ALU.mult, [PS(ob), fk], [tk])
                    TT("pool", odst, odst, tmp_o[e_], ALU.add, [tk, ok], [ok])

            SBK = [2, 3, 0, 1]

            def sel_chain(qt):
                TT("dve", impm, imp, forced[:, qt * 4:(qt + 1) * 4, :], ALU.max, ["imp", "consts"], ["impm"])
                TT("dve", impm, impm, future[:, qt * 4:(qt + 1) * 4, :], ALU.min, ["impm", "consts"], ["impm"])
                for qs in range(4):
                    P.add("dve", lambda e, qs=qs: e.max(out=m8[:, qs, :], in_=impm[:, qs, :]), reads=["impm"], writes=["m8"])
                TS(thr.unsqueeze(2), m8[:, :, 7:8], 0.0, None, ALU.max, None, ["m8"], ["thr"])
                for qs in range(4):
                    TS(selm_w[:, qs, 64:96], impm[:, qs, :], thr[:, qs:qs + 1], 1.0, ALU.is_ge, ALU.subtract,
                       ["impm", "thr"], ["selm"])

            def sel_transposes(Q0):
                for qs in range(4):
                    MM(psum[6][:, qs * 128:(qs + 1) * 128], selm_w[:, qs, :], identb, True, True, ["selm", "consts"], [PS(6)])
                for hl in range(4):
                    CP(q_aug[hl][64:96, Q0:Q0 + 512], psum[6][64:96, :], [PS(6)], [("qa_s", hl)], scale=30000.0)

            def gate_path(qt, os_):
                p6b = psum[6][:, :].bitcast(BF16)
                for qs in range(4):
                    TT("dve", ogb[qs], o_tm[os_][:, qs, :], sz[qs], ALU.mult,
                       [("sz", qs)] + [("o_tm", os_, h_) for h_ in range(4)], [("ogb", qs)])
                for qs in range(4):
                    for cc in range(2):
                        P.add("pe", lambda e, cc=cc, qs=qs: e.transpose(
                            out=p6b[:, qs * 256 + cc * 128:qs * 256 + (cc + 1) * 128],
                            in_=ogb[qs][:, cc * 128:(cc + 1) * 128], identity=identb),
                            reads=[("ogb", qs), "consts"], writes=[PS(6)])
                CP(ogT[:, 2 * g:2 * g + 2, qt * 512:(qt + 1) * 512].rearrange("p c (q t) -> p c q t", q=4),
                   p6b.rearrange("p (q c t) -> p c q t", q=4, c=2), [PS(6)], [("ogT", g)])

            def zproj_tiles(qt):
                for qs in range(4):
                    jt = 4 * qt + qs

                    def qk_f(sb, jt=jt):
                        for kc in range(8):
                            MM(psum[sb][:, 0:256], x1T[:, kc, jt * 128:(jt + 1) * 128], W2[w2s][:, kc, 0:256],
                               kc == 0, kc == 7, [w2k_, "x1T"], [PS(sb)])

                    def act_f(sb, qs=qs):
                        ACT(sz[qs], psum[sb][:, 0:256], AF.Silu, [PS(sb)], [("sz", qs)])

                    mk_tile(qk_f, 0, 256, (lambda pr: None), None)
                    units[-1]["tiles"][0]["act"] = act_f

            units = []

            def mk_tile(qk, c0, c1, pv, post=None, pairable=False):
                t = {"qk": qk, "c0": c0, "c1": c1, "pv": pv, "post": post}
                if pairable and units and units[-1]["pair_open"]:
                    units[-1]["tiles"].append(t)
                    units[-1]["pair_open"] = False
                else:
                    units.append({"tiles": [t], "pair_open": pairable, "marker": None})

            def close_pair():
                if units:
                    units[-1]["pair_open"] = False

            def mk_marker(fn):
                close_pair()
                units.append({"tiles": [], "pair_open": False, "marker": fn})

            for qt in range(4):
                Q0 = qt * 512
                os_ = (gi * 4 + qt) % 2
                for hl in range(4):
                    ob = (4, 5, 7)[nxt("o", 3)]
                    qk = [("qa_q", hl), ("qa_s", hl), ("qa_a", hl)]

                    def qk_f(sb, hl=hl, Q0=Q0, qk=qk):
                        MM(psum[sb][:, :], kcmp[0:103, :], q_aug[hl][0:103, Q0:Q0 + 512], True, False,
                           ["kcmp", "kcmp_c"] + qk, [PS(sb)])
                        MM(psum[sb][:, :], identb, cmpmask[:, Q0:Q0 + 512], False, True, ["consts"], [PS(sb)])

                    def pv_f(pr, ob=ob):
                        for qs in range(4):
                            MM(psum[ob][:, qs * 97:(qs + 1) * 97], Pbig[:, pr * 512 + qs * 128:pr * 512 + (qs + 1) * 128],
                               vcmp[:, 0:97], True, True, [("Pb", pr), "vcmp", "vcmp_c"], [PS(ob)])

                    def post_f(hl=hl, ob=ob, qt=qt, os_=os_):
                        branch_epilogue(hl, 0, ob, 97, qt, os_, True, True)
                        if hl == 3:
                            sel_chain(qt)

                    mk_tile(qk_f, 0, 512, pv_f, post_f, pairable=False)
                close_pair()
                for hl in range(4):
                    qk = [("qa_q", hl), ("qa_s", hl), ("qa_a", hl)]
                    ob = (4, 5, 7)[nxt("o", 3)]
                    plan = []
                    for kt in range(max(0, 4 * qt - 2), 4 * qt + 4):
                        K0 = kt * 128
                        lo, hi = max(K0, Q0), min(K0 + 384, Q0 + 512)
                        if hi > lo:
                            plan.append((kt, K0, lo, hi))
                    lastk = {}
                    for (kt, K0, lo, hi) in plan:
                        for qs in range((lo - Q0) // 128, (hi - Q0) // 128):
                            lastk[qs] = kt
                    firstmm = True
                    for ti, (kt, K0, lo, hi) in enumerate(plan):
                        c0, c1, m0 = lo - Q0, hi - Q0, lo - K0

                        def qk_f(sb, hl=hl, qk=qk, K0=K0, lo=lo, hi=hi, c0=c0, c1=c1, m0=m0):
                            MM(psum[sb][:, c0:c1], kwin[0:103, K0:K0 + 128], q_aug[hl][0:103, lo:hi], True, False,
                               ["kwin", "kwin_c"] + qk, [PS(sb)])
                            MM(psum[sb][:, c0:c1], identb, winmask[:, m0:m0 + (hi - lo)], False, True, ["consts"], [PS(sb)])

                        pvl = []
                        for qs in range(c0 // 128, c1 // 128):
                            pvl.append((qs, firstmm, lastk[qs] == kt))
                            firstmm = False

                        def pv_f(pr, ob=ob, kt=kt, pvl=pvl):
                            for (qs, st_, sp_) in pvl:
                                MM(psum[ob][:, qs * 65:(qs + 1) * 65], Pbig[:, pr * 512 + qs * 128:pr * 512 + (qs + 1) * 128],
                                   vwin[:, kt, :], st_, sp_, [("Pb", pr), "vwin", "vwin_c"], [PS(ob)])

                        post_f = None
                        if ti == len(plan) - 1:
                            def post_f(hl=hl, ob=ob, qt=qt, os_=os_):
                                branch_epilogue(hl, 2, ob, 65, qt, os_, False, False)
                        mk_tile(qk_f, c0, c1, pv_f, post_f)
                mk_marker(lambda Q0=Q0: sel_transposes(Q0))
                zproj_tiles(qt)
                for hl in range(4):
                    qk = [("qa_q", hl), ("qa_s", hl), ("qa_a", hl)]
                    ob = (4, 5, 7)[nxt("o", 3)]
                    nk = 4 * qt + 4
                    firstmm = True
                    for kt in range(nk):
                        K0 = kt * 128
                        dq_ = kt - 4 * qt
                        c0 = max(dq_, 0) * 128

                        def qk_f(sb, hl=hl, qk=qk, K0=K0, Q0=Q0, c0=c0, dq_=dq_):
                            MM(psum[sb][:, c0:512], ksel[0:103, K0:K0 + 128], q_aug[hl][0:103, Q0 + c0:Q0 + 512], True, dq_ < 0,
                               ["ksel", "ksel_c"] + qk, [PS(sb)])
                            if dq_ >= 0:
                                MM(psum[sb][:, c0:c0 + 128], identb, trimask, False, True, ["consts"], [PS(sb)])

                        pvl = []
                        for qs in range(c0 // 128, 4):
                            pvl.append((qs, firstmm, kt == 4 * qt + qs))
                            firstmm = False

                        def pv_f(pr, ob=ob, kt=kt, pvl=pvl):
                            for (qs, st_, sp_) in pvl:
                                MM(psum[ob][:, qs * 65:(qs + 1) * 65], Pbig[:, pr * 512 + qs * 128:pr * 512 + (qs + 1) * 128],
                                   vsel[:, kt, :], st_, sp_, [("Pb", pr), "vsel", "vsel_c"], [PS(ob)])

                        post_f = None
                        if kt == nk - 1:
                            def post_f(hl=hl, ob=ob, qt=qt, os_=os_):
                                branch_epilogue(hl, 1, ob, 65, qt, os_, False, False)
                                if hl == 3:
                                    gate_path(qt, os_)
                        mk_tile(qk_f, c0, 512, pv_f, post_f, pairable=False)
                    close_pair()

            inflight = []

            def flush_one():
                u, need, pos = inflight.pop(0)
                for t, p in zip(u["tiles"], pos):
                    t["pv"](p)
                    if t["post"] is not None:
                        t["post"]()

            for u in units:
                if u["marker"] is not None:
                    u["marker"]()
                    continue
                nb = len(u["tiles"])
                skip = 1 if (nb == 2 and rr["ring"] % 2 == 1) else 0
                need = nb + skip
                while inflight and (sum(x[1] for x in inflight) + need > 4 or len(inflight) >= 3):
                    flush_one()
                rr["ring"] = (rr["ring"] + skip) % 4
                pos = [(rr["ring"] + k) % 4 for k in range(nb)]
                rr["ring"] = (rr["ring"] + nb) % 4
                for t, p in zip(u["tiles"], pos):
                    t["qk"](SBK[p])
                if nb == 2:
                    d = SBK[pos[0]] // 2
                    ACT(Pbig[:, pos[0] * 512:(pos[0] + 2) * 512], psum2[d][:, :], AF.Exp,
                        [PS(SBK[pos[0]]), PS(SBK[pos[1]])], [("Pb", pos[0]), ("Pb", pos[1])])
                else:
                    t = u["tiles"][0]
                    p = pos[0]
                    if t.get("act") is not None:
                        t["act"](SBK[p])
                    else:
                        ACT(Pbig[:, p * 512 + t["c0"]:p * 512 + t["c1"]], psum[SBK[p]][:, t["c0"]:t["c1"]], AF.Exp,
                            [PS(SBK[p])], [("Pb", p)])
                inflight.append((u, need, pos))
            while inflight:
                flush_one()

        for jt in range(16):
            pending.append((s, jt))
        if s + 1 < NSEQ:
            for jt in range(16):
                x1T_tile(s + 1, jt)
                if jt % 2 == 1:
                    drain_pending(1)
        else:
            drain_pending(16)


def build_program(do_l0=True, do_l1=True):
    nc = bass.Bass("TRN2", target_bir_lowering=False)
    dt = {}

    def din(name, shape, dtype=F32):
        dt[name] = nc.dram_tensor(name, list(shape), dtype, kind="ExternalInput").ap()
        return dt[name]

    x_d = din("x", [NSEQ, S, D])
    lng_d = din("lng", [2, 128, D])
    lnb_d = din("lnb", [2, 128, D])
    w_in0_d = din("w_in0", [128, 8, 4096])
    w_grp_d = din("w_grp", [128, 16, 512])
    w_out0_d = din("w_out0", [128, 16, 1024])
    scale_d = din("pscale", [128, 16])
    poolA_d = din("poolA", [128, 4 * 3 * 128], BF16)
    invc_d = din("invc", [128, 4 * 128])
    identf_d = din("identf", [128, 128])
    identb_d = din("identb", [128, 128], BF16)
    x1kind = "Internal" if (do_l0 and do_l1) else ("ExternalOutput" if do_l0 else "ExternalInput")
    x1_d = nc.dram_tensor("x1s", [NSEQ, S, D], F32, kind=x1kind).ap()
    out_d = nc.dram_tensor("out", [NSEQ, S, D], F32, kind="ExternalOutput").ap()
    L1D = _l1_dram(nc, din)

    stack = contextlib.ExitStack()
    with stack:
        ACOLS = 52800
        ar_t = stack.enter_context(nc.sbuf_tensor("arena", [128, ACOLS], F32))
        AR = Arena(ar_t[:], ACOLS)
        psum2 = [stack.enter_context(nc.psum_tensor("ps%d" % i, [128, 1024], F32)) for i in range(4)]
        psum = [psum2[i // 2][:, (i % 2) * 512:(i % 2 + 1) * 512] for i in range(8)]
        P = Prog(nc)

        def PS(i):
            return ("ps", i)

        if do_l0:
            AR.reset()
            w_in0 = AR.b(8, 4096)
            w_grp = AR.b(16, 512)
            w_out0 = AR.b(16, 1024)
            poolA = AR.b(12, 128)
            identb = AR.b(128)
            xT0 = [AR.b(8, 256) for _ in range(2)]
            u_tm = AR.b(12, 512)
            mT = [AR.b(4, 256) for _ in range(2)]
            gT = [AR.b(16, 256) for _ in range(2)]
            identf = AR.f(128)
            lng0 = AR.f(1024)
            lnb0 = AR.f(1024)
            invc = AR.f(4, 128)
            pscale = AR.f(16)
            epsc = AR.f(1)
            xs = [AR.f(2, 1024) for _ in range(2)]
            siluz = [AR.f(4, 256) for _ in range(2)]
            rbuf = [AR.f(1024) for _ in range(2)]
            stat = [AR.f(16) for _ in range(2)]

            P.add("sp", lambda e: e.dma_start(out=identf, in_=identf_d), writes=["consts"], dma_key="c0")
            P.add("sp", lambda e: e.dma_start(out=identb, in_=identb_d), writes=["consts"], dma_key="c0")
            P.add("sp", lambda e: e.dma_start(out=poolA.rearrange("p a b -> p (a b)"), in_=poolA_d), writes=["consts"], dma_key="c0")
            P.add("sp", lambda e: e.dma_start(out=invc.rearrange("p a b -> p (a b)"), in_=invc_d), writes=["consts"], dma_key="c0")
            P.add("sp", lambda e: e.dma_start(out=pscale, in_=scale_d), writes=["consts"], dma_key="c0")
            P.add("sp", lambda e: e.dma_start(out=lng0, in_=lng_d[0]), writes=["lnc"], dma_key="c0")
            P.add("sp", lambda e: e.dma_start(out=lnb0, in_=lnb_d[0]), writes=["lnc"], dma_key="c0")
            P.add("dve", lambda e: e.memset(epsc, LN_EPS), writes=["consts"])
            for gq in range(4):
                for cb in (gq, 4 + gq):
                    P.add("pool", lambda e, cb=cb: e.dma_start(out=w_in0[:, :, cb * 512:(cb + 1) * 512],
                                                             in_=w_in0_d[:, :, cb * 512:(cb + 1) * 512]),
                          writes=[("w_in0", cb)], dma_key="w_in0_%d" % cb)
                P.add("pool", lambda e, gq=gq: e.dma_start(out=w_grp[:, gq * 4:(gq + 1) * 4, :],
                                                         in_=w_grp_d[:, gq * 4:(gq + 1) * 4, :]),
                      writes=[("w_grp", gq)], dma_key="w_grp_%d" % gq)
            for c in range(0, 16, 2):
                P.add("pool", lambda e, c=c: e.dma_start(out=w_out0[:, c:c + 2, :], in_=w_out0_d[:, c:c + 2, :]),
                      writes=["w_out0"], dma_key="w_out0")

            NB = S // 256
            NIT = NSEQ * NB

            def geo(it):
                s, b = divmod(it, NB)
                return s, b, it % 2, b * 256

            def stage_T(it):
                s, b, sl, t0 = geo(it)
                xk, xtk = ("xs", sl), ("xT0", sl)
                for j in range(2):
                    P.add("sp", lambda e, j=j: e.dma_start(
                        out=xs[sl][:, j, :], in_=x_d[s, t0 + j * 128:t0 + (j + 1) * 128, :]),
                        writes=[xk], dma_key="xs%d" % sl)
                for j in range(2):
                    for q4 in range(2):
                        pb = (j * 2 + q4) % 2
                        for i4 in range(4):
                            kc = q4 * 4 + i4
                            P.add("pe", lambda e, j=j, kc=kc, i4=i4, pb=pb: e.transpose(
                                out=psum[pb][:, i4 * 128:(i4 + 1) * 128], in_=xs[sl][:, j, kc * 128:(kc + 1) * 128],
                                identity=identf), reads=[xk, "consts"], writes=[PS(pb)])
                        P.add("dve", lambda e, j=j, q4=q4, pb=pb: e.tensor_copy(
                            out=xT0[sl][:, q4 * 4:(q4 + 1) * 4, j * 128:(j + 1) * 128],
                            in_=psum[pb][:, :].rearrange("p (a b) -> p a b", a=4)),
                            reads=[PS(pb)], writes=[xtk])

            def stage_U(it, g):
                s, b, sl, t0 = geo(it)
                xtk = ("xT0", sl)
                for j in range(2):
                    n = b * 2 + j
                    slot = n % 3
                    pb = 2 + (j % 2)
                    for kc in range(8):
                        P.add("pe", lambda e, kc=kc, j=j, pb=pb: e.matmul(
                            psum[pb][:, :], lhsT=xT0[sl][:, kc, j * 128:(j + 1) * 128],
                            rhs=w_in0[:, kc, g * 512:(g + 1) * 512], start=(kc == 0), stop=(kc == 7)),
                            reads=[xtk, ("w_in0", g)], writes=[PS(pb)])
                    P.add("act", lambda e, slot=slot, pb=pb: e.activation(
                        out=u_tm[:, g * 3 + slot, :], in_=psum[pb][:, :], func=AF.Copy),
                        reads=[PS(pb)], writes=[("u", g, slot)])

            def stage_Z(it, g):
                s, b, sl, t0 = geo(it)
                xtk = ("xT0", sl)
                gs = (it * 4 + g) % 2
                szk = ("siluz", gs)
                for c2 in range(2):
                    pb = 4 + c2
                    for ci in range(2):
                        c = c2 * 2 + ci
                        col = 2048 + g * 512 + c * 128
                        for kc in range(8):
                            P.add("pe", lambda e, kc=kc, col=col, ci=ci, pb=pb: e.matmul(
                                psum[pb][:, ci * 256:(ci + 1) * 256], lhsT=w_in0[:, kc, col:col + 128],
                                rhs=xT0[sl][:, kc, :], start=(kc == 0), stop=(kc == 7)),
                                reads=[xtk, ("w_in0", 4 + g)], writes=[PS(pb)])
                    P.add("act", lambda e, c2=c2, pb=pb: e.activation(
                        out=siluz[gs][:, c2 * 2:(c2 + 1) * 2, :],
                        in_=psum[pb][:, :].rearrange("p (a b) -> p a b", a=2), func=AF.Silu),
                        reads=[PS(pb)], writes=[szk])

            def stage_PM(it, g):
                s, b, sl, t0 = geo(it)
                gs = (it * 4 + g) % 2
                mk = ("mT", gs)
                for c2 in range(2):
                    pb = 6 + c2
                    for ci in range(2):
                        c = c2 * 2 + ci
                        for j in range(2):
                            n = b * 2 + j
                            slot = n % 3
                            pslot = (n - 1) % 3
                            o = psum[pb][:, ci * 256 + j * 128: ci * 256 + (j + 1) * 128]
                            if n == 0:
                                P.add("pe", lambda e, o=o, slot=slot, c=c: e.matmul(
                                    o, lhsT=u_tm[:, g * 3 + slot, c * 128:(c + 1) * 128],
                                    rhs=poolA[:, g * 3 + 2, :], start=True, stop=True),
                                    reads=[("u", g, slot), "consts"], writes=[PS(pb)])
                            else:
                                P.add("pe", lambda e, o=o, slot=slot, c=c: e.matmul(
                                    o, lhsT=u_tm[:, g * 3 + slot, c * 128:(c + 1) * 128],
                                    rhs=poolA[:, g * 3 + 0, :], start=True, stop=False),
                                    reads=[("u", g, slot), "consts"], writes=[PS(pb)])
                                P.add("pe", lambda e, o=o, pslot=pslot, c=c: e.matmul(
                                    o, lhsT=u_tm[:, g * 3 + pslot, c * 128:(c + 1) * 128],
                                    rhs=poolA[:, g * 3 + 1, :], start=False, stop=True),
                                    reads=[("u", g, pslot), "consts"], writes=[PS(pb)])
                    P.add("act", lambda e, c2=c2, pb=pb: e.activation(
                        out=mT[gs][:, c2 * 2:(c2 + 1) * 2, :],
                        in_=psum[pb][:, :].rearrange("p (a b) -> p a b", a=2), func=AF.Copy),
                        reads=[PS(pb)], writes=[mk])
                    if b == 0:
                        P.add("dve", lambda e, c2=c2, pb=pb: e.tensor_tensor(
                            out=mT[gs][:, c2 * 2:(c2 + 1) * 2, 0:128],
                            in0=psum[pb][:, :].rearrange("p (a b) -> p a b", a=2)[:, :, 0:128],
                            in1=invc[:, g:g + 1, :].broadcast_to([128, 2, 128]), op=ALU.mult),
                            reads=[PS(pb), "consts"], writes=[mk])

            def stage_G(it, g):
                gs = (it * 4 + g) % 2
                szk, mk = ("siluz", gs), ("mT", gs)
                gb = it % 2
                for d2 in range(2):
                    pb = d2
                    for di in range(2):
                        d = d2 * 2 + di
                        for cc in range(4):
                            P.add("pe", lambda e, cc=cc, d=d, di=di, pb=pb: e.matmul(
                                psum[pb][:, di * 256:(di + 1) * 256],
                                lhsT=w_grp[:, g * 4 + cc, d * 128:(d + 1) * 128], rhs=mT[gs][:, cc, :],
                                start=(cc == 0), stop=(cc == 3)), reads=[mk, ("w_grp", g)], writes=[PS(pb)])
                    for di in range(2):
                        d = d2 * 2 + di
                        ch = g * 4 + d
                        P.add("dve", lambda e, d=d, di=di, ch=ch, pb=pb: e.scalar_tensor_tensor(
                            out=gT[gb][:, ch, :], in0=psum[pb][:, di * 256:(di + 1) * 256],
                            scalar=pscale[:, ch:ch + 1], in1=siluz[gs][:, d, :], op0=ALU.mult, op1=ALU.mult),
                            reads=[PS(pb), szk, "consts"], writes=[("gT", gb, ch)])

            def stage_O(it):
                s, b, sl, t0 = geo(it)
                xk = ("xs", sl)
                gb = it % 2
                for j in range(2):
                    rs = (it * 2 + j) % 2
                    rk = ("r", rs)
                    for hh in range(2):
                        pb = 2 + hh
                        for ch in range(16):
                            P.add("pe", lambda e, ch=ch, j=j, hh=hh, pb=pb: e.matmul(
                                psum[pb][:, :], lhsT=gT[gb][:, ch, j * 128:(j + 1) * 128],
                                rhs=w_out0[:, ch, hh * 512:(hh + 1) * 512], start=(ch == 0), stop=(ch == 15)),
                                reads=[("gT", gb, ch), "w_out0"], writes=[PS(pb)])
                        P.add("dve", lambda e, j=j, hh=hh, pb=pb, rs=rs: e.scalar_tensor_tensor(
                            out=rbuf[rs][:, hh * 512:(hh + 1) * 512], in0=xs[sl][:, j, hh * 512:(hh + 1) * 512],
                            scalar=DN_ALPHA, in1=psum[pb][:, :], op0=ALU.mult, op1=ALU.add),
                            reads=[PS(pb), xk], writes=[rk])
                    layer_norm_tile(P, rbuf[rs], rk, lng0, lnb0, stat[rs], ("stat", rs), epsc)
                    P.add("pool", lambda e, rs=rs, j=j: e.dma_start(
                        out=x1_d[s, t0 + j * 128:t0 + (j + 1) * 128, :], in_=rbuf[rs]),
                        reads=[rk], dma_key="st%d" % rs)

            for it in range(NIT):
                stage_T(it)
                stage_U(it, 0)
                if it > 0:
                    stage_G(it - 1, 3)
                stage_Z(it, 0)
                if it > 0:
                    stage_O(it - 1)
                stage_PM(it, 0)
                for g in range(1, 4):
                    stage_U(it, g)
                    stage_Z(it, g)
                    stage_G(it, g - 1)
                    stage_PM(it, g)
            stage_G(NIT - 1, 3)
            stage_O(NIT - 1)

        P.barrier()
        if do_l1:
            AR.reset()
            _build_l1(nc, P, AR, psum, psum2, L1D, x1_d, out_d, lng_d, lnb_d, identf_d, identb_d)
        P.emit(stack)
    return nc


def _host_common(inputs):
    f = lambda a: np.ascontiguousarray(np.asarray(a, dtype=np.float32))
    m = {}
    lng = f(inputs["ln_g"])
    lnb = f(inputs["ln_b"])
    m["lng"] = np.ascontiguousarray(np.broadcast_to(lng[:, None, :], (2, 128, D)))
    m["lnb"] = np.ascontiguousarray(np.broadcast_to(lnb[:, None, :], (2, 128, D)))
    w = f(inputs["pool_w_in"])[0]
    m["w_in0"] = np.ascontiguousarray(w.reshape(8, 128, 4096).transpose(1, 0, 2))
    wg = f(inputs["pool_w_grp"])[0]
    m["w_grp"] = np.ascontiguousarray(wg.reshape(4, 4, 128, 512).transpose(2, 0, 1, 3).reshape(128, 16, 512))
    wo = f(inputs["pool_w_out"])[0]
    m["w_out0"] = np.ascontiguousarray(wo.reshape(16, 128, 1024).transpose(1, 0, 2))
    m["pscale"] = np.ascontiguousarray(f(inputs["pool_scale"])[0].reshape(16, 128).T)
    pa, inv = _pool_tables()
    m["poolA"] = pa
    m["invc"] = inv
    m["identf"] = np.eye(128, dtype=np.float32)
    m["identb"] = _bf(np.eye(128))
    m.update(_host_l1(inputs))
    return m


_NC_CACHE = {}


def kernel(**inputs):
    x = np.ascontiguousarray(np.asarray(inputs["x"], dtype=np.float32))
    common = _host_common(inputs)
    if "nc" not in _NC_CACHE:
        _NC_CACHE["nc"] = build_program()
    nc = _NC_CACHE["nc"]
    in_maps = []
    for c in range(NCORES):
        m = dict(common)
        m["x"] = x[c * NSEQ:(c + 1) * NSEQ]
        in_maps.append(m)
    res = run_bass_kernel_spmd(nc, in_maps, core_ids=list(range(NCORES)))
    out = np.concatenate([np.asarray(r["out"]) for r in res.results], axis=0)
    return out.astype(np.float32)
```

```python
import contextlib
import numpy as np
import ml_dtypes
import concourse.bass as bass
import concourse.mybir as mybir
from concourse.bass_utils import run_bass_kernel_spmd

F32 = mybir.dt.float32
BF16 = mybir.dt.bfloat16
AF = mybir.ActivationFunctionType
ALU = mybir.AluOpType

D = 1024
S = 2048
NSEQ = 2
NCORES = 8
DN_ALPHA = float((2.0 * 2) ** 0.25)
LN_EPS = 1e-5
POOL_WINDOWS = (2, 4, 8, 16)
NEGM = -30000.0


class _Op:
    __slots__ = ("eng", "fn", "deps", "is_dma", "key", "awaited", "count", "idx")


class Prog:
    ENGS = ("pe", "act", "dve", "pool", "sp")

    def __init__(self, nc):
        self.nc = nc
        self.ops = {e: [] for e in self.ENGS}
        self.last_w = {}
        self.readers = {}
        self.dma_counts = {}
        self.last_dma = {}
        self.bar = {e: [] for e in self.ENGS}

    def _dep(self, op, a):
        if a is None or a is op:
            return
        if (not a.is_dma) and a.eng == op.eng and a.eng == "pe":
            return
        if a.is_dma and op.is_dma and a.key == op.key:
            return
        op.deps.append(a)
        if not a.is_dma:
            a.awaited = True

    def add(self, eng, fn, reads=(), writes=(), dma_key=None):
        op = _Op()
        op.eng = eng
        op.fn = fn
        op.deps = []
        op.is_dma = dma_key is not None
        op.key = dma_key
        op.awaited = False
        op.count = None
        for a in self.bar[eng]:
            self._dep(op, a)
        self.bar[eng] = []
        if eng != "pe":
            extra = [("psx", r[1]) for r in reads if isinstance(r, tuple) and r[0] == "ps"]
            extra += [("psx", w[1]) for w in writes if isinstance(w, tuple) and w[0] == "ps"]
            writes = list(writes) + extra
        for r in reads:
            self._dep(op, self.last_w.get(r))
        for w in writes:
            self._dep(op, self.last_w.get(w))
            for a in self.readers.get(w, ()):
                self._dep(op, a)
        for r in reads:
            self.readers.setdefault(r, []).append(op)
        for w in writes:
            self.last_w[w] = op
            self.readers[w] = []
        if op.is_dma:
            c = self.dma_counts.get(dma_key, 0) + 1
            self.dma_counts[dma_key] = c
            op.count = c
            self.last_dma[dma_key] = op
        op.idx = len(self.ops[eng])
        self.ops[eng].append(op)
        return op

    def barrier(self):
        deps = []
        for e in self.ENGS:
            for op in reversed(self.ops[e]):
                if not op.is_dma:
                    deps.append(op)
                    break
        deps.extend(self.last_dma.values())
        for e in self.ENGS:
            self.bar[e] = list(deps)

    def emit(self, stack):
        nc = self.nc
        esem = {e: stack.enter_context(nc.semaphore("s_" + e)) for e in self.ENGS}
        dsem = {k: stack.enter_context(nc.semaphore("d_%d" % i)) for i, k in enumerate(self.dma_counts)}
        for e in self.ENGS:
            c = 0
            for op in self.ops[e]:
                if (not op.is_dma) and op.awaited:
                    c += 1
                    op.count = c
        block = stack.enter_context(nc.Block())
        final = [(dsem[k], 16 * c) for k, c in self.dma_counts.items()]

        def run(ename, eh, is_last=False):
            waited = {}
            for op in self.ops[ename]:
                for a in op.deps:
                    if a.is_dma:
                        sem, val = dsem[a.key], 16 * a.count
                    else:
                        sem, val = esem[a.eng], a.count
                    sid = id(sem)
                    if waited.get(sid, 0) >= val:
                        continue
                    waited[sid] = val
                    eh.wait_ge(sem, val)
                ins = op.fn(eh)
                if op.is_dma:
                    ins.then_inc(dsem[op.key], 16)
                elif op.awaited:
                    ins.then_inc(esem[ename], 1)
            if is_last:
                for sem, val in final:
                    eh.wait_ge(sem, val)

        @block.tensor
        def _(eh):
            run("pe", eh)

        @block.scalar
        def _(eh):
            run("act", eh)

        @block.vector
        def _(eh):
            run("dve", eh)

        @block.gpsimd
        def _(eh):
            run("pool", eh)

        @block.sync
        def _(eh):
            run("sp", eh, is_last=True)


class Arena:
    def __init__(self, ap, ncols):
        self.ap = ap
        self.n = ncols
        self.off = 0

    def reset(self):
        self.off = 0

    def _shape(self, v, shape):
        if len(shape) == 2:
            v = v.rearrange("p (a b) -> p a b", a=shape[0])
        elif len(shape) == 3:
            v = v.rearrange("p (a b c) -> p a b c", a=shape[0], b=shape[1])
        return v

    def f(self, *shape):
        cols = int(np.prod(shape))
        assert self.off + cols <= self.n, ("arena overflow", self.off, cols, self.n)
        v = self.ap[:, self.off:self.off + cols]
        self.off += cols
        return self._shape(v, shape)

    def b(self, *shape):
        cols = int(np.prod(shape))
        c32 = (cols + 1) // 2
        assert self.off + c32 <= self.n, ("arena overflow", self.off, c32, self.n)
        v = self.ap[:, self.off:self.off + c32].bitcast(BF16)[:, 0:cols]
        self.off += c32
        return self._shape(v, shape)


def _bf(a):
    return np.ascontiguousarray(np.asarray(a, dtype=np.float32).astype(ml_dtypes.bfloat16))


def _pool_tables():
    A = np.zeros((4, 3, 128, 128), np.float32)
    inv = np.zeros((4, 128), np.float32)
    for g, w in enumerate(POOL_WINDOWS):
        for t in range(128):
            for tp in range(t - w + 1, t + 1):
                if tp >= 0:
                    A[g, 0, tp, t] += 1.0 / w
                else:
                    A[g, 1, tp + 128, t] += 1.0 / w
            A[g, 0, t, t] -= 1.0
            cnt = min(t + 1, w)
            for tp in range(max(0, t - w + 1), t + 1):
                A[g, 2, tp, t] += 1.0
            A[g, 2, t, t] -= cnt
            inv[g, t] = 1.0 / cnt
    At = np.transpose(A, (2, 0, 1, 3)).reshape(128, 4 * 3 * 128)
    invb = np.broadcast_to(inv.reshape(1, 4 * 128), (128, 4 * 128))
    return _bf(At), np.ascontiguousarray(invb, dtype=np.float32)


def layer_norm_tile(P, r_ap, rkey, lng, lnb, stat, skey, epsc):
    st6 = stat[:, 0:12].rearrange("p (a b) -> p a b", a=2)
    mv = stat[:, 12:14]
    P.add("dve", lambda e: e.bn_stats(out=st6[:, 0, :], in_=r_ap[:, 0:512]), reads=[rkey], writes=[skey])
    P.add("dve", lambda e: e.bn_stats(out=st6[:, 1, :], in_=r_ap[:, 512:1024]), reads=[rkey], writes=[skey])
    P.add("dve", lambda e: e.bn_aggr(out=mv, in_=stat[:, 0:12]), reads=[skey], writes=[skey])
    P.add("act", lambda e: e.activation(out=stat[:, 14:15], in_=stat[:, 13:14], func=AF.Sqrt,
                                        bias=epsc, scale=1.0), reads=[skey, "consts"], writes=[skey])
    P.add("dve", lambda e: e.reciprocal(out=stat[:, 14:15], in_=stat[:, 14:15]), reads=[skey], writes=[skey])
    P.add("dve", lambda e: e.scalar_tensor_tensor(out=stat[:, 15:16], in0=stat[:, 12:13], scalar=-1.0,
                                                  in1=stat[:, 14:15], op0=ALU.mult, op1=ALU.mult),
          reads=[skey], writes=[skey])
    P.add("act", lambda e: e.activation(out=r_ap, in_=r_ap, func=AF.Identity, bias=stat[:, 15:16],
                                        scale=stat[:, 14:15]), reads=[skey, rkey], writes=[rkey])
    P.add("pool", lambda e: e.tensor_tensor(out=r_ap, in0=r_ap, in1=lng, op=ALU.mult),
          reads=[rkey, "lnc"], writes=[rkey])
    P.add("pool", lambda e: e.tensor_tensor(out=r_ap, in0=r_ap, in1=lnb, op=ALU.add),
          reads=[rkey, "lnc"], writes=[rkey])


def _slopes():
    h = np.arange(1, 17, dtype=np.float32)
    return (2.0 ** (-8.0 * h / 16.0)).astype(np.float32)


def _l1_tables():
    t = {}
    sl = _slopes()
    tq = np.arange(S, dtype=np.float64)
    qal = np.zeros((16, 7, S), np.float32)
    for h in range(16):
        s0 = float(sl[h])
        s1 = float(np.float32(s0).astype(ml_dtypes.bfloat16))
        s2 = float(np.float32(s0 - s1).astype(ml_dtypes.bfloat16))
        s3 = float(np.float32(s0 - s1 - s2).astype(ml_dtypes.bfloat16))
        qal[h, 0] = -s0 * tq
        for i, si in enumerate((s1, s2, s3)):
            qal[h, 1 + i] = 64.0 * si
            qal[h, 4 + i] = si
    t["qalibi"] = _bf(qal)
    kal = np.zeros((7, S), np.float32)
    kal[0] = 1.0
    kal[1:4] = (np.arange(S) // 64)[None, :]
    kal[4:7] = (np.arange(S) % 64)[None, :]
    t["kalibi"] = _bf(kal)
    kc = np.zeros((7, 128), np.float32)
    ce = np.arange(127) * 16 + 31
    kc[0] = 1.0
    kc[1:4, :127] = (ce // 64)[None, :]
    kc[4:7, :127] = (ce % 64)[None, :]
    t["kalibic"] = _bf(kc)
    E = np.zeros((32, S), np.float32)
    E[np.arange(S) // 64, np.arange(S)] = 1.0
    t["eoh"] = _bf(E)
    cm = np.full((128, S), NEGM, np.float32)
    cm[:127] = np.where(np.arange(S)[None, :] >= ce[:, None], 0.0, NEGM)
    t["cmpmask"] = _bf(cm)
    dk = np.arange(128)[:, None]
    dq = np.arange(384)[None, :]
    t["winmask"] = _bf(np.where((dq - dk >= 0) & (dq - dk < 256), 0.0, NEGM))
    dq = np.arange(128)[None, :]
    t["trimask"] = _bf(np.where(dk <= dq, 0.0, NEGM))
    vcc = np.zeros((128, 33), np.float32)
    vcc[:, 0] = 1.0
    c0 = np.arange(127)[:, None] * 16
    j0 = np.arange(32)[None, :] * 64
    vcc[:127, 1:] = ((c0 < j0 + 64) & (c0 + 32 > j0)).astype(np.float32)
    t["vcc"] = _bf(vcc)
    q = np.arange(128)[:, None, None]
    qt = np.arange(16)[None, :, None]
    j = np.arange(32)[None, None, :]
    cur = (qt * 128 + q) // 64
    forced = (j == 0) | (j == cur) | (j == cur - 1)
    t["forced"] = np.ascontiguousarray(np.where(forced, 1e9, 0.0).astype(np.float32).reshape(128, 512))
    t["future"] = np.ascontiguousarray(np.where(j > cur, -1e30, 3e38).astype(np.float32).reshape(128, 512))
    return t


def _l1_dram(nc, din):
    L = {}
    L["wg1"] = din("wg1", [4, 128, 8, 640])
    L["wg2"] = din("wg2", [4, 128, 8, 268])
    L["w1k"] = din("w1k", [64, 32, 256])
    L["w1v"] = din("w1v", [64, 32, 256])
    L["w2k"] = din("w2k", [128, 2, 64])
    L["w2v"] = din("w2v", [128, 2, 64])
    L["posk"] = din("posk", [64, 32])
    L["posv"] = din("posv", [64, 32])
    L["wout1"] = din("wout1", [128, 8, 1024])
    L["qalibi"] = din("qalibi", [16, 7, S], BF16)
    L["kalibi"] = din("kalibi", [7, S], BF16)
    L["kalibic"] = din("kalibic", [7, 128], BF16)
    L["eoh"] = din("eoh", [32, S], BF16)
    L["cmpmask"] = din("cmpmask", [128, S], BF16)
    L["winmask"] = din("winmask", [128, 384], BF16)
    L["trimask"] = din("trimask", [128, 128], BF16)
    L["vcc"] = din("vcc", [128, 33], BF16)
    L["forced"] = din("forced", [128, 512])
    L["future"] = din("future", [128, 512])
    return L


def _host_l1(inputs):
    f = lambda a: np.ascontiguousarray(np.asarray(a, dtype=np.float32))
    m = {}
    W = f(inputs["nsa_w_in"])[0].reshape(8, 128, 3632).transpose(1, 0, 2)
    wg1 = np.zeros((4, 128, 8, 640), np.float32)
    wg2 = np.zeros((4, 128, 8, 268), np.float32)
    for g in range(4):
        wg1[g, :, :, 0:256] = W[:, :, 256 * g:256 * g + 256]
        wg1[g, :, :, 256:320] = W[:, :, 1024 + 64 * g:1024 + 64 * g + 64]
        wg1[g, :, :, 320:384] = W[:, :, 1280 + 64 * g:1280 + 64 * g + 64]
        wg1[g, :, :, 384:448] = W[:, :, 1536 + 64 * g:1536 + 64 * g + 64]
        wg1[g, :, :, 448:512] = W[:, :, 2048 + 64 * g:2048 + 64 * g + 64]
        wg1[g, :, :, 512:576] = W[:, :, 1792 + 64 * g:1792 + 64 * g + 64]
        wg1[g, :, :, 576:640] = W[:, :, 2304 + 64 * g:2304 + 64 * g + 64]
        wg2[g, :, :, 0:256] = W[:, :, 2560 + 256 * g:2560 + 256 * g + 256]
        wg2[g, :, :, 256:268] = W[:, :, 3584 + 12 * g:3584 + 12 * g + 12]
    m["wg1"] = wg1
    m["wg2"] = wg2
    m["w1k"] = np.ascontiguousarray(f(inputs["nsa_cmp_w1_k"])[0].reshape(32, 64, 256).transpose(1, 0, 2))
    m["w1v"] = np.ascontiguousarray(f(inputs["nsa_cmp_w1_v"])[0].reshape(32, 64, 256).transpose(1, 0, 2))
    m["w2k"] = np.ascontiguousarray(f(inputs["nsa_cmp_w2_k"])[0].reshape(2, 128, 64).transpose(1, 0, 2))
    m["w2v"] = np.ascontiguousarray(f(inputs["nsa_cmp_w2_v"])[0].reshape(2, 128, 64).transpose(1, 0, 2))
    m["posk"] = np.ascontiguousarray(f(inputs["nsa_cmp_pos_k"])[0].T)
    m["posv"] = np.ascontiguousarray(f(inputs["nsa_cmp_pos_v"])[0].T)
    m["wout1"] = np.ascontiguousarray(f(inputs["nsa_w_out"])[0].reshape(8, 128, 1024).transpose(1, 0, 2))
    m.update(_l1_tables())
    return m


L1_STAGE = 99


def _build_l1(nc, P, AR, psum, psum2, L, x1_d, out_d, lng_d, lnb_d, identf_d, identb_d):
    STG = L1_STAGE
    def PS(i):
        return ("ps", i)

    def MM(out, lhsT, rhs, start, stop, reads, writes):
        P.add("pe", lambda e: e.matmul(out, lhsT=lhsT, rhs=rhs, start=start, stop=stop), reads=reads, writes=writes)

    def ACT(out, in_, func, reads, writes, **kw):
        P.add("act", lambda e: e.activation(out=out, in_=in_, func=func, **kw), reads=reads, writes=writes)

    def TT(eng, out, in0, in1, op, reads, writes):
        P.add(eng, lambda e: e.tensor_tensor(out=out, in0=in0, in1=in1, op=op), reads=reads, writes=writes)

    def TS(out, in0, s1, s2, op0, op1, reads, writes):
        if op1 is None:
            P.add("dve", lambda e: e.tensor_scalar(out=out, in0=in0, scalar1=s1, scalar2=None, op0=op0),
                  reads=reads, writes=writes)
        else:
            P.add("dve", lambda e: e.tensor_scalar(out=out, in0=in0, scalar1=s1, scalar2=s2, op0=op0, op1=op1),
                  reads=reads, writes=writes)

    def CP(out, in_, reads, writes, scale=None):
        if scale is None:
            P.add("dve", lambda e: e.tensor_copy(out=out, in_=in_), reads=reads, writes=writes)
        else:
            P.add("dve", lambda e: e.tensor_scalar(out=out, in0=in_, scalar1=scale, scalar2=None, op0=ALU.mult),
                  reads=reads, writes=writes)

    def DMA(q, out, in_, reads, writes, key):
        P.add(q, lambda e: e.dma_start(out=out, in_=in_), reads=reads, writes=writes, dma_key=key)

    W1 = AR.b(8, 640)
    W2 = [AR.b(8, 268) for _ in range(2)]
    w1 = AR.b(32, 256)
    w2 = [AR.b(2, 64) for _ in range(2)]
    pos = AR.b(32)
    wout1 = AR.b(8, 1024)
    identb = AR.b(128)
    x1T = AR.b(8, S)
    ogT = AR.b(8, S)
    q_aug = [AR.b(S) for _ in range(4)]
    ksel = AR.b(S)
    kwin = AR.b(S)
    kcmp = AR.b(128)
    pair = AR.b(S)
    vsel = AR.b(16, 65)
    vwin = AR.b(16, 65)
    vcmp = AR.b(97)
    hs = [AR.b(2, 128) for _ in range(2)]
    cmpmask = AR.b(S)
    winmask = AR.b(384)
    trimask = AR.b(128)
    Pbig = AR.b(2048)
    selm_w = AR.b(4, 128)
    ogb = [AR.b(256) for _ in range(4)]
    identf = AR.f(128)
    lng1 = AR.f(1024)
    lnb1 = AR.f(1024)
    xr = [AR.f(1024) for _ in range(2)]
    xs2 = [AR.f(1024) for _ in range(2)]
    stat = [AR.f(16) for _ in range(2)]
    o_tm = [AR.f(4, 256) for _ in range(2)]
    gates = AR.f(16, 12)
    imp = AR.f(4, 32)
    impm = AR.f(4, 32)
    forced = AR.f(16, 32)
    future = AR.f(16, 32)
    m8 = AR.f(4, 8)
    thr = AR.f(4)
    rden = [AR.f(4) for _ in range(2)]
    fcol = [AR.f(4) for _ in range(2)]
    tmp_o = [AR.f(4, 64) for _ in range(2)]
    tmp_i = AR.f(4, 32)
    sz = [AR.f(256) for _ in range(4)]
    bh = AR.f(4)
    epsc = AR.f(1)

    for t_, k_ in ((ksel, "ksel_c"), (kwin, "kwin_c"), (kcmp, "kcmp_c"), (vcmp, "vcmp_c"),
                   (selm_w.rearrange("p a b -> p (a b)"), "selm")):
        P.add("pool", lambda e, t_=t_: e.memset(t_, 0.0), writes=[k_])
    for hl in range(4):
        P.add("pool", lambda e, hl=hl: e.memset(q_aug[hl], 0.0), writes=[("qa_q", hl), ("qa_s", hl), ("qa_a", hl)])
    P.add("pool", lambda e: e.memset(vsel[:, :, 64:65], 1.0), writes=["vsel_c"])
    P.add("pool", lambda e: e.memset(vwin[:, :, 64:65], 1.0), writes=["vwin_c"])
    P.add("dve", lambda e: e.memset(epsc, LN_EPS), writes=["consts"])
    DMA("sp", identb, identb_d, [], ["consts"], "c1")
    DMA("sp", identf, identf_d, [], ["consts"], "c1")
    DMA("sp", lng1, lng_d[1], [], ["lnc"], "c1l")
    DMA("sp", lnb1, lnb_d[1], [], ["lnc"], "c1l")
    DMA("sp", cmpmask, L["cmpmask"], [], ["consts"], "c1")
    DMA("sp", winmask, L["winmask"], [], ["consts"], "c1")
    DMA("sp", trimask, L["trimask"], [], ["consts"], "c1")
    DMA("sp", forced.rearrange("p a b -> p (a b)"), L["forced"], [], ["consts"], "c1")
    DMA("sp", future.rearrange("p a b -> p (a b)"), L["future"], [], ["consts"], "c1")
    DMA("sp", ksel[64:96, :], L["eoh"], [], ["ksel_c"], "c2a")
    DMA("sp", ksel[96:103, :], L["kalibi"], [], ["ksel_c"], "c2a")
    DMA("sp", kwin[96:103, :], L["kalibi"], [], ["kwin_c"], "c2b")
    DMA("sp", kcmp[96:103, :], L["kalibic"], [], ["kcmp_c"], "c2c")
    DMA("sp", vcmp[:, 64:97], L["vcc"], [], ["vcmp_c"], "c2d")
    for kc in range(0, 8, 2):
        DMA("pool", W1[:, kc:kc + 2, :], L["wg1"][0, :, kc:kc + 2, :], [], ["W1"], "W1")
    for kc in range(0, 8, 4):
        DMA("pool", W2[0][:, kc:kc + 4, :], L["wg2"][0, :, kc:kc + 4, :], [], [("W2", 0)], "W20")
    for kv, (wn, w2n, pn) in enumerate((("w1k", "w2k", "posk"), ("w1v", "w2v", "posv"))):
        for i in range(0, 32, 8):
            DMA("pool", w1[64 * kv:64 * kv + 64, i:i + 8, :], L[wn][:, i:i + 8, :], [], ["wcmp"], "wcmp")
        DMA("pool", w2[kv], L[w2n], [], ["wcmp"], "wcmp")
        DMA("pool", pos[64 * kv:64 * kv + 64, :], L[pn], [], ["wcmp"], "wcmp")
    for c in range(8):
        DMA("pool", wout1[:, c, :], L["wout1"][:, c, :], [], ["wout1"], "wout1")
    for kv in range(2):
        for hc in range(2):
            col = kv * 2 + hc
            for i in range(32):
                MM(psum[0][:, col:col + 1], w1[64 * kv:64 * kv + 64, i, hc * 128:(hc + 1) * 128],
                   pos[64 * kv:64 * kv + 64, i:i + 1], i == 0, i == 31, ["wcmp"], [PS(0)])
    ACT(bh, psum[0][:, 0:4], AF.Copy, [PS(0)], ["bh"])

    if STG < 1:
        return
    rr = {"s": 0, "o": 0, "p": 0, "a": 0, "e": 0, "ring": 0}

    def nxt(k, n):
        v = rr[k]
        rr[k] = (v + 1) % n
        return v

    def x1T_tile(s, j):
        if s == 0:
            xs_ = j % 4
            stg = (xs2[0], xs2[1], xr[0], xr[1])[xs_]
            xk = (("xs2", 0), ("xs2", 1), ("xr", 0), ("xr", 1))[xs_]
            dk = ("xs20", "xs21", "xr0", "xr1")[xs_]
        else:
            xs_ = j % 2
            stg = xs2[xs_]
            xk = ("xs2", xs_)
            dk = "xs2%d" % xs_
        DMA("sp", stg, x1_d[s, j * 128:(j + 1) * 128, :], [], [xk], dk)
        for q4 in range(2):
            pb = nxt("a", 2)
            for i4 in range(4):
                kc = q4 * 4 + i4
                P.add("pe", lambda e, kc=kc, i4=i4, pb=pb, stg=stg: e.transpose(
                    out=psum[pb][:, i4 * 128:(i4 + 1) * 128], in_=stg[:, kc * 128:(kc + 1) * 128],
                    identity=identf), reads=[xk, "consts"], writes=[PS(pb)])
            CP(x1T[:, q4 * 4:(q4 + 1) * 4, j * 128:(j + 1) * 128],
               psum[pb][:, :].rearrange("p (a b) -> p a b", a=4), [PS(pb)], ["x1T"])

    xr.append(o_tm[0].rearrange("p a b -> p (a b)"))
    xr.append(o_tm[1].rearrange("p a b -> p (a b)"))
    stat.append(AR.f(16))
    stat.append(AR.f(16))

    def outproj_tile(s, jt):
        xs_ = jt % 4
        xk = ("xr", xs_)
        alias = [("o_tm", xs_ - 2, h_) for h_ in range(4)] if xs_ >= 2 else []
        DMA("sp", xr[xs_], x1_d[s, jt * 128:(jt + 1) * 128, :], [], [xk] + alias, "xr%d" % xs_)
        for hh in range(2):
            pb = nxt("a", 2)
            for c in range(8):
                MM(psum[pb][:, :], ogT[:, c, jt * 128:(jt + 1) * 128], wout1[:, c, hh * 512:(hh + 1) * 512],
                   c == 0, c == 7, [("ogT", c // 2), "wout1"], [PS(pb)])
            P.add("dve", lambda e, hh=hh, pb=pb, xs_=xs_: e.scalar_tensor_tensor(
                out=xr[xs_][:, hh * 512:(hh + 1) * 512], in0=xr[xs_][:, hh * 512:(hh + 1) * 512],
                scalar=DN_ALPHA, in1=psum[pb][:, :], op0=ALU.mult, op1=ALU.add), reads=[PS(pb), xk], writes=[xk])
        layer_norm_tile(P, xr[xs_], xk, lng1, lnb1, stat[xs_], ("stat1", xs_), epsc)
        DMA("pool", out_d[s, jt * 128:(jt + 1) * 128, :], xr[xs_], [xk], [], "ost%d" % xs_)

    def load_group_weights(gi_):
        g_ = gi_ % 4
        for kc in range(0, 8, 2):
            DMA("pool", W1[:, kc:kc + 2, :], L["wg1"][g_, :, kc:kc + 2, :], [], ["W1"], "W1")
        for kc in range(0, 8, 4):
            DMA("pool", W2[gi_ % 2][:, kc:kc + 4, :], L["wg2"][g_, :, kc:kc + 4, :], [], [("W2", gi_ % 2)], "W2%d" % (gi_ % 2))

    pending = []

    def drain_pending(n):
        for _ in range(n):
            if pending:
                s_, jt_ = pending.pop(0)
                outproj_tile(s_, jt_)

    for s in range(NSEQ):
        if s == 0:
            for j in range(16):
                x1T_tile(0, j)
        if STG < 2:
            return
        for g in range(4):
            gi = s * 4 + g
            w2s = gi % 2
            w2k_ = ("W2", w2s)
            for hl in range(4):
                DMA("sp", q_aug[hl][96:103, :], L["qalibi"][4 * g + hl], [], [("qa_a", hl)], "qal%d" % hl)
            for pc in range(2):
                for tb in range(4):
                    pb = nxt("a", 2)
                    for kc in range(8):
                        MM(psum[pb][:, :], W1[:, kc, pc * 128:(pc + 1) * 128], x1T[:, kc, tb * 512:(tb + 1) * 512],
                           kc == 0, kc == 7, ["W1", "x1T"], [PS(pb)])
                    CP(q_aug[2 * pc][0:64, tb * 512:(tb + 1) * 512], psum[pb][0:64, :],
                       [PS(pb)], [("qa_q", 2 * pc)], scale=0.125)
                    CP(q_aug[2 * pc + 1][0:64, tb * 512:(tb + 1) * 512], psum[pb][64:128, :],
                       [PS(pb)], [("qa_q", 2 * pc + 1)], scale=0.125)
                    drain_pending(1)
            if STG < 3:
                return
            for tb in range(4):
                c0, c1 = tb * 512, (tb + 1) * 512
                pb = nxt("a", 2)
                for kc in range(8):
                    MM(psum[pb][:, :], W1[:, kc, 256:384], x1T[:, kc, c0:c1], kc == 0, kc == 7, ["W1", "x1T"], [PS(pb)])
                CP(pair.rearrange("p (ph c) -> p ph c", ph=16)[:, :, tb * 32:(tb + 1) * 32],
                   psum[pb][:, :].rearrange("p (c ph) -> p ph c", ph=16), [PS(pb)], ["pair"])
                pb = nxt("a", 2)
                for kc in range(8):
                    MM(psum[pb][:, :], W1[:, kc, 384:512], x1T[:, kc, c0:c1], kc == 0, kc == 7, ["W1", "x1T"], [PS(pb)])
                P.add("dve", lambda e, pb=pb, c0=c0, c1=c1: e.tensor_copy(out=ksel[0:64, c0:c1], in_=psum[pb][0:64, :]),
                      reads=[PS(pb)], writes=["ksel"])
                CP(kwin[0:64, c0:c1], psum[pb][64:128, :], [PS(pb)], ["kwin"])
            if STG < 4:
                return
            for j4 in range(4):
                pb = nxt("a", 2)
                for jj in range(4):
                    j = j4 * 4 + jj
                    for kc in range(8):
                        MM(psum[pb][:, jj * 128:(jj + 1) * 128], x1T[:, kc, j * 128:(j + 1) * 128], W1[:, kc, 512:640],
                           kc == 0, kc == 7, ["W1", "x1T"], [PS(pb)])
                pv = psum[pb][:, :].rearrange("p (a b) -> p a b", a=4)
                P.add("dve", lambda e, pv=pv, j4=j4: e.tensor_copy(out=vsel[:, j4 * 4:(j4 + 1) * 4, 0:64], in_=pv[:, :, 0:64]),
                      reads=[PS(pb)], writes=["vsel"])
                ACT(vwin[:, j4 * 4:(j4 + 1) * 4, 0:64], pv[:, :, 64:128], AF.Copy, [PS(pb)], ["vwin"])
            if STG < 5:
                return
            if gi + 1 < NSEQ * 4:
                load_group_weights(gi + 1)
            pb = nxt("a", 2)
            for j in range(16):
                for kc in range(8):
                    MM(psum[pb][:, j * 12:(j + 1) * 12], x1T[:, kc, j * 128:(j + 1) * 128], W2[w2s][:, kc, 256:268],
                       kc == 0, kc == 7, [w2k_, "x1T"], [PS(pb)])
            ACT(gates.rearrange("p a b -> p (a b)"), psum[pb][:, 0:192], AF.Sigmoid, [PS(pb)], ["gates"])
            if STG < 6:
                return
            for kv in range(2):
                for hc in range(2):
                    pb = nxt("a", 2)
                    for i in range(32):
                        MM(psum[pb][:, 0:127], w1[64 * kv:64 * kv + 64, i, hc * 128:(hc + 1) * 128],
                           pair[64 * kv:64 * kv + 64, (i % 16) * 128 + i // 16:(i % 16) * 128 + i // 16 + 127],
                           i == 0, i == 31, ["wcmp", "pair"], [PS(pb)])
                    ACT(hs[kv][:, hc, 0:127], psum[pb][:, 0:127], AF.Silu, [PS(pb), "bh"], [("hs", kv)],
                        bias=bh[:, kv * 2 + hc:kv * 2 + hc + 1])
            pb = nxt("a", 2)
            for hc in range(2):
                MM(psum[pb][0:64, 0:127], w2[0][:, hc, :], hs[0][:, hc, 0:127], hc == 0, hc == 1,
                   ["wcmp", ("hs", 0)], [PS(pb)])
            CP(kcmp[0:64, 0:127], psum[pb][0:64, 0:127], [PS(pb)], ["kcmp"])
            pb = nxt("a", 2)
            for hc in range(2):
                MM(psum[pb][0:127, 0:64], hs[1][:, hc, 0:127], w2[1][:, hc, :], hc == 0, hc == 1,
                   ["wcmp", ("hs", 1)], [PS(pb)])
            CP(vcmp[0:127, 0:64], psum[pb][0:127, 0:64], [PS(pb)], ["vcmp"])

            if STG < 7:
                return
            def branch_epilogue(hl, br, ob, Wd, qt, os_, first, with_imp):
                O = psum[ob][:, 0:4 * Wd].rearrange("p (a b) -> p a b", a=4)
                e_ = nxt("e", 2)
                rk, fk, tk = ("rden", e_), ("fcol", e_), ("tmp_o", e_)
                TS(rden[e_].unsqueeze(2), O[:, :, 64:65], 1e-30, None, ALU.max, None, [PS(ob)], [rk])
                P.add("dve", lambda e: e.reciprocal(out=rden[e_], in_=rden[e_]), reads=[rk], writes=[rk])
                if with_imp:
                    rb = rden[e_].unsqueeze(2).broadcast_to([128, 4, 32])
                    if hl == 0:
                        TT("dve", imp, O[:, :, 65:97], rb, ALU.mult, [PS(ob), rk], ["imp"])
                    else:
                        TT("dve", tmp_i, O[:, :, 65:97], rb, ALU.mult, [PS(ob), rk], ["tmp_i"])
                        TT("pool", imp, imp, tmp_i, ALU.add, ["tmp_i", "imp"], ["imp"])
                gcol = gates[:, qt * 4:(qt + 1) * 4, hl * 3 + br]
                TT("dve", fcol[e_], rden[e_], gcol, ALU.mult, [rk, "gates"], [fk])
                fb = fcol[e_].unsqueeze(2).broadcast_to([128, 4, 64])
                odst = o_tm[os_][:, :, hl * 64:(hl + 1) * 64]
                ok = ("o_tm", os_, hl)
                if first:
                    TT("dve", odst, O[:, :, 0:64], fb, ALU.mult, [PS(ob), fk], [ok])
                else:
                    TT("dve", tmp_o[e_], O[:, :, 0:64], fb, ALU.mult, [PS(ob), fk], [tk])
                    TT("pool", odst, odst, tmp_o[e_], ALU.add, [tk, ok], [ok])

            SBK = [2, 3, 0, 1]

            def sel_chain(qt):
                TT("dve", impm, imp, forced[:, qt * 4:(qt + 1) * 4, :], ALU.max, ["imp", "consts"], ["impm"])
                TT("dve", impm, impm, future[:, qt * 4:(qt + 1) * 4, :], ALU.min, ["impm", "consts"], ["impm"])
                for qs in range(4):
                    P.add("dve", lambda e, qs=qs: e.max(out=m8[:, qs, :], in_=impm[:, qs, :]), reads=["impm"], writes=["m8"])
                TS(thr.unsqueeze(2), m8[:, :, 7:8], 0.0, None, ALU.max, None, ["m8"], ["thr"])
                for qs in range(4):
                    TS(selm_w[:, qs, 64:96], impm[:, qs, :], thr[:, qs:qs + 1], 1.0, ALU.is_ge, ALU.subtract,
                       ["impm", "thr"], ["selm"])

            def sel_transposes(Q0):
                for qs in range(4):
                    MM(psum[6][:, qs * 128:(qs + 1) * 128], selm_w[:, qs, :], identb, True, True, ["selm", "consts"], [PS(6)])
                for hl in range(4):
                    CP(q_aug[hl][64:96, Q0:Q0 + 512], psum[6][64:96, :], [PS(6)], [("qa_s", hl)], scale=30000.0)

            def gate_path(qt, os_):
                p6b = psum[6][:, :].bitcast(BF16)
                for qs in range(4):
                    TT("dve", ogb[qs], o_tm[os_][:, qs, :], sz[qs], ALU.mult,
                       [("sz", qs)] + [("o_tm", os_, h_) for h_ in range(4)], [("ogb", qs)])
                for qs in range(4):
                    for cc in range(2):
                        P.add("pe", lambda e, cc=cc, qs=qs: e.transpose(
                            out=p6b[:, qs * 256 + cc * 128:qs * 256 + (cc + 1) * 128],
                            in_=ogb[qs][:, cc * 128:(cc + 1) * 128], identity=identb),
                            reads=[("ogb", qs), "consts"], writes=[PS(6)])
                CP(ogT[:, 2 * g:2 * g + 2, qt * 512:(qt + 1) * 512].rearrange("p c (q t) -> p c q t", q=4),
                   p6b.rearrange("p (q c t) -> p c q t", q=4, c=2), [PS(6)], [("ogT", g)])

            def zproj_tiles(qt):
                for qs in range(4):
                    jt = 4 * qt + qs

                    def qk_f(sb, jt=jt):
                        for kc in range(8):
                            MM(psum[sb][:, 0:256], x1T[:, kc, jt * 128:(jt + 1) * 128], W2[w2s][:, kc, 0:256],
                               kc == 0, kc == 7, [w2k_, "x1T"], [PS(sb)])

                    def act_f(sb, qs=qs):
                        ACT(sz[qs], psum[sb][:, 0:256], AF.Silu, [PS(sb)], [("sz", qs)])

                    mk_tile(qk_f, 0, 256, (lambda pr: None), None)
                    units[-1]["tiles"][0]["act"] = act_f

            units = []

            def mk_tile(qk, c0, c1, pv, post=None, pairable=False):
                t = {"qk": qk, "c0": c0, "c1": c1, "pv": pv, "post": post}
                if pairable and units and units[-1]["pair_open"]:
                    units[-1]["tiles"].append(t)
                    units[-1]["pair_open"] = False
                else:
                    units.append({"tiles": [t], "pair_open": pairable, "marker": None})

            def close_pair():
                if units:
                    units[-1]["pair_open"] = False

            def mk_marker(fn):
                close_pair()
                units.append({"tiles": [], "pair_open": False, "marker": fn})

            for qt in range(4):
                Q0 = qt * 512
                os_ = (gi * 4 + qt) % 2
                for hl in range(4):
                    ob = (4, 5, 7)[nxt("o", 3)]
                    qk = [("qa_q", hl), ("qa_s", hl), ("qa_a", hl)]

                    def qk_f(sb, hl=hl, Q0=Q0, qk=qk):
                        MM(psum[sb][:, :], kcmp[0:103, :], q_aug[hl][0:103, Q0:Q0 + 512], True, False,
                           ["kcmp", "kcmp_c"] + qk, [PS(sb)])
                        MM(psum[sb][:, :], identb, cmpmask[:, Q0:Q0 + 512], False, True, ["consts"], [PS(sb)])

                    def pv_f(pr, ob=ob):
                        for qs in range(4):
                            MM(psum[ob][:, qs * 97:(qs + 1) * 97], Pbig[:, pr * 512 + qs * 128:pr * 512 + (qs + 1) * 128],
                               vcmp[:, 0:97], True, True, [("Pb", pr), "vcmp", "vcmp_c"], [PS(ob)])

                    def post_f(hl=hl, ob=ob, qt=qt, os_=os_):
                        branch_epilogue(hl, 0, ob, 97, qt, os_, True, True)
                        if hl == 3:
                            sel_chain(qt)

                    mk_tile(qk_f, 0, 512, pv_f, post_f, pairable=False)
                close_pair()
                for hl in range(4):
                    qk = [("qa_q", hl), ("qa_s", hl), ("qa_a", hl)]
                    ob = (4, 5, 7)[nxt("o", 3)]
                    plan = []
                    for kt in range(max(0, 4 * qt - 2), 4 * qt + 4):
                        K0 = kt * 128
                        lo, hi = max(K0, Q0), min(K0 + 384, Q0 + 512)
                        if hi > lo:
                            plan.append((kt, K0, lo, hi))
                    lastk = {}
                    for (kt, K0, lo, hi) in plan:
                        for qs in range((lo - Q0) // 128, (hi - Q0) // 128):
                            lastk[qs] = kt
                    firstmm = True
                    for ti, (kt, K0, lo, hi) in enumerate(plan):
                        c0, c1, m0 = lo - Q0, hi - Q0, lo - K0

                        def qk_f(sb, hl=hl, qk=qk, K0=K0, lo=lo, hi=hi, c0=c0, c1=c1, m0=m0):
                            MM(psum[sb][:, c0:c1], kwin[0:103, K0:K0 + 128], q_aug[hl][0:103, lo:hi], True, False,
                               ["kwin", "kwin_c"] + qk, [PS(sb)])
                            MM(psum[sb][:, c0:c1], identb, winmask[:, m0:m0 + (hi - lo)], False, True, ["consts"], [PS(sb)])

                        pvl = []
                        for qs in range(c0 // 128, c1 // 128):
                            pvl.append((qs, firstmm, lastk[qs] == kt))
                            firstmm = False

                        def pv_f(pr, ob=ob, kt=kt, pvl=pvl):
                            for (qs, st_, sp_) in pvl:
                                MM(psum[ob][:, qs * 65:(qs + 1) * 65], Pbig[:, pr * 512 + qs * 128:pr * 512 + (qs + 1) * 128],
                                   vwin[:, kt, :], st_, sp_, [("Pb", pr), "vwin", "vwin_c"], [PS(ob)])

                        post_f = None
                        if ti == len(plan) - 1:
                            def post_f(hl=hl, ob=ob, qt=qt, os_=os_):
                                branch_epilogue(hl, 2, ob, 65, qt, os_, False, False)
                        mk_tile(qk_f, c0, c1, pv_f, post_f)
                mk_marker(lambda Q0=Q0: sel_transposes(Q0))
                zproj_tiles(qt)
                for hl in range(4):
                    qk = [("qa_q", hl), ("qa_s", hl), ("qa_a", hl)]
                    ob = (4, 5, 7)[nxt("o", 3)]
                    nk = 4 * qt + 4
                    firstmm = True
                    for kt in range(nk):
                        K0 = kt * 128
                        dq_ = kt - 4 * qt
                        c0 = max(dq_, 0) * 128

                        def qk_f(sb, hl=hl, qk=qk, K0=K0, Q0=Q0, c0=c0, dq_=dq_):
                            MM(psum[sb][:, c0:512], ksel[0:103, K0:K0 + 128], q_aug[hl][0:103, Q0 + c0:Q0 + 512], True, dq_ < 0,
                               ["ksel", "ksel_c"] + qk, [PS(sb)])
                            if dq_ >= 0:
                                MM(psum[sb][:, c0:c0 + 128], identb, trimask, False, True, ["consts"], [PS(sb)])

                        pvl = []
                        for qs in range(c0 // 128, 4):
                            pvl.append((qs, firstmm, kt == 4 * qt + qs))
                            firstmm = False

                        def pv_f(pr, ob=ob, kt=kt, pvl=pvl):
                            for (qs, st_, sp_) in pvl:
                                MM(psum[ob][:, qs * 65:(qs + 1) * 65], Pbig[:, pr * 512 + qs * 128:pr * 512 + (qs + 1) * 128],
                                   vsel[:, kt, :], st_, sp_, [("Pb", pr), "vsel", "vsel_c"], [PS(ob)])

                        post_f = None
                        if kt == nk - 1:
                            def post_f(hl=hl, ob=ob, qt=qt, os_=os_):
                                branch_epilogue(hl, 1, ob, 65, qt, os_, False, False)
                                if hl == 3:
                                    gate_path(qt, os_)
                        mk_tile(qk_f, c0, 512, pv_f, post_f, pairable=False)
                    close_pair()

            inflight = []

            def flush_one():
                u, need, pos = inflight.pop(0)
                for t, p in zip(u["tiles"], pos):
                    t["pv"](p)
                    if t["post"] is not None:
                        t["post"]()

            for u in units:
                if u["marker"] is not None:
                    u["marker"]()
                    continue
                nb = len(u["tiles"])
                skip = 1 if (nb == 2 and rr["ring"] % 2 == 1) else 0
                need = nb + skip
                while inflight and (sum(x[1] for x in inflight) + need > 4 or len(inflight) >= 3):
                    flush_one()
                rr["ring"] = (rr["ring"] + skip) % 4
                pos = [(rr["ring"] + k) % 4 for k in range(nb)]
                rr["ring"] = (rr["ring"] + nb) % 4
                for t, p in zip(u["tiles"], pos):
                    t["qk"](SBK[p])
                if nb == 2:
                    d = SBK[pos[0]] // 2
                    ACT(Pbig[:, pos[0] * 512:(pos[0] + 2) * 512], psum2[d][:, :], AF.Exp,
                        [PS(SBK[pos[0]]), PS(SBK[pos[1]])], [("Pb", pos[0]), ("Pb", pos[1])])
                else:
                    t = u["tiles"][0]
                    p = pos[0]
                    if t.get("act") is not None:
                        t["act"](SBK[p])
                    else:
                        ACT(Pbig[:, p * 512 + t["c0"]:p * 512 + t["c1"]], psum[SBK[p]][:, t["c0"]:t["c1"]], AF.Exp,
                            [PS(SBK[p])], [("Pb", p)])
                inflight.append((u, need, pos))
            while inflight:
                flush_one()

        for jt in range(16):
            pending.append((s, jt))
        if s + 1 < NSEQ:
            for jt in range(16):
                x1T_tile(s + 1, jt)
                if jt % 2 == 1:
                    drain_pending(1)
        else:
            drain_pending(16)


def build_program(do_l0=True, do_l1=True):
    nc = bass.Bass("TRN2", target_bir_lowering=False)
    dt = {}

    def din(name, shape, dtype=F32):
        dt[name] = nc.dram_tensor(name, list(shape), dtype, kind="ExternalInput").ap()
        return dt[name]

    x_d = din("x", [NSEQ, S, D])
    lng_d = din("lng", [2, 128, D])
    lnb_d = din("lnb", [2, 128, D])
    w_in0_d = din("w_in0", [128, 8, 4096])
    w_grp_d = din("w_grp", [128, 16, 512])
    w_out0_d = din("w_out0", [128, 16, 1024])
    scale_d = din("pscale", [128, 16])
    poolA_d = din("poolA", [128, 4 * 3 * 128], BF16)
    invc_d = din("invc", [128, 4 * 128])
    identf_d = din("identf", [128, 128])
    identb_d = din("identb", [128, 128], BF16)
    x1kind = "Internal" if (do_l0 and do_l1) else ("ExternalOutput" if do_l0 else "ExternalInput")
    x1_d = nc.dram_tensor("x1s", [NSEQ, S, D], F32, kind=x1kind).ap()
    out_d = nc.dram_tensor("out", [NSEQ, S, D], F32, kind="ExternalOutput").ap()
    L1D = _l1_dram(nc, din)

    stack = contextlib.ExitStack()
    with stack:
        ACOLS = 52800
        ar_t = stack.enter_context(nc.sbuf_tensor("arena", [128, ACOLS], F32))
        AR = Arena(ar_t[:], ACOLS)
        psum2 = [stack.enter_context(nc.psum_tensor("ps%d" % i, [128, 1024], F32)) for i in range(4)]
        psum = [psum2[i // 2][:, (i % 2) * 512:(i % 2 + 1) * 512] for i in range(8)]
        P = Prog(nc)

        def PS(i):
            return ("ps", i)

        if do_l0:
            AR.reset()
            w_in0 = AR.b(8, 4096)
            w_grp = AR.b(16, 512)
            w_out0 = AR.b(16, 1024)
            poolA = AR.b(12, 128)
            identb = AR.b(128)
            xT0 = [AR.b(8, 256) for _ in range(2)]
            u_tm = AR.b(12, 512)
            mT = [AR.b(4, 256) for _ in range(2)]
            gT = [AR.b(16, 256) for _ in range(2)]
            identf = AR.f(128)
            lng0 = AR.f(1024)
            lnb0 = AR.f(1024)
            invc = AR.f(4, 128)
            pscale = AR.f(16)
            epsc = AR.f(1)
            xs = [AR.f(2, 1024) for _ in range(2)]
            siluz = [AR.f(4, 256) for _ in range(2)]
            rbuf = [AR.f(1024) for _ in range(2)]
            stat = [AR.f(16) for _ in range(2)]

            P.add("sp", lambda e: e.dma_start(out=identf, in_=identf_d), writes=["consts"], dma_key="c0")
            P.add("sp", lambda e: e.dma_start(out=identb, in_=identb_d), writes=["consts"], dma_key="c0")
            P.add("sp", lambda e: e.dma_start(out=poolA.rearrange("p a b -> p (a b)"), in_=poolA_d), writes=["consts"], dma_key="c0")
            P.add("sp", lambda e: e.dma_start(out=invc.rearrange("p a b -> p (a b)"), in_=invc_d), writes=["consts"], dma_key="c0")
            P.add("sp", lambda e: e.dma_start(out=pscale, in_=scale_d), writes=["consts"], dma_key="c0")
            P.add("sp", lambda e: e.dma_start(out=lng0, in_=lng_d[0]), writes=["lnc"], dma_key="c0l")
            P.add("sp", lambda e: e.dma_start(out=lnb0, in_=lnb_d[0]), writes=["lnc"], dma_key="c0l")
            P.add("dve", lambda e: e.memset(epsc, LN_EPS), writes=["consts"])
            for gq in range(4):
                for cb in (gq, 4 + gq):
                    P.add("pool", lambda e, cb=cb: e.dma_start(out=w_in0[:, :, cb * 512:(cb + 1) * 512],
                                                             in_=w_in0_d[:, :, cb * 512:(cb + 1) * 512]),
                          writes=[("w_in0", cb)], dma_key="w_in0_%d" % cb)
                P.add("pool", lambda e, gq=gq: e.dma_start(out=w_grp[:, gq * 4:(gq + 1) * 4, :],
                                                         in_=w_grp_d[:, gq * 4:(gq + 1) * 4, :]),
                      writes=[("w_grp", gq)], dma_key="w_grp_%d" % gq)
            for c in range(0, 16, 2):
                P.add("pool", lambda e, c=c: e.dma_start(out=w_out0[:, c:c + 2, :], in_=w_out0_d[:, c:c + 2, :]),
                      writes=["w_out0"], dma_key="w_out0")

            NB = S // 256
            NIT = NSEQ * NB

            def geo(it):
                s, b = divmod(it, NB)
                return s, b, it % 2, b * 256

            def stage_T(it):
                s, b, sl, t0 = geo(it)
                xk, xtk = ("xs", sl), ("xT0", sl)
                for j in range(2):
                    P.add("sp", lambda e, j=j: e.dma_start(
                        out=xs[sl][:, j, :], in_=x_d[s, t0 + j * 128:t0 + (j + 1) * 128, :]),
                        writes=[xk], dma_key="xs%d" % sl)
                for j in range(2):
                    for q4 in range(2):
                        pb = (j * 2 + q4) % 2
                        for i4 in range(4):
                            kc = q4 * 4 + i4
                            P.add("pe", lambda e, j=j, kc=kc, i4=i4, pb=pb: e.transpose(
                                out=psum[pb][:, i4 * 128:(i4 + 1) * 128], in_=xs[sl][:, j, kc * 128:(kc + 1) * 128],
                                identity=identf), reads=[xk, "consts"], writes=[PS(pb)])
                        P.add("dve", lambda e, j=j, q4=q4, pb=pb: e.tensor_copy(
                            out=xT0[sl][:, q4 * 4:(q4 + 1) * 4, j * 128:(j + 1) * 128],
                            in_=psum[pb][:, :].rearrange("p (a b) -> p a b", a=4)),
                            reads=[PS(pb)], writes=[xtk])

            def stage_U(it, g):
                s, b, sl, t0 = geo(it)
                xtk = ("xT0", sl)
                for j in range(2):
                    n = b * 2 + j
                    slot = n % 3
                    pb = 2 + (j % 2)
                    for kc in range(8):
                        P.add("pe", lambda e, kc=kc, j=j, pb=pb: e.matmul(
                            psum[pb][:, :], lhsT=xT0[sl][:, kc, j * 128:(j + 1) * 128],
                            rhs=w_in0[:, kc, g * 512:(g + 1) * 512], start=(kc == 0), stop=(kc == 7)),
                            reads=[xtk, ("w_in0", g)], writes=[PS(pb)])
                    P.add("act", lambda e, slot=slot, pb=pb: e.activation(
                        out=u_tm[:, g * 3 + slot, :], in_=psum[pb][:, :], func=AF.Copy),
                        reads=[PS(pb)], writes=[("u", g, slot)])

            def stage_Z(it, g):
                s, b, sl, t0 = geo(it)
                xtk = ("xT0", sl)
                gs = (it * 4 + g) % 2
                szk = ("siluz", gs)
                for c2 in range(2):
                    pb = 4 + c2
                    for ci in range(2):
                        c = c2 * 2 + ci
                        col = 2048 + g * 512 + c * 128
                        for kc in range(8):
                            P.add("pe", lambda e, kc=kc, col=col, ci=ci, pb=pb: e.matmul(
                                psum[pb][:, ci * 256:(ci + 1) * 256], lhsT=w_in0[:, kc, col:col + 128],
                                rhs=xT0[sl][:, kc, :], start=(kc == 0), stop=(kc == 7)),
                                reads=[xtk, ("w_in0", 4 + g)], writes=[PS(pb)])
                    P.add("act", lambda e, c2=c2, pb=pb: e.activation(
                        out=siluz[gs][:, c2 * 2:(c2 + 1) * 2, :],
                        in_=psum[pb][:, :].rearrange("p (a b) -> p a b", a=2), func=AF.Silu),
                        reads=[PS(pb)], writes=[szk])

            def stage_PM(it, g):
                s, b, sl, t0 = geo(it)
                gs = (it * 4 + g) % 2
                mk = ("mT", gs)
                for c2 in range(2):
                    pb = 6 + c2
                    for ci in range(2):
                        c = c2 * 2 + ci
                        for j in range(2):
                            n = b * 2 + j
                            slot = n % 3
                            pslot = (n - 1) % 3
                            o = psum[pb][:, ci * 256 + j * 128: ci * 256 + (j + 1) * 128]
                            if n == 0:
                                P.add("pe", lambda e, o=o, slot=slot, c=c: e.matmul(
                                    o, lhsT=u_tm[:, g * 3 + slot, c * 128:(c + 1) * 128],
                                    rhs=poolA[:, g * 3 + 2, :], start=True, stop=True),
                                    reads=[("u", g, slot), "consts"], writes=[PS(pb)])
                            else:
                                P.add("pe", lambda e, o=o, slot=slot, c=c: e.matmul(
                                    o, lhsT=u_tm[:, g * 3 + slot, c * 128:(c + 1) * 128],
                                    rhs=poolA[:, g * 3 + 0, :], start=True, stop=False),
                                    reads=[("u", g, slot), "consts"], writes=[PS(pb)])
                                P.add("pe", lambda e, o=o, pslot=pslot, c=c: e.matmul(
                                    o, lhsT=u_tm[:, g * 3 + pslot, c * 128:(c + 1) * 128],
                                    rhs=poolA[:, g * 3 + 1, :], start=False, stop=True),
                                    reads=[("u", g, pslot), "consts"], writes=[PS(pb)])
                    P.add("act", lambda e, c2=c2, pb=pb: e.activation(
                        out=mT[gs][:, c2 * 2:(c2 + 1) * 2, :],
                        in_=psum[pb][:, :].rearrange("p (a b) -> p a b", a=2), func=AF.Copy),
                        reads=[PS(pb)], writes=[mk])
                    if b == 0:
                        P.add("dve", lambda e, c2=c2, pb=pb: e.tensor_tensor(
                            out=mT[gs][:, c2 * 2:(c2 + 1) * 2, 0:128],
                            in0=psum[pb][:, :].rearrange("p (a b) -> p a b", a=2)[:, :, 0:128],
                            in1=invc[:, g:g + 1, :].broadcast_to([128, 2, 128]), op=ALU.mult),
                            reads=[PS(pb), "consts"], writes=[mk])

            def stage_G(it, g):
                gs = (it * 4 + g) % 2
                szk, mk = ("siluz", gs), ("mT", gs)
                gb = it % 2
                for d2 in range(2):
                    pb = d2
                    for di in range(2):
                        d = d2 * 2 + di
                        for cc in range(4):
                            P.add("pe", lambda e, cc=cc, d=d, di=di, pb=pb: e.matmul(
                                psum[pb][:, di * 256:(di + 1) * 256],
                                lhsT=w_grp[:, g * 4 + cc, d * 128:(d + 1) * 128], rhs=mT[gs][:, cc, :],
                                start=(cc == 0), stop=(cc == 3)), reads=[mk, ("w_grp", g)], writes=[PS(pb)])
                    for di in range(2):
                        d = d2 * 2 + di
                        ch = g * 4 + d
                        P.add("dve", lambda e, d=d, di=di, ch=ch, pb=pb: e.scalar_tensor_tensor(
                            out=gT[gb][:, ch, :], in0=psum[pb][:, di * 256:(di + 1) * 256],
                            scalar=pscale[:, ch:ch + 1], in1=siluz[gs][:, d, :], op0=ALU.mult, op1=ALU.mult),
                            reads=[PS(pb), szk, "consts"], writes=[("gT", gb, ch)])

            def stage_O(it):
                s, b, sl, t0 = geo(it)
                xk = ("xs", sl)
                gb = it % 2
                for j in range(2):
                    rs = (it * 2 + j) % 2
                    rk = ("r", rs)
                    for hh in range(2):
                        pb = 2 + hh
                        for ch in range(16):
                            P.add("pe", lambda e, ch=ch, j=j, hh=hh, pb=pb: e.matmul(
                                psum[pb][:, :], lhsT=gT[gb][:, ch, j * 128:(j + 1) * 128],
                                rhs=w_out0[:, ch, hh * 512:(hh + 1) * 512], start=(ch == 0), stop=(ch == 15)),
                                reads=[("gT", gb, ch), "w_out0"], writes=[PS(pb)])
                        P.add("dve", lambda e, j=j, hh=hh, pb=pb, rs=rs: e.scalar_tensor_tensor(
                            out=rbuf[rs][:, hh * 512:(hh + 1) * 512], in0=xs[sl][:, j, hh * 512:(hh + 1) * 512],
                            scalar=DN_ALPHA, in1=psum[pb][:, :], op0=ALU.mult, op1=ALU.add),
                            reads=[PS(pb), xk], writes=[rk])
                    layer_norm_tile(P, rbuf[rs], rk, lng0, lnb0, stat[rs], ("stat", rs), epsc)
                    P.add("pool", lambda e, rs=rs, j=j: e.dma_start(
                        out=x1_d[s, t0 + j * 128:t0 + (j + 1) * 128, :], in_=rbuf[rs]),
                        reads=[rk], dma_key="st%d" % rs)

            for it in range(NIT):
                stage_T(it)
                stage_U(it, 0)
                if it > 0:
                    stage_G(it - 1, 3)
                stage_Z(it, 0)
                if it > 0:
                    stage_O(it - 1)
                stage_PM(it, 0)
                for g in range(1, 4):
                    stage_U(it, g)
                    stage_Z(it, g)
                    stage_G(it, g - 1)
                    stage_PM(it, g)
            stage_G(NIT - 1, 3)
            stage_O(NIT - 1)

        P.barrier()
        if do_l1:
            AR.reset()
            _build_l1(nc, P, AR, psum, psum2, L1D, x1_d, out_d, lng_d, lnb_d, identf_d, identb_d)
        P.emit(stack)
    return nc


def _host_common(inputs):
    f = lambda a: np.ascontiguousarray(np.asarray(a, dtype=np.float32))
    m = {}
    lng = f(inputs["ln_g"])
    lnb = f(inputs["ln_b"])
    m["lng"] = np.ascontiguousarray(np.broadcast_to(lng[:, None, :], (2, 128, D)))
    m["lnb"] = np.ascontiguousarray(np.broadcast_to(lnb[:, None, :], (2, 128, D)))
    w = f(inputs["pool_w_in"])[0]
    m["w_in0"] = np.ascontiguousarray(w.reshape(8, 128, 4096).transpose(1, 0, 2))
    wg = f(inputs["pool_w_grp"])[0]
    m["w_grp"] = np.ascontiguousarray(wg.reshape(4, 4, 128, 512).transpose(2, 0, 1, 3).reshape(128, 16, 512))
    wo = f(inputs["pool_w_out"])[0]
    m["w_out0"] = np.ascontiguousarray(wo.reshape(16, 128, 1024).transpose(1, 0, 2))
    m["pscale"] = np.ascontiguousarray(f(inputs["pool_scale"])[0].reshape(16, 128).T)
    pa, inv = _pool_tables()
    m["poolA"] = pa
    m["invc"] = inv
    m["identf"] = np.eye(128, dtype=np.float32)
    m["identb"] = _bf(np.eye(128))
    m.update(_host_l1(inputs))
    return m


_NC_CACHE = {}


def kernel(**inputs):
    x = np.ascontiguousarray(np.asarray(inputs["x"], dtype=np.float32))
    common = _host_common(inputs)
    if "nc" not in _NC_CACHE:
        _NC_CACHE["nc"] = build_program()
    nc = _NC_CACHE["nc"]
    in_maps = []
    for c in range(NCORES):
        m = dict(common)
        m["x"] = x[c * NSEQ:(c + 1) * NSEQ]
        in_maps.append(m)
    res = run_bass_kernel_spmd(nc, in_maps, core_ids=list(range(NCORES)))
    out = np.concatenate([np.asarray(r["out"]) for r in res.results], axis=0)
    return out.astype(np.float32)
```

```python
import contextlib
import numpy as np
import ml_dtypes
import concourse.bass as bass
import concourse.mybir as mybir
from concourse.bass_utils import run_bass_kernel_spmd

F32 = mybir.dt.float32
BF16 = mybir.dt.bfloat16
AF = mybir.ActivationFunctionType
ALU = mybir.AluOpType

D = 1024
S = 2048
NSEQ = 2
NCORES = 8
DN_ALPHA = float((2.0 * 2) ** 0.25)
LN_EPS = 1e-5
POOL_WINDOWS = (2, 4, 8, 16)
NEGM = -30000.0


class _Op:
    __slots__ = ("eng", "fn", "deps", "is_dma", "key", "awaited", "count", "idx")


class Prog:
    ENGS = ("pe", "act", "dve", "pool", "sp")

    def __init__(self, nc):
        self.nc = nc
        self.ops = {e: [] for e in self.ENGS}
        self.last_w = {}
        self.readers = {}
        self.dma_counts = {}
        self.last_dma = {}
        self.bar = {e: [] for e in self.ENGS}

    def _dep(self, op, a):
        if a is None or a is op:
            return
        if (not a.is_dma) and a.eng == op.eng and a.eng == "pe":
            return
        if a.is_dma and op.is_dma and a.key == op.key:
            return
        op.deps.append(a)
        if not a.is_dma:
            a.awaited = True

    def add(self, eng, fn, reads=(), writes=(), dma_key=None):
        op = _Op()
        op.eng = eng
        op.fn = fn
        op.deps = []
        op.is_dma = dma_key is not None
        op.key = dma_key
        op.awaited = False
        op.count = None
        for a in self.bar[eng]:
            self._dep(op, a)
        self.bar[eng] = []
        if eng != "pe":
            extra = [("psx", r[1]) for r in reads if isinstance(r, tuple) and r[0] == "ps"]
            extra += [("psx", w[1]) for w in writes if isinstance(w, tuple) and w[0] == "ps"]
            writes = list(writes) + extra
        for r in reads:
            self._dep(op, self.last_w.get(r))
        for w in writes:
            self._dep(op, self.last_w.get(w))
            for a in self.readers.get(w, ()):
                self._dep(op, a)
        for r in reads:
            self.readers.setdefault(r, []).append(op)
        for w in writes:
            self.last_w[w] = op
            self.readers[w] = []
        if op.is_dma:
            c = self.dma_counts.get(dma_key, 0) + 1
            self.dma_counts[dma_key] = c
            op.count = c
            self.last_dma[dma_key] = op
        op.idx = len(self.ops[eng])
        self.ops[eng].append(op)
        return op

    def barrier(self):
        deps = []
        for e in self.ENGS:
            for op in reversed(self.ops[e]):
                if not op.is_dma:
                    deps.append(op)
                    break
        deps.extend(self.last_dma.values())
        for e in self.ENGS:
            self.bar[e] = list(deps)

    def emit(self, stack):
        nc = self.nc
        esem = {e: stack.enter_context(nc.semaphore("s_" + e)) for e in self.ENGS}
        dsem = {k: stack.enter_context(nc.semaphore("d_%d" % i)) for i, k in enumerate(self.dma_counts)}
        for e in self.ENGS:
            c = 0
            for op in self.ops[e]:
                if (not op.is_dma) and op.awaited:
                    c += 1
                    op.count = c
        block = stack.enter_context(nc.Block())
        final = [(dsem[k], 16 * c) for k, c in self.dma_counts.items()]

        def run(ename, eh, is_last=False):
            waited = {}
            for op in self.ops[ename]:
                for a in op.deps:
                    if a.is_dma:
                        sem, val = dsem[a.key], 16 * a.count
                    else:
                        sem, val = esem[a.eng], a.count
                    sid = id(sem)
                    if waited.get(sid, 0) >= val:
                        continue
                    waited[sid] = val
                    eh.wait_ge(sem, val)
                ins = op.fn(eh)
                if op.is_dma:
                    ins.then_inc(dsem[op.key], 16)
                elif op.awaited:
                    ins.then_inc(esem[ename], 1)
            if is_last:
                for sem, val in final:
                    eh.wait_ge(sem, val)

        @block.tensor
        def _(eh):
            run("pe", eh)

        @block.scalar
        def _(eh):
            run("act", eh)

        @block.vector
        def _(eh):
            run("dve", eh)

        @block.gpsimd
        def _(eh):
            run("pool", eh)

        @block.sync
        def _(eh):
            run("sp", eh, is_last=True)


class Arena:
    def __init__(self, ap, ncols):
        self.ap = ap
        self.n = ncols
        self.off = 0

    def reset(self):
        self.off = 0

    def _shape(self, v, shape):
        if len(shape) == 2:
            v = v.rearrange("p (a b) -> p a b", a=shape[0])
        elif len(shape) == 3:
            v = v.rearrange("p (a b c) -> p a b c", a=shape[0], b=shape[1])
        return v

    def f(self, *shape):
        cols = int(np.prod(shape))
        assert self.off + cols <= self.n, ("arena overflow", self.off, cols, self.n)
        v = self.ap[:, self.off:self.off + cols]
        self.off += cols
        return self._shape(v, shape)

    def b(self, *shape):
        cols = int(np.prod(shape))
        c32 = (cols + 1) // 2
        assert self.off + c32 <= self.n, ("arena overflow", self.off, c32, self.n)
        v = self.ap[:, self.off:self.off + c32].bitcast(BF16)[:, 0:cols]
        self.off += c32
        return self._shape(v, shape)


def _bf(a):
    return np.ascontiguousarray(np.asarray(a, dtype=np.float32).astype(ml_dtypes.bfloat16))


def _pool_tables():
    A = np.zeros((4, 3, 128, 128), np.float32)
    inv = np.zeros((4, 128), np.float32)
    for g, w in enumerate(POOL_WINDOWS):
        for t in range(128):
            for tp in range(t - w + 1, t + 1):
                if tp >= 0:
                    A[g, 0, tp, t] += 1.0 / w
                else:
                    A[g, 1, tp + 128, t] += 1.0 / w
            A[g, 0, t, t] -= 1.0
            cnt = min(t + 1, w)
            for tp in range(max(0, t - w + 1), t + 1):
                A[g, 2, tp, t] += 1.0
            A[g, 2, t, t] -= cnt
            inv[g, t] = 1.0 / cnt
    At = np.transpose(A, (2, 0, 1, 3)).reshape(128, 4 * 3 * 128)
    invb = np.broadcast_to(inv.reshape(1, 4 * 128), (128, 4 * 128))
    return _bf(At), np.ascontiguousarray(invb, dtype=np.float32)


def layer_norm_tile(P, r_ap, rkey, lng, lnb, stat, skey, epsc):
    st6 = stat[:, 0:12].rearrange("p (a b) -> p a b", a=2)
    mv = stat[:, 12:14]
    P.add("dve", lambda e: e.bn_stats(out=st6[:, 0, :], in_=r_ap[:, 0:512]), reads=[rkey], writes=[skey])
    P.add("dve", lambda e: e.bn_stats(out=st6[:, 1, :], in_=r_ap[:, 512:1024]), reads=[rkey], writes=[skey])
    P.add("dve", lambda e: e.bn_aggr(out=mv, in_=stat[:, 0:12]), reads=[skey], writes=[skey])
    P.add("act", lambda e: e.activation(out=stat[:, 14:15], in_=stat[:, 13:14], func=AF.Sqrt,
                                        bias=epsc, scale=1.0), reads=[skey, "consts"], writes=[skey])
    P.add("dve", lambda e: e.reciprocal(out=stat[:, 14:15], in_=stat[:, 14:15]), reads=[skey], writes=[skey])
    P.add("dve", lambda e: e.scalar_tensor_tensor(out=stat[:, 15:16], in0=stat[:, 12:13], scalar=-1.0,
                                                  in1=stat[:, 14:15], op0=ALU.mult, op1=ALU.mult),
          reads=[skey], writes=[skey])
    P.add("act", lambda e: e.activation(out=r_ap, in_=r_ap, func=AF.Identity, bias=stat[:, 15:16],
                                        scale=stat[:, 14:15]), reads=[skey, rkey], writes=[rkey])
    P.add("pool", lambda e: e.tensor_tensor(out=r_ap, in0=r_ap, in1=lng, op=ALU.mult),
          reads=[rkey, "lnc"], writes=[rkey])
    P.add("pool", lambda e: e.tensor_tensor(out=r_ap, in0=r_ap, in1=lnb, op=ALU.add),
          reads=[rkey, "lnc"], writes=[rkey])


def _slopes():
    h = np.arange(1, 17, dtype=np.float32)
    return (2.0 ** (-8.0 * h / 16.0)).astype(np.float32)


def _l1_tables():
    t = {}
    sl = _slopes()
    tq = np.arange(S, dtype=np.float64)
    qal = np.zeros((16, 7, S), np.float32)
    for h in range(16):
        s0 = float(sl[h])
        s1 = float(np.float32(s0).astype(ml_dtypes.bfloat16))
        s2 = float(np.float32(s0 - s1).astype(ml_dtypes.bfloat16))
        s3 = float(np.float32(s0 - s1 - s2).astype(ml_dtypes.bfloat16))
        qal[h, 0] = -s0 * tq
        for i, si in enumerate((s1, s2, s3)):
            qal[h, 1 + i] = 64.0 * si
            qal[h, 4 + i] = si
    t["qalibi"] = _bf(qal)
    kal = np.zeros((7, S), np.float32)
    kal[0] = 1.0
    kal[1:4] = (np.arange(S) // 64)[None, :]
    kal[4:7] = (np.arange(S) % 64)[None, :]
    t["kalibi"] = _bf(kal)
    kc = np.zeros((7, 128), np.float32)
    ce = np.arange(127) * 16 + 31
    kc[0] = 1.0
    kc[1:4, :127] = (ce // 64)[None, :]
    kc[4:7, :127] = (ce % 64)[None, :]
    t["kalibic"] = _bf(kc)
    E = np.zeros((32, S), np.float32)
    E[np.arange(S) // 64, np.arange(S)] = 1.0
    t["eoh"] = _bf(E)
    cm = np.full((128, S), NEGM, np.float32)
    cm[:127] = np.where(np.arange(S)[None, :] >= ce[:, None], 0.0, NEGM)
    t["cmpmask"] = _bf(cm)
    dk = np.arange(128)[:, None]
    dq = np.arange(384)[None, :]
    t["winmask"] = _bf(np.where((dq - dk >= 0) & (dq - dk < 256), 0.0, NEGM))
    dq = np.arange(128)[None, :]
    t["trimask"] = _bf(np.where(dk <= dq, 0.0, NEGM))
    vcc = np.zeros((128, 33), np.float32)
    vcc[:, 0] = 1.0
    c0 = np.arange(127)[:, None] * 16
    j0 = np.arange(32)[None, :] * 64
    vcc[:127, 1:] = ((c0 < j0 + 64) & (c0 + 32 > j0)).astype(np.float32)
    t["vcc"] = _bf(vcc)
    q = np.arange(128)[:, None, None]
    qt = np.arange(16)[None, :, None]
    j = np.arange(32)[None, None, :]
    cur = (qt * 128 + q) // 64
    forced = (j == 0) | (j == cur) | (j == cur - 1)
    t["forced"] = np.ascontiguousarray(np.where(forced, 1e9, 0.0).astype(np.float32).reshape(128, 512))
    t["future"] = np.ascontiguousarray(np.where(j > cur, -1e30, 3e38).astype(np.float32).reshape(128, 512))
    return t


def _l1_dram(nc, din):
    L = {}
    L["wg1"] = din("wg1", [4, 128, 8, 640])
    L["wg2"] = din("wg2", [4, 128, 8, 268])
    L["w1k"] = din("w1k", [64, 32, 256])
    L["w1v"] = din("w1v", [64, 32, 256])
    L["w2k"] = din("w2k", [128, 2, 64])
    L["w2v"] = din("w2v", [128, 2, 64])
    L["posk"] = din("posk", [64, 32])
    L["posv"] = din("posv", [64, 32])
    L["wout1"] = din("wout1", [128, 8, 1024])
    L["qalibi"] = din("qalibi", [16, 7, S], BF16)
    L["kalibi"] = din("kalibi", [7, S], BF16)
    L["kalibic"] = din("kalibic", [7, 128], BF16)
    L["eoh"] = din("eoh", [32, S], BF16)
    L["cmpmask"] = din("cmpmask", [128, S], BF16)
    L["winmask"] = din("winmask", [128, 384], BF16)
    L["trimask"] = din("trimask", [128, 128], BF16)
    L["vcc"] = din("vcc", [128, 33], BF16)
    L["forced"] = din("forced", [128, 512])
    L["future"] = din("future", [128, 512])
    return L


def _host_l1(inputs):
    f = lambda a: np.ascontiguousarray(np.asarray(a, dtype=np.float32))
    m = {}
    W = f(inputs["nsa_w_in"])[0].reshape(8, 128, 3632).transpose(1, 0, 2)
    wg1 = np.zeros((4, 128, 8, 640), np.float32)
    wg2 = np.zeros((4, 128, 8, 268), np.float32)
    for g in range(4):
        wg1[g, :, :, 0:256] = W[:, :, 256 * g:256 * g + 256]
        wg1[g, :, :, 256:320] = W[:, :, 1024 + 64 * g:1024 + 64 * g + 64]
        wg1[g, :, :, 320:384] = W[:, :, 1280 + 64 * g:1280 + 64 * g + 64]
        wg1[g, :, :, 384:448] = W[:, :, 1536 + 64 * g:1536 + 64 * g + 64]
        wg1[g, :, :, 448:512] = W[:, :, 2048 + 64 * g:2048 + 64 * g + 64]
        wg1[g, :, :, 512:576] = W[:, :, 1792 + 64 * g:1792 + 64 * g + 64]
        wg1[g, :, :, 576:640] = W[:, :, 2304 + 64 * g:2304 + 64 * g + 64]
        wg2[g, :, :, 0:256] = W[:, :, 2560 + 256 * g:2560 + 256 * g + 256]
        wg2[g, :, :, 256:268] = W[:, :, 3584 + 12 * g:3584 + 12 * g + 12]
    m["wg1"] = wg1
    m["wg2"] = wg2
    m["w1k"] = np.ascontiguousarray(f(inputs["nsa_cmp_w1_k"])[0].reshape(32, 64, 256).transpose(1, 0, 2))
    m["w1v"] = np.ascontiguousarray(f(inputs["nsa_cmp_w1_v"])[0].reshape(32, 64, 256).transpose(1, 0, 2))
    m["w2k"] = np.ascontiguousarray(f(inputs["nsa_cmp_w2_k"])[0].reshape(2, 128, 64).transpose(1, 0, 2))
    m["w2v"] = np.ascontiguousarray(f(inputs["nsa_cmp_w2_v"])[0].reshape(2, 128, 64).transpose(1, 0, 2))
    m["posk"] = np.ascontiguousarray(f(inputs["nsa_cmp_pos_k"])[0].T)
    m["posv"] = np.ascontiguousarray(f(inputs["nsa_cmp_pos_v"])[0].T)
    m["wout1"] = np.ascontiguousarray(f(inputs["nsa_w_out"])[0].reshape(8, 128, 1024).transpose(1, 0, 2))
    m.update(_l1_tables())
    return m


L1_STAGE = 99


def _build_l1(nc, P, AR, psum, psum2, L, x1_d, out_d, lng_d, lnb_d, identf_d, identb_d):
    STG = L1_STAGE
    def PS(i):
        return ("ps", i)

    def MM(out, lhsT, rhs, start, stop, reads, writes, skip=False):
        P.add("pe", lambda e: e.matmul(out, lhsT=lhsT, rhs=rhs, start=start, stop=stop, skip_group_check=skip),
              reads=reads, writes=writes)

    def ACT(out, in_, func, reads, writes, **kw):
        P.add("act", lambda e: e.activation(out=out, in_=in_, func=func, **kw), reads=reads, writes=writes)

    def TT(eng, out, in0, in1, op, reads, writes):
        P.add(eng, lambda e: e.tensor_tensor(out=out, in0=in0, in1=in1, op=op), reads=reads, writes=writes)

    def TS(out, in0, s1, s2, op0, op1, reads, writes):
        if op1 is None:
            P.add("dve", lambda e: e.tensor_scalar(out=out, in0=in0, scalar1=s1, scalar2=None, op0=op0),
                  reads=reads, writes=writes)
        else:
            P.add("dve", lambda e: e.tensor_scalar(out=out, in0=in0, scalar1=s1, scalar2=s2, op0=op0, op1=op1),
                  reads=reads, writes=writes)

    def CP(out, in_, reads, writes, scale=None):
        if scale is None:
            P.add("dve", lambda e: e.tensor_copy(out=out, in_=in_), reads=reads, writes=writes)
        else:
            P.add("dve", lambda e: e.tensor_scalar(out=out, in0=in_, scalar1=scale, scalar2=None, op0=ALU.mult),
                  reads=reads, writes=writes)

    def DMA(q, out, in_, reads, writes, key):
        P.add(q, lambda e: e.dma_start(out=out, in_=in_), reads=reads, writes=writes, dma_key=key)

    W1 = AR.b(8, 640)
    W2 = [AR.b(8, 268) for _ in range(2)]
    w1 = AR.b(32, 256)
    w2 = [AR.b(2, 64) for _ in range(2)]
    pos = AR.b(32)
    wout1 = AR.b(8, 1024)
    identb = AR.b(128)
    x1T = AR.b(8, S)
    ogT = AR.b(8, S)
    q_aug = [AR.b(S) for _ in range(4)]
    ksel = AR.b(S)
    kwin = AR.b(S)
    kcmp = AR.b(128)
    pair = AR.b(S)
    vsel = AR.b(16, 65)
    vwin = AR.b(16, 65)
    vcmp = AR.b(97)
    hs = [AR.b(2, 128) for _ in range(2)]
    cmpmask = AR.b(S)
    winmask = AR.b(384)
    trimask = AR.b(128)
    Pbig = AR.b(2048)
    selm_w = AR.b(4, 128)
    ogb = [AR.b(256) for _ in range(4)]
    identf = AR.f(128)
    lng1 = AR.f(1024)
    lnb1 = AR.f(1024)
    xr = [AR.f(1024) for _ in range(2)]
    xs2 = [AR.f(1024) for _ in range(2)]
    stat = [AR.f(16) for _ in range(2)]
    o_tm = [AR.f(4, 256) for _ in range(2)]
    gates = AR.f(16, 12)
    imp = AR.f(4, 32)
    impm = AR.f(4, 32)
    forced = AR.f(16, 32)
    future = AR.f(16, 32)
    m8 = AR.f(4, 8)
    thr = AR.f(4)
    rden = [AR.f(4) for _ in range(2)]
    fcol = [AR.f(4) for _ in range(2)]
    tmp_o = [AR.f(4, 64) for _ in range(2)]
    tmp_i = AR.f(4, 32)
    sz = [AR.f(256) for _ in range(4)]
    bh = AR.f(4)
    epsc = AR.f(1)

    for kc in range(0, 8, 2):
        DMA("pool", W1[:, kc:kc + 2, :], L["wg1"][0, :, kc:kc + 2, :], [], ["W1"], "W1")
    for kc in range(0, 8, 4):
        DMA("pool", W2[0][:, kc:kc + 4, :], L["wg2"][0, :, kc:kc + 4, :], [], [("W2", 0)], "W20")
    for t_, k_ in ((ksel, "ksel_c"), (kwin, "kwin_c"), (kcmp, "kcmp_c"), (vcmp, "vcmp_c"),
                   (selm_w.rearrange("p a b -> p (a b)"), "selm")):
        P.add("pool", lambda e, t_=t_: e.memset(t_, 0.0), writes=[k_])
    for hl in range(4):
        P.add("pool", lambda e, hl=hl: e.memset(q_aug[hl], 0.0), writes=[("qa_q", hl), ("qa_s", hl), ("qa_a", hl)])
    P.add("pool", lambda e: e.memset(vsel[:, :, 64:65], 1.0), writes=["vsel_c"])
    P.add("pool", lambda e: e.memset(vwin[:, :, 64:65], 1.0), writes=["vwin_c"])
    P.add("dve", lambda e: e.memset(epsc, LN_EPS), writes=["consts"])
    DMA("sp", identb, identb_d, [], ["consts"], "c1")
    DMA("sp", identf, identf_d, [], ["consts"], "c1")
    DMA("sp", lng1, lng_d[1], [], ["lnc"], "c1l")
    DMA("sp", lnb1, lnb_d[1], [], ["lnc"], "c1l")
    DMA("sp", cmpmask, L["cmpmask"], [], ["consts"], "c1")
    DMA("sp", winmask, L["winmask"], [], ["consts"], "c1")
    DMA("sp", trimask, L["trimask"], [], ["consts"], "c1")
    DMA("sp", forced.rearrange("p a b -> p (a b)"), L["forced"], [], ["consts"], "c1")
    DMA("sp", future.rearrange("p a b -> p (a b)"), L["future"], [], ["consts"], "c1")
    DMA("sp", ksel[64:96, :], L["eoh"], [], ["ksel_c"], "c2a")
    DMA("sp", ksel[96:103, :], L["kalibi"], [], ["ksel_c"], "c2a")
    DMA("sp", kwin[96:103, :], L["kalibi"], [], ["kwin_c"], "c2b")
    DMA("sp", kcmp[96:103, :], L["kalibic"], [], ["kcmp_c"], "c2c")
    DMA("sp", vcmp[:, 64:97], L["vcc"], [], ["vcmp_c"], "c2d")
    for kv, (wn, w2n, pn) in enumerate((("w1k", "w2k", "posk"), ("w1v", "w2v", "posv"))):
        for i in range(0, 32, 8):
            DMA("pool", w1[64 * kv:64 * kv + 64, i:i + 8, :], L[wn][:, i:i + 8, :], [], ["wcmp"], "wcmp")
        DMA("pool", w2[kv], L[w2n], [], ["wcmp"], "wcmp")
        DMA("pool", pos[64 * kv:64 * kv + 64, :], L[pn], [], ["wcmp"], "wcmp")
    for c in range(8):
        DMA("pool", wout1[:, c, :], L["wout1"][:, c, :], [], ["wout1"], "wout1")
    for kv in range(2):
        for hc in range(2):
            col = kv * 2 + hc
            for i in range(32):
                MM(psum[0][:, col:col + 1], w1[64 * kv:64 * kv + 64, i, hc * 128:(hc + 1) * 128],
                   pos[64 * kv:64 * kv + 64, i:i + 1], i == 0, i == 31, ["wcmp"], [PS(0)])
    ACT(bh, psum[0][:, 0:4], AF.Copy, [PS(0)], ["bh"])

    if STG < 1:
        return
    rr = {"s": 0, "o": 0, "p": 0, "a": 0, "e": 0, "ring": 0}

    def nxt(k, n):
        v = rr[k]
        rr[k] = (v + 1) % n
        return v

    def x1T_tile(s, j):
        if s == 0:
            xs_ = j % 4
            stg = (xs2[0], xs2[1], xr[0], xr[1])[xs_]
            xk = (("xs2", 0), ("xs2", 1), ("xr", 0), ("xr", 1))[xs_]
            dk = ("xs20", "xs21", "xr0", "xr1")[xs_]
        else:
            xs_ = j % 2
            stg = xs2[xs_]
            xk = ("xs2", xs_)
            dk = "xs2%d" % xs_
        DMA("sp", stg, x1_d[s, j * 128:(j + 1) * 128, :], [], [xk], dk)
        for q4 in range(2):
            pb = nxt("a", 2)
            for i4 in range(4):
                kc = q4 * 4 + i4
                P.add("pe", lambda e, kc=kc, i4=i4, pb=pb, stg=stg: e.transpose(
                    out=psum[pb][:, i4 * 128:(i4 + 1) * 128], in_=stg[:, kc * 128:(kc + 1) * 128],
                    identity=identf), reads=[xk, "consts"], writes=[PS(pb)])
            CP(x1T[:, q4 * 4:(q4 + 1) * 4, j * 128:(j + 1) * 128],
               psum[pb][:, :].rearrange("p (a b) -> p a b", a=4), [PS(pb)], ["x1T"])

    xr.append(o_tm[0].rearrange("p a b -> p (a b)"))
    xr.append(o_tm[1].rearrange("p a b -> p (a b)"))
    stat.append(AR.f(16))
    stat.append(AR.f(16))

    def outproj_tile(s, jt):
        xs_ = jt % 4
        xk = ("xr", xs_)
        alias = [("o_tm", xs_ - 2, h_) for h_ in range(4)] if xs_ >= 2 else []
        DMA("sp", xr[xs_], x1_d[s, jt * 128:(jt + 1) * 128, :], [], [xk] + alias, "xr%d" % xs_)
        for hh in range(2):
            pb = nxt("a", 2)
            for c in range(8):
                MM(psum[pb][:, :], ogT[:, c, jt * 128:(jt + 1) * 128], wout1[:, c, hh * 512:(hh + 1) * 512],
                   c == 0, c == 7, [("ogT", c // 2), "wout1"], [PS(pb)])
            P.add("dve", lambda e, hh=hh, pb=pb, xs_=xs_: e.scalar_tensor_tensor(
                out=xr[xs_][:, hh * 512:(hh + 1) * 512], in0=xr[xs_][:, hh * 512:(hh + 1) * 512],
                scalar=DN_ALPHA, in1=psum[pb][:, :], op0=ALU.mult, op1=ALU.add), reads=[PS(pb), xk], writes=[xk])
        layer_norm_tile(P, xr[xs_], xk, lng1, lnb1, stat[xs_], ("stat1", xs_), epsc)
        DMA("pool", out_d[s, jt * 128:(jt + 1) * 128, :], xr[xs_], [xk], [], "ost%d" % xs_)

    def load_group_weights(gi_):
        g_ = gi_ % 4
        for kc in range(0, 8, 2):
            DMA("pool", W1[:, kc:kc + 2, :], L["wg1"][g_, :, kc:kc + 2, :], [], ["W1"], "W1")
        for kc in range(0, 8, 4):
            DMA("pool", W2[gi_ % 2][:, kc:kc + 4, :], L["wg2"][g_, :, kc:kc + 4, :], [], [("W2", gi_ % 2)], "W2%d" % (gi_ % 2))

    pending = []

    def drain_pending(n):
        for _ in range(n):
            if pending:
                s_, jt_ = pending.pop(0)
                outproj_tile(s_, jt_)

    for s in range(NSEQ):
        if s == 0:
            for j in range(16):
                x1T_tile(0, j)
        if STG < 2:
            return
        for g in range(4):
            gi = s * 4 + g
            w2s = gi % 2
            w2k_ = ("W2", w2s)
            for hl in range(4):
                DMA("sp", q_aug[hl][96:103, :], L["qalibi"][4 * g + hl], [], [("qa_a", hl)], "qal%d" % hl)
            for pc in range(2):
                for tb in range(4):
                    pb = nxt("a", 2)
                    for kc in range(8):
                        MM(psum[pb][:, :], W1[:, kc, pc * 128:(pc + 1) * 128], x1T[:, kc, tb * 512:(tb + 1) * 512],
                           kc == 0, kc == 7, ["W1", "x1T"], [PS(pb)])
                    CP(q_aug[2 * pc][0:64, tb * 512:(tb + 1) * 512], psum[pb][0:64, :],
                       [PS(pb)], [("qa_q", 2 * pc)], scale=0.125)
                    CP(q_aug[2 * pc + 1][0:64, tb * 512:(tb + 1) * 512], psum[pb][64:128, :],
                       [PS(pb)], [("qa_q", 2 * pc + 1)], scale=0.125)
                    drain_pending(1)
            if STG < 3:
                return
            for tb in range(4):
                c0, c1 = tb * 512, (tb + 1) * 512
                pb = nxt("a", 2)
                for kc in range(8):
                    MM(psum[pb][:, :], W1[:, kc, 256:384], x1T[:, kc, c0:c1], kc == 0, kc == 7, ["W1", "x1T"], [PS(pb)])
                CP(pair.rearrange("p (ph c) -> p ph c", ph=16)[:, :, tb * 32:(tb + 1) * 32],
                   psum[pb][:, :].rearrange("p (c ph) -> p ph c", ph=16), [PS(pb)], ["pair"])
                pb = nxt("a", 2)
                for kc in range(8):
                    MM(psum[pb][:, :], W1[:, kc, 384:512], x1T[:, kc, c0:c1], kc == 0, kc == 7, ["W1", "x1T"], [PS(pb)])
                P.add("dve", lambda e, pb=pb, c0=c0, c1=c1: e.tensor_copy(out=ksel[0:64, c0:c1], in_=psum[pb][0:64, :]),
                      reads=[PS(pb)], writes=["ksel"])
                CP(kwin[0:64, c0:c1], psum[pb][64:128, :], [PS(pb)], ["kwin"])
            if STG < 4:
                return
            for j4 in range(4):
                pb = nxt("a", 2)
                for jj in range(4):
                    j = j4 * 4 + jj
                    for kc in range(8):
                        MM(psum[pb][:, jj * 128:(jj + 1) * 128], x1T[:, kc, j * 128:(j + 1) * 128], W1[:, kc, 512:640],
                           kc == 0, kc == 7, ["W1", "x1T"], [PS(pb)])
                pv = psum[pb][:, :].rearrange("p (a b) -> p a b", a=4)
                P.add("dve", lambda e, pv=pv, j4=j4: e.tensor_copy(out=vsel[:, j4 * 4:(j4 + 1) * 4, 0:64], in_=pv[:, :, 0:64]),
                      reads=[PS(pb)], writes=["vsel"])
                ACT(vwin[:, j4 * 4:(j4 + 1) * 4, 0:64], pv[:, :, 64:128], AF.Copy, [PS(pb)], ["vwin"])
            if STG < 5:
                return
            if gi + 1 < NSEQ * 4:
                load_group_weights(gi + 1)
            pb = nxt("a", 2)
            for j in range(16):
                for kc in range(8):
                    MM(psum[pb][:, j * 12:(j + 1) * 12], x1T[:, kc, j * 128:(j + 1) * 128], W2[w2s][:, kc, 256:268],
                       kc == 0, kc == 7, [w2k_, "x1T"], [PS(pb)])
            ACT(gates.rearrange("p a b -> p (a b)"), psum[pb][:, 0:192], AF.Sigmoid, [PS(pb)], ["gates"])
            if STG < 6:
                return
            for kv in range(2):
                for hc in range(2):
                    pb = nxt("a", 2)
                    for i in range(32):
                        MM(psum[pb][:, 0:127], w1[64 * kv:64 * kv + 64, i, hc * 128:(hc + 1) * 128],
                           pair[64 * kv:64 * kv + 64, (i % 16) * 128 + i // 16:(i % 16) * 128 + i // 16 + 127],
                           i == 0, i == 31, ["wcmp", "pair"], [PS(pb)])
                    ACT(hs[kv][:, hc, 0:127], psum[pb][:, 0:127], AF.Silu, [PS(pb), "bh"], [("hs", kv)],
                        bias=bh[:, kv * 2 + hc:kv * 2 + hc + 1])
            pb = nxt("a", 2)
            for hc in range(2):
                MM(psum[pb][0:64, 0:127], w2[0][:, hc, :], hs[0][:, hc, 0:127], hc == 0, hc == 1,
                   ["wcmp", ("hs", 0)], [PS(pb)])
            CP(kcmp[0:64, 0:127], psum[pb][0:64, 0:127], [PS(pb)], ["kcmp"])
            pb = nxt("a", 2)
            for hc in range(2):
                MM(psum[pb][0:127, 0:64], hs[1][:, hc, 0:127], w2[1][:, hc, :], hc == 0, hc == 1,
                   ["wcmp", ("hs", 1)], [PS(pb)])
            CP(vcmp[0:127, 0:64], psum[pb][0:127, 0:64], [PS(pb)], ["vcmp"])

            if STG < 7:
                return
            def branch_epilogue(hl, br, ob, Wd, qt, os_, first, with_imp):
                O = psum[ob][:, 0:4 * Wd].rearrange("p (a b) -> p a b", a=4)
                e_ = nxt("e", 2)
                rk, fk, tk = ("rden", e_), ("fcol", e_), ("tmp_o", e_)
                if br == 0:
                    TS(rden[e_].unsqueeze(2), O[:, :, 64:65], 1e-30, None, ALU.max, None, [PS(ob)], [rk])
                    P.add("dve", lambda e: e.reciprocal(out=rden[e_], in_=rden[e_]), reads=[rk], writes=[rk])
                else:
                    P.add("dve", lambda e: e.reciprocal(out=rden[e_].unsqueeze(2), in_=O[:, :, 64:65]),
                          reads=[PS(ob)], writes=[rk])
                if with_imp:
                    rb = rden[e_].unsqueeze(2).broadcast_to([128, 4, 32])
                    if hl == 0:
                        TT("dve", imp, O[:, :, 65:97], rb, ALU.mult, [PS(ob), rk], ["imp"])
                    else:
                        TT("dve", tmp_i, O[:, :, 65:97], rb, ALU.mult, [PS(ob), rk], ["tmp_i"])
                        TT("pool", imp, imp, tmp_i, ALU.add, ["tmp_i", "imp"], ["imp"])
                gcol = gates[:, qt * 4:(qt + 1) * 4, hl * 3 + br]
                TT("dve", fcol[e_], rden[e_], gcol, ALU.mult, [rk, "gates"], [fk])
                fb = fcol[e_].unsqueeze(2).broadcast_to([128, 4, 64])
                odst = o_tm[os_][:, :, hl * 64:(hl + 1) * 64]
                ok = ("o_tm", os_, hl)
                if first:
                    TT("dve", odst, O[:, :, 0:64], fb, ALU.mult, [PS(ob), fk], [ok])
                else:
                    TT("dve", tmp_o[e_], O[:, :, 0:64], fb, ALU.mult, [PS(ob), fk], [tk])
                    TT("pool", odst, odst, tmp_o[e_], ALU.add, [tk, ok], [ok])

            SBK = [2, 3, 0, 1]

            def sel_chain(qt):
                TT("dve", impm, imp, forced[:, qt * 4:(qt + 1) * 4, :], ALU.max, ["imp", "consts"], ["impm"])
                TT("dve", impm, impm, future[:, qt * 4:(qt + 1) * 4, :], ALU.min, ["impm", "consts"], ["impm"])
                for qs in range(4):
                    P.add("dve", lambda e, qs=qs: e.max(out=m8[:, qs, :], in_=impm[:, qs, :]), reads=["impm"], writes=["m8"])
                TS(thr.unsqueeze(2), m8[:, :, 7:8], 0.0, None, ALU.max, None, ["m8"], ["thr"])
                for qs in range(4):
                    TS(selm_w[:, qs, 64:96], impm[:, qs, :], thr[:, qs:qs + 1], 1.0, ALU.is_ge, ALU.subtract,
                       ["impm", "thr"], ["selm"])

            def sel_transposes(Q0):
                for qs in range(4):
                    MM(psum[6][:, qs * 128:(qs + 1) * 128], selm_w[:, qs, :], identb, True, True, ["selm", "consts"], [PS(6)])
                for hl in range(4):
                    CP(q_aug[hl][64:96, Q0:Q0 + 512], psum[6][64:96, :], [PS(6)], [("qa_s", hl)], scale=30000.0)

            def gate_path(qt, os_):
                p6b = psum[6][:, :].bitcast(BF16)
                for qs in range(4):
                    TT("dve", ogb[qs], o_tm[os_][:, qs, :], sz[qs], ALU.mult,
                       [("sz", qs)] + [("o_tm", os_, h_) for h_ in range(4)], [("ogb", qs)])
                for qs in range(4):
                    for cc in range(2):
                        P.add("pe", lambda e, cc=cc, qs=qs: e.transpose(
                            out=p6b[:, qs * 256 + cc * 128:qs * 256 + (cc + 1) * 128],
                            in_=ogb[qs][:, cc * 128:(cc + 1) * 128], identity=identb),
                            reads=[("ogb", qs), "consts"], writes=[PS(6)])
                CP(ogT[:, 2 * g:2 * g + 2, qt * 512:(qt + 1) * 512].rearrange("p c (q t) -> p c q t", q=4),
                   p6b.rearrange("p (q c t) -> p c q t", q=4, c=2), [PS(6)], [("ogT", g)])

            def zproj_tiles(qt):
                for qs in range(4):
                    jt = 4 * qt + qs

                    def qk_f(sb, jt=jt):
                        for kc in range(8):
                            MM(psum[sb][:, 0:256], x1T[:, kc, jt * 128:(jt + 1) * 128], W2[w2s][:, kc, 0:256],
                               kc == 0, kc == 7, [w2k_, "x1T"], [PS(sb)])

                    def act_f(sb, qs=qs):
                        ACT(sz[qs], psum[sb][:, 0:256], AF.Silu, [PS(sb)], [("sz", qs)])

                    mk_tile(qk_f, 0, 256, (lambda pr: None), None)
                    units[-1]["tiles"][0]["act"] = act_f

            units = []

            def mk_tile(qk, c0, c1, pv, post=None, pairable=False):
                t = {"qk": qk, "c0": c0, "c1": c1, "pv": pv, "post": post}
                if pairable and units and units[-1]["pair_open"]:
                    units[-1]["tiles"].append(t)
                    units[-1]["pair_open"] = False
                else:
                    units.append({"tiles": [t], "pair_open": pairable, "marker": None})

            def close_pair():
                if units:
                    units[-1]["pair_open"] = False

            def mk_marker(fn):
                close_pair()
                units.append({"tiles": [], "pair_open": False, "marker": fn})

            for qt in range(4):
                Q0 = qt * 512
                os_ = (gi * 4 + qt) % 2
                for hl in range(4):
                    ob = (4, 5, 7)[nxt("o", 3)]
                    qk = [("qa_q", hl), ("qa_s", hl), ("qa_a", hl)]

                    def qk_f(sb, hl=hl, Q0=Q0, qk=qk):
                        MM(psum[sb][:, :], kcmp[0:103, :], q_aug[hl][0:103, Q0:Q0 + 512], True, False,
                           ["kcmp", "kcmp_c"] + qk, [PS(sb)])
                        MM(psum[sb][:, :], identb, cmpmask[:, Q0:Q0 + 512], False, True, ["consts"], [PS(sb)])

                    def pv_f(pr, ob=ob):
                        for qs in range(4):
                            MM(psum[ob][:, qs * 97:(qs + 1) * 97], Pbig[:, pr * 512 + qs * 128:pr * 512 + (qs + 1) * 128],
                               vcmp[:, 0:97], True, True, [("Pb", pr), "vcmp", "vcmp_c"], [PS(ob)])

                    def post_f(hl=hl, ob=ob, qt=qt, os_=os_):
                        branch_epilogue(hl, 0, ob, 97, qt, os_, True, True)
                        if hl == 3:
                            sel_chain(qt)

                    mk_tile(qk_f, 0, 512, pv_f, post_f, pairable=False)
                close_pair()
                for hl in range(4):
                    qk = [("qa_q", hl), ("qa_s", hl), ("qa_a", hl)]
                    ob = (4, 5, 7)[nxt("o", 3)]
                    plan = []
                    for kt in range(max(0, 4 * qt - 2), 4 * qt + 4):
                        K0 = kt * 128
                        lo, hi = max(K0, Q0), min(K0 + 384, Q0 + 512)
                        if hi > lo:
                            plan.append((kt, K0, lo, hi))
                    lastk = {}
                    for (kt, K0, lo, hi) in plan:
                        for qs in range((lo - Q0) // 128, (hi - Q0) // 128):
                            lastk[qs] = kt
                    firstmm = True
                    for ti, (kt, K0, lo, hi) in enumerate(plan):
                        c0, c1, m0 = lo - Q0, hi - Q0, lo - K0

                        def qk_f(sb, hl=hl, qk=qk, K0=K0, lo=lo, hi=hi, c0=c0, c1=c1, m0=m0):
                            MM(psum[sb][:, c0:c1], kwin[0:103, K0:K0 + 128], q_aug[hl][0:103, lo:hi], True, False,
                               ["kwin", "kwin_c"] + qk, [PS(sb)])
                            MM(psum[sb][:, c0:c1], identb, winmask[:, m0:m0 + (hi - lo)], False, True, ["consts"], [PS(sb)])

                        pvl = []
                        for qs in range(c0 // 128, c1 // 128):
                            pvl.append((qs, firstmm, lastk[qs] == kt))
                            firstmm = False

                        def pv_f(pr, ob=ob, kt=kt, pvl=pvl):
                            for (qs, st_, sp_) in pvl:
                                MM(psum[ob][:, qs * 65:(qs + 1) * 65], Pbig[:, pr * 512 + qs * 128:pr * 512 + (qs + 1) * 128],
                                   vwin[:, kt, :], st_, sp_, [("Pb", pr), "vwin", "vwin_c"], [PS(ob)], skip=True)

                        post_f = None
                        if ti == len(plan) - 1:
                            def post_f(hl=hl, ob=ob, qt=qt, os_=os_):
                                branch_epilogue(hl, 2, ob, 65, qt, os_, False, False)
                        mk_tile(qk_f, c0, c1, pv_f, post_f)
                mk_marker(lambda Q0=Q0: sel_transposes(Q0))
                zproj_tiles(qt)
                for hl in range(4):
                    qk = [("qa_q", hl), ("qa_s", hl), ("qa_a", hl)]
                    ob = (4, 5, 7)[nxt("o", 3)]
                    nk = 4 * qt + 4
                    firstmm = True
                    for kt in range(nk):
                        K0 = kt * 128
                        dq_ = kt - 4 * qt
                        c0 = max(dq_, 0) * 128

                        def qk_f(sb, hl=hl, qk=qk, K0=K0, Q0=Q0, c0=c0, dq_=dq_):
                            MM(psum[sb][:, c0:512], ksel[0:103, K0:K0 + 128], q_aug[hl][0:103, Q0 + c0:Q0 + 512], True, dq_ < 0,
                               ["ksel", "ksel_c"] + qk, [PS(sb)])
                            if dq_ >= 0:
                                MM(psum[sb][:, c0:c0 + 128], identb, trimask, False, True, ["consts"], [PS(sb)])

                        pvl = []
                        for qs in range(c0 // 128, 4):
                            pvl.append((qs, firstmm, kt == 4 * qt + qs))
                            firstmm = False

                        def pv_f(pr, ob=ob, kt=kt, pvl=pvl):
                            for (qs, st_, sp_) in pvl:
                                MM(psum[ob][:, qs * 65:(qs + 1) * 65], Pbig[:, pr * 512 + qs * 128:pr * 512 + (qs + 1) * 128],
                                   vsel[:, kt, :], st_, sp_, [("Pb", pr), "vsel", "vsel_c"], [PS(ob)], skip=True)

                        post_f = None
                        if kt == nk - 1:
                            def post_f(hl=hl, ob=ob, qt=qt, os_=os_):
                                branch_epilogue(hl, 1, ob, 65, qt, os_, False, False)
                                if hl == 3:
                                    gate_path(qt, os_)
                        mk_tile(qk_f, c0, 512, pv_f, post_f, pairable=False)
                    close_pair()

            inflight = []

            def flush_one():
                u, need, pos = inflight.pop(0)
                for t, p in zip(u["tiles"], pos):
                    t["pv"](p)
                    if t["post"] is not None:
                        t["post"]()

            for u in units:
                if u["marker"] is not None:
                    u["marker"]()
                    continue
                nb = len(u["tiles"])
                skip = 1 if (nb == 2 and rr["ring"] % 2 == 1) else 0
                need = nb + skip
                while inflight and (sum(x[1] for x in inflight) + need > 4 or len(inflight) >= 3):
                    flush_one()
                rr["ring"] = (rr["ring"] + skip) % 4
                pos = [(rr["ring"] + k) % 4 for k in range(nb)]
                rr["ring"] = (rr["ring"] + nb) % 4
                for t, p in zip(u["tiles"], pos):
                    t["qk"](SBK[p])
                if nb == 2:
                    d = SBK[pos[0]] // 2
                    ACT(Pbig[:, pos[0] * 512:(pos[0] + 2) * 512], psum2[d][:, :], AF.Exp,
                        [PS(SBK[pos[0]]), PS(SBK[pos[1]])], [("Pb", pos[0]), ("Pb", pos[1])])
                else:
                    t = u["tiles"][0]
                    p = pos[0]
                    if t.get("act") is not None:
                        t["act"](SBK[p])
                    else:
                        ACT(Pbig[:, p * 512 + t["c0"]:p * 512 + t["c1"]], psum[SBK[p]][:, t["c0"]:t["c1"]], AF.Exp,
                            [PS(SBK[p])], [("Pb", p)])
                inflight.append((u, need, pos))
            while inflight:
                flush_one()

        for jt in range(16):
            pending.append((s, jt))
        if s + 1 < NSEQ:
            for jt in range(16):
                x1T_tile(s + 1, jt)
                if jt % 2 == 1:
                    drain_pending(1)
        else:
            drain_pending(16)


def build_program(do_l0=True, do_l1=True):
    nc = bass.Bass("TRN2", target_bir_lowering=False)
    dt = {}

    def din(name, shape, dtype=F32):
        dt[name] = nc.dram_tensor(name, list(shape), dtype, kind="ExternalInput").ap()
        return dt[name]

    x_d = din("x", [NSEQ, S, D])
    lng_d = din("lng", [2, 128, D])
    lnb_d = din("lnb", [2, 128, D])
    w_in0_d = din("w_in0", [128, 8, 4096])
    w_grp_d = din("w_grp", [128, 16, 512])
    w_out0_d = din("w_out0", [128, 16, 1024])
    scale_d = din("pscale", [128, 16])
    poolA_d = din("poolA", [128, 4 * 3 * 128], BF16)
    invc_d = din("invc", [128, 4 * 128])
    identf_d = din("identf", [128, 128])
    identb_d = din("identb", [128, 128], BF16)
    x1kind = "Internal" if (do_l0 and do_l1) else ("ExternalOutput" if do_l0 else "ExternalInput")
    x1_d = nc.dram_tensor("x1s", [NSEQ, S, D], F32, kind=x1kind).ap()
    out_d = nc.dram_tensor("out", [NSEQ, S, D], F32, kind="ExternalOutput").ap()
    L1D = _l1_dram(nc, din)

    stack = contextlib.ExitStack()
    with stack:
        ACOLS = 52800
        ar_t = stack.enter_context(nc.sbuf_tensor("arena", [128, ACOLS], F32))
        AR = Arena(ar_t[:], ACOLS)
        psum2 = [stack.enter_context(nc.psum_tensor("ps%d" % i, [128, 1024], F32)) for i in range(4)]
        psum = [psum2[i // 2][:, (i % 2) * 512:(i % 2 + 1) * 512] for i in range(8)]
        P = Prog(nc)

        def PS(i):
            return ("ps", i)

        if do_l0:
            AR.reset()
            w_in0 = AR.b(8, 4096)
            w_grp = AR.b(16, 512)
            w_out0 = AR.b(16, 1024)
            poolA = AR.b(12, 128)
            identb = AR.b(128)
            xT0 = [AR.b(8, 256) for _ in range(2)]
            u_tm = AR.b(12, 512)
            mT = [AR.b(4, 256) for _ in range(2)]
            gT = [AR.b(16, 256) for _ in range(2)]
            identf = AR.f(128)
            lng0 = AR.f(1024)
            lnb0 = AR.f(1024)
            invc = AR.f(4, 128)
            pscale = AR.f(16)
            epsc = AR.f(1)
            xs = [AR.f(2, 1024) for _ in range(2)]
            siluz = [AR.f(4, 256) for _ in range(2)]
            rbuf = [AR.f(1024) for _ in range(2)]
            stat = [AR.f(16) for _ in range(2)]

            P.add("sp", lambda e: e.dma_start(out=identf, in_=identf_d), writes=["consts"], dma_key="c0")
            P.add("sp", lambda e: e.dma_start(out=identb, in_=identb_d), writes=["consts"], dma_key="c0")
            P.add("sp", lambda e: e.dma_start(out=poolA.rearrange("p a b -> p (a b)"), in_=poolA_d), writes=["consts"], dma_key="c0")
            P.add("sp", lambda e: e.dma_start(out=invc.rearrange("p a b -> p (a b)"), in_=invc_d), writes=["consts"], dma_key="c0")
            P.add("sp", lambda e: e.dma_start(out=pscale, in_=scale_d), writes=["consts"], dma_key="c0")
            P.add("sp", lambda e: e.dma_start(out=lng0, in_=lng_d[0]), writes=["lnc"], dma_key="c0l")
            P.add("sp", lambda e: e.dma_start(out=lnb0, in_=lnb_d[0]), writes=["lnc"], dma_key="c0l")
            P.add("dve", lambda e: e.memset(epsc, LN_EPS), writes=["consts"])
            for gq in range(4):
                for cb in (gq, 4 + gq):
                    P.add("pool", lambda e, cb=cb: e.dma_start(out=w_in0[:, :, cb * 512:(cb + 1) * 512],
                                                             in_=w_in0_d[:, :, cb * 512:(cb + 1) * 512]),
                          writes=[("w_in0", cb)], dma_key="w_in0_%d" % cb)
                P.add("pool", lambda e, gq=gq: e.dma_start(out=w_grp[:, gq * 4:(gq + 1) * 4, :],
                                                         in_=w_grp_d[:, gq * 4:(gq + 1) * 4, :]),
                      writes=[("w_grp", gq)], dma_key="w_grp_%d" % gq)
            for c in range(0, 16, 2):
                P.add("pool", lambda e, c=c: e.dma_start(out=w_out0[:, c:c + 2, :], in_=w_out0_d[:, c:c + 2, :]),
                      writes=["w_out0"], dma_key="w_out0")

            NB = S // 256
            NIT = NSEQ * NB

            def geo(it):
                s, b = divmod(it, NB)
                return s, b, it % 2, b * 256

            def stage_T(it):
                s, b, sl, t0 = geo(it)
                xk, xtk = ("xs", sl), ("xT0", sl)
                for j in range(2):
                    P.add("sp", lambda e, j=j: e.dma_start(
                        out=xs[sl][:, j, :], in_=x_d[s, t0 + j * 128:t0 + (j + 1) * 128, :]),
                        writes=[xk], dma_key="xs%d" % sl)
                for j in range(2):
                    for q4 in range(2):
                        pb = (j * 2 + q4) % 2
                        for i4 in range(4):
                            kc = q4 * 4 + i4
                            P.add("pe", lambda e, j=j, kc=kc, i4=i4, pb=pb: e.transpose(
                                out=psum[pb][:, i4 * 128:(i4 + 1) * 128], in_=xs[sl][:, j, kc * 128:(kc + 1) * 128],
                                identity=identf), reads=[xk, "consts"], writes=[PS(pb)])
                        P.add("dve", lambda e, j=j, q4=q4, pb=pb: e.tensor_copy(
                            out=xT0[sl][:, q4 * 4:(q4 + 1) * 4, j * 128:(j + 1) * 128],
                            in_=psum[pb][:, :].rearrange("p (a b) -> p a b", a=4)),
                            reads=[PS(pb)], writes=[xtk])

            def stage_U(it, g):
                s, b, sl, t0 = geo(it)
                xtk = ("xT0", sl)
                for j in range(2):
                    n = b * 2 + j
                    slot = n % 3
                    pb = 2 + (j % 2)
                    for kc in range(8):
                        P.add("pe", lambda e, kc=kc, j=j, pb=pb: e.matmul(
                            psum[pb][:, :], lhsT=xT0[sl][:, kc, j * 128:(j + 1) * 128],
                            rhs=w_in0[:, kc, g * 512:(g + 1) * 512], start=(kc == 0), stop=(kc == 7)),
                            reads=[xtk, ("w_in0", g)], writes=[PS(pb)])
                    P.add("act", lambda e, slot=slot, pb=pb: e.activation(
                        out=u_tm[:, g * 3 + slot, :], in_=psum[pb][:, :], func=AF.Copy),
                        reads=[PS(pb)], writes=[("u", g, slot)])

            def stage_Z(it, g):
                s, b, sl, t0 = geo(it)
                xtk = ("xT0", sl)
                gs = (it * 4 + g) % 2
                szk = ("siluz", gs)
                for c2 in range(2):
                    pb = 4 + c2
                    for ci in range(2):
                        c = c2 * 2 + ci
                        col = 2048 + g * 512 + c * 128
                        for kc in range(8):
                            P.add("pe", lambda e, kc=kc, col=col, ci=ci, pb=pb: e.matmul(
                                psum[pb][:, ci * 256:(ci + 1) * 256], lhsT=w_in0[:, kc, col:col + 128],
                                rhs=xT0[sl][:, kc, :], start=(kc == 0), stop=(kc == 7)),
                                reads=[xtk, ("w_in0", 4 + g)], writes=[PS(pb)])
                    P.add("act", lambda e, c2=c2, pb=pb: e.activation(
                        out=siluz[gs][:, c2 * 2:(c2 + 1) * 2, :],
                        in_=psum[pb][:, :].rearrange("p (a b) -> p a b", a=2), func=AF.Silu),
                        reads=[PS(pb)], writes=[szk])

            def stage_PM(it, g):
                s, b, sl, t0 = geo(it)
                gs = (it * 4 + g) % 2
                mk = ("mT", gs)
                for c2 in range(2):
                    pb = 6 + c2
                    for ci in range(2):
                        c = c2 * 2 + ci
                        for j in range(2):
                            n = b * 2 + j
                            slot = n % 3
                            pslot = (n - 1) % 3
                            o = psum[pb][:, ci * 256 + j * 128: ci * 256 + (j + 1) * 128]
                            if n == 0:
                                P.add("pe", lambda e, o=o, slot=slot, c=c: e.matmul(
                                    o, lhsT=u_tm[:, g * 3 + slot, c * 128:(c + 1) * 128],
                                    rhs=poolA[:, g * 3 + 2, :], start=True, stop=True),
                                    reads=[("u", g, slot), "consts"], writes=[PS(pb)])
                            else:
                                P.add("pe", lambda e, o=o, slot=slot, c=c: e.matmul(
                                    o, lhsT=u_tm[:, g * 3 + slot, c * 128:(c + 1) * 128],
                                    rhs=poolA[:, g * 3 + 0, :], start=True, stop=False),
                                    reads=[("u", g, slot), "consts"], writes=[PS(pb)])
                                P.add("pe", lambda e, o=o, pslot=pslot, c=c: e.matmul(
                                    o, lhsT=u_tm[:, g * 3 + pslot, c * 128:(c + 1) * 128],
                                    rhs=poolA[:, g * 3 + 1, :], start=False, stop=True),
                                    reads=[("u", g, pslot), "consts"], writes=[PS(pb)])
                    P.add("act", lambda e, c2=c2, pb=pb: e.activation(
                        out=mT[gs][:, c2 * 2:(c2 + 1) * 2, :],
                        in_=psum[pb][:, :].rearrange("p (a b) -> p a b", a=2), func=AF.Copy),
                        reads=[PS(pb)], writes=[mk])
                    if b == 0:
                        P.add("dve", lambda e, c2=c2, pb=pb: e.tensor_tensor(
                            out=mT[gs][:, c2 * 2:(c2 + 1) * 2, 0:128],
                            in0=psum[pb][:, :].rearrange("p (a b) -> p a b", a=2)[:, :, 0:128],
                            in1=invc[:, g:g + 1, :].broadcast_to([128, 2, 128]), op=ALU.mult),
                            reads=[PS(pb), "consts"], writes=[mk])

            def stage_G(it, g):
                gs = (it * 4 + g) % 2
                szk, mk = ("siluz", gs), ("mT", gs)
                gb = it % 2
                for d2 in range(2):
                    pb = d2
                    for di in range(2):
                        d = d2 * 2 + di
                        for cc in range(4):
                            P.add("pe", lambda e, cc=cc, d=d, di=di, pb=pb: e.matmul(
                                psum[pb][:, di * 256:(di + 1) * 256],
                                lhsT=w_grp[:, g * 4 + cc, d * 128:(d + 1) * 128], rhs=mT[gs][:, cc, :],
                                start=(cc == 0), stop=(cc == 3)), reads=[mk, ("w_grp", g)], writes=[PS(pb)])
                    for di in range(2):
                        d = d2 * 2 + di
                        ch = g * 4 + d
                        P.add("dve", lambda e, d=d, di=di, ch=ch, pb=pb: e.scalar_tensor_tensor(
                            out=gT[gb][:, ch, :], in0=psum[pb][:, di * 256:(di + 1) * 256],
                            scalar=pscale[:, ch:ch + 1], in1=siluz[gs][:, d, :], op0=ALU.mult, op1=ALU.mult),
                            reads=[PS(pb), szk, "consts"], writes=[("gT", gb, ch)])

            def stage_O(it):
                s, b, sl, t0 = geo(it)
                xk = ("xs", sl)
                gb = it % 2
                for j in range(2):
                    rs = (it * 2 + j) % 2
                    rk = ("r", rs)
                    for hh in range(2):
                        pb = 2 + hh
                        for ch in range(16):
                            P.add("pe", lambda e, ch=ch, j=j, hh=hh, pb=pb: e.matmul(
                                psum[pb][:, :], lhsT=gT[gb][:, ch, j * 128:(j + 1) * 128],
                                rhs=w_out0[:, ch, hh * 512:(hh + 1) * 512], start=(ch == 0), stop=(ch == 15)),
                                reads=[("gT", gb, ch), "w_out0"], writes=[PS(pb)])
                        P.add("dve", lambda e, j=j, hh=hh, pb=pb, rs=rs: e.scalar_tensor_tensor(
                            out=rbuf[rs][:, hh * 512:(hh + 1) * 512], in0=xs[sl][:, j, hh * 512:(hh + 1) * 512],
                            scalar=DN_ALPHA, in1=psum[pb][:, :], op0=ALU.mult, op1=ALU.add),
                            reads=[PS(pb), xk], writes=[rk])
                    layer_norm_tile(P, rbuf[rs], rk, lng0, lnb0, stat[rs], ("stat", rs), epsc)
                    P.add("pool", lambda e, rs=rs, j=j: e.dma_start(
                        out=x1_d[s, t0 + j * 128:t0 + (j + 1) * 128, :], in_=rbuf[rs]),
                        reads=[rk], dma_key="st%d" % rs)

            for it in range(NIT):
                stage_T(it)
                stage_U(it, 0)
                if it > 0:
                    stage_G(it - 1, 3)
                stage_Z(it, 0)
                if it > 0:
                    stage_O(it - 1)
                stage_PM(it, 0)
                for g in range(1, 4):
                    stage_U(it, g)
                    stage_Z(it, g)
                    stage_G(it, g - 1)
                    stage_PM(it, g)
            stage_G(NIT - 1, 3)
            stage_O(NIT - 1)

        P.barrier()
        if do_l1:
            AR.reset()
            _build_l1(nc, P, AR, psum, psum2, L1D, x1_d, out_d, lng_d, lnb_d, identf_d, identb_d)
        P.emit(stack)
    return nc


def _host_common(inputs):
    f = lambda a: np.ascontiguousarray(np.asarray(a, dtype=np.float32))
    m = {}
    lng = f(inputs["ln_g"])
    lnb = f(inputs["ln_b"])
    m["lng"] = np.ascontiguousarray(np.broadcast_to(lng[:, None, :], (2, 128, D)))
    m["lnb"] = np.ascontiguousarray(np.broadcast_to(lnb[:, None, :], (2, 128, D)))
    w = f(inputs["pool_w_in"])[0]
    m["w_in0"] = np.ascontiguousarray(w.reshape(8, 128, 4096).transpose(1, 0, 2))
    wg = f(inputs["pool_w_grp"])[0]
    m["w_grp"] = np.ascontiguousarray(wg.reshape(4, 4, 128, 512).transpose(2, 0, 1, 3).reshape(128, 16, 512))
    wo = f(inputs["pool_w_out"])[0]
    m["w_out0"] = np.ascontiguousarray(wo.reshape(16, 128, 1024).transpose(1, 0, 2))
    m["pscale"] = np.ascontiguousarray(f(inputs["pool_scale"])[0].reshape(16, 128).T)
    pa, inv = _pool_tables()
    m["poolA"] = pa
    m["invc"] = inv
    m["identf"] = np.eye(128, dtype=np.float32)
    m["identb"] = _bf(np.eye(128))
    m.update(_host_l1(inputs))
    return m


_NC_CACHE = {}


def kernel(**inputs):
    x = np.ascontiguousarray(np.asarray(inputs["x"], dtype=np.float32))
    common = _host_common(inputs)
    if "nc" not in _NC_CACHE:
        _NC_CACHE["nc"] = build_program()
    nc = _NC_CACHE["nc"]
    in_maps = []
    for c in range(NCORES):
        m = dict(common)
        m["x"] = x[c * NSEQ:(c + 1) * NSEQ]
        in_maps.append(m)
    res = run_bass_kernel_spmd(nc, in_maps, core_ids=list(range(NCORES)))
    out = np.concatenate([np.asarray(r["out"]) for r in res.results], axis=0)
    return out.astype(np.float32)
```

```python
import contextlib
import numpy as np
import ml_dtypes
import concourse.bass as bass
import concourse.mybir as mybir
from concourse.bass_utils import run_bass_kernel_spmd

F32 = mybir.dt.float32
BF16 = mybir.dt.bfloat16
AF = mybir.ActivationFunctionType
ALU = mybir.AluOpType

D = 1024
S = 2048
NSEQ = 2
NCORES = 8
DN_ALPHA = float((2.0 * 2) ** 0.25)
LN_EPS = 1e-5
POOL_WINDOWS = (2, 4, 8, 16)
NEGM = -30000.0


class _Op:
    __slots__ = ("eng", "fn", "deps", "is_dma", "key", "awaited", "count", "idx")


class Prog:
    ENGS = ("pe", "act", "dve", "pool", "sp")

    def __init__(self, nc):
        self.nc = nc
        self.ops = {e: [] for e in self.ENGS}
        self.last_w = {}
        self.readers = {}
        self.dma_counts = {}
        self.last_dma = {}
        self.bar = {e: [] for e in self.ENGS}

    def _dep(self, op, a):
        if a is None or a is op:
            return
        if (not a.is_dma) and a.eng == op.eng and a.eng == "pe":
            return
        if a.is_dma and op.is_dma and a.key == op.key:
            return
        op.deps.append(a)
        if not a.is_dma:
            a.awaited = True

    def add(self, eng, fn, reads=(), writes=(), dma_key=None):
        op = _Op()
        op.eng = eng
        op.fn = fn
        op.deps = []
        op.is_dma = dma_key is not None
        op.key = dma_key
        op.awaited = False
        op.count = None
        for a in self.bar[eng]:
            self._dep(op, a)
        self.bar[eng] = []
        if eng != "pe":
            extra = [("psx", r[1]) for r in reads if isinstance(r, tuple) and r[0] == "ps"]
            extra += [("psx", w[1]) for w in writes if isinstance(w, tuple) and w[0] == "ps"]
            writes = list(writes) + extra
        for r in reads:
            self._dep(op, self.last_w.get(r))
        for w in writes:
            self._dep(op, self.last_w.get(w))
            for a in self.readers.get(w, ()):
                self._dep(op, a)
        for r in reads:
            self.readers.setdefault(r, []).append(op)
        for w in writes:
            self.last_w[w] = op
            self.readers[w] = []
        if op.is_dma:
            c = self.dma_counts.get(dma_key, 0) + 1
            self.dma_counts[dma_key] = c
            op.count = c
            self.last_dma[dma_key] = op
        op.idx = len(self.ops[eng])
        self.ops[eng].append(op)
        return op

    def barrier(self):
        deps = []
        for e in self.ENGS:
            for op in reversed(self.ops[e]):
                if not op.is_dma:
                    deps.append(op)
                    break
        deps.extend(self.last_dma.values())
        for e in self.ENGS:
            self.bar[e] = list(deps)

    def emit(self, stack):
        nc = self.nc
        esem = {e: stack.enter_context(nc.semaphore("s_" + e)) for e in self.ENGS}
        dsem = {k: stack.enter_context(nc.semaphore("d_%d" % i)) for i, k in enumerate(self.dma_counts)}
        for e in self.ENGS:
            c = 0
            for op in self.ops[e]:
                if (not op.is_dma) and op.awaited:
                    c += 1
                    op.count = c
        block = stack.enter_context(nc.Block())
        final = [(dsem[k], 16 * c) for k, c in self.dma_counts.items()]

        def run(ename, eh, is_last=False):
            waited = {}
            for op in self.ops[ename]:
                for a in op.deps:
                    if a.is_dma:
                        sem, val = dsem[a.key], 16 * a.count
                    else:
                        sem, val = esem[a.eng], a.count
                    sid = id(sem)
                    if waited.get(sid, 0) >= val:
                        continue
                    waited[sid] = val
                    eh.wait_ge(sem, val)
                ins = op.fn(eh)
                if op.is_dma:
                    ins.then_inc(dsem[op.key], 16)
                elif op.awaited:
                    ins.then_inc(esem[ename], 1)
            if is_last:
                for sem, val in final:
                    eh.wait_ge(sem, val)

        @block.tensor
        def _(eh):
            run("pe", eh)

        @block.scalar
        def _(eh):
            run("act", eh)

        @block.vector
        def _(eh):
            run("dve", eh)

        @block.gpsimd
        def _(eh):
            run("pool", eh)

        @block.sync
        def _(eh):
            run("sp", eh, is_last=True)


class Arena:
    def __init__(self, ap, ncols):
        self.ap = ap
        self.n = ncols
        self.off = 0

    def reset(self):
        self.off = 0

    def _shape(self, v, shape):
        if len(shape) == 2:
            v = v.rearrange("p (a b) -> p a b", a=shape[0])
        elif len(shape) == 3:
            v = v.rearrange("p (a b c) -> p a b c", a=shape[0], b=shape[1])
        return v

    def f(self, *shape):
        cols = int(np.prod(shape))
        assert self.off + cols <= self.n, ("arena overflow", self.off, cols, self.n)
        v = self.ap[:, self.off:self.off + cols]
        self.off += cols
        return self._shape(v, shape)

    def b(self, *shape):
        cols = int(np.prod(shape))
        c32 = (cols + 1) // 2
        assert self.off + c32 <= self.n, ("arena overflow", self.off, c32, self.n)
        v = self.ap[:, self.off:self.off + c32].bitcast(BF16)[:, 0:cols]
        self.off += c32
        return self._shape(v, shape)


def _bf(a):
    return np.ascontiguousarray(np.asarray(a, dtype=np.float32).astype(ml_dtypes.bfloat16))


def _pool_tables():
    A = np.zeros((4, 3, 128, 128), np.float32)
    inv = np.zeros((4, 128), np.float32)
    for g, w in enumerate(POOL_WINDOWS):
        for t in range(128):
            for tp in range(t - w + 1, t + 1):
                if tp >= 0:
                    A[g, 0, tp, t] += 1.0 / w
                else:
                    A[g, 1, tp + 128, t] += 1.0 / w
            A[g, 0, t, t] -= 1.0
            cnt = min(t + 1, w)
            for tp in range(max(0, t - w + 1), t + 1):
                A[g, 2, tp, t] += 1.0
            A[g, 2, t, t] -= cnt
            inv[g, t] = 1.0 / cnt
    At = np.transpose(A, (2, 0, 1, 3)).reshape(128, 4 * 3 * 128)
    invb = np.broadcast_to(inv.reshape(1, 4 * 128), (128, 4 * 128))
    return _bf(At), np.ascontiguousarray(invb, dtype=np.float32)


def layer_norm_tile(P, r_ap, rkey, lng, lnb, stat, skey, epsc):
    st6 = stat[:, 0:12].rearrange("p (a b) -> p a b", a=2)
    mv = stat[:, 12:14]
    P.add("dve", lambda e: e.bn_stats(out=st6[:, 0, :], in_=r_ap[:, 0:512]), reads=[rkey], writes=[skey])
    P.add("dve", lambda e: e.bn_stats(out=st6[:, 1, :], in_=r_ap[:, 512:1024]), reads=[rkey], writes=[skey])
    P.add("dve", lambda e: e.bn_aggr(out=mv, in_=stat[:, 0:12]), reads=[skey], writes=[skey])
    P.add("act", lambda e: e.activation(out=stat[:, 14:15], in_=stat[:, 13:14], func=AF.Sqrt,
                                        bias=epsc, scale=1.0), reads=[skey, "consts"], writes=[skey])
    P.add("dve", lambda e: e.reciprocal(out=stat[:, 14:15], in_=stat[:, 14:15]), reads=[skey], writes=[skey])
    P.add("dve", lambda e: e.scalar_tensor_tensor(out=stat[:, 15:16], in0=stat[:, 12:13], scalar=-1.0,
                                                  in1=stat[:, 14:15], op0=ALU.mult, op1=ALU.mult),
          reads=[skey], writes=[skey])
    P.add("act", lambda e: e.activation(out=r_ap, in_=r_ap, func=AF.Identity, bias=stat[:, 15:16],
                                        scale=stat[:, 14:15]), reads=[skey, rkey], writes=[rkey])
    P.add("pool", lambda e: e.tensor_tensor(out=r_ap, in0=r_ap, in1=lng, op=ALU.mult),
          reads=[rkey, "lnc"], writes=[rkey])
    P.add("pool", lambda e: e.tensor_tensor(out=r_ap, in0=r_ap, in1=lnb, op=ALU.add),
          reads=[rkey, "lnc"], writes=[rkey])


def _slopes():
    h = np.arange(1, 17, dtype=np.float32)
    return (2.0 ** (-8.0 * h / 16.0)).astype(np.float32)


def _l1_tables():
    t = {}
    sl = _slopes()
    tq = np.arange(S, dtype=np.float64)
    qal = np.zeros((16, 7, S), np.float32)
    for h in range(16):
        s0 = float(sl[h])
        s1 = float(np.float32(s0).astype(ml_dtypes.bfloat16))
        s2 = float(np.float32(s0 - s1).astype(ml_dtypes.bfloat16))
        s3 = float(np.float32(s0 - s1 - s2).astype(ml_dtypes.bfloat16))
        qal[h, 0] = -s0 * tq
        for i, si in enumerate((s1, s2, s3)):
            qal[h, 1 + i] = 64.0 * si
            qal[h, 4 + i] = si
    t["qalibi"] = _bf(qal)
    kal = np.zeros((7, S), np.float32)
    kal[0] = 1.0
    kal[1:4] = (np.arange(S) // 64)[None, :]
    kal[4:7] = (np.arange(S) % 64)[None, :]
    t["kalibi"] = _bf(kal)
    kc = np.zeros((7, 128), np.float32)
    ce = np.arange(127) * 16 + 31
    kc[0] = 1.0
    kc[1:4, :127] = (ce // 64)[None, :]
    kc[4:7, :127] = (ce % 64)[None, :]
    t["kalibic"] = _bf(kc)
    E = np.zeros((32, S), np.float32)
    E[np.arange(S) // 64, np.arange(S)] = 1.0
    t["eoh"] = _bf(E)
    cm = np.full((128, S), NEGM, np.float32)
    cm[:127] = np.where(np.arange(S)[None, :] >= ce[:, None], 0.0, NEGM)
    t["cmpmask"] = _bf(cm)
    dk = np.arange(128)[:, None]
    dq = np.arange(384)[None, :]
    t["winmask"] = _bf(np.where((dq - dk >= 0) & (dq - dk < 256), 0.0, NEGM))
    dq = np.arange(128)[None, :]
    t["trimask"] = _bf(np.where(dk <= dq, 0.0, NEGM))
    vcc = np.zeros((128, 33), np.float32)
    vcc[:, 0] = 1.0
    c0 = np.arange(127)[:, None] * 16
    j0 = np.arange(32)[None, :] * 64
    vcc[:127, 1:] = ((c0 < j0 + 64) & (c0 + 32 > j0)).astype(np.float32)
    t["vcc"] = _bf(vcc)
    q = np.arange(128)[:, None, None]
    qt = np.arange(16)[None, :, None]
    j = np.arange(32)[None, None, :]
    cur = (qt * 128 + q) // 64
    forced = (j == 0) | (j == cur) | (j == cur - 1)
    t["forced"] = np.ascontiguousarray(np.where(forced, 1e9, 0.0).astype(np.float32).reshape(128, 512))
    t["future"] = np.ascontiguousarray(np.where(j > cur, -1e30, 3e38).astype(np.float32).reshape(128, 512))
    return t


def _l1_dram(nc, din):
    L = {}
    L["wg1"] = din("wg1", [4, 128, 8, 640])
    L["wg2"] = din("wg2", [4, 128, 8, 268])
    L["w1k"] = din("w1k", [64, 32, 256])
    L["w1v"] = din("w1v", [64, 32, 256])
    L["w2k"] = din("w2k", [128, 2, 64])
    L["w2v"] = din("w2v", [128, 2, 64])
    L["posk"] = din("posk", [64, 32])
    L["posv"] = din("posv", [64, 32])
    L["wout1"] = din("wout1", [128, 8, 1024])
    L["qalibi"] = din("qalibi", [16, 7, S], BF16)
    L["kalibi"] = din("kalibi", [7, S], BF16)
    L["kalibic"] = din("kalibic", [7, 128], BF16)
    L["eoh"] = din("eoh", [32, S], BF16)
    L["cmpmask"] = din("cmpmask", [128, S], BF16)
    L["winmask"] = din("winmask", [128, 384], BF16)
    L["trimask"] = din("trimask", [128, 128], BF16)
    L["vcc"] = din("vcc", [128, 33], BF16)
    L["forced"] = din("forced", [128, 512])
    L["future"] = din("future", [128, 512])
    return L


def _host_l1(inputs):
    f = lambda a: np.ascontiguousarray(np.asarray(a, dtype=np.float32))
    m = {}
    W = f(inputs["nsa_w_in"])[0].reshape(8, 128, 3632).transpose(1, 0, 2)
    wg1 = np.zeros((4, 128, 8, 640), np.float32)
    wg2 = np.zeros((4, 128, 8, 268), np.float32)
    for g in range(4):
        wg1[g, :, :, 0:256] = W[:, :, 256 * g:256 * g + 256]
        wg1[g, :, :, 256:320] = W[:, :, 1024 + 64 * g:1024 + 64 * g + 64]
        wg1[g, :, :, 320:384] = W[:, :, 1280 + 64 * g:1280 + 64 * g + 64]
        wg1[g, :, :, 384:448] = W[:, :, 1536 + 64 * g:1536 + 64 * g + 64]
        wg1[g, :, :, 448:512] = W[:, :, 2048 + 64 * g:2048 + 64 * g + 64]
        wg1[g, :, :, 512:576] = W[:, :, 1792 + 64 * g:1792 + 64 * g + 64]
        wg1[g, :, :, 576:640] = W[:, :, 2304 + 64 * g:2304 + 64 * g + 64]
        wg2[g, :, :, 0:256] = W[:, :, 2560 + 256 * g:2560 + 256 * g + 256]
        wg2[g, :, :, 256:268] = W[:, :, 3584 + 12 * g:3584 + 12 * g + 12]
    m["wg1"] = wg1
    m["wg2"] = wg2
    m["w1k"] = np.ascontiguousarray(f(inputs["nsa_cmp_w1_k"])[0].reshape(32, 64, 256).transpose(1, 0, 2))
    m["w1v"] = np.ascontiguousarray(f(inputs["nsa_cmp_w1_v"])[0].reshape(32, 64, 256).transpose(1, 0, 2))
    m["w2k"] = np.ascontiguousarray(f(inputs["nsa_cmp_w2_k"])[0].reshape(2, 128, 64).transpose(1, 0, 2))
    m["w2v"] = np.ascontiguousarray(f(inputs["nsa_cmp_w2_v"])[0].reshape(2, 128, 64).transpose(1, 0, 2))
    m["posk"] = np.ascontiguousarray(f(inputs["nsa_cmp_pos_k"])[0].T)
    m["posv"] = np.ascontiguousarray(f(inputs["nsa_cmp_pos_v"])[0].T)
    m["wout1"] = np.ascontiguousarray(f(inputs["nsa_w_out"])[0].reshape(8, 128, 1024).transpose(1, 0, 2))
    m.update(_l1_tables())
    return m


L1_STAGE = 99


def _build_l1(nc, P, AR, psum, psum2, L, x1_d, out_d, lng_d, lnb_d, identf_d, identb_d):
    STG = L1_STAGE
    def PS(i):
        return ("ps", i)

    def MM(out, lhsT, rhs, start, stop, reads, writes, skip=False):
        P.add("pe", lambda e: e.matmul(out, lhsT=lhsT, rhs=rhs, start=start, stop=stop, skip_group_check=skip),
              reads=reads, writes=writes)

    def ACT(out, in_, func, reads, writes, **kw):
        P.add("act", lambda e: e.activation(out=out, in_=in_, func=func, **kw), reads=reads, writes=writes)

    def TT(eng, out, in0, in1, op, reads, writes):
        P.add(eng, lambda e: e.tensor_tensor(out=out, in0=in0, in1=in1, op=op), reads=reads, writes=writes)

    def TS(out, in0, s1, s2, op0, op1, reads, writes):
        if op1 is None:
            P.add("dve", lambda e: e.tensor_scalar(out=out, in0=in0, scalar1=s1, scalar2=None, op0=op0),
                  reads=reads, writes=writes)
        else:
            P.add("dve", lambda e: e.tensor_scalar(out=out, in0=in0, scalar1=s1, scalar2=s2, op0=op0, op1=op1),
                  reads=reads, writes=writes)

    def CP(out, in_, reads, writes, scale=None):
        if scale is None:
            P.add("dve", lambda e: e.tensor_copy(out=out, in_=in_), reads=reads, writes=writes)
        else:
            P.add("dve", lambda e: e.tensor_scalar(out=out, in0=in_, scalar1=scale, scalar2=None, op0=ALU.mult),
                  reads=reads, writes=writes)

    def DMA(q, out, in_, reads, writes, key):
        P.add(q, lambda e: e.dma_start(out=out, in_=in_), reads=reads, writes=writes, dma_key=key)

    W1 = AR.b(8, 640)
    W2 = [AR.b(8, 268) for _ in range(2)]
    w1 = AR.b(32, 256)
    w2 = [AR.b(2, 64) for _ in range(2)]
    pos = AR.b(32)
    wout1 = AR.b(8, 1024)
    identb = AR.b(128)
    x1T = AR.b(8, S)
    ogT = AR.b(8, S)
    q_aug = [AR.b(S) for _ in range(4)]
    ksel = AR.b(S)
    kwin = AR.b(S)
    kcmp = AR.b(128)
    pair = AR.b(S)
    vsel = AR.b(16, 65)
    vwin = AR.b(16, 65)
    vcmp = AR.b(97)
    hs = [AR.b(2, 128) for _ in range(2)]
    cmpmask = AR.b(S)
    winmask = AR.b(384)
    trimask = AR.b(128)
    Pbig = AR.b(2048)
    selm_w = AR.b(4, 128)
    ogb = [AR.b(256) for _ in range(4)]
    identf = AR.f(128)
    lng1 = AR.f(1024)
    lnb1 = AR.f(1024)
    xr = [AR.f(1024) for _ in range(2)]
    xs2 = [AR.f(1024) for _ in range(2)]
    stat = [AR.f(16) for _ in range(2)]
    o_tm = [AR.f(4, 256) for _ in range(2)]
    gates = AR.f(16, 12)
    imp = AR.f(4, 32)
    impm = AR.f(4, 32)
    forced = AR.f(16, 32)
    future = AR.f(16, 32)
    m8 = AR.f(4, 8)
    thr = AR.f(4)
    rden = [AR.f(4) for _ in range(2)]
    fcol = [AR.f(4) for _ in range(2)]
    tmp_o = [AR.f(4, 64) for _ in range(2)]
    tmp_i = AR.f(4, 32)
    sz = [AR.f(256) for _ in range(4)]
    bh = AR.f(4)
    epsc = AR.f(1)

    for kc in range(0, 8, 2):
        DMA("pool", W1[:, kc:kc + 2, :], L["wg1"][0, :, kc:kc + 2, :], [], ["W1"], "W1")
    for kc in range(0, 8, 4):
        DMA("pool", W2[0][:, kc:kc + 4, :], L["wg2"][0, :, kc:kc + 4, :], [], [("W2", 0)], "W20")
    for t_, k_ in ((ksel, "ksel_c"), (kwin, "kwin_c"), (kcmp, "kcmp_c"), (vcmp, "vcmp_c"),
                   (selm_w.rearrange("p a b -> p (a b)"), "selm")):
        P.add("pool", lambda e, t_=t_: e.memset(t_, 0.0), writes=[k_])
    for hl in range(4):
        P.add("pool", lambda e, hl=hl: e.memset(q_aug[hl], 0.0), writes=[("qa_q", hl), ("qa_s", hl), ("qa_a", hl)])
    P.add("pool", lambda e: e.memset(vsel[:, :, 64:65], 1.0), writes=["vsel_c"])
    P.add("pool", lambda e: e.memset(vwin[:, :, 64:65], 1.0), writes=["vwin_c"])
    P.add("dve", lambda e: e.memset(epsc, LN_EPS), writes=["consts"])
    DMA("sp", identb, identb_d, [], ["consts"], "c1")
    DMA("sp", identf, identf_d, [], ["consts"], "c1")
    DMA("sp", lng1, lng_d[1], [], ["lnc"], "c1l")
    DMA("sp", lnb1, lnb_d[1], [], ["lnc"], "c1l")
    DMA("sp", cmpmask, L["cmpmask"], [], ["consts"], "c1")
    DMA("sp", winmask, L["winmask"], [], ["consts"], "c1")
    DMA("sp", trimask, L["trimask"], [], ["consts"], "c1")
    DMA("sp", forced.rearrange("p a b -> p (a b)"), L["forced"], [], ["consts"], "c1")
    DMA("sp", future.rearrange("p a b -> p (a b)"), L["future"], [], ["consts"], "c1")
    DMA("sp", ksel[64:96, :], L["eoh"], [], ["ksel_c"], "c2a")
    DMA("sp", ksel[96:103, :], L["kalibi"], [], ["ksel_c"], "c2a")
    DMA("sp", kwin[96:103, :], L["kalibi"], [], ["kwin_c"], "c2b")
    DMA("sp", kcmp[96:103, :], L["kalibic"], [], ["kcmp_c"], "c2c")
    DMA("sp", vcmp[:, 64:97], L["vcc"], [], ["vcmp_c"], "c2d")
    for kv, (wn, w2n, pn) in enumerate((("w1k", "w2k", "posk"), ("w1v", "w2v", "posv"))):
        for i in range(0, 32, 8):
            DMA("pool", w1[64 * kv:64 * kv + 64, i:i + 8, :], L[wn][:, i:i + 8, :], [], ["wcmp"], "wcmp")
        DMA("pool", w2[kv], L[w2n], [], ["wcmp"], "wcmp")
        DMA("pool", pos[64 * kv:64 * kv + 64, :], L[pn], [], ["wcmp"], "wcmp")
    for c in range(8):
        DMA("pool", wout1[:, c, :], L["wout1"][:, c, :], [], ["wout1"], "wout1")
    for kv in range(2):
        for hc in range(2):
            col = kv * 2 + hc
            for i in range(32):
                MM(psum[0][:, col:col + 1], w1[64 * kv:64 * kv + 64, i, hc * 128:(hc + 1) * 128],
                   pos[64 * kv:64 * kv + 64, i:i + 1], i == 0, i == 31, ["wcmp"], [PS(0)])
    ACT(bh, psum[0][:, 0:4], AF.Copy, [PS(0)], ["bh"])

    if STG < 1:
        return
    rr = {"s": 0, "o": 0, "p": 0, "a": 0, "e": 0, "ring": 0}

    def nxt(k, n):
        v = rr[k]
        rr[k] = (v + 1) % n
        return v

    def x1T_tile(s, j):
        if s == 0:
            xs_ = j % 4
            stg = (xs2[0], xs2[1], xr[0], xr[1])[xs_]
            xk = (("xs2", 0), ("xs2", 1), ("xr", 0), ("xr", 1))[xs_]
            dk = ("xs20", "xs21", "xr0", "xr1")[xs_]
        else:
            xs_ = j % 2
            stg = xs2[xs_]
            xk = ("xs2", xs_)
            dk = "xs2%d" % xs_
        DMA("sp", stg, x1_d[s, j * 128:(j + 1) * 128, :], [], [xk], dk)
        for q4 in range(2):
            pb = nxt("a", 2)
            for i4 in range(4):
                kc = q4 * 4 + i4
                P.add("pe", lambda e, kc=kc, i4=i4, pb=pb, stg=stg: e.transpose(
                    out=psum[pb][:, i4 * 128:(i4 + 1) * 128], in_=stg[:, kc * 128:(kc + 1) * 128],
                    identity=identf), reads=[xk, "consts"], writes=[PS(pb)])
            CP(x1T[:, q4 * 4:(q4 + 1) * 4, j * 128:(j + 1) * 128],
               psum[pb][:, :].rearrange("p (a b) -> p a b", a=4), [PS(pb)], ["x1T"])

    xr.append(o_tm[0].rearrange("p a b -> p (a b)"))
    xr.append(o_tm[1].rearrange("p a b -> p (a b)"))
    stat.append(AR.f(16))
    stat.append(AR.f(16))

    def outproj_tile(s, jt):
        xs_ = jt % 4
        xk = ("xr", xs_)
        alias = [("o_tm", xs_ - 2, h_) for h_ in range(4)] if xs_ >= 2 else []
        DMA("sp", xr[xs_], x1_d[s, jt * 128:(jt + 1) * 128, :], [], [xk] + alias, "xr%d" % xs_)
        for hh in range(2):
            pb = nxt("a", 2)
            for c in range(8):
                MM(psum[pb][:, :], ogT[:, c, jt * 128:(jt + 1) * 128], wout1[:, c, hh * 512:(hh + 1) * 512],
                   c == 0, c == 7, [("ogT", c // 2), "wout1"], [PS(pb)])
            P.add("dve", lambda e, hh=hh, pb=pb, xs_=xs_: e.scalar_tensor_tensor(
                out=xr[xs_][:, hh * 512:(hh + 1) * 512], in0=xr[xs_][:, hh * 512:(hh + 1) * 512],
                scalar=DN_ALPHA, in1=psum[pb][:, :], op0=ALU.mult, op1=ALU.add), reads=[PS(pb), xk], writes=[xk])
        layer_norm_tile(P, xr[xs_], xk, lng1, lnb1, stat[xs_], ("stat1", xs_), epsc)
        DMA("pool", out_d[s, jt * 128:(jt + 1) * 128, :], xr[xs_], [xk] + alias, [], "ost%d" % xs_)

    def load_group_weights(gi_):
        g_ = gi_ % 4
        for kc in range(0, 8, 2):
            DMA("pool", W1[:, kc:kc + 2, :], L["wg1"][g_, :, kc:kc + 2, :], [], ["W1"], "W1")
        for kc in range(0, 8, 4):
            DMA("pool", W2[gi_ % 2][:, kc:kc + 4, :], L["wg2"][g_, :, kc:kc + 4, :], [], [("W2", gi_ % 2)], "W2%d" % (gi_ % 2))

    pending = []

    def drain_pending(n):
        for _ in range(n):
            if pending:
                s_, jt_ = pending.pop(0)
                outproj_tile(s_, jt_)

    for s in range(NSEQ):
        if s == 0:
            for j in range(16):
                x1T_tile(0, j)
        if STG < 2:
            return
        for g in range(4):
            gi = s * 4 + g
            w2s = gi % 2
            w2k_ = ("W2", w2s)
            for hl in range(4):
                DMA("sp", q_aug[hl][96:103, :], L["qalibi"][4 * g + hl], [], [("qa_a", hl)], "qal%d" % hl)
            for pc in range(2):
                for tb in range(4):
                    pb = nxt("a", 2)
                    for kc in range(8):
                        MM(psum[pb][:, :], W1[:, kc, pc * 128:(pc + 1) * 128], x1T[:, kc, tb * 512:(tb + 1) * 512],
                           kc == 0, kc == 7, ["W1", "x1T"], [PS(pb)])
                    CP(q_aug[2 * pc][0:64, tb * 512:(tb + 1) * 512], psum[pb][0:64, :],
                       [PS(pb)], [("qa_q", 2 * pc)], scale=0.125)
                    CP(q_aug[2 * pc + 1][0:64, tb * 512:(tb + 1) * 512], psum[pb][64:128, :],
                       [PS(pb)], [("qa_q", 2 * pc + 1)], scale=0.125)
                    drain_pending(1)
            if STG < 3:
                return
            for tb in range(4):
                c0, c1 = tb * 512, (tb + 1) * 512
                pb = nxt("a", 2)
                for kc in range(8):
                    MM(psum[pb][:, :], W1[:, kc, 256:384], x1T[:, kc, c0:c1], kc == 0, kc == 7, ["W1", "x1T"], [PS(pb)])
                CP(pair.rearrange("p (ph c) -> p ph c", ph=16)[:, :, tb * 32:(tb + 1) * 32],
                   psum[pb][:, :].rearrange("p (c ph) -> p ph c", ph=16), [PS(pb)], ["pair"])
                pb = nxt("a", 2)
                for kc in range(8):
                    MM(psum[pb][:, :], W1[:, kc, 384:512], x1T[:, kc, c0:c1], kc == 0, kc == 7, ["W1", "x1T"], [PS(pb)])
                P.add("dve", lambda e, pb=pb, c0=c0, c1=c1: e.tensor_copy(out=ksel[0:64, c0:c1], in_=psum[pb][0:64, :]),
                      reads=[PS(pb)], writes=["ksel"])
                CP(kwin[0:64, c0:c1], psum[pb][64:128, :], [PS(pb)], ["kwin"])
            if STG < 4:
                return
            for j4 in range(4):
                pb = nxt("a", 2)
                for jj in range(4):
                    j = j4 * 4 + jj
                    for kc in range(8):
                        MM(psum[pb][:, jj * 128:(jj + 1) * 128], x1T[:, kc, j * 128:(j + 1) * 128], W1[:, kc, 512:640],
                           kc == 0, kc == 7, ["W1", "x1T"], [PS(pb)])
                pv = psum[pb][:, :].rearrange("p (a b) -> p a b", a=4)
                P.add("dve", lambda e, pv=pv, j4=j4: e.tensor_copy(out=vsel[:, j4 * 4:(j4 + 1) * 4, 0:64], in_=pv[:, :, 0:64]),
                      reads=[PS(pb)], writes=["vsel"])
                ACT(vwin[:, j4 * 4:(j4 + 1) * 4, 0:64], pv[:, :, 64:128], AF.Copy, [PS(pb)], ["vwin"])
            if STG < 5:
                return
            if gi + 1 < NSEQ * 4:
                load_group_weights(gi + 1)
            pb = nxt("a", 2)
            for j in range(16):
                for kc in range(8):
                    MM(psum[pb][:, j * 12:(j + 1) * 12], x1T[:, kc, j * 128:(j + 1) * 128], W2[w2s][:, kc, 256:268],
                       kc == 0, kc == 7, [w2k_, "x1T"], [PS(pb)])
            ACT(gates.rearrange("p a b -> p (a b)"), psum[pb][:, 0:192], AF.Sigmoid, [PS(pb)], ["gates"])
            if STG < 6:
                return
            for kv in range(2):
                for hc in range(2):
                    pb = nxt("a", 2)
                    for i in range(32):
                        MM(psum[pb][:, 0:127], w1[64 * kv:64 * kv + 64, i, hc * 128:(hc + 1) * 128],
                           pair[64 * kv:64 * kv + 64, (i % 16) * 128 + i // 16:(i % 16) * 128 + i // 16 + 127],
                           i == 0, i == 31, ["wcmp", "pair"], [PS(pb)])
                    ACT(hs[kv][:, hc, 0:127], psum[pb][:, 0:127], AF.Silu, [PS(pb), "bh"], [("hs", kv)],
                        bias=bh[:, kv * 2 + hc:kv * 2 + hc + 1])
            pb = nxt("a", 2)
            for hc in range(2):
                MM(psum[pb][0:64, 0:127], w2[0][:, hc, :], hs[0][:, hc, 0:127], hc == 0, hc == 1,
                   ["wcmp", ("hs", 0)], [PS(pb)])
            CP(kcmp[0:64, 0:127], psum[pb][0:64, 0:127], [PS(pb)], ["kcmp"])
            pb = nxt("a", 2)
            for hc in range(2):
                MM(psum[pb][0:127, 0:64], hs[1][:, hc, 0:127], w2[1][:, hc, :], hc == 0, hc == 1,
                   ["wcmp", ("hs", 1)], [PS(pb)])
            CP(vcmp[0:127, 0:64], psum[pb][0:127, 0:64], [PS(pb)], ["vcmp"])

            if STG < 7:
                return
            def branch_epilogue(hl, br, ob, Wd, qt, os_, first, with_imp):
                O = psum[ob][:, 0:4 * Wd].rearrange("p (a b) -> p a b", a=4)
                e_ = nxt("e", 2)
                rk, fk, tk = ("rden", e_), ("fcol", e_), ("tmp_o", e_)
                if br == 0:
                    TS(rden[e_].unsqueeze(2), O[:, :, 64:65], 1e-30, None, ALU.max, None, [PS(ob)], [rk])
                    P.add("dve", lambda e: e.reciprocal(out=rden[e_], in_=rden[e_]), reads=[rk], writes=[rk])
                else:
                    P.add("dve", lambda e: e.reciprocal(out=rden[e_].unsqueeze(2), in_=O[:, :, 64:65]),
                          reads=[PS(ob)], writes=[rk])
                if with_imp:
                    rb = rden[e_].unsqueeze(2).broadcast_to([128, 4, 32])
                    if hl == 0:
                        TT("dve", imp, O[:, :, 65:97], rb, ALU.mult, [PS(ob), rk], ["imp"])
                    else:
                        TT("dve", tmp_i, O[:, :, 65:97], rb, ALU.mult, [PS(ob), rk], ["tmp_i"])
                        TT("pool", imp, imp, tmp_i, ALU.add, ["tmp_i", "imp"], ["imp"])
                gcol = gates[:, qt * 4:(qt + 1) * 4, hl * 3 + br]
                TT("dve", fcol[e_], rden[e_], gcol, ALU.mult, [rk, "gates"], [fk])
                fb = fcol[e_].unsqueeze(2).broadcast_to([128, 4, 64])
                odst = o_tm[os_][:, :, hl * 64:(hl + 1) * 64]
                ok = ("o_tm", os_, hl)
                if first:
                    TT("dve", odst, O[:, :, 0:64], fb, ALU.mult, [PS(ob), fk], [ok])
                else:
                    TT("dve", tmp_o[e_], O[:, :, 0:64], fb, ALU.mult, [PS(ob), fk], [tk])
                    TT("pool", odst, odst, tmp_o[e_], ALU.add, [tk, ok], [ok])

            SBK = [2, 3, 0, 1]

            def sel_chain(qt):
                TT("dve", impm, imp, forced[:, qt * 4:(qt + 1) * 4, :], ALU.max, ["imp", "consts"], ["impm"])
                TT("dve", impm, impm, future[:, qt * 4:(qt + 1) * 4, :], ALU.min, ["impm", "consts"], ["impm"])
                for qs in range(4):
                    P.add("dve", lambda e, qs=qs: e.max(out=m8[:, qs, :], in_=impm[:, qs, :]), reads=["impm"], writes=["m8"])
                TS(thr.unsqueeze(2), m8[:, :, 7:8], 0.0, None, ALU.max, None, ["m8"], ["thr"])
                for qs in range(4):
                    TS(selm_w[:, qs, 64:96], impm[:, qs, :], thr[:, qs:qs + 1], 1.0, ALU.is_ge, ALU.subtract,
                       ["impm", "thr"], ["selm"])

            def sel_transposes(Q0):
                for qs in range(4):
                    MM(psum[6][:, qs * 128:(qs + 1) * 128], selm_w[:, qs, :], identb, True, True, ["selm", "consts"], [PS(6)])
                for hl in range(4):
                    CP(q_aug[hl][64:96, Q0:Q0 + 512], psum[6][64:96, :], [PS(6)], [("qa_s", hl)], scale=30000.0)

            def gate_path(qt, os_):
                p6b = psum[6][:, :].bitcast(BF16)
                for qs in range(4):
                    TT("dve", ogb[qs], o_tm[os_][:, qs, :], sz[qs], ALU.mult,
                       [("sz", qs)] + [("o_tm", os_, h_) for h_ in range(4)], [("ogb", qs)])
                for qs in range(4):
                    for cc in range(2):
                        P.add("pe", lambda e, cc=cc, qs=qs: e.transpose(
                            out=p6b[:, qs * 256 + cc * 128:qs * 256 + (cc + 1) * 128],
                            in_=ogb[qs][:, cc * 128:(cc + 1) * 128], identity=identb),
                            reads=[("ogb", qs), "consts"], writes=[PS(6)])
                CP(ogT[:, 2 * g:2 * g + 2, qt * 512:(qt + 1) * 512].rearrange("p c (q t) -> p c q t", q=4),
                   p6b.rearrange("p (q c t) -> p c q t", q=4, c=2), [PS(6)], [("ogT", g)])

            def zproj_tiles(qt):
                for qs in range(4):
                    jt = 4 * qt + qs

                    def qk_f(sb, jt=jt):
                        for kc in range(8):
                            MM(psum[sb][:, 0:256], x1T[:, kc, jt * 128:(jt + 1) * 128], W2[w2s][:, kc, 0:256],
                               kc == 0, kc == 7, [w2k_, "x1T"], [PS(sb)])

                    def act_f(sb, qs=qs):
                        ACT(sz[qs], psum[sb][:, 0:256], AF.Silu, [PS(sb)], [("sz", qs)])

                    mk_tile(qk_f, 0, 256, (lambda pr: None), None)
                    units[-1]["tiles"][0]["act"] = act_f

            units = []

            def mk_tile(qk, c0, c1, pv, post=None, pairable=False):
                t = {"qk": qk, "c0": c0, "c1": c1, "pv": pv, "post": post}
                if pairable and units and units[-1]["pair_open"]:
                    units[-1]["tiles"].append(t)
                    units[-1]["pair_open"] = False
                else:
                    units.append({"tiles": [t], "pair_open": pairable, "marker": None})

            def close_pair():
                if units:
                    units[-1]["pair_open"] = False

            def mk_marker(fn):
                close_pair()
                units.append({"tiles": [], "pair_open": False, "marker": fn})

            for qt in range(4):
                Q0 = qt * 512
                os_ = (gi * 4 + qt) % 2
                for hl in range(4):
                    ob = (4, 5, 7)[nxt("o", 3)]
                    qk = [("qa_q", hl), ("qa_s", hl), ("qa_a", hl)]

                    def qk_f(sb, hl=hl, Q0=Q0, qk=qk):
                        MM(psum[sb][:, :], kcmp[0:103, :], q_aug[hl][0:103, Q0:Q0 + 512], True, False,
                           ["kcmp", "kcmp_c"] + qk, [PS(sb)])
                        MM(psum[sb][:, :], identb, cmpmask[:, Q0:Q0 + 512], False, True, ["consts"], [PS(sb)])

                    def pv_f(pr, ob=ob):
                        for qs in range(4):
                            MM(psum[ob][:, qs * 97:(qs + 1) * 97], Pbig[:, pr * 512 + qs * 128:pr * 512 + (qs + 1) * 128],
                               vcmp[:, 0:97], True, True, [("Pb", pr), "vcmp", "vcmp_c"], [PS(ob)])

                    def post_f(hl=hl, ob=ob, qt=qt, os_=os_):
                        branch_epilogue(hl, 0, ob, 97, qt, os_, True, True)
                        if hl == 3:
                            sel_chain(qt)

                    mk_tile(qk_f, 0, 512, pv_f, post_f, pairable=False)
                close_pair()
                for hl in range(4):
                    qk = [("qa_q", hl), ("qa_s", hl), ("qa_a", hl)]
                    ob = (4, 5, 7)[nxt("o", 3)]
                    plan = []
                    for kt in range(max(0, 4 * qt - 2), 4 * qt + 4):
                        K0 = kt * 128
                        lo, hi = max(K0, Q0), min(K0 + 384, Q0 + 512)
                        if hi > lo:
                            plan.append((kt, K0, lo, hi))
                    lastk = {}
                    for (kt, K0, lo, hi) in plan:
                        for qs in range((lo - Q0) // 128, (hi - Q0) // 128):
                            lastk[qs] = kt
                    firstmm = True
                    for ti, (kt, K0, lo, hi) in enumerate(plan):
                        c0, c1, m0 = lo - Q0, hi - Q0, lo - K0

                        def qk_f(sb, hl=hl, qk=qk, K0=K0, lo=lo, hi=hi, c0=c0, c1=c1, m0=m0):
                            MM(psum[sb][:, c0:c1], kwin[0:103, K0:K0 + 128], q_aug[hl][0:103, lo:hi], True, False,
                               ["kwin", "kwin_c"] + qk, [PS(sb)])
                            MM(psum[sb][:, c0:c1], identb, winmask[:, m0:m0 + (hi - lo)], False, True, ["consts"], [PS(sb)])

                        pvl = []
                        for qs in range(c0 // 128, c1 // 128):
                            pvl.append((qs, firstmm, lastk[qs] == kt))
                            firstmm = False

                        def pv_f(pr, ob=ob, kt=kt, pvl=pvl):
                            for (qs, st_, sp_) in pvl:
                                MM(psum[ob][:, qs * 65:(qs + 1) * 65], Pbig[:, pr * 512 + qs * 128:pr * 512 + (qs + 1) * 128],
                                   vwin[:, kt, :], st_, sp_, [("Pb", pr), "vwin", "vwin_c"], [PS(ob)], skip=True)

                        post_f = None
                        if ti == len(plan) - 1:
                            def post_f(hl=hl, ob=ob, qt=qt, os_=os_):
                                branch_epilogue(hl, 2, ob, 65, qt, os_, False, False)
                        mk_tile(qk_f, c0, c1, pv_f, post_f)
                mk_marker(lambda Q0=Q0: sel_transposes(Q0))
                zproj_tiles(qt)
                for hl in range(4):
                    qk = [("qa_q", hl), ("qa_s", hl), ("qa_a", hl)]
                    ob = (4, 5, 7)[nxt("o", 3)]
                    nk = 4 * qt + 4
                    firstmm = True
                    for kt in range(nk):
                        K0 = kt * 128
                        dq_ = kt - 4 * qt
                        c0 = max(dq_, 0) * 128

                        def qk_f(sb, hl=hl, qk=qk, K0=K0, Q0=Q0, c0=c0, dq_=dq_):
                            MM(psum[sb][:, c0:512], ksel[0:103, K0:K0 + 128], q_aug[hl][0:103, Q0 + c0:Q0 + 512], True, dq_ < 0,
                               ["ksel", "ksel_c"] + qk, [PS(sb)])
                            if dq_ >= 0:
                                MM(psum[sb][:, c0:c0 + 128], identb, trimask, False, True, ["consts"], [PS(sb)])

                        pvl = []
                        for qs in range(c0 // 128, 4):
                            pvl.append((qs, firstmm, kt == 4 * qt + qs))
                            firstmm = False

                        def pv_f(pr, ob=ob, kt=kt, pvl=pvl):
                            for (qs, st_, sp_) in pvl:
                                MM(psum[ob][:, qs * 65:(qs + 1) * 65], Pbig[:, pr * 512 + qs * 128:pr * 512 + (qs + 1) * 128],
                                   vsel[:, kt, :], st_, sp_, [("Pb", pr), "vsel", "vsel_c"], [PS(ob)], skip=True)

                        post_f = None
                        if kt == nk - 1:
                            def post_f(hl=hl, ob=ob, qt=qt, os_=os_):
                                branch_epilogue(hl, 1, ob, 65, qt, os_, False, False)
                                if hl == 3:
                                    gate_path(qt, os_)
                        mk_tile(qk_f, c0, 512, pv_f, post_f, pairable=False)
                    close_pair()

            inflight = []

            def flush_one():
                u, need, pos = inflight.pop(0)
                for t, p in zip(u["tiles"], pos):
                    t["pv"](p)
                    if t["post"] is not None:
                        t["post"]()

            for u in units:
                if u["marker"] is not None:
                    u["marker"]()
                    continue
                nb = len(u["tiles"])
                skip = 1 if (nb == 2 and rr["ring"] % 2 == 1) else 0
                need = nb + skip
                while inflight and (sum(x[1] for x in inflight) + need > 4 or len(inflight) >= 3):
                    flush_one()
                rr["ring"] = (rr["ring"] + skip) % 4
                pos = [(rr["ring"] + k) % 4 for k in range(nb)]
                rr["ring"] = (rr["ring"] + nb) % 4
                for t, p in zip(u["tiles"], pos):
                    t["qk"](SBK[p])
                if nb == 2:
                    d = SBK[pos[0]] // 2
                    ACT(Pbig[:, pos[0] * 512:(pos[0] + 2) * 512], psum2[d][:, :], AF.Exp,
                        [PS(SBK[pos[0]]), PS(SBK[pos[1]])], [("Pb", pos[0]), ("Pb", pos[1])])
                else:
                    t = u["tiles"][0]
                    p = pos[0]
                    if t.get("act") is not None:
                        t["act"](SBK[p])
                    else:
                        ACT(Pbig[:, p * 512 + t["c0"]:p * 512 + t["c1"]], psum[SBK[p]][:, t["c0"]:t["c1"]], AF.Exp,
                            [PS(SBK[p])], [("Pb", p)])
                inflight.append((u, need, pos))
            while inflight:
                flush_one()

        for jt in range(16):
            pending.append((s, jt))
        if s + 1 < NSEQ:
            for jt in range(16):
                x1T_tile(s + 1, jt)
                if jt % 2 == 1:
                    drain_pending(1)
        else:
            drain_pending(16)


def build_program(do_l0=True, do_l1=True):
    nc = bass.Bass("TRN2", target_bir_lowering=False)
    dt = {}

    def din(name, shape, dtype=F32):
        dt[name] = nc.dram_tensor(name, list(shape), dtype, kind="ExternalInput").ap()
        return dt[name]

    x_d = din("x", [NSEQ, S, D])
    lng_d = din("lng", [2, 128, D])
    lnb_d = din("lnb", [2, 128, D])
    w_in0_d = din("w_in0", [128, 8, 4096])
    w_grp_d = din("w_grp", [128, 16, 512])
    w_out0_d = din("w_out0", [128, 16, 1024])
    scale_d = din("pscale", [128, 16])
    poolA_d = din("poolA", [128, 4 * 3 * 128], BF16)
    invc_d = din("invc", [128, 4 * 128])
    identf_d = din("identf", [128, 128])
    identb_d = din("identb", [128, 128], BF16)
    x1kind = "Internal" if (do_l0 and do_l1) else ("ExternalOutput" if do_l0 else "ExternalInput")
    x1_d = nc.dram_tensor("x1s", [NSEQ, S, D], F32, kind=x1kind).ap()
    out_d = nc.dram_tensor("out", [NSEQ, S, D], F32, kind="ExternalOutput").ap()
    L1D = _l1_dram(nc, din)

    stack = contextlib.ExitStack()
    with stack:
        ACOLS = 52800
        ar_t = stack.enter_context(nc.sbuf_tensor("arena", [128, ACOLS], F32))
        AR = Arena(ar_t[:], ACOLS)
        psum2 = [stack.enter_context(nc.psum_tensor("ps%d" % i, [128, 1024], F32)) for i in range(4)]
        psum = [psum2[i // 2][:, (i % 2) * 512:(i % 2 + 1) * 512] for i in range(8)]
        P = Prog(nc)

        def PS(i):
            return ("ps", i)

        if do_l0:
            AR.reset()
            w_in0 = AR.b(8, 4096)
            w_grp = AR.b(16, 512)
            w_out0 = AR.b(16, 1024)
            poolA = AR.b(12, 128)
            identb = AR.b(128)
            xT0 = [AR.b(8, 256) for _ in range(2)]
            u_tm = AR.b(12, 512)
            mT = [AR.b(4, 256) for _ in range(2)]
            gT = [AR.b(16, 256) for _ in range(2)]
            identf = AR.f(128)
            lng0 = AR.f(1024)
            lnb0 = AR.f(1024)
            invc = AR.f(4, 128)
            pscale = AR.f(16)
            epsc = AR.f(1)
            xs = [AR.f(2, 1024) for _ in range(2)]
            siluz = [AR.f(4, 256) for _ in range(2)]
            rbuf = [AR.f(1024) for _ in range(2)]
            stat = [AR.f(16) for _ in range(2)]

            P.add("sp", lambda e: e.dma_start(out=identf, in_=identf_d), writes=["consts"], dma_key="c0")
            P.add("sp", lambda e: e.dma_start(out=identb, in_=identb_d), writes=["consts"], dma_key="c0")
            P.add("sp", lambda e: e.dma_start(out=poolA.rearrange("p a b -> p (a b)"), in_=poolA_d), writes=["consts"], dma_key="c0")
            P.add("sp", lambda e: e.dma_start(out=invc.rearrange("p a b -> p (a b)"), in_=invc_d), writes=["consts"], dma_key="c0")
            P.add("sp", lambda e: e.dma_start(out=pscale, in_=scale_d), writes=["consts"], dma_key="c0")
            P.add("sp", lambda e: e.dma_start(out=lng0, in_=lng_d[0]), writes=["lnc"], dma_key="c0l")
            P.add("sp", lambda e: e.dma_start(out=lnb0, in_=lnb_d[0]), writes=["lnc"], dma_key="c0l")
            P.add("dve", lambda e: e.memset(epsc, LN_EPS), writes=["consts"])
            for gq in range(4):
                for cb in (gq, 4 + gq):
                    P.add("pool", lambda e, cb=cb: e.dma_start(out=w_in0[:, :, cb * 512:(cb + 1) * 512],
                                                             in_=w_in0_d[:, :, cb * 512:(cb + 1) * 512]),
                          writes=[("w_in0", cb)], dma_key="w_in0_%d" % cb)
                P.add("pool", lambda e, gq=gq: e.dma_start(out=w_grp[:, gq * 4:(gq + 1) * 4, :],
                                                         in_=w_grp_d[:, gq * 4:(gq + 1) * 4, :]),
                      writes=[("w_grp", gq)], dma_key="w_grp_%d" % gq)
            for c in range(0, 16, 2):
                P.add("pool", lambda e, c=c: e.dma_start(out=w_out0[:, c:c + 2, :], in_=w_out0_d[:, c:c + 2, :]),
                      writes=["w_out0"], dma_key="w_out0")

            NB = S // 256
            NIT = NSEQ * NB

            def geo(it):
                s, b = divmod(it, NB)
                return s, b, it % 2, b * 256

            def stage_T(it):
                s, b, sl, t0 = geo(it)
                xk, xtk = ("xs", sl), ("xT0", sl)
                for j in range(2):
                    P.add("sp", lambda e, j=j: e.dma_start(
                        out=xs[sl][:, j, :], in_=x_d[s, t0 + j * 128:t0 + (j + 1) * 128, :]),
                        writes=[xk], dma_key="xs%d" % sl)
                for j in range(2):
                    for q4 in range(2):
                        pb = (j * 2 + q4) % 2
                        for i4 in range(4):
                            kc = q4 * 4 + i4
                            P.add("pe", lambda e, j=j, kc=kc, i4=i4, pb=pb: e.transpose(
                                out=psum[pb][:, i4 * 128:(i4 + 1) * 128], in_=xs[sl][:, j, kc * 128:(kc + 1) * 128],
                                identity=identf), reads=[xk, "consts"], writes=[PS(pb)])
                        P.add("dve", lambda e, j=j, q4=q4, pb=pb: e.tensor_copy(
                            out=xT0[sl][:, q4 * 4:(q4 + 1) * 4, j * 128:(j + 1) * 128],
                            in_=psum[pb][:, :].rearrange("p (a b) -> p a b", a=4)),
                            reads=[PS(pb)], writes=[xtk])

            def stage_U(it, g):
                s, b, sl, t0 = geo(it)
                xtk = ("xT0", sl)
                for j in range(2):
                    n = b * 2 + j
                    slot = n % 3
                    pb = 2 + (j % 2)
                    for kc in range(8):
                        P.add("pe", lambda e, kc=kc, j=j, pb=pb: e.matmul(
                            psum[pb][:, :], lhsT=xT0[sl][:, kc, j * 128:(j + 1) * 128],
                            rhs=w_in0[:, kc, g * 512:(g + 1) * 512], start=(kc == 0), stop=(kc == 7)),
                            reads=[xtk, ("w_in0", g)], writes=[PS(pb)])
                    P.add("act", lambda e, slot=slot, pb=pb: e.activation(
                        out=u_tm[:, g * 3 + slot, :], in_=psum[pb][:, :], func=AF.Copy),
                        reads=[PS(pb)], writes=[("u", g, slot)])

            def stage_Z(it, g):
                s, b, sl, t0 = geo(it)
                xtk = ("xT0", sl)
                gs = (it * 4 + g) % 2
                szk = ("siluz", gs)
                for c2 in range(2):
                    pb = 4 + c2
                    for ci in range(2):
                        c = c2 * 2 + ci
                        col = 2048 + g * 512 + c * 128
                        for kc in range(8):
                            P.add("pe", lambda e, kc=kc, col=col, ci=ci, pb=pb: e.matmul(
                                psum[pb][:, ci * 256:(ci + 1) * 256], lhsT=w_in0[:, kc, col:col + 128],
                                rhs=xT0[sl][:, kc, :], start=(kc == 0), stop=(kc == 7)),
                                reads=[xtk, ("w_in0", 4 + g)], writes=[PS(pb)])
                    P.add("act", lambda e, c2=c2, pb=pb: e.activation(
                        out=siluz[gs][:, c2 * 2:(c2 + 1) * 2, :],
                        in_=psum[pb][:, :].rearrange("p (a b) -> p a b", a=2), func=AF.Silu),
                        reads=[PS(pb)], writes=[szk])

            def stage_PM(it, g):
                s, b, sl, t0 = geo(it)
                gs = (it * 4 + g) % 2
                mk = ("mT", gs)
                for c2 in range(2):
                    pb = 6 + c2
                    for ci in range(2):
                        c = c2 * 2 + ci
                        for j in range(2):
                            n = b * 2 + j
                            slot = n % 3
                            pslot = (n - 1) % 3
                            o = psum[pb][:, ci * 256 + j * 128: ci * 256 + (j + 1) * 128]
                            if n == 0:
                                P.add("pe", lambda e, o=o, slot=slot, c=c: e.matmul(
                                    o, lhsT=u_tm[:, g * 3 + slot, c * 128:(c + 1) * 128],
                                    rhs=poolA[:, g * 3 + 2, :], start=True, stop=True),
                                    reads=[("u", g, slot), "consts"], writes=[PS(pb)])
                            else:
                                P.add("pe", lambda e, o=o, slot=slot, c=c: e.matmul(
                                    o, lhsT=u_tm[:, g * 3 + slot, c * 128:(c + 1) * 128],
                                    rhs=poolA[:, g * 3 + 0, :], start=True, stop=False),
                                    reads=[("u", g, slot), "consts"], writes=[PS(pb)])
                                P.add("pe", lambda e, o=o, pslot=pslot, c=c: e.matmul(
                                    o, lhsT=u_tm[:, g * 3 + pslot, c * 128:(c + 1) * 128],
                                    rhs=poolA[:, g * 3 + 1, :], start=False, stop=True),
                                    reads=[("u", g, pslot), "consts"], writes=[PS(pb)])
                    P.add("act", lambda e, c2=c2, pb=pb: e.activation(
                        out=mT[gs][:, c2 * 2:(c2 + 1) * 2, :],
                        in_=psum[pb][:, :].rearrange("p (a b) -> p a b", a=2), func=AF.Copy),
                        reads=[PS(pb)], writes=[mk])
                    if b == 0:
                        P.add("dve", lambda e, c2=c2, pb=pb: e.tensor_tensor(
                            out=mT[gs][:, c2 * 2:(c2 + 1) * 2, 0:128],
                            in0=psum[pb][:, :].rearrange("p (a b) -> p a b", a=2)[:, :, 0:128],
                            in1=invc[:, g:g + 1, :].broadcast_to([128, 2, 128]), op=ALU.mult),
                            reads=[PS(pb), "consts"], writes=[mk])

            def stage_G(it, g):
                gs = (it * 4 + g) % 2
                szk, mk = ("siluz", gs), ("mT", gs)
                gb = it % 2
                for d2 in range(2):
                    pb = d2
                    for di in range(2):
                        d = d2 * 2 + di
                        for cc in range(4):
                            P.add("pe", lambda e, cc=cc, d=d, di=di, pb=pb: e.matmul(
                                psum[pb][:, di * 256:(di + 1) * 256],
                                lhsT=w_grp[:, g * 4 + cc, d * 128:(d + 1) * 128], rhs=mT[gs][:, cc, :],
                                start=(cc == 0), stop=(cc == 3)), reads=[mk, ("w_grp", g)], writes=[PS(pb)])
                    for di in range(2):
                        d = d2 * 2 + di
                        ch = g * 4 + d
                        P.add("dve", lambda e, d=d, di=di, ch=ch, pb=pb: e.scalar_tensor_tensor(
                            out=gT[gb][:, ch, :], in0=psum[pb][:, di * 256:(di + 1) * 256],
                            scalar=pscale[:, ch:ch + 1], in1=siluz[gs][:, d, :], op0=ALU.mult, op1=ALU.mult),
                            reads=[PS(pb), szk, "consts"], writes=[("gT", gb, ch)])

            def stage_O(it):
                s, b, sl, t0 = geo(it)
                xk = ("xs", sl)
                gb = it % 2
                for j in range(2):
                    rs = (it * 2 + j) % 2
                    rk = ("r", rs)
                    for hh in range(2):
                        pb = 2 + hh
                        for ch in range(16):
                            P.add("pe", lambda e, ch=ch, j=j, hh=hh, pb=pb: e.matmul(
                                psum[pb][:, :], lhsT=gT[gb][:, ch, j * 128:(j + 1) * 128],
                                rhs=w_out0[:, ch, hh * 512:(hh + 1) * 512], start=(ch == 0), stop=(ch == 15)),
                                reads=[("gT", gb, ch), "w_out0"], writes=[PS(pb)])
                        P.add("dve", lambda e, j=j, hh=hh, pb=pb, rs=rs: e.scalar_tensor_tensor(
                            out=rbuf[rs][:, hh * 512:(hh + 1) * 512], in0=xs[sl][:, j, hh * 512:(hh + 1) * 512],
                            scalar=DN_ALPHA, in1=psum[pb][:, :], op0=ALU.mult, op1=ALU.add),
                            reads=[PS(pb), xk], writes=[rk])
                    layer_norm_tile(P, rbuf[rs], rk, lng0, lnb0, stat[rs], ("stat", rs), epsc)
                    P.add("pool", lambda e, rs=rs, j=j: e.dma_start(
                        out=x1_d[s, t0 + j * 128:t0 + (j + 1) * 128, :], in_=rbuf[rs]),
                        reads=[rk], dma_key="st%d" % rs)

            for it in range(NIT):
                stage_T(it)
                stage_U(it, 0)
                if it > 0:
                    stage_G(it - 1, 3)
                stage_Z(it, 0)
                if it > 0:
                    stage_O(it - 1)
                stage_PM(it, 0)
                for g in range(1, 4):
                    stage_U(it, g)
                    stage_Z(it, g)
                    stage_G(it, g - 1)
                    stage_PM(it, g)
            stage_G(NIT - 1, 3)
            stage_O(NIT - 1)

        P.barrier()
        if do_l1:
            AR.reset()
            _build_l1(nc, P, AR, psum, psum2, L1D, x1_d, out_d, lng_d, lnb_d, identf_d, identb_d)
        P.emit(stack)
    return nc


def _host_common(inputs):
    f = lambda a: np.ascontiguousarray(np.asarray(a, dtype=np.float32))
    m = {}
    lng = f(inputs["ln_g"])
    lnb = f(inputs["ln_b"])
    m["lng"] = np.ascontiguousarray(np.broadcast_to(lng[:, None, :], (2, 128, D)))
    m["lnb"] = np.ascontiguousarray(np.broadcast_to(lnb[:, None, :], (2, 128, D)))
    w = f(inputs["pool_w_in"])[0]
    m["w_in0"] = np.ascontiguousarray(w.reshape(8, 128, 4096).transpose(1, 0, 2))
    wg = f(inputs["pool_w_grp"])[0]
    m["w_grp"] = np.ascontiguousarray(wg.reshape(4, 4, 128, 512).transpose(2, 0, 1, 3).reshape(128, 16, 512))
    wo = f(inputs["pool_w_out"])[0]
    m["w_out0"] = np.ascontiguousarray(wo.reshape(16, 128, 1024).transpose(1, 0, 2))
    m["pscale"] = np.ascontiguousarray(f(inputs["pool_scale"])[0].reshape(16, 128).T)
    pa, inv = _pool_tables()
    m["poolA"] = pa
    m["invc"] = inv
    m["identf"] = np.eye(128, dtype=np.float32)
    m["identb"] = _bf(np.eye(128))
    m.update(_host_l1(inputs))
    return m


_NC_CACHE = {}


def kernel(**inputs):
    x = np.ascontiguousarray(np.asarray(inputs["x"], dtype=np.float32))
    common = _host_common(inputs)
    if "nc" not in _NC_CACHE:
        _NC_CACHE["nc"] = build_program()
    nc = _NC_CACHE["nc"]
    in_maps = []
    for c in range(NCORES):
        m = dict(common)
        m["x"] = x[c * NSEQ:(c + 1) * NSEQ]
        in_maps.append(m)
    res = run_bass_kernel_spmd(nc, in_maps, core_ids=list(range(NCORES)))
    out = np.concatenate([np.asarray(r["out"]) for r in res.results], axis=0)
    return out.astype(np.float32)
```
